# Optimizing a Trainium2 kernel written in Bass

```python
import math
import jax, jax.numpy as jnp
from jax import lax
import numpy as np

D_MODEL = 1024
BATCH = 2
SEQ = 8192
DEPTH = 1

N_META = 16
ATTN_WIDTH = D_MODEL // 2
SSM_WIDTH = D_MODEL - ATTN_WIDTH
N_ATTN_HEADS = 4
QK_DIM = ATTN_WIDTH // N_ATTN_HEADS // 2
V_DIM = 2 * QK_DIM
ROT_DIM = QK_DIM // 4
ROPE_THETA = 500000.0
SSM_GROUP = 16
N_SSM_GROUPS = SSM_WIDTH // SSM_GROUP
SSM_STATE = 64
D_FF = -(-8 * D_MODEL // (3 * 256)) * 256
Q_BLOCK = 128
EPS = 1e-6
QK_COLS = N_ATTN_HEADS * QK_DIM
V_COLS = N_ATTN_HEADS * V_DIM
IN_COLS = 4 * QK_COLS + V_COLS + SSM_WIDTH

kernel_name = "hybrid_diffattn_s5_parallel_heads"


def rms_norm(x, g):
    x32 = x.astype(jnp.float32)
    y = x32 * lax.rsqrt(jnp.mean(x32 * x32, axis=-1, keepdims=True) + EPS)
    return (y * g.astype(jnp.float32)).astype(x.dtype)


def lambda_init(layer_idx):
    return 0.8 - 0.6 * math.exp(-0.3 * layer_idx)


def rope_tables(length):
    pos = jnp.arange(length, dtype=jnp.float32)
    inv_freq = ROPE_THETA ** (-jnp.arange(0, ROT_DIM, 2, dtype=jnp.float32) / ROT_DIM)
    ang = pos[:, None] * inv_freq[None, :]
    ang = jnp.concatenate([ang, ang], axis=-1)
    return jnp.cos(ang), jnp.sin(ang)


def partial_rope(t, cos, sin):
    t_rot, t_pass = t[..., :ROT_DIM], t[..., ROT_DIM:]
    half = ROT_DIM // 2
    rot_half = jnp.concatenate([-t_rot[..., half:], t_rot[..., :half]], axis=-1)
    t_rot = (t_rot * cos + rot_half * sin).astype(t.dtype)
    return jnp.concatenate([t_rot, t_pass], axis=-1)


def diff_attention(q1, q2, k1, k2, v, lam):
    b, h, lp, d = q1.shape
    nb = lp // Q_BLOCK
    scale = 1.0 / math.sqrt(d)
    k_pos = jnp.arange(lp)
    v32 = v.astype(jnp.float32)

    def to_blocks(t):
        return t.reshape(b, h, nb, Q_BLOCK, t.shape[-1]).transpose(2, 0, 1, 3, 4)

    def block(args):
        q1b, q2b, start = args
        q_pos = start + jnp.arange(Q_BLOCK)
        causal = k_pos[None, :] <= q_pos[:, None]

        def probs(qb, k):
            s = jnp.einsum('bhqd,bhkd->bhqk', qb, k, preferred_element_type=jnp.float32) * scale
            return jax.nn.softmax(jnp.where(causal, s, -jnp.inf), axis=-1)

        attn = probs(q1b, k1) - lam * probs(q2b, k2)
        return jnp.einsum('bhqk,bhkd->bhqd', attn, v32)

    starts = jnp.arange(nb, dtype=jnp.int32) * Q_BLOCK
    out = lax.map(block, (to_blocks(q1), to_blocks(q2), starts))
    return out.transpose(1, 2, 0, 3, 4).reshape(b, h, lp, v.shape[-1])


def s5_ssm(u, a_re, a_im, log_dt, b_re, b_im, c_re, c_im, d_skip):
    u = u.astype(jnp.float32)
    dt = jnp.exp(log_dt.astype(jnp.float32))[:, None]
    a_re = a_re.astype(jnp.float32)
    a_im = a_im.astype(jnp.float32)
    mag = jnp.exp(a_re * dt)
    ang = a_im * dt
    ab_re, ab_im = mag * jnp.cos(ang), mag * jnp.sin(ang)
    den = a_re * a_re + a_im * a_im
    nr, ni = ab_re - 1.0, ab_im
    f_re = (nr * a_re + ni * a_im) / den
    f_im = (ni * a_re - nr * a_im) / den
    b_re = b_re.astype(jnp.float32)
    b_im = b_im.astype(jnp.float32)
    bb_re = f_re[..., None] * b_re - f_im[..., None] * b_im
    bb_im = f_re[..., None] * b_im + f_im[..., None] * b_re
    bu_re = jnp.einsum('blgh,gph->blgp', u, bb_re)
    bu_im = jnp.einsum('blgh,gph->blgp', u, bb_im)
    ar = jnp.broadcast_to(ab_re, bu_re.shape)
    ai = jnp.broadcast_to(ab_im, bu_re.shape)

    def combine(e1, e2):
        ar1, ai1, br1, bi1 = e1
        ar2, ai2, br2, bi2 = e2
        return (ar2 * ar1 - ai2 * ai1,
                ar2 * ai1 + ai2 * ar1,
                ar2 * br1 - ai2 * bi1 + br2,
                ar2 * bi1 + ai2 * br1 + bi2)

    _, _, x_re, x_im = lax.associative_scan(combine, (ar, ai, bu_re, bu_im), axis=1)
    y = (jnp.einsum('blgp,ghp->blgh', x_re, c_re.astype(jnp.float32))
         - jnp.einsum('blgp,ghp->blgh', x_im, c_im.astype(jnp.float32)))
    y = y + d_skip.astype(jnp.float32).reshape(N_SSM_GROUPS, SSM_GROUP) * u
    bsz, length = u.shape[0], u.shape[1]
    return y.reshape(bsz, length, N_SSM_GROUPS * SSM_GROUP)


def setup_inputs(seed: int = 0) -> dict:
    key = jax.random.key(seed)
    ks = jax.random.split(key, 32)
    f32 = jnp.float32

    def nrm(k, shape, scale):
        return jax.random.normal(k, shape, f32) * scale

    def gain(k, shape):
        return 1.0 + 0.02 * jax.random.normal(k, shape, f32)

    n_idx = jnp.arange(SSM_STATE, dtype=f32)
    a_re = -0.5 + 0.01 * jax.random.normal(ks[8], (DEPTH, N_SSM_GROUPS, SSM_STATE), f32)
    a_im = math.pi * n_idx[None, None, :] + 0.01 * jax.random.normal(ks[9], (DEPTH, N_SSM_GROUPS, SSM_STATE), f32)
    log_dt = jax.random.uniform(ks[10], (DEPTH, N_SSM_GROUPS), f32, math.log(1e-3), math.log(1e-1))
    return {
        "x": nrm(ks[0], (BATCH, SEQ, D_MODEL), 1.0),
        "meta": nrm(ks[1], (N_META, D_MODEL), 1.0),
        "pre_mix_g": gain(ks[2], (DEPTH, D_MODEL)),
        "w_in": nrm(ks[3], (DEPTH, D_MODEL, IN_COLS), D_MODEL ** -0.5),
        "lambda_q1": nrm(ks[4], (DEPTH, QK_DIM), 0.1),
        "lambda_k1": nrm(ks[5], (DEPTH, QK_DIM), 0.1),
        "lambda_q2": nrm(ks[6], (DEPTH, QK_DIM), 0.1),
        "lambda_k2": nrm(ks[7], (DEPTH, QK_DIM), 0.1),
        "subln_g": gain(ks[11], (DEPTH, V_DIM)),
        "a_re": a_re,
        "a_im": a_im,
        "log_dt": log_dt,
        "b_re": nrm(ks[12], (DEPTH, N_SSM_GROUPS, SSM_STATE, SSM_GROUP), (2.0 * SSM_GROUP) ** -0.5),
        "b_im": nrm(ks[13], (DEPTH, N_SSM_GROUPS, SSM_STATE, SSM_GROUP), (2.0 * SSM_GROUP) ** -0.5),
        "c_re": nrm(ks[14], (DEPTH, N_SSM_GROUPS, SSM_GROUP, SSM_STATE), (2.0 * SSM_STATE) ** -0.5),
        "c_im": nrm(ks[15], (DEPTH, N_SSM_GROUPS, SSM_GROUP, SSM_STATE), (2.0 * SSM_STATE) ** -0.5),
        "d_skip": nrm(ks[16], (DEPTH, SSM_WIDTH), 1.0),
        "w_glu": nrm(ks[17], (DEPTH, SSM_WIDTH, SSM_WIDTH), SSM_WIDTH ** -0.5),
        "b_glu": nrm(ks[18], (DEPTH, SSM_WIDTH), 0.02),
        "ssm_out_g": gain(ks[19], (DEPTH, SSM_WIDTH)),
        "w_out": nrm(ks[20], (DEPTH, D_MODEL, D_MODEL), D_MODEL ** -0.5),
        "post_mix_g": gain(ks[21], (DEPTH, D_MODEL)),
        "pre_ffn_g": gain(ks[22], (DEPTH, D_MODEL)),
        "w_gate": nrm(ks[23], (DEPTH, D_MODEL, D_FF), D_MODEL ** -0.5),
        "w_up": nrm(ks[24], (DEPTH, D_MODEL, D_FF), D_MODEL ** -0.5),
        "w_down": nrm(ks[25], (DEPTH, D_FF, D_MODEL), D_FF ** -0.5),
        "post_ffn_g": gain(ks[26], (DEPTH, D_MODEL)),
    }


def reference(x, meta, pre_mix_g, w_in, lambda_q1, lambda_k1, lambda_q2, lambda_k2, subln_g,
              a_re, a_im, log_dt, b_re, b_im, c_re, c_im, d_skip, w_glu, b_glu, ssm_out_g,
              w_out, post_mix_g, pre_ffn_g, w_gate, w_up, w_down, post_ffn_g):
    bsz = x.shape[0]
    meta_b = jnp.broadcast_to(meta.astype(x.dtype)[None], (bsz, N_META, x.shape[-1]))
    h_res = jnp.concatenate([meta_b, x], axis=1)
    length = h_res.shape[1]
    l_pad = -(-length // Q_BLOCK) * Q_BLOCK
    cos, sin = rope_tables(length)

    def heads(t, dim):
        t = t.reshape(bsz, length, N_ATTN_HEADS, dim).transpose(0, 2, 1, 3)
        return t

    def pad(t):
        return jnp.pad(t, ((0, 0), (0, 0), (0, l_pad - length), (0, 0)))

    for l in range(DEPTH):
        lam_init = lambda_init(l)
        h = rms_norm(h_res, pre_mix_g[l])
        proj = jnp.einsum('bld,dc->blc', h, w_in[l])
        q1, q2, k1, k2, v, u = jnp.split(
            proj, [QK_COLS, 2 * QK_COLS, 3 * QK_COLS, 4 * QK_COLS, 4 * QK_COLS + V_COLS], axis=-1)

        q1 = pad(partial_rope(heads(q1, QK_DIM), cos, sin))
        q2 = pad(partial_rope(heads(q2, QK_DIM), cos, sin))
        k1 = pad(partial_rope(heads(k1, QK_DIM), cos, sin))
        k2 = pad(partial_rope(heads(k2, QK_DIM), cos, sin))
        vh = pad(heads(v, V_DIM))
        lam = (jnp.exp(jnp.sum(lambda_q1[l].astype(jnp.float32) * lambda_k1[l].astype(jnp.float32)))
               - jnp.exp(jnp.sum(lambda_q2[l].astype(jnp.float32) * lambda_k2[l].astype(jnp.float32)))
               + lam_init)
        o = diff_attention(q1, q2, k1, k2, vh, lam)[:, :, :length]
        o = rms_norm(o, subln_g[l]) * (1.0 - lam_init)
        o = o.transpose(0, 2, 1, 3).reshape(bsz, length, V_COLS).astype(h.dtype)

        y = s5_ssm(u.reshape(bsz, length, N_SSM_GROUPS, SSM_GROUP), a_re[l], a_im[l], log_dt[l],
                   b_re[l], b_im[l], c_re[l], c_im[l], d_skip[l])
        y = jax.nn.gelu(y, approximate=False)
        y = y * jax.nn.sigmoid(jnp.einsum('blc,cd->bld', y, w_glu[l].astype(jnp.float32))
                               + b_glu[l].astype(jnp.float32))
        y = rms_norm(y, ssm_out_g[l]).astype(h.dtype)

        mix = jnp.einsum('blc,cd->bld', jnp.concatenate([o, y], axis=-1), w_out[l])
        h_res = h_res + rms_norm(mix, post_mix_g[l])

        h = rms_norm(h_res, pre_ffn_g[l])
        f = jax.nn.silu(jnp.einsum('bld,df->blf', h, w_gate[l])) * jnp.einsum('bld,df->blf', h, w_up[l])
        f = jnp.einsum('blf,fd->bld', f, w_down[l])
        h_res = h_res + rms_norm(f, post_ffn_g[l])

    return h_res[:, N_META:]
```

```python
import math
import os
from contextlib import ExitStack

import numpy as np
import concourse.bass as bass
import concourse.mybir as mybir
from concourse.bass_utils import run_bass_kernel_spmd

F32 = mybir.dt.float32
BF16 = mybir.dt.bfloat16
I32 = mybir.dt.int32
AF = mybir.ActivationFunctionType
ALU = mybir.AluOpType

D = 1024
SEQ = 8192
NMETA = 16
L = SEQ + NMETA
LP = 8320
NBLK = 65
NT1 = 17
DFF = 2816
NF = 22
EPS = 1e-6
LAM_INIT = 0.8 - 0.6 * math.exp(0.0)
TWO_PI = 2.0 * math.pi
C1 = 6.28125
C2 = TWO_PI - C1
PI_LO = 3.141592
TOK2 = 2048
T2 = 256
NT2 = TOK2 // T2
FRONT_STEPS = 54


class Res:
    __slots__ = ("name", "w", "r", "excl")

    def __init__(self, name, excl=False):
        self.name = name
        self.w = None
        self.r = []
        self.excl = excl


class Sched:
    ENGS = ("pe", "act", "dve", "pool", "sp")

    def __init__(self, nc, stack):
        self.nc = nc
        self.ops = {e: [] for e in self.ENGS}
        self.sem = {e: stack.enter_context(nc.semaphore("s_" + e)) for e in self.ENGS}
        self.cnt = {e: 0 for e in self.ENGS}
        self.waited = {e: {} for e in self.ENGS}
        self.stack = stack
        self.dma_sems = {}
        self.dma_cnt = {}
        self.last = {e: None for e in self.ENGS}
        self.limit = int(os.environ.get("K_LIMIT", "1000000000"))
        self.nops = 0
        self.marks = []

    def _waits(self, eng, deps):
        need = {}
        for ev in deps:
            if ev is None:
                continue
            sem, val, src = ev
            key = sem.name
            if self.waited[eng].get(key, 0) >= val:
                continue
            if key not in need or need[key][1] < val:
                need[key] = (sem, val)
        for key, (sem, val) in need.items():
            self.waited[eng][key] = val
            self.ops[eng].append(("wait", sem, val))

    @staticmethod
    def _deps_for(reads, writes, extra):
        deps = list(extra or [])
        for r in reads or []:
            if r.w is not None:
                deps.append(r.w)
            if r.excl:
                deps.extend(r.r)
        for w in writes or []:
            if w.w is not None:
                deps.append(w.w)
            deps.extend(w.r)
        return deps

    @staticmethod
    def _commit(ev, reads, writes):
        for r in reads or []:
            r.r.append(ev)
        for w in writes or []:
            w.w = ev
            w.r = []

    def op(self, eng, fn, reads=None, writes=None, extra=None):
        return self.group(eng, [fn], reads, writes, extra)

    def group(self, eng, fns, reads=None, writes=None, extra=None):
        self.nops += 1
        if self.nops > self.limit:
            return None
        self._waits(eng, self._deps_for(reads, writes, extra))
        self.cnt[eng] += 1
        ev = (self.sem[eng], self.cnt[eng], eng)
        for f in fns[:-1]:
            self.ops[eng].append(("op", f, False))
        self.ops[eng].append(("op", fns[-1], True))
        self._commit(ev, reads, writes)
        self.last[eng] = ev
        return ev

    def dma(self, eng, out, in_, slot, reads=None, writes=None, extra=None):
        if slot not in self.dma_sems:
            self.dma_sems[slot] = self.stack.enter_context(self.nc.semaphore("d_" + slot))
            self.dma_cnt[slot] = 0
        self.nops += 1
        if self.nops > self.limit:
            return None
        self._waits(eng, self._deps_for(reads, writes, extra))
        self.dma_cnt[slot] += 16
        sem = self.dma_sems[slot]
        ev = (sem, self.dma_cnt[slot], "dma")
        self.ops[eng].append(("dma", out, in_, sem))
        self._commit(ev, reads, writes)
        return ev

    def raw(self, eng, fn):
        self.ops[eng].append(("raw", fn))

    def wait_all(self, eng, events):
        self._waits(eng, events)

    def run(self, block):
        sched = self

        def replay(engname, e):
            for item in sched.ops[engname]:
                if item[0] == "wait":
                    e.wait_ge(item[1], item[2])
                elif item[0] == "op":
                    ins = item[1](e)
                    if item[2]:
                        ins.then_inc(sched.sem[engname], 1)
                elif item[0] == "dma":
                    e.dma_start(out=item[1], in_=item[2]).then_inc(item[3], 16)
                elif item[0] == "raw":
                    item[1](e)

        @block.tensor
        def _(e):
            replay("pe", e)

        @block.scalar
        def _(e):
            replay("act", e)

        @block.vector
        def _(e):
            replay("dve", e)

        @block.gpsimd
        def _(e):
            replay("pool", e)

        @block.sync
        def _(e):
            replay("sp", e)


def MM(out, lhsT, rhs, start, stop):
    return lambda e: e.matmul(out, lhsT, rhs, start=start, stop=stop)


def TR(out, in_, ident):
    return lambda e: e.transpose(out, in_, ident)


def ACT(out, in_, func, **kw):
    return lambda e: e.activation(out=out, in_=in_, func=func, **kw)


def TT(out, a, b, op):
    return lambda e: e.tensor_tensor(out=out, in0=a, in1=b, op=op)


def TS(out, a, s1, op0, s2=None, op1=None):
    if op1 is None:
        return lambda e: e.tensor_scalar(out=out, in0=a, scalar1=s1, scalar2=None, op0=op0)
    return lambda e: e.tensor_scalar(out=out, in0=a, scalar1=s1, scalar2=s2, op0=op0, op1=op1)


def STT(out, a, s, b, op0, op1):
    return lambda e: e.scalar_tensor_tensor(out=out, in0=a, scalar=s, in1=b, op0=op0, op1=op1)


def CP(out, in_):
    return lambda e: e.tensor_copy(out=out, in_=in_)


def MEMSET(ap, v):
    return lambda e: e.memset(ap, v)


def RECIP(out, in_):
    return lambda e: e.reciprocal(out=out, in_=in_)


def SCAN(out, d0, d1, init):
    return lambda e: e.tensor_tensor_scan(out=out, data0=d0, data1=d1, initial=init, op0=ALU.mult, op1=ALU.add)


def build_program(nt1=NT1, do_phase2=True, debug=False, stop=9, p2mode="full"):
    nc = bass.Bass("TRN2", target_bir_lowering=False)

    def din(name, shape, dt=F32):
        return nc.dram_tensor(name, list(shape), dt, kind="ExternalInput").ap()

    ident_d = din("ident", [128, 128])
    if nt1 > 0:
        x_d = din("x", [SEQ, D])
        meta_d = din("meta", [NMETA, D])
        win_d = din("win", [D, 512])
        gpre_d = din("gpre", [128, 8])
        invf_d = din("invf", [128, 1])
        iota512_d = din("iota512", [512])
        tile_iota_d = din("tile_iota", [NT1])
        rmatT_d = din("rmatT", [128, 128])
        swap_d = din("swapm", [128, 128])
        sgn_d = din("sgn", [128, 1])
        tri_d = din("tri", [128, 128])
        gmask_d = din("gmask", [128, 8])
        lam_d = din("lamv", [4, 64])
        subg_d = din("subg", [128, 1])
        areT_d = din("areT", [64, 8])
        aimT_d = din("aimT", [64, 8])
        logdt_d = din("logdt", [8])
        bre_d = din("bre", [8, 64, 16])
        bim_d = din("bim", [8, 64, 16])
        creT_d = din("creT", [64, 128])
        cimT_d = din("cimT", [64, 128])
        dskip_d = din("dskip", [128, 1])
    if do_phase2:
        wglu_d = din("wglu", [512, 512])
        bglu_d = din("bglu", [128, 4])
        gssm_d = din("gssm", [128, 4])
        wout_d = din("wout", [D, D])
        gpost_d = din("gpost", [D])
        gffn_d = din("gffn", [D])
        wgate_d = din("wgate", [D, DFF])
        wup_d = din("wup", [D, DFF])
        wdown_d = din("wdown", [DFF, D])
        gpffn_d = din("gpffn", [D])
        xres_d = din("xres", [TOK2, D])
    out_d = nc.dram_tensor("out", [TOK2, D], F32, kind="ExternalOutput").ap() if do_phase2 else None

    xout_dbg_d = din("xout_dbg", [1024, TOK2]) if p2mode == "nogather" else None
    xin_t = [nc.dram_tensor("xchg_in%d" % q, [256, TOK2], BF16) for q in range(4)]
    xout_t = nc.dram_tensor("xchg_out", [4 * 1024, TOK2], BF16)
    xin = [t.ap() for t in xin_t]
    xout = xout_t.ap()
    dbg_d = None
    if debug:
        dbg_d = nc.dram_tensor("dbg", [256, SEQ], F32, kind="ExternalOutput").ap()

    with ExitStack() as top:
        S = Sched(nc, top)

        def ps_bank(name):
            return top.enter_context(nc.psum_tensor(name, [128, 512], F32))

        use_cc = do_phase2 and p2mode == "full"
        cc_sem = top.enter_context(nc.semaphore("cc_sem")) if use_cc else None
        q_events = [[] for _ in range(4)]

        def xchg_write(row0, lo, hi, src, src_col0, slot, res):
            evs = []
            c = lo
            while c < hi:
                q = c // TOK2
                ce = min(hi, TOK2 * (q + 1))
                ev = S.dma("sp", xin[q][row0:row0 + 128, c - TOK2 * q:ce - TOK2 * q],
                           src[:, src_col0 + (c - lo):src_col0 + (ce - lo)], slot, reads=[res])
                q_events[q].append(ev)
                evs.append(ev)
                c = ce
            return evs

        banks = [ps_bank("bank%d" % i) for i in range(8)]
        rbank = [Res("bank%d" % i, excl=True) for i in range(8)]

        def _phase1():
            with ExitStack() as p1:
                def sb(name, shape, dt=F32):
                    return p1.enter_context(nc.sbuf_tensor("a_" + name, list(shape), dt))

                ident_f = sb("ident_f", [128, 128]); r_ident_f = Res("ident_f")
                ident_b = sb("ident_b", [128, 128], BF16); r_ident_b = Res("ident_b")
                rmatT_b = sb("rmatT_b", [128, 128], BF16); r_rmat = Res("rmat")
                swap_f = sb("swap_f", [128, 128]); r_swap = Res("swap")
                tri_b = sb("tri_b", [128, 128], BF16); r_tri = Res("tri")
                ones_b = sb("ones_b", [128, 128], BF16); r_ones = Res("ones")
                sgn = sb("sgn", [128, 1]); r_sgn = Res("sgn")
                gmask = sb("gmask", [128, 8]); r_gmask = Res("gmask")
                invf = sb("invf", [128, 1]); r_invf = Res("invf")
                iota512 = sb("iota512", [128, 512]); r_iota = Res("iota512")
                tile_iota = sb("tile_iota", [128, NT1]); r_tiota = Res("tile_iota")
                gpre = sb("gpre", [128, 8]); r_gpre = Res("gpre")
                subg = sb("subg", [128, 1]); r_subg = Res("subg")
                dskip = sb("dskip", [128, 1]); r_dskip = Res("dskip")
                epsc = sb("epsc", [128, 1]); r_eps = Res("eps")
                lamv = sb("lamv", [128, 4, 64]); r_lamv = Res("lamv")
                neglam = sb("neglam", [128, 1]); r_neglam = Res("neglam")

                S.dma("sp", ident_f[:], ident_d, "c0", writes=[r_ident_f])
                S.dma("pool", ident_b[:], ident_d, "c1", writes=[r_ident_b])
                S.dma("pool", rmatT_b[:], rmatT_d, "c2", writes=[r_rmat])
                S.dma("sp", swap_f[:], swap_d, "c3", writes=[r_swap])
                S.dma("pool", tri_b[:], tri_d, "c4", writes=[r_tri])
                S.dma("sp", sgn[:], sgn_d, "c5", writes=[r_sgn])
                S.dma("sp", gmask[:], gmask_d, "c6", writes=[r_gmask])
                S.dma("sp", invf[:], invf_d, "c7", writes=[r_invf])
                S.dma("sp", iota512[:], iota512_d.partition_broadcast(128), "c8", writes=[r_iota])
                S.dma("sp", tile_iota[:], tile_iota_d.partition_broadcast(128), "c9", writes=[r_tiota])
                S.dma("sp", gpre[:], gpre_d, "c10", writes=[r_gpre])
                S.dma("sp", subg[:], subg_d, "c11", writes=[r_subg])
                S.dma("sp", dskip[:], dskip_d, "c12", writes=[r_dskip])
                S.dma("sp", lamv[:].rearrange("p a b -> p (a b)"),
                      lam_d.rearrange("a b -> (a b)").partition_broadcast(128), "c13", writes=[r_lamv])
                S.op("dve", MEMSET(ones_b[:], 1.0), writes=[r_ones])
                S.op("dve", MEMSET(epsc[:], EPS), writes=[r_eps])

                scr = [sb("scr%d" % i, [128, 512]) for i in range(4)]
                r_scr = [Res("scr%d" % i) for i in range(4)]
                ki_t = sb("ki_t", [128, 512], I32); r_ki = Res("ki")

                def range_reduce(eng, out_ap, arg_ap, n, r_out, r_arg):
                    kf = scr[3][:, 0:n]
                    S.op(eng, TS(ki_t[:, 0:n], arg_ap, 1.0 / TWO_PI, ALU.mult), reads=[r_arg], writes=[r_ki])
                    S.op(eng, CP(kf, ki_t[:, 0:n]), reads=[r_ki], writes=[r_scr[3]])
                    S.op(eng, STT(out_ap, kf, -C1, arg_ap, ALU.mult, ALU.add), reads=[r_scr[3], r_arg], writes=[r_out])
                    S.op(eng, STT(out_ap, kf, -C2, out_ap, ALU.mult, ALU.add), reads=[r_scr[3], r_out], writes=[r_out])
                    S.op(eng, TS(out_ap, out_ap, PI_LO, ALU.min, -PI_LO, ALU.max), reads=[r_out], writes=[r_out])

                def sincos(arg_ap, n, r_arg, sin_out, cos_out, r_sin, r_cos):
                    red = scr[2][:, 0:n]
                    range_reduce("dve", red, arg_ap, n, r_scr[2], r_arg)
                    S.op("act", ACT(sin_out, red, AF.Sin), reads=[r_scr[2]], writes=[r_sin])
                    hs = scr[1][:, 0:n]
                    S.op("act", ACT(hs, red, AF.Sin, scale=0.5), reads=[r_scr[2]], writes=[r_scr[1]])
                    S.op("dve", TT(hs, hs, hs, ALU.mult), reads=[r_scr[1]], writes=[r_scr[1]])
                    S.op("dve", TS(cos_out, hs, -2.0, ALU.mult, 1.0, ALU.add), reads=[r_scr[1]], writes=[r_cos])

                lt = sb("lt", [128, 2, 64]); r_lt = Res("lt")
                ls = sb("ls", [128, 2]); r_ls = Res("ls")
                S.op("dve", TT(lt[:, 0, :], lamv[:, 0, :], lamv[:, 1, :], ALU.mult), reads=[r_lamv], writes=[r_lt])
                S.op("dve", TT(lt[:, 1, :], lamv[:, 2, :], lamv[:, 3, :], ALU.mult), reads=[r_lamv, r_lt], writes=[r_lt])
                S.op("dve", lambda e: e.reduce_sum(out=ls[:], in_=lt[:], axis=mybir.AxisListType.X), reads=[r_lt], writes=[r_ls])
                S.op("act", ACT(ls[:], ls[:], AF.Exp), reads=[r_ls], writes=[r_ls])
                S.op("dve", TT(neglam[:], ls[:, 1:2], ls[:, 0:1], ALU.subtract), reads=[r_ls], writes=[r_neglam])
                S.op("dve", TS(neglam[:], neglam[:], -LAM_INIT, ALU.add), reads=[r_neglam], writes=[r_neglam])
                S.op("dve", TS(subg[:], subg[:], 1.0 - LAM_INIT, ALU.mult), reads=[r_subg], writes=[r_subg])

                Wb = sb("Wb", [128, 8, 512], BF16); r_Wb = Res("Wb")
                wstg = [sb("wstg%d" % i, [128, 512]) for i in range(2)]
                r_wstg = [Res("wstg%d" % i) for i in range(2)]
                for k in range(8):
                    S.dma("sp", wstg[k % 2][:], win_d[128 * k:128 * (k + 1), :], "wstg%d" % (k % 2), writes=[r_wstg[k % 2]])
                    S.op("pool", TS(Wb[:, k, :], wstg[k % 2][:], gpre[:, k:k + 1], ALU.mult),
                         reads=[r_wstg[k % 2], r_gpre], writes=[r_Wb])

                C0 = sb("C0", [128, 512]); S0 = sb("S0", [128, 512]); r_C0 = Res("C0"); r_S0 = Res("S0")
                cI = sb("cI", [128, NT1]); sI = sb("sI", [128, NT1]); r_cI = Res("cI"); r_sI = Res("sI")
                S.op("dve", TS(scr[0][:], iota512[:], invf[:, 0:1], ALU.mult), reads=[r_iota, r_invf], writes=[r_scr[0]])
                sincos(scr[0][:], 512, r_scr[0], S0[:], C0[:], r_S0, r_C0)
                S.op("dve", TS(scr[0][:, 0:NT1], tile_iota[:], invf[:, 0:1], ALU.mult), reads=[r_tiota, r_invf], writes=[r_scr[0]])
                sincos(scr[0][:, 0:NT1], NT1, r_scr[0], sI[:], cI[:], r_sI, r_cI)
                nsI = sb("nsI", [128, NT1]); r_nsI = Res("nsI")
                S.op("dve", TS(nsI[:], sI[:], -1.0, ALU.mult), reads=[r_sI], writes=[r_nsI])

                are2 = sb("are2", [128, 8]); aim2 = sb("aim2", [128, 8]); dt2 = sb("dt2", [128, 8])
                r_are = Res("are2"); r_aim = Res("aim2"); r_dt = Res("dt2")
                S.dma("sp", are2[0:64, :], areT_d, "s0", writes=[r_are])
                S.dma("sp", are2[64:128, :], areT_d, "s0", writes=[r_are])
                S.dma("sp", aim2[0:64, :], aimT_d, "s1", writes=[r_aim])
                S.dma("sp", aim2[64:128, :], aimT_d, "s1", writes=[r_aim])
                S.dma("sp", dt2[:], logdt_d.partition_broadcast(128), "s2", writes=[r_dt])
                S.op("act", ACT(dt2[:], dt2[:], AF.Exp), reads=[r_dt], writes=[r_dt])
                rdec = sb("rdec", [128, 8]); r_rdec = Res("rdec")
                theta = sb("theta", [128, 8]); r_theta = Res("theta")
                S.op("dve", TT(rdec[:], are2[:], dt2[:], ALU.mult), reads=[r_are, r_dt], writes=[r_rdec])
                S.op("act", ACT(rdec[:], rdec[:], AF.Exp), reads=[r_rdec], writes=[r_rdec])
                S.op("dve", TT(theta[:], aim2[:], dt2[:], ALU.mult), reads=[r_aim, r_dt], writes=[r_theta])
                c1t = sb("c1t", [128, 8]); s1t = sb("s1t", [128, 8]); r_c1t = Res("c1t"); r_s1t = Res("s1t")
                cTt = sb("cTt", [128, 8]); sTt = sb("sTt", [128, 8]); r_cTt = Res("cTt"); r_sTt = Res("sTt")
                sincos(theta[:], 8, r_theta, s1t[:], c1t[:], r_s1t, r_c1t)
                S.op("dve", TS(scr[0][:, 0:8], theta[:], 512.0, ALU.mult), reads=[r_theta], writes=[r_scr[0]])
                sincos(scr[0][:, 0:8], 8, r_scr[0], sTt[:], cTt[:], r_sTt, r_cTt)
                S.op("dve", TT(sTt[:], sTt[:], sgn[:, 0:1].to_broadcast([128, 8]), ALU.mult), reads=[r_sTt, r_sgn], writes=[r_sTt])
                COSg = sb("COSg", [128, 8, 512]); SINg = sb("SINg", [128, 8, 512])
                r_COSg = [Res("COSg%d" % g) for g in range(8)]; r_SINg = [Res("SINg%d" % g) for g in range(8)]
                for g in range(8):
                    S.op("dve", TS(scr[0][:], iota512[:], theta[:, g:g + 1], ALU.mult), reads=[r_iota, r_theta], writes=[r_scr[0]])
                    sincos(scr[0][:], 512, r_scr[0], SINg[:, g, :], COSg[:, g, :], r_SINg[g], r_COSg[g])
                ROT = sb("ROT", [128, 8, 128]); r_ROT = Res("ROT")
                for g in range(8):
                    S.op("dve", TS(ROT[:, g, :], ident_f[:], cTt[:, g:g + 1], ALU.mult), reads=[r_ident_f, r_cTt], writes=[r_ROT])
                    S.op("dve", STT(ROT[:, g, :], swap_f[:], sTt[:, g:g + 1], ROT[:, g, :], ALU.mult, ALU.add),
                         reads=[r_swap, r_sTt, r_ROT], writes=[r_ROT])
                fre = sb("fre", [128, 8]); fim = sb("fim", [128, 8]); r_fre = Res("fre"); r_fim = Res("fim")
                nr = sb("nr", [128, 8]); ni = sb("ni", [128, 8]); den = sb("den", [128, 8]); tmp8 = sb("tmp8", [128, 8])
                r_nr = Res("nr"); r_ni = Res("ni"); r_den = Res("den"); r_tmp8 = Res("tmp8")
                S.op("dve", TT(nr[:], rdec[:], c1t[:], ALU.mult), reads=[r_rdec, r_c1t], writes=[r_nr])
                S.op("dve", TS(nr[:], nr[:], -1.0, ALU.add), reads=[r_nr], writes=[r_nr])
                S.op("dve", TT(ni[:], rdec[:], s1t[:], ALU.mult), reads=[r_rdec, r_s1t], writes=[r_ni])
                S.op("dve", TT(den[:], are2[:], are2[:], ALU.mult), reads=[r_are], writes=[r_den])
                S.op("dve", TT(tmp8[:], aim2[:], aim2[:], ALU.mult), reads=[r_aim], writes=[r_tmp8])
                S.op("dve", TT(den[:], den[:], tmp8[:], ALU.add), reads=[r_den, r_tmp8], writes=[r_den])
                S.op("dve", RECIP(den[:], den[:]), reads=[r_den], writes=[r_den])
                S.op("dve", TT(fre[:], nr[:], are2[:], ALU.mult), reads=[r_nr, r_are], writes=[r_fre])
                S.op("dve", TT(tmp8[:], ni[:], aim2[:], ALU.mult), reads=[r_ni, r_aim], writes=[r_tmp8])
                S.op("dve", TT(fre[:], fre[:], tmp8[:], ALU.add), reads=[r_fre, r_tmp8], writes=[r_fre])
                S.op("dve", TT(fre[:], fre[:], den[:], ALU.mult), reads=[r_fre, r_den], writes=[r_fre])
                S.op("dve", TT(fim[:], ni[:], are2[:], ALU.mult), reads=[r_ni, r_are], writes=[r_fim])
                S.op("dve", TT(tmp8[:], nr[:], aim2[:], ALU.mult), reads=[r_nr, r_aim], writes=[r_tmp8])
                S.op("dve", TT(fim[:], fim[:], tmp8[:], ALU.subtract), reads=[r_fim, r_tmp8], writes=[r_fim])
                S.op("dve", TT(fim[:], fim[:], den[:], ALU.mult), reads=[r_fim, r_den], writes=[r_fim])
                Bre = sb("Bre", [64, 8, 16]); Bim = sb("Bim", [64, 8, 16]); r_Bre = Res("Bre"); r_Bim = Res("Bim")
                S.dma("sp", Bre[:], bre_d.rearrange("g p h -> p g h"), "s3", writes=[r_Bre])
                S.dma("sp", Bim[:], bim_d.rearrange("g p h -> p g h"), "s4", writes=[r_Bim])
                BBre = sb("BBre", [64, 8, 16]); BBim = sb("BBim", [64, 8, 16]); tB = sb("tB", [64, 8, 16])
                r_BBre = Res("BBre"); r_BBim = Res("BBim"); r_tB = Res("tB")
                fre_b = fre[0:64, :].unsqueeze(2).to_broadcast([64, 8, 16])
                fim_b = fim[0:64, :].unsqueeze(2).to_broadcast([64, 8, 16])
                S.op("dve", TT(BBre[:], Bre[:], fre_b, ALU.mult), reads=[r_Bre, r_fre], writes=[r_BBre])
                S.op("dve", TT(tB[:], Bim[:], fim_b, ALU.mult), reads=[r_Bim, r_fim], writes=[r_tB])
                S.op("dve", TT(BBre[:], BBre[:], tB[:], ALU.subtract), reads=[r_BBre, r_tB], writes=[r_BBre])
                S.op("dve", TT(BBim[:], Bim[:], fre_b, ALU.mult), reads=[r_Bim, r_fre], writes=[r_BBim])
                S.op("dve", TT(tB[:], Bre[:], fim_b, ALU.mult), reads=[r_Bre, r_fim], writes=[r_tB])
                S.op("dve", TT(BBim[:], BBim[:], tB[:], ALU.add), reads=[r_BBim, r_tB], writes=[r_BBim])
                FB1 = sb("FB1", [128, 128]); FB2 = sb("FB2", [128, 128]); r_FB1 = Res("FB1"); r_FB2 = Res("FB2")
                S.group("pe", [TR(banks[0][:, 0:64], BBre[:].rearrange("p g h -> p (g h)"), ident_f[0:64, 0:64]),
                               TR(banks[0][:, 64:128], BBim[:].rearrange("p g h -> p (g h)"), ident_f[0:64, 0:64])],
                        reads=[r_BBre, r_BBim, r_ident_f], writes=[rbank[0]])
                S.op("dve", CP(FB1[:], banks[0][:, 0:128]), reads=[rbank[0]], writes=[r_FB1])
                S.op("dve", CP(FB2[:, 0:64], banks[0][:, 64:128]), reads=[rbank[0]], writes=[r_FB2])
                S.op("dve", TS(FB2[:, 64:128], banks[0][:, 0:64], -1.0, ALU.mult), reads=[rbank[0], r_FB2], writes=[r_FB2])
                LB1 = sb("LB1", [128, 8, 128], BF16); LB2 = sb("LB2", [128, 8, 128], BF16); r_LB = Res("LB")
                for g in range(8):
                    S.op("pool", TS(LB1[:, g, :], FB1[:], gmask[:, g:g + 1], ALU.mult), reads=[r_FB1, r_gmask], writes=[r_LB])
                    S.op("pool", TS(LB2[:, g, :], FB2[:], gmask[:, g:g + 1], ALU.mult), reads=[r_FB2, r_gmask], writes=[r_LB])
                CC1 = sb("CC1", [128, 128]); CC2 = sb("CC2", [128, 128]); r_CC1 = Res("CC1"); r_CC2 = Res("CC2")
                S.dma("sp", CC1[0:64, :], creT_d, "s5", writes=[r_CC1])
                S.dma("sp", CC1[64:128, :], cimT_d, "s5", writes=[r_CC1])
                S.dma("sp", CC2[0:64, :], cimT_d, "s6", writes=[r_CC2])
                S.dma("sp", CC2[64:128, :], creT_d, "s6", writes=[r_CC2])
                S.op("dve", TS(CC1[64:128, :], CC1[64:128, :], -1.0, ALU.mult), reads=[r_CC1], writes=[r_CC1])
                S.op("dve", TS(CC2[:], CC2[:], -1.0, ALU.mult), reads=[r_CC2], writes=[r_CC2])
                LC1 = sb("LC1", [128, 8, 128], BF16); LC2 = sb("LC2", [128, 8, 128], BF16); r_LC = Res("LC")
                S.op("pool", MEMSET(LC1[:], 0.0), writes=[r_LC])
                S.op("pool", MEMSET(LC2[:], 0.0), writes=[r_LC])
                for g in range(8):
                    S.op("dve", CP(LC1[:, g, 16 * g:16 * g + 16], CC1[:, 16 * g:16 * g + 16]), reads=[r_CC1], writes=[r_LC])
                    S.op("dve", CP(LC2[:, g, 16 * g:16 * g + 16], CC2[:, 16 * g:16 * g + 16]), reads=[r_CC2], writes=[r_LC])
                diagD = sb("diagD", [128, 128], BF16); r_diagD = Res("diagD")
                S.op("dve", TS(diagD[:], ident_f[:], dskip[:, 0:1], ALU.mult), reads=[r_ident_f, r_dskip], writes=[r_diagD])
                s0 = sb("s0", [128, 2, 8]); r_s0 = [Res("s0_0"), Res("s0_1")]
                S.op("dve", MEMSET(s0[:], 0.0), writes=r_s0)

                xt = [sb("xt%d" % i, [128, D]) for i in range(2)]; r_xt = [Res("xt%d" % i) for i in range(2)]
                junk = sb("junk", [128, D], BF16); r_junk = Res("junk")
                ssq = sb("ssq", [128, 4]); r_ssq = [Res("ssq%d" % i) for i in range(4)]
                xs = [sb("xs%d" % i, [128, D], BF16) for i in range(2)]; r_xs = [Res("xs%d" % i) for i in range(2)]
                hT = [sb("hT%d" % i, [128, 8, 512], BF16) for i in range(2)]; r_hT = [Res("hT%d" % i) for i in range(2)]
                COSt = sb("COSt", [128, 512]); SINt = sb("SINt", [128, 512]); r_COSt = Res("COSt"); r_SINt = Res("SINt")
                qraw = sb("qraw", [128, 512], BF16); r_qraw = Res("qraw")
                rt1 = sb("rt1", [128, 512]); rt2 = sb("rt2", [128, 512]); r_rt1 = Res("rt1"); r_rt2 = Res("rt2")
                QT = [sb("QT%d" % i, [128, 512], BF16) for i in range(2)]; r_QT = [Res("QT%d" % i) for i in range(2)]
                KTa = sb("KTa", [128, LP], BF16); KTb = sb("KTb", [128, LP], BF16)
                r_KT = [Res("KT%d" % i) for i in range(NT1)]
                r_KTz = Res("KTz")
                S.op("pool", MEMSET(KTa[64:128, :], 0.0), writes=[r_KTz])
                S.op("pool", MEMSET(KTb[0:64, :], 0.0), writes=[r_KTz])
                V = sb("V", [128, NBLK, 128], BF16); r_V = [Res("V%d" % i) for i in range(NT1)]
                uT = [sb("uT%d" % i, [128, 512], BF16) for i in range(2)]; r_uT = [Res("uT%d" % i) for i in range(2)]
                st1 = sb("st1", [128, 512]); st2 = sb("st2", [128, 512]); r_st1 = Res("st1"); r_st2 = Res("st2")
                bt = sb("bt", [128, 512]); r_bt = Res("bt")
                Xs = [sb("Xs%d" % i, [128, 512]) for i in range(2)]; r_Xs = [Res("Xs%d" % i) for i in range(2)]
                P1s = [sb("P1s%d" % i, [128, 512], BF16) for i in range(2)]; r_P1s = [Res("P1s%d" % i) for i in range(2)]
                P2s = [sb("P2s%d" % i, [128, 512], BF16) for i in range(2)]; r_P2s = [Res("P2s%d" % i) for i in range(2)]
                ybuf = [sb("ybuf%d" % i, [128, 512], BF16) for i in range(2)]; r_ybuf = [Res("ybuf%d" % i) for i in range(2)]
                Pb = [sb("Pb%d" % i, [128, 2, 256], BF16) for i in range(3)]; r_Pb = [Res("Pb%d" % i) for i in range(3)]
                rL = sb("rL", [128, 2, 256]); r_rL = Res("rL")
                Lacc = [sb("Lacc%d" % i, [128, 2, 256]) for i in range(2)]; r_Lacc = [Res("Lacc%d" % i) for i in range(2)]
                ones_f = sb("ones_f", [128, 128]); r_ones_f = Res("ones_f")
                S.op("pool", MEMSET(ones_f[:], 1.0), writes=[r_ones_f])
                On = sb("On", [128, 2, 256]); r_On = Res("On")
                od = sb("od", [128, 256]); r_od = Res("od")
                osq = sb("osq", [128, 256], BF16); r_osq = Res("osq")
                orstd = sb("orstd", [128, 256]); r_orstd = Res("orstd")
                obuf = [sb("obuf%d" % i, [128, 256], BF16) for i in range(2)]; r_obuf = [Res("obuf%d" % i) for i in range(2)]

                F0, F1, F2, F3 = banks[0], banks[1], banks[2], banks[3]
                rF0, rF1, rF2, rF3 = rbank[0], rbank[1], rbank[2], rbank[3]
                Sb = [banks[4], banks[5]]; rSb = [rbank[4], rbank[5]]
                Ob, Lb = banks[6], banks[7]; rOb, rLb = rbank[6], rbank[7]
                F0b = F0[:].bitcast(BF16)


                pcount = [0]
                sbcount = [0]
                out_events = []

                def front_gen(ti):
                    nb = 4 if ti < 16 else 1
                    N = 128 * nb
                    tok0 = 512 * ti
                    hb = ti % 2
                    for j in range(nb):
                        s = 4 * ti + j
                        xb = sbcount[0] % 2
                        sq = sbcount[0] % 4
                        sbcount[0] += 1
                        if s == 0:
                            S.dma("sp", xt[xb][0:16, :], meta_d, "x%d" % xb, writes=[r_xt[xb]])
                            S.dma("sp", xt[xb][16:128, :], x_d[0:112, :], "x%d" % xb, writes=[r_xt[xb]])
                        elif s == 64:
                            S.op("pool", MEMSET(xt[xb][:], 0.0), writes=[r_xt[xb]])
                            S.dma("sp", xt[xb][0:16, :], x_d[SEQ - 16:SEQ, :], "x%d" % xb, writes=[r_xt[xb]])
                        else:
                            S.dma("sp", xt[xb][:], x_d[128 * s - 16:128 * s + 112, :], "x%d" % xb, writes=[r_xt[xb]])
                        S.op("act", ACT(junk[:], xt[xb][:], AF.Square, accum_out=ssq[:, sq:sq + 1]),
                             reads=[r_xt[xb]], writes=[r_junk, r_ssq[sq]])
                        S.op("act", ACT(ssq[:, sq:sq + 1], ssq[:, sq:sq + 1], AF.Ln, scale=1.0 / D, bias=epsc[:, 0:1]),
                             reads=[r_ssq[sq], r_eps], writes=[r_ssq[sq]])
                        S.op("act", ACT(ssq[:, sq:sq + 1], ssq[:, sq:sq + 1], AF.Exp, scale=-0.5),
                             reads=[r_ssq[sq]], writes=[r_ssq[sq]])
                        S.op("act", ACT(xs[xb][:], xt[xb][:], AF.Copy, scale=ssq[:, sq:sq + 1]),
                             reads=[r_xt[xb], r_ssq[sq]], writes=[r_xs[xb]])
                        yield
                        S.group("pe", [TR(F0b[:, 128 * k:128 * (k + 1)], xs[xb][:, 128 * k:128 * (k + 1)], ident_b[:]) for k in range(8)],
                                reads=[r_xs[xb], r_ident_b], writes=[rF0])
                        S.op("dve", CP(hT[hb][:, :, 128 * j:128 * (j + 1)], F0b.rearrange("p (k t) -> p k t", k=8)),
                             reads=[rF0], writes=[r_hT[hb]])
                        yield
                    S.op("dve", TS(COSt[:], C0[:], cI[:, ti:ti + 1], ALU.mult), reads=[r_C0, r_cI], writes=[r_COSt])
                    S.op("dve", STT(COSt[:], S0[:], nsI[:, ti:ti + 1], COSt[:], ALU.mult, ALU.add), reads=[r_S0, r_nsI, r_COSt], writes=[r_COSt])
                    S.op("dve", TS(SINt[:], S0[:], cI[:, ti:ti + 1], ALU.mult), reads=[r_S0, r_cI], writes=[r_SINt])
                    S.op("dve", STT(SINt[:], C0[:], sI[:, ti:ti + 1], SINt[:], ALU.mult, ALU.add), reads=[r_C0, r_sI, r_SINt], writes=[r_SINt])
                    yield

                    def proj_fm(c, bank, rb):
                        S.group("pe", [MM(bank[:, 0:N], Wb[:, k, 128 * c:128 * (c + 1)], hT[hb][:, k, 0:N], k == 0, k == 7) for k in range(8)],
                                reads=[r_Wb, r_hT[hb]], writes=[rb])

                    def rope_a():
                        S.op("act", ACT(qraw[:, 0:N], F1[:, 0:N], AF.Copy), reads=[rF1], writes=[r_qraw])
                        S.op("pe", MM(F2[:, 0:N], rmatT_b[:], qraw[:, 0:N], True, True), reads=[r_rmat, r_qraw], writes=[rF2])

                    def rope_b(dst_ap, r_dst, dst2=None):
                        S.op("dve", TT(rt1[:, 0:N], F1[:, 0:N], COSt[:, 0:N], ALU.mult), reads=[rF1, r_COSt], writes=[r_rt1])
                        S.op("dve", TT(rt2[:, 0:N], F2[:, 0:N], SINt[:, 0:N], ALU.mult), reads=[rF2, r_SINt], writes=[r_rt2])
                        if dst2 is None:
                            S.op("pool", TT(dst_ap, rt1[:, 0:N], rt2[:, 0:N], ALU.add), reads=[r_rt1, r_rt2], writes=[r_dst])
                        else:
                            S.op("pool", TT(dst_ap[0:64, :], rt1[0:64, 0:N], rt2[0:64, 0:N], ALU.add), reads=[r_rt1, r_rt2, r_KTz], writes=[r_dst])
                            S.op("pool", TT(dst2[64:128, :], rt1[64:128, 0:N], rt2[64:128, 0:N], ALU.add), reads=[r_rt1, r_rt2, r_KTz], writes=[r_dst])

                    proj_fm(0, F1, rF1)
                    yield
                    rope_a()
                    yield
                    rope_b(QT[hb][:, 0:N], r_QT[hb])
                    yield
                    proj_fm(1, F1, rF1)
                    yield
                    rope_a()
                    yield
                    rope_b(KTa[:, tok0:tok0 + N], r_KT[ti], KTb[:, tok0:tok0 + N])
                    yield
                    proj_fm(3, F1, rF1)
                    yield
                    S.op("act", ACT(uT[hb][:, 0:N], F1[:, 0:N], AF.Copy), reads=[rF1], writes=[r_uT[hb]])
                    fns = []
                    for j in range(nb):
                        for k in range(8):
                            fns.append(MM(F2[:, 128 * j:128 * (j + 1)], hT[hb][:, k, 128 * j:128 * (j + 1)], Wb[:, k, 256:384], k == 0, k == 7))
                    S.group("pe", fns, reads=[r_Wb, r_hT[hb]], writes=[rF2])
                    yield
                    S.op("act", ACT(V[:, 4 * ti:4 * ti + nb, :], F2[:, 0:N].rearrange("p (j d) -> p j d", j=nb), AF.Copy),
                         reads=[rF2], writes=[r_V[ti]])
                    yield
                    par = ti % 2
                    yb = ti % 2

                    def emit_y(g, pb):
                        S.group("pe", [MM(F3[:, 0:N], LC1[:, g, :], P1s[pb][:, 0:N], g == 0, False),
                                       MM(F3[:, 0:N], LC2[:, g, :], P2s[pb][:, 0:N], False, False)],
                                reads=[r_LC, r_P1s[pb], r_P2s[pb]], writes=[rF3])

                    prev = None
                    for g in range(8):
                        pb = pcount[0] % 2
                        pcount[0] += 1
                        S.op("pe", MM(F1[:, 0:N], LB1[:, g, :], uT[hb][:, 0:N], True, True), reads=[r_LB, r_uT[hb]], writes=[rF1])
                        S.op("pe", MM(F2[:, 0:N], LB2[:, g, :], uT[hb][:, 0:N], True, True), reads=[r_LB, r_uT[hb]], writes=[rF2])
                        yield
                        S.op("dve", TT(st1[:, 0:N], F1[:, 0:N], COSg[:, g, 0:N], ALU.mult), reads=[rF1, r_COSg[g]], writes=[r_st1])
                        S.op("dve", TT(st2[:, 0:N], F2[:, 0:N], SINg[:, g, 0:N], ALU.mult), reads=[rF2, r_SINg[g]], writes=[r_st2])
                        S.op("pool", TT(bt[:, 0:N], st1[:, 0:N], st2[:, 0:N], ALU.add), reads=[r_st1, r_st2], writes=[r_bt])
                        if prev is not None:
                            emit_y(*prev)
                        yield
                        S.op("dve", SCAN(Xs[pb][:, 0:N], rdec[:, g:g + 1].to_broadcast([128, N]), bt[:, 0:N], s0[:, par, g:g + 1]),
                             reads=[r_rdec, r_bt, r_s0[par]], writes=[r_Xs[pb]])
                        yield
                        if ti + 1 < nt1:
                            S.op("pe", MM(F0[:, g:g + 1], ROT[:, g, :], Xs[pb][:, N - 1:N], True, True), reads=[r_ROT, r_Xs[pb]], writes=[rF0])
                            S.op("act", ACT(s0[:, 1 - par, g:g + 1], F0[:, g:g + 1], AF.Copy), reads=[rF0], writes=[r_s0[1 - par]])
                        S.op("pool", TT(P1s[pb][:, 0:N], Xs[pb][:, 0:N], COSg[:, g, 0:N], ALU.mult), reads=[r_Xs[pb], r_COSg[g]], writes=[r_P1s[pb]])
                        S.op("dve", TT(P2s[pb][:, 0:N], Xs[pb][:, 0:N], SINg[:, g, 0:N], ALU.mult), reads=[r_Xs[pb], r_SINg[g]], writes=[r_P2s[pb]])
                        prev = (g, pb)
                        yield
                    emit_y(*prev)
                    S.op("pe", MM(F3[:, 0:N], diagD[:], uT[hb][:, 0:N], False, True), reads=[r_diagD, r_uT[hb]], writes=[rF3])
                    yield
                    S.op("act", ACT(ybuf[yb][:, 0:N], F3[:, 0:N], AF.Gelu), reads=[rF3], writes=[r_ybuf[yb]])
                    lo = max(tok0, NMETA); hi = min(tok0 + N, L)
                    out_events.extend(xchg_write(128, lo - NMETA, hi - NMETA, ybuf[yb], lo - tok0, "yo%d" % yb, r_ybuf[yb]))
                    yield

                def attention(ti, pump):
                    nb = 4 if ti < 16 else 1
                    tok0 = 512 * ti
                    hb = ti % 2
                    nqt = 2 if nb == 4 else 1
                    nbq = 2 if nb == 4 else 1
                    NQ = 128 * nbq
                    QTv = QT[hb]
                    for qi in range(nqt):
                        c = 2 * ti + qi
                        q0 = 256 * qi
                        nkb = 2 * c + nbq

                        def qk(kb):
                            j = kb - 2 * c
                            qs = 128 * j if j > 0 else 0
                            kt = kb // 4
                            sbk = kb % 2
                            pbk = kb % 3
                            Sv = Sb[sbk][:, 0:2 * NQ].rearrange("p (h q) -> p h q", h=2)
                            Pv = Pb[pbk][:].rearrange("p h q -> p (h q)")[:, 0:2 * NQ].rearrange("p (h q) -> p h q", h=2)
                            S.group("pe", [MM(Sv[:, 0, qs:NQ], KTa[:, 128 * kb:128 * (kb + 1)], QTv[:, q0 + qs:q0 + NQ], True, True),
                                           MM(Sv[:, 1, qs:NQ], KTb[:, 128 * kb:128 * (kb + 1)], QTv[:, q0 + qs:q0 + NQ], True, True)],
                                    reads=[r_KT[kt], r_QT[hb]], writes=[rSb[sbk]])
                            S.op("act", ACT(Pv[:, :, qs:NQ], Sv[:, :, qs:NQ], AF.Exp, scale=0.125), reads=[rSb[sbk]], writes=[r_Pb[pbk]])
                            if j >= 0:
                                S.op("pool", TT(Pv[:, :, qs:qs + 128], Pv[:, :, qs:qs + 128],
                                                tri_b[:].unsqueeze(1).to_broadcast([128, 2, 128]), ALU.mult),
                                     reads=[r_Pb[pbk], r_tri], writes=[r_Pb[pbk]])

                        def pv(kb):
                            j = kb - 2 * c
                            qs = 128 * j if j > 0 else 0
                            kt = kb // 4
                            pbk = kb % 3
                            Pv = Pb[pbk][:].rearrange("p h q -> p (h q)")[:, 0:2 * NQ].rearrange("p (h q) -> p h q", h=2)
                            Ov = Ob[:, 0:2 * NQ].rearrange("p (h q) -> p h q", h=2)
                            Lv = Lb[:, 0:2 * NQ].rearrange("p (h q) -> p h q", h=2)
                            first = (kb == 0); lastk = (kb == nkb - 1)
                            if qs == 0:
                                Pf = Pb[pbk][:].rearrange("p h q -> p (h q)")[:, 0:2 * NQ]
                                S.op("pe", MM(Ob[:, 0:2 * NQ], V[:, kb, :], Pf, first, lastk), reads=[r_V[kt], r_Pb[pbk]], writes=[rOb])
                            else:
                                assert not first
                                S.group("pe", [MM(Ov[:, 0, qs:NQ], V[:, kb, :], Pv[:, 0, qs:NQ], False, False),
                                               MM(Ov[:, 1, qs:NQ], V[:, kb, :], Pv[:, 1, qs:NQ], False, lastk)],
                                        reads=[r_V[kt], r_Pb[pbk]], writes=[rOb])
                            la = Lacc[c % 2][:].rearrange("p h q -> p (h q)")[:, 0:2 * NQ].rearrange("p (h q) -> p h q", h=2); r_la = r_Lacc[c % 2]
                            if first:
                                S.op("dve", CP(la[:, :, 0:NQ], Pv[:, :, 0:NQ]), reads=[r_Pb[pbk]], writes=[r_la])
                            else:
                                S.op("dve", TT(la[:, :, qs:NQ], la[:, :, qs:NQ], Pv[:, :, qs:NQ], ALU.add), reads=[r_Pb[pbk], r_la], writes=[r_la])

                        qk(0)
                        for kb in range(nkb):
                            if kb + 1 < nkb:
                                qk(kb + 1)
                            pv(kb)
                            pump()
                        Ov = Ob[:, 0:2 * NQ].rearrange("p (h q) -> p h q", h=2)
                        Lv = Lb[:, 0:2 * NQ].rearrange("p (h q) -> p h q", h=2)
                        S.op("pe", MM(Lb[:, 0:2 * NQ], ones_f[:], Lacc[c % 2][:].rearrange("p h q -> p (h q)")[:, 0:2 * NQ], True, True),
                             reads=[r_ones_f, r_Lacc[c % 2]], writes=[rLb])
                        S.op("dve", RECIP(rL[:, :, 0:NQ], Lv[:, :, 0:NQ]), reads=[rLb], writes=[r_rL])
                        S.op("dve", TT(On[:, :, 0:NQ], Ov[:, :, 0:NQ], rL[:, :, 0:NQ], ALU.mult), reads=[rOb, r_rL], writes=[r_On])
                        S.op("dve", STT(od[:, 0:NQ], On[:, 1, 0:NQ], neglam[:, 0:1], On[:, 0, 0:NQ], ALU.mult, ALU.add),
                             reads=[r_On, r_neglam], writes=[r_od])
                        S.op("act", ACT(osq[:, 0:NQ], od[:, 0:NQ], AF.Square), reads=[r_od], writes=[r_osq])
                        sbk = nkb % 2
                        S.op("pe", MM(Sb[sbk][:, 0:NQ], ones_b[:], osq[:, 0:NQ], True, True), reads=[r_ones, r_osq], writes=[rSb[sbk]])
                        S.op("act", ACT(orstd[:, 0:NQ], Sb[sbk][:, 0:NQ], AF.Ln, scale=1.0 / 128, bias=epsc[:, 0:1]),
                             reads=[rSb[sbk], r_eps], writes=[r_orstd])
                        S.op("act", ACT(orstd[:, 0:NQ], orstd[:, 0:NQ], AF.Exp, scale=-0.5), reads=[r_orstd], writes=[r_orstd])
                        ob = c % 2
                        S.op("dve", STT(obuf[ob][:, 0:NQ], od[:, 0:NQ], subg[:, 0:1], orstd[:, 0:NQ], ALU.mult, ALU.mult),
                             reads=[r_od, r_subg, r_orstd], writes=[r_obuf[ob]])
                        qt0 = tok0 + q0
                        lo = max(qt0, NMETA); hi = min(qt0 + NQ, L)
                        out_events.extend(xchg_write(0, lo - NMETA, hi - NMETA, obuf[ob], lo - qt0, "oo%d" % ob, r_obuf[ob]))

                for _ in front_gen(0):
                    pass
                for ti in range(nt1):
                    nxt = front_gen(ti + 1) if ti + 1 < nt1 else None
                    nbq_t = 2 if ti < 16 else 1
                    total_kb = sum(2 * (2 * ti + qi) + nbq_t for qi in range(2 if ti < 16 else 1))
                    state = {"steps_left": FRONT_STEPS if nxt is not None else 0, "kb_left": total_kb, "gen": nxt}

                    def pump(state=state):
                        if state["gen"] is None:
                            return
                        kbl = max(state["kb_left"], 1)
                        n = -(-state["steps_left"] // kbl)
                        state["kb_left"] -= 1
                        for _ in range(n):
                            try:
                                next(state["gen"])
                                state["steps_left"] -= 1
                            except StopIteration:
                                state["gen"] = None
                                state["steps_left"] = 0
                                return
                    attention(ti, pump)
                    if state["gen"] is not None:
                        for _ in state["gen"]:
                            pass
                    if use_cc and ti >= 4 and ti % 4 == 0:
                        qd = ti // 4 - 1
                        S.wait_all("pool", q_events[qd])
                        S.raw("pool", (lambda qd: (lambda e: e.collective_compute(
                            "AllGather", ALU.bypass, replica_groups=[[0, 1, 2, 3], [4, 5, 6, 7]],
                            ins=[xin_t[qd].ap().opt()], outs=[xout_t.ap()[1024 * qd:1024 * (qd + 1), :].opt()]).then_inc(cc_sem)))(qd))

            return out_events

        out_events = _phase1() if nt1 > 0 else []

        if debug:
            print('MARKS', S.marks, S.nops)
            S.limit = 10**9
            S.wait_all("pool", out_events)
            evs = [S.dma("pool", dbg_d[:, TOK2 * q:TOK2 * (q + 1)], xin[q], "dbg") for q in range(4)]
            S.wait_all("pool", evs)
        if do_phase2:
            S.wait_all("pool", out_events)
            if p2mode == "nogather":
                evd = S.dma("pool", xout[0:1024, :], xout_dbg_d, "xdbg")
                S.wait_all("pool", [evd])
            else:
                S.raw("pool", lambda e: e.wait_ge(cc_sem, 4))
            S.cnt["pool"] += 1
            gather_ev = (S.sem["pool"], S.cnt["pool"], "pool")
            S.raw("pool", lambda e: e.sem_inc(S.sem["pool"], 1))
            lasts = [S.last[e] for e in ("pe", "act", "dve", "pool") if S.last[e] is not None] + [gather_ev]
            for e in ("pe", "act", "dve", "sp", "pool"):
                S.wait_all(e, lasts)

            with ExitStack() as p2:
                def sb(name, shape, dt=F32):
                    return p2.enter_context(nc.sbuf_tensor("b_" + name, list(shape), dt))

                Wg = sb("Wg", [128, 8, DFF], BF16); r_Wg = Res("Wg")
                Wu = sb("Wu", [128, 8, DFF], BF16); r_Wu = Res("Wu")
                Wd = sb("Wd", [128, NF, D], BF16); r_Wd = Res("Wd")
                Wo = sb("Wo", [128, 8, D], BF16); r_Wo = Res("Wo")
                Wgl = sb("Wgl", [128, 4, 512], BF16); r_Wgl = Res("Wgl")
                S.dma("pool", Wgl[:], wglu_d.rearrange("(k p) c -> p k c", p=128), "w0", writes=[r_Wgl])
                S.dma("pool", Wo[:], wout_d.rearrange("(k p) c -> p k c", p=128), "w1", writes=[r_Wo])
                r_Wgc = [Res("Wg%d" % f) for f in range(NF)]
                r_Wuc = [Res("Wu%d" % f) for f in range(NF)]
                r_Wdc = [Res("Wd%d" % f) for f in range(NF)]
                for f in range(0, NF, 2):
                    S.dma("pool", Wg[:, :, 128 * f:128 * (f + 2)], wgate_d[:, 128 * f:128 * (f + 2)].rearrange("(k p) c -> p k c", p=128),
                          "wg%d" % (f // 2), writes=[r_Wgc[f], r_Wgc[f + 1]])
                    S.dma("pool", Wu[:, :, 128 * f:128 * (f + 2)], wup_d[:, 128 * f:128 * (f + 2)].rearrange("(k p) c -> p k c", p=128),
                          "wu%d" % (f // 2), writes=[r_Wuc[f], r_Wuc[f + 1]])
                for f in range(0, NF, 2):
                    S.dma("pool", Wd[:, f:f + 2, :], wdown_d[128 * f:128 * (f + 2), :].rearrange("(k p) c -> p k c", p=128),
                          "wd%d" % (f // 2), writes=[r_Wdc[f], r_Wdc[f + 1]])
                ident2 = sb("ident2", [128, 128], BF16); r_id2 = Res("ident2")
                ones2 = sb("ones2", [128, 128], BF16); r_ones2 = Res("ones2")
                eps2 = sb("eps2", [128, 1]); r_eps2 = Res("eps2")
                bglu = sb("bglu", [128, 4]); r_bglu = Res("bglu")
                gssm = sb("gssm", [128, 4]); r_gssm = Res("gssm")
                gpost_b = sb("gpost_b", [128, D]); r_gpost = Res("gpost")
                gffn_b = sb("gffn_b", [128, D]); r_gffn = Res("gffn")
                gpffn_b = sb("gpffn_b", [128, D]); r_gpffn = Res("gpffn")
                S.dma("pool", ident2[:], ident_d, "k0", writes=[r_id2])
                S.op("dve", MEMSET(ones2[:], 1.0), writes=[r_ones2])
                S.op("dve", MEMSET(eps2[:], EPS), writes=[r_eps2])
                S.dma("sp", bglu[:], bglu_d, "k1", writes=[r_bglu])
                S.dma("sp", gssm[:], gssm_d, "k2", writes=[r_gssm])
                S.dma("sp", gpost_b[:], gpost_d.partition_broadcast(128), "k3", writes=[r_gpost])
                S.dma("sp", gffn_b[:], gffn_d.partition_broadcast(128), "k4", writes=[r_gffn])
                S.dma("sp", gpffn_b[:], gpffn_d.partition_broadcast(128), "k5", writes=[r_gpffn])

                cT = [sb("cT%d" % i, [128, 8, T2], BF16) for i in range(1)]; r_cT = [Res("cT%d" % i) for i in range(1)]
                sig = sb("sig", [128, T2]); r_sig = Res("sig")
                yg = sb("yg", [128, 4, T2]); r_yg = Res("yg")
                sq2 = sb("sq2", [128, T2], BF16); r_sq2 = Res("sq2")
                rsy = sb("rsy", [128, T2]); r_rsy = Res("rsy")
                yn = sb("yn", [128, 4, T2], BF16); r_yn = Res("yn")
                hres = [sb("hres%d" % i, [128, D]) for i in range(2)]; r_hres = [Res("hres%d" % i) for i in range(2)]
                ssv = sb("ssv", [128, 8]); r_ssv = Res("ssv")
                tmpm = sb("tmpm", [128, 512]); r_tmpm = Res("tmpm")
                hs = sb("hs", [128, D], BF16); r_hs = Res("hs")
                junk2 = hs; r_junk2 = r_hs
                hT2 = sb("hT2", [128, 8, T2], BF16); r_hT2 = Res("hT2")
                sg = [sb("sg%d" % i, [128, T2], BF16) for i in range(2)]; r_sg = [Res("sg%d" % i) for i in range(2)]
                aT = sb("aT", [128, NF, T2], BF16); r_aT = Res("aT")

                pid = [None]

                B = banks
                rB = rbank
                final_events = []
                for tt in range(NT2):
                    cb = 0
                    def _ld(e, cb=cb, tt=tt):
                        if pid[0] is None:
                            pid[0] = e.partition_id() % 4
                        return e.dma_start(out=cT[cb][:], in_=xout[bass.ds(pid[0] * 1024, 1024), tt * T2:(tt + 1) * T2].rearrange("(k p) t -> p k t", p=128))
                    slot = "g%d" % cb
                    if slot not in S.dma_sems:
                        S.dma_sems[slot] = top.enter_context(nc.semaphore("d_" + slot))
                        S.dma_cnt[slot] = 0
                    S._waits("sp", S._deps_for(None, [r_cT[cb]], None))
                    S.dma_cnt[slot] += 16
                    ev = (S.dma_sems[slot], S.dma_cnt[slot], "dma")
                    S.raw("sp", (lambda sem: (lambda e, f=_ld: f(e).then_inc(sem, 16)))(S.dma_sems[slot]))
                    S._commit(ev, None, [r_cT[cb]])

                    for m in range(4):
                        gb = B[0]; rgb = rB[0]
                        S.group("pe", [MM(gb[:, 0:T2], Wgl[:, r, 128 * m:128 * (m + 1)], cT[cb][:, 2 * r + 1, :], r == 0, r == 3) for r in range(4)],
                                reads=[r_Wgl, r_cT[cb]], writes=[rgb])
                        S.op("act", ACT(sig[:], gb[:, 0:T2], AF.Sigmoid, bias=bglu[:, m:m + 1]), reads=[rgb, r_bglu], writes=[r_sig])
                        S.op("dve", TT(yg[:, m, :], cT[cb][:, 2 * m + 1, :], sig[:], ALU.mult), reads=[r_cT[cb], r_sig], writes=[r_yg])
                        S.op("act", ACT(sq2[:], yg[:, m, :], AF.Square), reads=[r_yg], writes=[r_sq2])
                        S.op("pe", MM(B[1][:, 0:T2], ones2[:], sq2[:], m == 0, m == 3), reads=[r_ones2, r_sq2], writes=[rB[1]])
                    S.op("act", ACT(rsy[:], B[1][:, 0:T2], AF.Ln, scale=1.0 / 512, bias=eps2[:, 0:1]), reads=[rB[1], r_eps2], writes=[r_rsy])
                    S.op("act", ACT(rsy[:], rsy[:], AF.Exp, scale=-0.5), reads=[r_rsy], writes=[r_rsy])
                    for m in range(4):
                        S.op("dve", STT(yn[:, m, :], yg[:, m, :], gssm[:, m:m + 1], rsy[:], ALU.mult, ALU.mult),
                             reads=[r_yg, r_gssm, r_rsy], writes=[r_yn])
                    for s in range(2):
                        row0 = tt * T2 + 128 * s
                        S.dma("sp", hres[s][:], xres_d[row0:row0 + 128, :], "xr%d" % s, writes=[r_hres[s]])
                        for hf in range(2):
                            bk = B[2 + hf]; rbk = rB[2 + hf]
                            fns = []
                            for k in range(8):
                                src = cT[cb][:, k, 128 * s:128 * (s + 1)] if k % 2 == 0 else yn[:, k // 2, 128 * s:128 * (s + 1)]
                                fns.append(MM(bk[:, :], src, Wo[:, k, 512 * hf:512 * (hf + 1)], k == 0, k == 7))
                            S.group("pe", fns, reads=[r_cT[cb], r_yn, r_Wo], writes=[rbk])
                            S.op("act", ACT(junk2[:, 0:512], bk[:, :], AF.Square, accum_out=ssv[:, hf:hf + 1]), reads=[rbk], writes=[r_junk2, r_ssv])
                        S.op("dve", TT(ssv[:, 2:3], ssv[:, 0:1], ssv[:, 1:2], ALU.add), reads=[r_ssv], writes=[r_ssv])
                        S.op("act", ACT(ssv[:, 2:3], ssv[:, 2:3], AF.Ln, scale=1.0 / D, bias=eps2[:, 0:1]), reads=[r_ssv, r_eps2], writes=[r_ssv])
                        S.op("act", ACT(ssv[:, 2:3], ssv[:, 2:3], AF.Exp, scale=-0.5), reads=[r_ssv], writes=[r_ssv])
                        for hf in range(2):
                            bk = B[2 + hf]; rbk = rB[2 + hf]
                            S.op("dve", STT(tmpm[:], bk[:, :], ssv[:, 2:3], gpost_b[:, 512 * hf:512 * (hf + 1)], ALU.mult, ALU.mult),
                                 reads=[rbk, r_ssv, r_gpost], writes=[r_tmpm])
                            S.op("dve", TT(hres[s][:, 512 * hf:512 * (hf + 1)], tmpm[:], hres[s][:, 512 * hf:512 * (hf + 1)], ALU.add),
                                 reads=[r_tmpm, r_hres[s]], writes=[r_hres[s]])
                        S.op("act", ACT(junk2[:], hres[s][:], AF.Square, accum_out=ssv[:, 3:4]), reads=[r_hres[s]], writes=[r_junk2, r_ssv])
                        S.op("act", ACT(ssv[:, 3:4], ssv[:, 3:4], AF.Ln, scale=1.0 / D, bias=eps2[:, 0:1]), reads=[r_ssv, r_eps2], writes=[r_ssv])
                        S.op("act", ACT(ssv[:, 3:4], ssv[:, 3:4], AF.Exp, scale=-0.5), reads=[r_ssv], writes=[r_ssv])
                        S.op("dve", STT(hs[:], hres[s][:], ssv[:, 3:4], gffn_b[:], ALU.mult, ALU.mult), reads=[r_hres[s], r_ssv, r_gffn], writes=[r_hs])
                        B4b = B[4][:].bitcast(BF16)
                        S.group("pe", [TR(B4b[:, 128 * k:128 * (k + 1)], hs[:, 128 * k:128 * (k + 1)], ident2[:]) for k in range(8)],
                                reads=[r_hs, r_id2], writes=[rB[4]])
                        S.op("act", ACT(hT2[:, :, 128 * s:128 * (s + 1)], B4b.rearrange("p (k t) -> p k t", k=8), AF.Copy),
                             reads=[rB[4]], writes=[r_hT2])
                    for f in range(NF):
                        gb2 = B[5 + (f % 2)]; rgb2 = rB[5 + (f % 2)]
                        gv = gb2[:].rearrange("p (h q) -> p h q", h=2)
                        S.group("pe", [MM(gv[:, 0, :], Wg[:, k, 128 * f:128 * (f + 1)], hT2[:, k, :], k == 0, k == 7) for k in range(8)] +
                                      [MM(gv[:, 1, :], Wu[:, k, 128 * f:128 * (f + 1)], hT2[:, k, :], k == 0, k == 7) for k in range(8)],
                                reads=[r_Wgc[f], r_Wuc[f], r_hT2], writes=[rgb2])
                        S.op("act", ACT(sg[f % 2][:], gv[:, 0, :], AF.Silu), reads=[rgb2], writes=[r_sg[f % 2]])
                        S.op("dve", TT(aT[:, f, :], gv[:, 1, :], sg[f % 2][:], ALU.mult), reads=[rgb2, r_sg[f % 2]], writes=[r_aT])
                    for s in range(2):
                        row0 = tt * T2 + 128 * s
                        for hf in range(2):
                            bk = B[2 + hf]; rbk = rB[2 + hf]
                            S.group("pe", [MM(bk[:, :], aT[:, f, 128 * s:128 * (s + 1)], Wd[:, f, 512 * hf:512 * (hf + 1)], f == 0, f == NF - 1) for f in range(NF)],
                                    reads=[r_aT] + r_Wdc, writes=[rbk])
                            S.op("act", ACT(junk2[:, 0:512], bk[:, :], AF.Square, accum_out=ssv[:, 4 + hf:5 + hf]), reads=[rbk], writes=[r_junk2, r_ssv])
                        S.op("dve", TT(ssv[:, 6:7], ssv[:, 4:5], ssv[:, 5:6], ALU.add), reads=[r_ssv], writes=[r_ssv])
                        S.op("act", ACT(ssv[:, 6:7], ssv[:, 6:7], AF.Ln, scale=1.0 / D, bias=eps2[:, 0:1]), reads=[r_ssv, r_eps2], writes=[r_ssv])
                        S.op("act", ACT(ssv[:, 6:7], ssv[:, 6:7], AF.Exp, scale=-0.5), reads=[r_ssv], writes=[r_ssv])
                        for hf in range(2):
                            bk = B[2 + hf]; rbk = rB[2 + hf]
                            S.op("dve", STT(tmpm[:], bk[:, :], ssv[:, 6:7], gpffn_b[:, 512 * hf:512 * (hf + 1)], ALU.mult, ALU.mult),
                                 reads=[rbk, r_ssv, r_gpffn], writes=[r_tmpm])
                            S.op("dve", TT(hres[s][:, 512 * hf:512 * (hf + 1)], tmpm[:], hres[s][:, 512 * hf:512 * (hf + 1)], ALU.add),
                                 reads=[r_tmpm, r_hres[s]], writes=[r_hres[s]])
                        final_events.append(S.dma("sp", out_d[row0:row0 + 128, :], hres[s][:], "fo%d" % s, reads=[r_hres[s]]))
                S.wait_all("sp", final_events)

        with nc.Block() as block:
            S.run(block)
    return nc


def _consts():
    invf8 = (500000.0 ** (-np.arange(0, 16, 2, dtype=np.float32) / np.float32(16))).astype(np.float32)
    invf = np.zeros((128, 1), np.float32)
    for base in (0, 64):
        for i in range(16):
            invf[base + i, 0] = invf8[i % 8]
    R = np.zeros((128, 128), np.float32)
    for base in (0, 64):
        for i in range(8):
            R[base + i, base + i + 8] = -1.0
            R[base + i + 8, base + i] = 1.0
    rmatT = np.ascontiguousarray(R.T)
    ident = np.eye(128, dtype=np.float32)
    swapm = np.zeros((128, 128), np.float32)
    for k in range(128):
        swapm[k, (k + 64) % 128] = 1.0
    sgn = np.ones((128, 1), np.float32); sgn[64:] = -1.0
    tri = (np.arange(128)[:, None] <= np.arange(128)[None, :]).astype(np.float32)
    gmask = (np.arange(128)[:, None] // 16 == np.arange(8)[None, :]).astype(np.float32)
    return dict(invf=invf, rmatT=rmatT, ident=ident, swapm=swapm, sgn=sgn, tri=tri, gmask=gmask,
                iota512=np.arange(512, dtype=np.float32), tile_iota=(512.0 * np.arange(NT1)).astype(np.float32))


def make_in_maps(inp):
    c = _consts()
    f = lambda a: np.ascontiguousarray(np.asarray(a, dtype=np.float32))
    w_in = f(inp["w_in"])[0]
    maps = []
    w_out = f(inp["w_out"])[0]
    perm = []
    for r in range(4):
        perm += list(range(128 * r, 128 * r + 128)) + list(range(512 + 128 * r, 512 + 128 * r + 128))
    w_out_p = np.ascontiguousarray(w_out[perm, :])
    for core in range(8):
        b, h = core // 4, core % 4
        cols = (list(range(64 * h, 64 * h + 64)) + list(range(256 + 64 * h, 256 + 64 * h + 64)) +
                list(range(512 + 64 * h, 512 + 64 * h + 64)) + list(range(768 + 64 * h, 768 + 64 * h + 64)) +
                list(range(1024 + 128 * h, 1024 + 128 * h + 128)) + list(range(1536 + 128 * h, 1536 + 128 * h + 128)))
        gs = slice(8 * h, 8 * h + 8)
        m = dict(c)
        m["x"] = f(inp["x"][b])
        m["meta"] = f(inp["meta"])
        m["win"] = np.ascontiguousarray(w_in[:, cols])
        m["gpre"] = np.ascontiguousarray(f(inp["pre_mix_g"])[0].reshape(8, 128).T)
        m["lamv"] = np.stack([f(inp["lambda_q1"])[0], f(inp["lambda_k1"])[0], f(inp["lambda_q2"])[0], f(inp["lambda_k2"])[0]])
        m["subg"] = f(inp["subln_g"])[0].reshape(128, 1)
        m["areT"] = np.ascontiguousarray(f(inp["a_re"])[0, gs].T)
        m["aimT"] = np.ascontiguousarray(f(inp["a_im"])[0, gs].T)
        m["logdt"] = f(inp["log_dt"])[0, gs]
        m["bre"] = f(inp["b_re"])[0, gs]
        m["bim"] = f(inp["b_im"])[0, gs]
        m["creT"] = np.ascontiguousarray(f(inp["c_re"])[0, gs].transpose(2, 0, 1).reshape(64, 128))
        m["cimT"] = np.ascontiguousarray(f(inp["c_im"])[0, gs].transpose(2, 0, 1).reshape(64, 128))
        m["dskip"] = f(inp["d_skip"])[0, 128 * h:128 * h + 128].reshape(128, 1)
        m["wglu"] = f(inp["w_glu"])[0]
        m["bglu"] = np.ascontiguousarray(f(inp["b_glu"])[0].reshape(4, 128).T)
        m["gssm"] = np.ascontiguousarray(f(inp["ssm_out_g"])[0].reshape(4, 128).T)
        m["wout"] = w_out_p
        m["gpost"] = f(inp["post_mix_g"])[0]
        m["gffn"] = f(inp["pre_ffn_g"])[0]
        m["wgate"] = f(inp["w_gate"])[0]
        m["wup"] = f(inp["w_up"])[0]
        m["wdown"] = f(inp["w_down"])[0]
        m["gpffn"] = f(inp["post_ffn_g"])[0]
        m["xres"] = f(inp["x"][b, TOK2 * h:TOK2 * (h + 1)])
        maps.append(m)
    return maps


def kernel(**inputs):
    nc = build_program()
    maps = make_in_maps(inputs)
    res = run_bass_kernel_spmd(nc, maps, core_ids=list(range(8)))
    out = np.zeros((2, SEQ, D), np.float32)
    for core in range(8):
        b, h = core // 4, core % 4
        out[b, TOK2 * h:TOK2 * (h + 1)] = res.results[core]["out"]
    return out
```

```python
import math
import os
from contextlib import ExitStack

import numpy as np
import concourse.bass as bass
import concourse.mybir as mybir
from concourse.bass_utils import run_bass_kernel_spmd

F32 = mybir.dt.float32
BF16 = mybir.dt.bfloat16
I32 = mybir.dt.int32
AF = mybir.ActivationFunctionType
ALU = mybir.AluOpType

D = 1024
SEQ = 8192
NMETA = 16
L = SEQ + NMETA
LP = 8320
NBLK = 65
NT1 = 17
DFF = 2816
NF = 22
EPS = 1e-6
LAM_INIT = 0.8 - 0.6 * math.exp(0.0)
TWO_PI = 2.0 * math.pi
C1 = 6.28125
C2 = TWO_PI - C1
PI_LO = 3.141592
TOK2 = 2048
T2 = 256
NT2 = TOK2 // T2
FRONT_STEPS = 54


class Res:
    __slots__ = ("name", "w", "r", "excl")

    def __init__(self, name, excl=False):
        self.name = name
        self.w = None
        self.r = []
        self.excl = excl


class Sched:
    ENGS = ("pe", "act", "dve", "pool", "sp")

    def __init__(self, nc, stack):
        self.nc = nc
        self.ops = {e: [] for e in self.ENGS}
        self.sem = {e: stack.enter_context(nc.semaphore("s_" + e)) for e in self.ENGS}
        self.cnt = {e: 0 for e in self.ENGS}
        self.waited = {e: {} for e in self.ENGS}
        self.stack = stack
        self.dma_sems = {}
        self.dma_cnt = {}
        self.last = {e: None for e in self.ENGS}
        self.limit = int(os.environ.get("K_LIMIT", "1000000000"))
        self.nops = 0
        self.marks = []

    def _waits(self, eng, deps):
        need = {}
        for ev in deps:
            if ev is None:
                continue
            sem, val, src = ev
            key = sem.name
            if self.waited[eng].get(key, 0) >= val:
                continue
            if key not in need or need[key][1] < val:
                need[key] = (sem, val)
        for key, (sem, val) in need.items():
            self.waited[eng][key] = val
            self.ops[eng].append(("wait", sem, val))

    @staticmethod
    def _deps_for(reads, writes, extra):
        deps = list(extra or [])
        for r in reads or []:
            if r.w is not None:
                deps.append(r.w)
            if r.excl:
                deps.extend(r.r)
        for w in writes or []:
            if w.w is not None:
                deps.append(w.w)
            deps.extend(w.r)
        return deps

    @staticmethod
    def _commit(ev, reads, writes):
        for r in reads or []:
            r.r.append(ev)
        for w in writes or []:
            w.w = ev
            w.r = []

    def op(self, eng, fn, reads=None, writes=None, extra=None):
        return self.group(eng, [fn], reads, writes, extra)

    def group(self, eng, fns, reads=None, writes=None, extra=None):
        self.nops += 1
        if self.nops > self.limit:
            return None
        self._waits(eng, self._deps_for(reads, writes, extra))
        self.cnt[eng] += 1
        ev = (self.sem[eng], self.cnt[eng], eng)
        for f in fns[:-1]:
            self.ops[eng].append(("op", f, False))
        self.ops[eng].append(("op", fns[-1], True))
        self._commit(ev, reads, writes)
        self.last[eng] = ev
        return ev

    def dma(self, eng, out, in_, slot, reads=None, writes=None, extra=None):
        if slot not in self.dma_sems:
            self.dma_sems[slot] = self.stack.enter_context(self.nc.semaphore("d_" + slot))
            self.dma_cnt[slot] = 0
        self.nops += 1
        if self.nops > self.limit:
            return None
        self._waits(eng, self._deps_for(reads, writes, extra))
        self.dma_cnt[slot] += 16
        sem = self.dma_sems[slot]
        ev = (sem, self.dma_cnt[slot], "dma")
        self.ops[eng].append(("dma", out, in_, sem))
        self._commit(ev, reads, writes)
        return ev

    def raw(self, eng, fn):
        self.ops[eng].append(("raw", fn))

    def wait_all(self, eng, events):
        self._waits(eng, events)

    def run(self, block):
        sched = self

        def replay(engname, e):
            for item in sched.ops[engname]:
                if item[0] == "wait":
                    e.wait_ge(item[1], item[2])
                elif item[0] == "op":
                    ins = item[1](e)
                    if item[2]:
                        ins.then_inc(sched.sem[engname], 1)
                elif item[0] == "dma":
                    e.dma_start(out=item[1], in_=item[2]).then_inc(item[3], 16)
                elif item[0] == "raw":
                    item[1](e)

        @block.tensor
        def _(e):
            replay("pe", e)

        @block.scalar
        def _(e):
            replay("act", e)

        @block.vector
        def _(e):
            replay("dve", e)

        @block.gpsimd
        def _(e):
            replay("pool", e)

        @block.sync
        def _(e):
            replay("sp", e)


def MM(out, lhsT, rhs, start, stop):
    return lambda e: e.matmul(out, lhsT, rhs, start=start, stop=stop)


def TR(out, in_, ident):
    return lambda e: e.transpose(out, in_, ident)


def ACT(out, in_, func, **kw):
    return lambda e: e.activation(out=out, in_=in_, func=func, **kw)


def TT(out, a, b, op):
    return lambda e: e.tensor_tensor(out=out, in0=a, in1=b, op=op)


def TS(out, a, s1, op0, s2=None, op1=None):
    if op1 is None:
        return lambda e: e.tensor_scalar(out=out, in0=a, scalar1=s1, scalar2=None, op0=op0)
    return lambda e: e.tensor_scalar(out=out, in0=a, scalar1=s1, scalar2=s2, op0=op0, op1=op1)


def STT(out, a, s, b, op0, op1):
    return lambda e: e.scalar_tensor_tensor(out=out, in0=a, scalar=s, in1=b, op0=op0, op1=op1)


def CP(out, in_):
    return lambda e: e.tensor_copy(out=out, in_=in_)


def MEMSET(ap, v):
    return lambda e: e.memset(ap, v)


def RECIP(out, in_):
    return lambda e: e.reciprocal(out=out, in_=in_)


def SCAN(out, d0, d1, init):
    return lambda e: e.tensor_tensor_scan(out=out, data0=d0, data1=d1, initial=init, op0=ALU.mult, op1=ALU.add)


def build_program(nt1=NT1, do_phase2=True, debug=False, stop=9, p2mode="full"):
    nc = bass.Bass("TRN2", target_bir_lowering=False)

    def din(name, shape, dt=F32):
        return nc.dram_tensor(name, list(shape), dt, kind="ExternalInput").ap()

    ident_d = din("ident", [128, 128])
    if nt1 > 0:
        x_d = din("x", [SEQ, D])
        meta_d = din("meta", [NMETA, D])
        win_d = din("win", [D, 512])
        gpre_d = din("gpre", [128, 8])
        invf_d = din("invf", [128, 1])
        iota512_d = din("iota512", [512])
        tile_iota_d = din("tile_iota", [NT1])
        rmatT_d = din("rmatT", [128, 128])
        swap_d = din("swapm", [128, 128])
        sgn_d = din("sgn", [128, 1])
        tri_d = din("tri", [128, 128])
        gmask_d = din("gmask", [128, 8])
        lam_d = din("lamv", [4, 64])
        subg_d = din("subg", [128, 1])
        areT_d = din("areT", [64, 8])
        aimT_d = din("aimT", [64, 8])
        logdt_d = din("logdt", [8])
        bre_d = din("bre", [8, 64, 16])
        bim_d = din("bim", [8, 64, 16])
        creT_d = din("creT", [64, 128])
        cimT_d = din("cimT", [64, 128])
        dskip_d = din("dskip", [128, 1])
    if do_phase2:
        wglu_d = din("wglu", [512, 512])
        bglu_d = din("bglu", [128, 4])
        gssm_d = din("gssm", [128, 4])
        wout_d = din("wout", [D, D])
        gpost_d = din("gpost", [D])
        gffn_d = din("gffn", [D])
        wgate_d = din("wgate", [D, DFF])
        wup_d = din("wup", [D, DFF])
        wdown_d = din("wdown", [DFF, D])
        gpffn_d = din("gpffn", [D])
        xres_d = din("xres", [TOK2, D])
    out_d = nc.dram_tensor("out", [TOK2, D], F32, kind="ExternalOutput").ap() if do_phase2 else None

    xout_dbg_d = din("xout_dbg", [1024, TOK2]) if p2mode == "nogather" else None
    xin_t = [nc.dram_tensor("xchg_in%d" % q, [256, TOK2], BF16) for q in range(4)]
    xout_t = nc.dram_tensor("xchg_out", [4 * 1024, TOK2], BF16)
    xin = [t.ap() for t in xin_t]
    xout = xout_t.ap()
    dbg_d = None
    if debug:
        dbg_d = nc.dram_tensor("dbg", [256, SEQ], F32, kind="ExternalOutput").ap()

    with ExitStack() as top:
        S = Sched(nc, top)

        def ps_bank(name):
            return top.enter_context(nc.psum_tensor(name, [128, 512], F32))

        use_cc = do_phase2 and p2mode == "full"
        cc_sem = top.enter_context(nc.semaphore("cc_sem")) if use_cc else None
        q_events = [[] for _ in range(4)]

        def xchg_write(row0, lo, hi, src, src_col0, slot, res):
            evs = []
            c = lo
            while c < hi:
                q = c // TOK2
                ce = min(hi, TOK2 * (q + 1))
                ev = S.dma("sp", xin[q][row0:row0 + 128, c - TOK2 * q:ce - TOK2 * q],
                           src[:, src_col0 + (c - lo):src_col0 + (ce - lo)], slot, reads=[res])
                q_events[q].append(ev)
                evs.append(ev)
                c = ce
            return evs

        banks = [ps_bank("bank%d" % i) for i in range(8)]
        rbank = [Res("bank%d" % i, excl=True) for i in range(8)]

        def _phase1():
            with ExitStack() as p1:
                def sb(name, shape, dt=F32):
                    return p1.enter_context(nc.sbuf_tensor("a_" + name, list(shape), dt))

                ident_f = sb("ident_f", [128, 128]); r_ident_f = Res("ident_f")
                ident_b = sb("ident_b", [128, 128], BF16); r_ident_b = Res("ident_b")
                rmatT_b = sb("rmatT_b", [128, 128], BF16); r_rmat = Res("rmat")
                swap_f = sb("swap_f", [128, 128]); r_swap = Res("swap")
                tri_b = sb("tri_b", [128, 128], BF16); r_tri = Res("tri")
                ones_b = sb("ones_b", [128, 128], BF16); r_ones = Res("ones")
                sgn = sb("sgn", [128, 1]); r_sgn = Res("sgn")
                gmask = sb("gmask", [128, 8]); r_gmask = Res("gmask")
                invf = sb("invf", [128, 1]); r_invf = Res("invf")
                iota512 = sb("iota512", [128, 512]); r_iota = Res("iota512")
                tile_iota = sb("tile_iota", [128, NT1]); r_tiota = Res("tile_iota")
                gpre = sb("gpre", [128, 8]); r_gpre = Res("gpre")
                subg = sb("subg", [128, 1]); r_subg = Res("subg")
                dskip = sb("dskip", [128, 1]); r_dskip = Res("dskip")
                epsc = sb("epsc", [128, 1]); r_eps = Res("eps")
                lamv = sb("lamv", [128, 4, 64]); r_lamv = Res("lamv")
                neglam = sb("neglam", [128, 1]); r_neglam = Res("neglam")

                S.dma("sp", ident_f[:], ident_d, "c0", writes=[r_ident_f])
                S.dma("pool", ident_b[:], ident_d, "c1", writes=[r_ident_b])
                S.dma("pool", rmatT_b[:], rmatT_d, "c2", writes=[r_rmat])
                S.dma("sp", swap_f[:], swap_d, "c3", writes=[r_swap])
                S.dma("pool", tri_b[:], tri_d, "c4", writes=[r_tri])
                S.dma("sp", sgn[:], sgn_d, "c5", writes=[r_sgn])
                S.dma("sp", gmask[:], gmask_d, "c6", writes=[r_gmask])
                S.dma("sp", invf[:], invf_d, "c7", writes=[r_invf])
                S.dma("sp", iota512[:], iota512_d.partition_broadcast(128), "c8", writes=[r_iota])
                S.dma("sp", tile_iota[:], tile_iota_d.partition_broadcast(128), "c9", writes=[r_tiota])
                S.dma("sp", gpre[:], gpre_d, "c10", writes=[r_gpre])
                S.dma("sp", subg[:], subg_d, "c11", writes=[r_subg])
                S.dma("sp", dskip[:], dskip_d, "c12", writes=[r_dskip])
                S.dma("sp", lamv[:].rearrange("p a b -> p (a b)"),
                      lam_d.rearrange("a b -> (a b)").partition_broadcast(128), "c13", writes=[r_lamv])
                S.op("dve", MEMSET(ones_b[:], 1.0), writes=[r_ones])
                S.op("dve", MEMSET(epsc[:], EPS), writes=[r_eps])

                scr = [sb("scr%d" % i, [128, 512]) for i in range(4)]
                r_scr = [Res("scr%d" % i) for i in range(4)]
                ki_t = sb("ki_t", [128, 512], I32); r_ki = Res("ki")

                def range_reduce(eng, out_ap, arg_ap, n, r_out, r_arg):
                    kf = scr[3][:, 0:n]
                    S.op(eng, TS(ki_t[:, 0:n], arg_ap, 1.0 / TWO_PI, ALU.mult), reads=[r_arg], writes=[r_ki])
                    S.op(eng, CP(kf, ki_t[:, 0:n]), reads=[r_ki], writes=[r_scr[3]])
                    S.op(eng, STT(out_ap, kf, -C1, arg_ap, ALU.mult, ALU.add), reads=[r_scr[3], r_arg], writes=[r_out])
                    S.op(eng, STT(out_ap, kf, -C2, out_ap, ALU.mult, ALU.add), reads=[r_scr[3], r_out], writes=[r_out])
                    S.op(eng, TS(out_ap, out_ap, PI_LO, ALU.min, -PI_LO, ALU.max), reads=[r_out], writes=[r_out])

                def sincos(arg_ap, n, r_arg, sin_out, cos_out, r_sin, r_cos):
                    red = scr[2][:, 0:n]
                    range_reduce("dve", red, arg_ap, n, r_scr[2], r_arg)
                    S.op("act", ACT(sin_out, red, AF.Sin), reads=[r_scr[2]], writes=[r_sin])
                    hs = scr[1][:, 0:n]
                    S.op("act", ACT(hs, red, AF.Sin, scale=0.5), reads=[r_scr[2]], writes=[r_scr[1]])
                    S.op("dve", TT(hs, hs, hs, ALU.mult), reads=[r_scr[1]], writes=[r_scr[1]])
                    S.op("dve", TS(cos_out, hs, -2.0, ALU.mult, 1.0, ALU.add), reads=[r_scr[1]], writes=[r_cos])

                lt = sb("lt", [128, 2, 64]); r_lt = Res("lt")
                ls = sb("ls", [128, 2]); r_ls = Res("ls")
                S.op("dve", TT(lt[:, 0, :], lamv[:, 0, :], lamv[:, 1, :], ALU.mult), reads=[r_lamv], writes=[r_lt])
                S.op("dve", TT(lt[:, 1, :], lamv[:, 2, :], lamv[:, 3, :], ALU.mult), reads=[r_lamv, r_lt], writes=[r_lt])
                S.op("dve", lambda e: e.reduce_sum(out=ls[:], in_=lt[:], axis=mybir.AxisListType.X), reads=[r_lt], writes=[r_ls])
                S.op("act", ACT(ls[:], ls[:], AF.Exp), reads=[r_ls], writes=[r_ls])
                S.op("dve", TT(neglam[:], ls[:, 1:2], ls[:, 0:1], ALU.subtract), reads=[r_ls], writes=[r_neglam])
                S.op("dve", TS(neglam[:], neglam[:], -LAM_INIT, ALU.add), reads=[r_neglam], writes=[r_neglam])
                S.op("dve", TS(subg[:], subg[:], 1.0 - LAM_INIT, ALU.mult), reads=[r_subg], writes=[r_subg])

                Wb = sb("Wb", [128, 8, 512], BF16); r_Wb = Res("Wb")
                wstg = [sb("wstg%d" % i, [128, 512]) for i in range(2)]
                r_wstg = [Res("wstg%d" % i) for i in range(2)]
                for k in range(8):
                    S.dma("sp", wstg[k % 2][:], win_d[128 * k:128 * (k + 1), :], "wstg%d" % (k % 2), writes=[r_wstg[k % 2]])
                    S.op("pool", TS(Wb[:, k, :], wstg[k % 2][:], gpre[:, k:k + 1], ALU.mult),
                         reads=[r_wstg[k % 2], r_gpre], writes=[r_Wb])

                C0 = sb("C0", [128, 512]); S0 = sb("S0", [128, 512]); r_C0 = Res("C0"); r_S0 = Res("S0")
                cI = sb("cI", [128, NT1]); sI = sb("sI", [128, NT1]); r_cI = Res("cI"); r_sI = Res("sI")
                S.op("dve", TS(scr[0][:], iota512[:], invf[:, 0:1], ALU.mult), reads=[r_iota, r_invf], writes=[r_scr[0]])
                sincos(scr[0][:], 512, r_scr[0], S0[:], C0[:], r_S0, r_C0)
                S.op("dve", TS(scr[0][:, 0:NT1], tile_iota[:], invf[:, 0:1], ALU.mult), reads=[r_tiota, r_invf], writes=[r_scr[0]])
                sincos(scr[0][:, 0:NT1], NT1, r_scr[0], sI[:], cI[:], r_sI, r_cI)
                nsI = sb("nsI", [128, NT1]); r_nsI = Res("nsI")
                S.op("dve", TS(nsI[:], sI[:], -1.0, ALU.mult), reads=[r_sI], writes=[r_nsI])

                are2 = sb("are2", [128, 8]); aim2 = sb("aim2", [128, 8]); dt2 = sb("dt2", [128, 8])
                r_are = Res("are2"); r_aim = Res("aim2"); r_dt = Res("dt2")
                S.dma("sp", are2[0:64, :], areT_d, "s0", writes=[r_are])
                S.dma("sp", are2[64:128, :], areT_d, "s0", writes=[r_are])
                S.dma("sp", aim2[0:64, :], aimT_d, "s1", writes=[r_aim])
                S.dma("sp", aim2[64:128, :], aimT_d, "s1", writes=[r_aim])
                S.dma("sp", dt2[:], logdt_d.partition_broadcast(128), "s2", writes=[r_dt])
                S.op("act", ACT(dt2[:], dt2[:], AF.Exp), reads=[r_dt], writes=[r_dt])
                rdec = sb("rdec", [128, 8]); r_rdec = Res("rdec")
                theta = sb("theta", [128, 8]); r_theta = Res("theta")
                S.op("dve", TT(rdec[:], are2[:], dt2[:], ALU.mult), reads=[r_are, r_dt], writes=[r_rdec])
                S.op("act", ACT(rdec[:], rdec[:], AF.Exp), reads=[r_rdec], writes=[r_rdec])
                S.op("dve", TT(theta[:], aim2[:], dt2[:], ALU.mult), reads=[r_aim, r_dt], writes=[r_theta])
                c1t = sb("c1t", [128, 8]); s1t = sb("s1t", [128, 8]); r_c1t = Res("c1t"); r_s1t = Res("s1t")
                cTt = sb("cTt", [128, 8]); sTt = sb("sTt", [128, 8]); r_cTt = Res("cTt"); r_sTt = Res("sTt")
                sincos(theta[:], 8, r_theta, s1t[:], c1t[:], r_s1t, r_c1t)
                S.op("dve", TS(scr[0][:, 0:8], theta[:], 512.0, ALU.mult), reads=[r_theta], writes=[r_scr[0]])
                sincos(scr[0][:, 0:8], 8, r_scr[0], sTt[:], cTt[:], r_sTt, r_cTt)
                S.op("dve", TT(sTt[:], sTt[:], sgn[:, 0:1].to_broadcast([128, 8]), ALU.mult), reads=[r_sTt, r_sgn], writes=[r_sTt])
                COSg = sb("COSg", [128, 8, 512]); SINg = sb("SINg", [128, 8, 512])
                r_COSg = [Res("COSg%d" % g) for g in range(8)]; r_SINg = [Res("SINg%d" % g) for g in range(8)]
                for g in range(8):
                    S.op("dve", TS(scr[0][:], iota512[:], theta[:, g:g + 1], ALU.mult), reads=[r_iota, r_theta], writes=[r_scr[0]])
                    sincos(scr[0][:], 512, r_scr[0], SINg[:, g, :], COSg[:, g, :], r_SINg[g], r_COSg[g])
                ROT = sb("ROT", [128, 8, 128]); r_ROT = Res("ROT")
                for g in range(8):
                    S.op("dve", TS(ROT[:, g, :], ident_f[:], cTt[:, g:g + 1], ALU.mult), reads=[r_ident_f, r_cTt], writes=[r_ROT])
                    S.op("dve", STT(ROT[:, g, :], swap_f[:], sTt[:, g:g + 1], ROT[:, g, :], ALU.mult, ALU.add),
                         reads=[r_swap, r_sTt, r_ROT], writes=[r_ROT])
                fre = sb("fre", [128, 8]); fim = sb("fim", [128, 8]); r_fre = Res("fre"); r_fim = Res("fim")
                nr = sb("nr", [128, 8]); ni = sb("ni", [128, 8]); den = sb("den", [128, 8]); tmp8 = sb("tmp8", [128, 8])
                r_nr = Res("nr"); r_ni = Res("ni"); r_den = Res("den"); r_tmp8 = Res("tmp8")
                S.op("dve", TT(nr[:], rdec[:], c1t[:], ALU.mult), reads=[r_rdec, r_c1t], writes=[r_nr])
                S.op("dve", TS(nr[:], nr[:], -1.0, ALU.add), reads=[r_nr], writes=[r_nr])
                S.op("dve", TT(ni[:], rdec[:], s1t[:], ALU.mult), reads=[r_rdec, r_s1t], writes=[r_ni])
                S.op("dve", TT(den[:], are2[:], are2[:], ALU.mult), reads=[r_are], writes=[r_den])
                S.op("dve", TT(tmp8[:], aim2[:], aim2[:], ALU.mult), reads=[r_aim], writes=[r_tmp8])
                S.op("dve", TT(den[:], den[:], tmp8[:], ALU.add), reads=[r_den, r_tmp8], writes=[r_den])
                S.op("dve", RECIP(den[:], den[:]), reads=[r_den], writes=[r_den])
                S.op("dve", TT(fre[:], nr[:], are2[:], ALU.mult), reads=[r_nr, r_are], writes=[r_fre])
                S.op("dve", TT(tmp8[:], ni[:], aim2[:], ALU.mult), reads=[r_ni, r_aim], writes=[r_tmp8])
                S.op("dve", TT(fre[:], fre[:], tmp8[:], ALU.add), reads=[r_fre, r_tmp8], writes=[r_fre])
                S.op("dve", TT(fre[:], fre[:], den[:], ALU.mult), reads=[r_fre, r_den], writes=[r_fre])
                S.op("dve", TT(fim[:], ni[:], are2[:], ALU.mult), reads=[r_ni, r_are], writes=[r_fim])
                S.op("dve", TT(tmp8[:], nr[:], aim2[:], ALU.mult), reads=[r_nr, r_aim], writes=[r_tmp8])
                S.op("dve", TT(fim[:], fim[:], tmp8[:], ALU.subtract), reads=[r_fim, r_tmp8], writes=[r_fim])
                S.op("dve", TT(fim[:], fim[:], den[:], ALU.mult), reads=[r_fim, r_den], writes=[r_fim])
                Bre = sb("Bre", [64, 8, 16]); Bim = sb("Bim", [64, 8, 16]); r_Bre = Res("Bre"); r_Bim = Res("Bim")
                S.dma("sp", Bre[:], bre_d.rearrange("g p h -> p g h"), "s3", writes=[r_Bre])
                S.dma("sp", Bim[:], bim_d.rearrange("g p h -> p g h"), "s4", writes=[r_Bim])
                BBre = sb("BBre", [64, 8, 16]); BBim = sb("BBim", [64, 8, 16]); tB = sb("tB", [64, 8, 16])
                r_BBre = Res("BBre"); r_BBim = Res("BBim"); r_tB = Res("tB")
                fre_b = fre[0:64, :].unsqueeze(2).to_broadcast([64, 8, 16])
                fim_b = fim[0:64, :].unsqueeze(2).to_broadcast([64, 8, 16])
                S.op("dve", TT(BBre[:], Bre[:], fre_b, ALU.mult), reads=[r_Bre, r_fre], writes=[r_BBre])
                S.op("dve", TT(tB[:], Bim[:], fim_b, ALU.mult), reads=[r_Bim, r_fim], writes=[r_tB])
                S.op("dve", TT(BBre[:], BBre[:], tB[:], ALU.subtract), reads=[r_BBre, r_tB], writes=[r_BBre])
                S.op("dve", TT(BBim[:], Bim[:], fre_b, ALU.mult), reads=[r_Bim, r_fre], writes=[r_BBim])
                S.op("dve", TT(tB[:], Bre[:], fim_b, ALU.mult), reads=[r_Bre, r_fim], writes=[r_tB])
                S.op("dve", TT(BBim[:], BBim[:], tB[:], ALU.add), reads=[r_BBim, r_tB], writes=[r_BBim])
                FB1 = sb("FB1", [128, 128]); FB2 = sb("FB2", [128, 128]); r_FB1 = Res("FB1"); r_FB2 = Res("FB2")
                S.group("pe", [TR(banks[0][:, 0:64], BBre[:].rearrange("p g h -> p (g h)"), ident_f[0:64, 0:64]),
                               TR(banks[0][:, 64:128], BBim[:].rearrange("p g h -> p (g h)"), ident_f[0:64, 0:64])],
                        reads=[r_BBre, r_BBim, r_ident_f], writes=[rbank[0]])
                S.op("dve", CP(FB1[:], banks[0][:, 0:128]), reads=[rbank[0]], writes=[r_FB1])
                S.op("dve", CP(FB2[:, 0:64], banks[0][:, 64:128]), reads=[rbank[0]], writes=[r_FB2])
                S.op("dve", TS(FB2[:, 64:128], banks[0][:, 0:64], -1.0, ALU.mult), reads=[rbank[0], r_FB2], writes=[r_FB2])
                LB1 = sb("LB1", [128, 8, 128], BF16); LB2 = sb("LB2", [128, 8, 128], BF16); r_LB = Res("LB")
                for g in range(8):
                    S.op("pool", TS(LB1[:, g, :], FB1[:], gmask[:, g:g + 1], ALU.mult), reads=[r_FB1, r_gmask], writes=[r_LB])
                    S.op("pool", TS(LB2[:, g, :], FB2[:], gmask[:, g:g + 1], ALU.mult), reads=[r_FB2, r_gmask], writes=[r_LB])
                CC1 = sb("CC1", [128, 128]); CC2 = sb("CC2", [128, 128]); r_CC1 = Res("CC1"); r_CC2 = Res("CC2")
                S.dma("sp", CC1[0:64, :], creT_d, "s5", writes=[r_CC1])
                S.dma("sp", CC1[64:128, :], cimT_d, "s5", writes=[r_CC1])
                S.dma("sp", CC2[0:64, :], cimT_d, "s6", writes=[r_CC2])
                S.dma("sp", CC2[64:128, :], creT_d, "s6", writes=[r_CC2])
                S.op("dve", TS(CC1[64:128, :], CC1[64:128, :], -1.0, ALU.mult), reads=[r_CC1], writes=[r_CC1])
                S.op("dve", TS(CC2[:], CC2[:], -1.0, ALU.mult), reads=[r_CC2], writes=[r_CC2])
                LC1 = sb("LC1", [128, 8, 128], BF16); LC2 = sb("LC2", [128, 8, 128], BF16); r_LC = Res("LC")
                S.op("pool", MEMSET(LC1[:], 0.0), writes=[r_LC])
                S.op("pool", MEMSET(LC2[:], 0.0), writes=[r_LC])
                for g in range(8):
                    S.op("dve", CP(LC1[:, g, 16 * g:16 * g + 16], CC1[:, 16 * g:16 * g + 16]), reads=[r_CC1], writes=[r_LC])
                    S.op("dve", CP(LC2[:, g, 16 * g:16 * g + 16], CC2[:, 16 * g:16 * g + 16]), reads=[r_CC2], writes=[r_LC])
                diagD = sb("diagD", [128, 128], BF16); r_diagD = Res("diagD")
                S.op("dve", TS(diagD[:], ident_f[:], dskip[:, 0:1], ALU.mult), reads=[r_ident_f, r_dskip], writes=[r_diagD])
                s0 = sb("s0", [128, 2, 8]); r_s0 = [Res("s0_0"), Res("s0_1")]
                S.op("dve", MEMSET(s0[:], 0.0), writes=r_s0)

                xt = [sb("xt%d" % i, [128, D]) for i in range(2)]; r_xt = [Res("xt%d" % i) for i in range(2)]
                junk = sb("junk", [128, D], BF16); r_junk = Res("junk")
                ssq = sb("ssq", [128, 4]); r_ssq = [Res("ssq%d" % i) for i in range(4)]
                xs = [sb("xs%d" % i, [128, D], BF16) for i in range(2)]; r_xs = [Res("xs%d" % i) for i in range(2)]
                hT = [sb("hT%d" % i, [128, 8, 512], BF16) for i in range(2)]; r_hT = [Res("hT%d" % i) for i in range(2)]
                COSt = sb("COSt", [128, 512]); SINt = sb("SINt", [128, 512]); r_COSt = Res("COSt"); r_SINt = Res("SINt")
                qraw = sb("qraw", [128, 512], BF16); r_qraw = Res("qraw")
                rt1 = sb("rt1", [128, 512]); rt2 = sb("rt2", [128, 512]); r_rt1 = Res("rt1"); r_rt2 = Res("rt2")
                QT = [sb("QT%d" % i, [128, 512], BF16) for i in range(2)]; r_QT = [Res("QT%d" % i) for i in range(2)]
                KTa = sb("KTa", [128, LP], BF16); KTb = sb("KTb", [128, LP], BF16)
                r_KT = [Res("KT%d" % i) for i in range(NT1)]
                r_KTz = Res("KTz")
                S.op("pool", MEMSET(KTa[64:128, :], 0.0), writes=[r_KTz])
                S.op("pool", MEMSET(KTb[0:64, :], 0.0), writes=[r_KTz])
                V = sb("V", [128, NBLK, 128], BF16); r_V = [Res("V%d" % i) for i in range(NT1)]
                uT = [sb("uT%d" % i, [128, 512], BF16) for i in range(2)]; r_uT = [Res("uT%d" % i) for i in range(2)]
                st1 = sb("st1", [128, 512]); st2 = sb("st2", [128, 512]); r_st1 = Res("st1"); r_st2 = Res("st2")
                bt = sb("bt", [128, 512]); r_bt = Res("bt")
                Xs = [sb("Xs%d" % i, [128, 512]) for i in range(2)]; r_Xs = [Res("Xs%d" % i) for i in range(2)]
                P1s = [sb("P1s%d" % i, [128, 512], BF16) for i in range(2)]; r_P1s = [Res("P1s%d" % i) for i in range(2)]
                P2s = [sb("P2s%d" % i, [128, 512], BF16) for i in range(2)]; r_P2s = [Res("P2s%d" % i) for i in range(2)]
                ybuf = [sb("ybuf%d" % i, [128, 512], BF16) for i in range(2)]; r_ybuf = [Res("ybuf%d" % i) for i in range(2)]
                Pb = [sb("Pb%d" % i, [128, 2, 256], BF16) for i in range(3)]; r_Pb = [Res("Pb%d" % i) for i in range(3)]
                rL = sb("rL", [128, 2, 256]); r_rL = Res("rL")
                On = sb("On", [128, 2, 256]); r_On = Res("On")
                od = sb("od", [128, 256]); r_od = Res("od")
                osq = sb("osq", [128, 256], BF16); r_osq = Res("osq")
                orstd = sb("orstd", [128, 256]); r_orstd = Res("orstd")
                obuf = [sb("obuf%d" % i, [128, 256], BF16) for i in range(2)]; r_obuf = [Res("obuf%d" % i) for i in range(2)]

                F0, F1, F2, F3 = banks[0], banks[1], banks[2], banks[3]
                rF0, rF1, rF2, rF3 = rbank[0], rbank[1], rbank[2], rbank[3]
                Sb = [banks[4], banks[5]]; rSb = [rbank[4], rbank[5]]
                Ob, Lb = banks[6], banks[7]; rOb, rLb = rbank[6], rbank[7]
                F0b = F0[:].bitcast(BF16)


                pcount = [0]
                sbcount = [0]
                out_events = []

                def front_gen(ti):
                    nb = 4 if ti < 16 else 1
                    N = 128 * nb
                    tok0 = 512 * ti
                    hb = ti % 2
                    for j in range(nb):
                        s = 4 * ti + j
                        xb = sbcount[0] % 2
                        sq = sbcount[0] % 4
                        sbcount[0] += 1
                        if s == 0:
                            S.dma("sp", xt[xb][0:16, :], meta_d, "x%d" % xb, writes=[r_xt[xb]])
                            S.dma("sp", xt[xb][16:128, :], x_d[0:112, :], "x%d" % xb, writes=[r_xt[xb]])
                        elif s == 64:
                            S.op("pool", MEMSET(xt[xb][:], 0.0), writes=[r_xt[xb]])
                            S.dma("sp", xt[xb][0:16, :], x_d[SEQ - 16:SEQ, :], "x%d" % xb, writes=[r_xt[xb]])
                        else:
                            S.dma("sp", xt[xb][:], x_d[128 * s - 16:128 * s + 112, :], "x%d" % xb, writes=[r_xt[xb]])
                        S.op("act", ACT(junk[:], xt[xb][:], AF.Square, accum_out=ssq[:, sq:sq + 1]),
                             reads=[r_xt[xb]], writes=[r_junk, r_ssq[sq]])
                        S.op("act", ACT(ssq[:, sq:sq + 1], ssq[:, sq:sq + 1], AF.Ln, scale=1.0 / D, bias=epsc[:, 0:1]),
                             reads=[r_ssq[sq], r_eps], writes=[r_ssq[sq]])
                        S.op("act", ACT(ssq[:, sq:sq + 1], ssq[:, sq:sq + 1], AF.Exp, scale=-0.5),
                             reads=[r_ssq[sq]], writes=[r_ssq[sq]])
                        S.op("act", ACT(xs[xb][:], xt[xb][:], AF.Copy, scale=ssq[:, sq:sq + 1]),
                             reads=[r_xt[xb], r_ssq[sq]], writes=[r_xs[xb]])
                        yield
                        S.group("pe", [TR(F0b[:, 128 * k:128 * (k + 1)], xs[xb][:, 128 * k:128 * (k + 1)], ident_b[:]) for k in range(8)],
                                reads=[r_xs[xb], r_ident_b], writes=[rF0])
                        S.op("dve", CP(hT[hb][:, :, 128 * j:128 * (j + 1)], F0b.rearrange("p (k t) -> p k t", k=8)),
                             reads=[rF0], writes=[r_hT[hb]])
                        yield
                    S.op("dve", TS(COSt[:], C0[:], cI[:, ti:ti + 1], ALU.mult), reads=[r_C0, r_cI], writes=[r_COSt])
                    S.op("dve", STT(COSt[:], S0[:], nsI[:, ti:ti + 1], COSt[:], ALU.mult, ALU.add), reads=[r_S0, r_nsI, r_COSt], writes=[r_COSt])
                    S.op("dve", TS(SINt[:], S0[:], cI[:, ti:ti + 1], ALU.mult), reads=[r_S0, r_cI], writes=[r_SINt])
                    S.op("dve", STT(SINt[:], C0[:], sI[:, ti:ti + 1], SINt[:], ALU.mult, ALU.add), reads=[r_C0, r_sI, r_SINt], writes=[r_SINt])
                    yield

                    def proj_fm(c, bank, rb):
                        S.group("pe", [MM(bank[:, 0:N], Wb[:, k, 128 * c:128 * (c + 1)], hT[hb][:, k, 0:N], k == 0, k == 7) for k in range(8)],
                                reads=[r_Wb, r_hT[hb]], writes=[rb])

                    def rope_a():
                        S.op("act", ACT(qraw[:, 0:N], F1[:, 0:N], AF.Copy), reads=[rF1], writes=[r_qraw])
                        S.op("pe", MM(F2[:, 0:N], rmatT_b[:], qraw[:, 0:N], True, True), reads=[r_rmat, r_qraw], writes=[rF2])

                    def rope_b(dst_ap, r_dst, dst2=None):
                        S.op("dve", TT(rt1[:, 0:N], F1[:, 0:N], COSt[:, 0:N], ALU.mult), reads=[rF1, r_COSt], writes=[r_rt1])
                        S.op("dve", TT(rt2[:, 0:N], F2[:, 0:N], SINt[:, 0:N], ALU.mult), reads=[rF2, r_SINt], writes=[r_rt2])
                        if dst2 is None:
                            S.op("pool", TT(dst_ap, rt1[:, 0:N], rt2[:, 0:N], ALU.add), reads=[r_rt1, r_rt2], writes=[r_dst])
                        else:
                            S.op("pool", TT(dst_ap[0:64, :], rt1[0:64, 0:N], rt2[0:64, 0:N], ALU.add), reads=[r_rt1, r_rt2, r_KTz], writes=[r_dst])
                            S.op("pool", TT(dst2[64:128, :], rt1[64:128, 0:N], rt2[64:128, 0:N], ALU.add), reads=[r_rt1, r_rt2, r_KTz], writes=[r_dst])

                    proj_fm(0, F1, rF1)
                    yield
                    rope_a()
                    yield
                    rope_b(QT[hb][:, 0:N], r_QT[hb])
                    yield
                    proj_fm(1, F1, rF1)
                    yield
                    rope_a()
                    yield
                    rope_b(KTa[:, tok0:tok0 + N], r_KT[ti], KTb[:, tok0:tok0 + N])
                    yield
                    proj_fm(3, F1, rF1)
                    yield
                    S.op("act", ACT(uT[hb][:, 0:N], F1[:, 0:N], AF.Copy), reads=[rF1], writes=[r_uT[hb]])
                    fns = []
                    for j in range(nb):
                        for k in range(8):
                            fns.append(MM(F2[:, 128 * j:128 * (j + 1)], hT[hb][:, k, 128 * j:128 * (j + 1)], Wb[:, k, 256:384], k == 0, k == 7))
                    S.group("pe", fns, reads=[r_Wb, r_hT[hb]], writes=[rF2])
                    yield
                    S.op("act", ACT(V[:, 4 * ti:4 * ti + nb, :], F2[:, 0:N].rearrange("p (j d) -> p j d", j=nb), AF.Copy),
                         reads=[rF2], writes=[r_V[ti]])
                    yield
                    par = ti % 2
                    yb = ti % 2

                    def emit_y(g, pb):
                        S.group("pe", [MM(F3[:, 0:N], LC1[:, g, :], P1s[pb][:, 0:N], g == 0, False),
                                       MM(F3[:, 0:N], LC2[:, g, :], P2s[pb][:, 0:N], False, False)],
                                reads=[r_LC, r_P1s[pb], r_P2s[pb]], writes=[rF3])

                    prev = None
                    for g in range(8):
                        pb = pcount[0] % 2
                        pcount[0] += 1
                        S.op("pe", MM(F1[:, 0:N], LB1[:, g, :], uT[hb][:, 0:N], True, True), reads=[r_LB, r_uT[hb]], writes=[rF1])
                        S.op("pe", MM(F2[:, 0:N], LB2[:, g, :], uT[hb][:, 0:N], True, True), reads=[r_LB, r_uT[hb]], writes=[rF2])
                        yield
                        S.op("dve", TT(st1[:, 0:N], F1[:, 0:N], COSg[:, g, 0:N], ALU.mult), reads=[rF1, r_COSg[g]], writes=[r_st1])
                        S.op("dve", TT(st2[:, 0:N], F2[:, 0:N], SINg[:, g, 0:N], ALU.mult), reads=[rF2, r_SINg[g]], writes=[r_st2])
                        S.op("pool", TT(bt[:, 0:N], st1[:, 0:N], st2[:, 0:N], ALU.add), reads=[r_st1, r_st2], writes=[r_bt])
                        if prev is not None:
                            emit_y(*prev)
                        yield
                        S.op("dve", SCAN(Xs[pb][:, 0:N], rdec[:, g:g + 1].to_broadcast([128, N]), bt[:, 0:N], s0[:, par, g:g + 1]),
                             reads=[r_rdec, r_bt, r_s0[par]], writes=[r_Xs[pb]])
                        yield
                        if ti + 1 < nt1:
                            S.op("pe", MM(F0[:, g:g + 1], ROT[:, g, :], Xs[pb][:, N - 1:N], True, True), reads=[r_ROT, r_Xs[pb]], writes=[rF0])
                            S.op("act", ACT(s0[:, 1 - par, g:g + 1], F0[:, g:g + 1], AF.Copy), reads=[rF0], writes=[r_s0[1 - par]])
                        S.op("pool", TT(P1s[pb][:, 0:N], Xs[pb][:, 0:N], COSg[:, g, 0:N], ALU.mult), reads=[r_Xs[pb], r_COSg[g]], writes=[r_P1s[pb]])
                        S.op("dve", TT(P2s[pb][:, 0:N], Xs[pb][:, 0:N], SINg[:, g, 0:N], ALU.mult), reads=[r_Xs[pb], r_SINg[g]], writes=[r_P2s[pb]])
                        prev = (g, pb)
                        yield
                    emit_y(*prev)
                    S.op("pe", MM(F3[:, 0:N], diagD[:], uT[hb][:, 0:N], False, True), reads=[r_diagD, r_uT[hb]], writes=[rF3])
                    yield
                    S.op("act", ACT(ybuf[yb][:, 0:N], F3[:, 0:N], AF.Gelu), reads=[rF3], writes=[r_ybuf[yb]])
                    lo = max(tok0, NMETA); hi = min(tok0 + N, L)
                    out_events.extend(xchg_write(128, lo - NMETA, hi - NMETA, ybuf[yb], lo - tok0, "yo%d" % yb, r_ybuf[yb]))
                    yield

                def attention(ti, pump):
                    nb = 4 if ti < 16 else 1
                    tok0 = 512 * ti
                    hb = ti % 2
                    nqt = 2 if nb == 4 else 1
                    nbq = 2 if nb == 4 else 1
                    NQ = 128 * nbq
                    QTv = QT[hb]
                    for qi in range(nqt):
                        c = 2 * ti + qi
                        q0 = 256 * qi
                        nkb = 2 * c + nbq

                        def qk(kb):
                            j = kb - 2 * c
                            qs = 128 * j if j > 0 else 0
                            kt = kb // 4
                            sbk = kb % 2
                            pbk = kb % 3
                            Sv = Sb[sbk][:, 0:2 * NQ].rearrange("p (h q) -> p h q", h=2)
                            Pv = Pb[pbk][:].rearrange("p h q -> p (h q)")[:, 0:2 * NQ].rearrange("p (h q) -> p h q", h=2)
                            S.group("pe", [MM(Sv[:, 0, qs:NQ], KTa[:, 128 * kb:128 * (kb + 1)], QTv[:, q0 + qs:q0 + NQ], True, True),
                                           MM(Sv[:, 1, qs:NQ], KTb[:, 128 * kb:128 * (kb + 1)], QTv[:, q0 + qs:q0 + NQ], True, True)],
                                    reads=[r_KT[kt], r_QT[hb]], writes=[rSb[sbk]])
                            S.op("act", ACT(Pv[:, :, qs:NQ], Sv[:, :, qs:NQ], AF.Exp, scale=0.125), reads=[rSb[sbk]], writes=[r_Pb[pbk]])
                            if j >= 0:
                                S.op("pool", TT(Pv[:, :, qs:qs + 128], Pv[:, :, qs:qs + 128],
                                                tri_b[:].unsqueeze(1).to_broadcast([128, 2, 128]), ALU.mult),
                                     reads=[r_Pb[pbk], r_tri], writes=[r_Pb[pbk]])

                        def pv(kb):
                            j = kb - 2 * c
                            qs = 128 * j if j > 0 else 0
                            kt = kb // 4
                            pbk = kb % 3
                            Pv = Pb[pbk][:].rearrange("p h q -> p (h q)")[:, 0:2 * NQ].rearrange("p (h q) -> p h q", h=2)
                            Ov = Ob[:, 0:2 * NQ].rearrange("p (h q) -> p h q", h=2)
                            Lv = Lb[:, 0:2 * NQ].rearrange("p (h q) -> p h q", h=2)
                            first = (kb == 0); lastk = (kb == nkb - 1)
                            if qs == 0:
                                Pf = Pb[pbk][:].rearrange("p h q -> p (h q)")[:, 0:2 * NQ]
                                S.op("pe", MM(Ob[:, 0:2 * NQ], V[:, kb, :], Pf, first, lastk), reads=[r_V[kt], r_Pb[pbk]], writes=[rOb])
                                S.op("pe", MM(Lb[:, 0:2 * NQ], ones_b[:], Pf, first, lastk), reads=[r_ones, r_Pb[pbk]], writes=[rLb])
                            else:
                                assert not first
                                S.group("pe", [MM(Ov[:, 0, qs:NQ], V[:, kb, :], Pv[:, 0, qs:NQ], False, False),
                                               MM(Ov[:, 1, qs:NQ], V[:, kb, :], Pv[:, 1, qs:NQ], False, lastk)],
                                        reads=[r_V[kt], r_Pb[pbk]], writes=[rOb])
                                S.group("pe", [MM(Lv[:, 0, qs:NQ], ones_b[:], Pv[:, 0, qs:NQ], False, False),
                                               MM(Lv[:, 1, qs:NQ], ones_b[:], Pv[:, 1, qs:NQ], False, lastk)],
                                        reads=[r_ones, r_Pb[pbk]], writes=[rLb])

                        qk(0)
                        for kb in range(nkb):
                            if kb + 1 < nkb:
                                qk(kb + 1)
                            pv(kb)
                            pump()
                        Ov = Ob[:, 0:2 * NQ].rearrange("p (h q) -> p h q", h=2)
                        Lv = Lb[:, 0:2 * NQ].rearrange("p (h q) -> p h q", h=2)
                        S.op("dve", RECIP(rL[:, :, 0:NQ], Lv[:, :, 0:NQ]), reads=[rLb], writes=[r_rL])
                        S.op("dve", TT(On[:, :, 0:NQ], Ov[:, :, 0:NQ], rL[:, :, 0:NQ], ALU.mult), reads=[rOb, r_rL], writes=[r_On])
                        S.op("dve", STT(od[:, 0:NQ], On[:, 1, 0:NQ], neglam[:, 0:1], On[:, 0, 0:NQ], ALU.mult, ALU.add),
                             reads=[r_On, r_neglam], writes=[r_od])
                        S.op("act", ACT(osq[:, 0:NQ], od[:, 0:NQ], AF.Square), reads=[r_od], writes=[r_osq])
                        sbk = nkb % 2
                        S.op("pe", MM(Sb[sbk][:, 0:NQ], ones_b[:], osq[:, 0:NQ], True, True), reads=[r_ones, r_osq], writes=[rSb[sbk]])
                        S.op("act", ACT(orstd[:, 0:NQ], Sb[sbk][:, 0:NQ], AF.Ln, scale=1.0 / 128, bias=epsc[:, 0:1]),
                             reads=[rSb[sbk], r_eps], writes=[r_orstd])
                        S.op("act", ACT(orstd[:, 0:NQ], orstd[:, 0:NQ], AF.Exp, scale=-0.5), reads=[r_orstd], writes=[r_orstd])
                        ob = c % 2
                        S.op("dve", STT(obuf[ob][:, 0:NQ], od[:, 0:NQ], subg[:, 0:1], orstd[:, 0:NQ], ALU.mult, ALU.mult),
                             reads=[r_od, r_subg, r_orstd], writes=[r_obuf[ob]])
                        qt0 = tok0 + q0
                        lo = max(qt0, NMETA); hi = min(qt0 + NQ, L)
                        out_events.extend(xchg_write(0, lo - NMETA, hi - NMETA, obuf[ob], lo - qt0, "oo%d" % ob, r_obuf[ob]))

                for _ in front_gen(0):
                    pass
                for ti in range(nt1):
                    nxt = front_gen(ti + 1) if ti + 1 < nt1 else None
                    nbq_t = 2 if ti < 16 else 1
                    total_kb = sum(2 * (2 * ti + qi) + nbq_t for qi in range(2 if ti < 16 else 1))
                    state = {"steps_left": FRONT_STEPS if nxt is not None else 0, "kb_left": total_kb, "gen": nxt}

                    def pump(state=state):
                        if state["gen"] is None:
                            return
                        kbl = max(state["kb_left"], 1)
                        n = -(-state["steps_left"] // kbl)
                        state["kb_left"] -= 1
                        for _ in range(n):
                            try:
                                next(state["gen"])
                                state["steps_left"] -= 1
                            except StopIteration:
                                state["gen"] = None
                                state["steps_left"] = 0
                                return
                    attention(ti, pump)
                    if state["gen"] is not None:
                        for _ in state["gen"]:
                            pass
                    if use_cc and ti >= 4 and ti % 4 == 0:
                        qd = ti // 4 - 1
                        S.wait_all("pool", q_events[qd])
                        S.raw("pool", (lambda qd: (lambda e: e.collective_compute(
                            "AllGather", ALU.bypass, replica_groups=[[0, 1, 2, 3], [4, 5, 6, 7]],
                            ins=[xin_t[qd].ap().opt()], outs=[xout_t.ap()[1024 * qd:1024 * (qd + 1), :].opt()]).then_inc(cc_sem)))(qd))

            return out_events

        out_events = _phase1() if nt1 > 0 else []

        if debug:
            print('MARKS', S.marks, S.nops)
            S.limit = 10**9
            S.wait_all("pool", out_events)
            evs = [S.dma("pool", dbg_d[:, TOK2 * q:TOK2 * (q + 1)], xin[q], "dbg") for q in range(4)]
            S.wait_all("pool", evs)
        if do_phase2:
            S.wait_all("pool", out_events)
            if p2mode == "nogather":
                evd = S.dma("pool", xout[0:1024, :], xout_dbg_d, "xdbg")
                S.wait_all("pool", [evd])
            else:
                S.raw("pool", lambda e: e.wait_ge(cc_sem, 4))
            S.cnt["pool"] += 1
            gather_ev = (S.sem["pool"], S.cnt["pool"], "pool")
            S.raw("pool", lambda e: e.sem_inc(S.sem["pool"], 1))
            lasts = [S.last[e] for e in ("pe", "act", "dve", "pool") if S.last[e] is not None] + [gather_ev]
            for e in ("pe", "act", "dve", "sp", "pool"):
                S.wait_all(e, lasts)

            with ExitStack() as p2:
                def sb(name, shape, dt=F32):
                    return p2.enter_context(nc.sbuf_tensor("b_" + name, list(shape), dt))

                Wg = sb("Wg", [128, 8, DFF], BF16); r_Wg = Res("Wg")
                Wu = sb("Wu", [128, 8, DFF], BF16); r_Wu = Res("Wu")
                Wd = sb("Wd", [128, NF, D], BF16); r_Wd = Res("Wd")
                Wo = sb("Wo", [128, 8, D], BF16); r_Wo = Res("Wo")
                Wgl = sb("Wgl", [128, 4, 512], BF16); r_Wgl = Res("Wgl")
                S.dma("pool", Wgl[:], wglu_d.rearrange("(k p) c -> p k c", p=128), "w0", writes=[r_Wgl])
                S.dma("pool", Wo[:], wout_d.rearrange("(k p) c -> p k c", p=128), "w1", writes=[r_Wo])
                r_Wgc = [Res("Wg%d" % f) for f in range(NF)]
                r_Wuc = [Res("Wu%d" % f) for f in range(NF)]
                r_Wdc = [Res("Wd%d" % f) for f in range(NF)]
                for f in range(0, NF, 2):
                    S.dma("pool", Wg[:, :, 128 * f:128 * (f + 2)], wgate_d[:, 128 * f:128 * (f + 2)].rearrange("(k p) c -> p k c", p=128),
                          "wg%d" % (f // 2), writes=[r_Wgc[f], r_Wgc[f + 1]])
                    S.dma("pool", Wu[:, :, 128 * f:128 * (f + 2)], wup_d[:, 128 * f:128 * (f + 2)].rearrange("(k p) c -> p k c", p=128),
                          "wu%d" % (f // 2), writes=[r_Wuc[f], r_Wuc[f + 1]])
                for f in range(0, NF, 2):
                    S.dma("pool", Wd[:, f:f + 2, :], wdown_d[128 * f:128 * (f + 2), :].rearrange("(k p) c -> p k c", p=128),
                          "wd%d" % (f // 2), writes=[r_Wdc[f], r_Wdc[f + 1]])
                ident2 = sb("ident2", [128, 128], BF16); r_id2 = Res("ident2")
                ones2 = sb("ones2", [128, 128], BF16); r_ones2 = Res("ones2")
                eps2 = sb("eps2", [128, 1]); r_eps2 = Res("eps2")
                bglu = sb("bglu", [128, 4]); r_bglu = Res("bglu")
                gssm = sb("gssm", [128, 4]); r_gssm = Res("gssm")
                gpost_b = sb("gpost_b", [128, D]); r_gpost = Res("gpost")
                gffn_b = sb("gffn_b", [128, D]); r_gffn = Res("gffn")
                gpffn_b = sb("gpffn_b", [128, D]); r_gpffn = Res("gpffn")
                S.dma("pool", ident2[:], ident_d, "k0", writes=[r_id2])
                S.op("dve", MEMSET(ones2[:], 1.0), writes=[r_ones2])
                S.op("dve", MEMSET(eps2[:], EPS), writes=[r_eps2])
                S.dma("sp", bglu[:], bglu_d, "k1", writes=[r_bglu])
                S.dma("sp", gssm[:], gssm_d, "k2", writes=[r_gssm])
                S.dma("sp", gpost_b[:], gpost_d.partition_broadcast(128), "k3", writes=[r_gpost])
                S.dma("sp", gffn_b[:], gffn_d.partition_broadcast(128), "k4", writes=[r_gffn])
                S.dma("sp", gpffn_b[:], gpffn_d.partition_broadcast(128), "k5", writes=[r_gpffn])

                cT = [sb("cT%d" % i, [128, 8, T2], BF16) for i in range(1)]; r_cT = [Res("cT%d" % i) for i in range(1)]
                sig = sb("sig", [128, T2], BF16); r_sig = Res("sig")
                yg = sb("yg", [128, 4, T2]); r_yg = Res("yg")
                sq2 = sb("sq2", [128, T2], BF16); r_sq2 = Res("sq2")
                rsy = sb("rsy", [128, T2]); r_rsy = Res("rsy")
                yn = sb("yn", [128, 4, T2], BF16); r_yn = Res("yn")
                hres = [sb("hres%d" % i, [128, D]) for i in range(2)]; r_hres = [Res("hres%d" % i) for i in range(2)]
                ssvA = sb("ssvA", [128, 8]); r_ssvA = Res("ssvA")
                ssvB = sb("ssvB", [128, 4]); r_ssvB = Res("ssvB")
                tmpA = sb("tmpA", [128, 512]); r_tmpA = Res("tmpA")
                tmpB = tmpA; r_tmpB = r_tmpA
                hs = [sb("hs%d" % i, [128, D], BF16) for i in range(2)]; r_hs = [Res("hs%d" % i) for i in range(2)]
                junkA = sb("junkA", [128, 512], BF16); r_junkA = Res("junkA")
                junkB = junkA; r_junkB = r_junkA
                hT2 = sb("hT2", [128, 8, T2], BF16); r_hT2 = Res("hT2")
                sg = [sb("sg%d" % i, [128, T2], BF16) for i in range(2)]; r_sg = [Res("sg%d" % i) for i in range(2)]
                aT = sb("aT", [128, NF, T2], BF16); r_aT = Res("aT")

                pid = [None]
                B = banks
                rB = rbank
                final_events = []

                def h1_gen(tt):
                    cb = 0

                    def _ld(e, tt=tt):
                        if pid[0] is None:
                            pid[0] = e.partition_id() % 4
                        return e.dma_start(out=cT[cb][:], in_=xout[bass.ds(pid[0] * 1024, 1024), tt * T2:(tt + 1) * T2].rearrange("(k p) t -> p k t", p=128))
                    slot = "g%d" % cb
                    if slot not in S.dma_sems:
                        S.dma_sems[slot] = top.enter_context(nc.semaphore("d_" + slot))
                        S.dma_cnt[slot] = 0
                    S._waits("sp", S._deps_for(None, [r_cT[cb]], None))
                    S.dma_cnt[slot] += 16
                    ev = (S.dma_sems[slot], S.dma_cnt[slot], "dma")
                    S.raw("sp", (lambda sem: (lambda e, f=_ld: f(e).then_inc(sem, 16)))(S.dma_sems[slot]))
                    S._commit(ev, None, [r_cT[cb]])
                    yield None
                    for m in range(4):
                        S.group("pe", [MM(B[0][:, 0:T2], Wgl[:, r, 128 * m:128 * (m + 1)], cT[cb][:, 2 * r + 1, :], r == 0, r == 3) for r in range(4)],
                                reads=[r_Wgl, r_cT[cb]], writes=[rB[0]])
                        yield None
                        S.op("act", ACT(sig[:], B[0][:, 0:T2], AF.Sigmoid, bias=bglu[:, m:m + 1]), reads=[rB[0], r_bglu], writes=[r_sig])
                        S.op("dve", TT(yg[:, m, :], cT[cb][:, 2 * m + 1, :], sig[:], ALU.mult), reads=[r_cT[cb], r_sig], writes=[r_yg])
                        S.op("act", ACT(sq2[:], yg[:, m, :], AF.Square), reads=[r_yg], writes=[r_sq2])
                        yield None
                        S.op("pe", MM(B[1][:, 0:T2], ones2[:], sq2[:], m == 0, m == 3), reads=[r_ones2, r_sq2], writes=[rB[1]])
                    yield None
                    S.op("act", ACT(rsy[:], B[1][:, 0:T2], AF.Ln, scale=1.0 / 512, bias=eps2[:, 0:1]), reads=[rB[1], r_eps2], writes=[r_rsy])
                    S.op("act", ACT(rsy[:], rsy[:], AF.Exp, scale=-0.5), reads=[r_rsy], writes=[r_rsy])
                    for m in range(4):
                        S.op("dve", STT(yn[:, m, :], yg[:, m, :], gssm[:, m:m + 1], rsy[:], ALU.mult, ALU.mult),
                             reads=[r_yg, r_gssm, r_rsy], writes=[r_yn])
                    yield None
                    for s in range(2):
                        hr = hres[s]; r_hr = r_hres[s]
                        c0 = 4 * s
                        for hf in range(2):
                            bk = B[2 + hf]; rbk = rB[2 + hf]
                            fns = []
                            for k in range(8):
                                src = cT[cb][:, k, 128 * s:128 * (s + 1)] if k % 2 == 0 else yn[:, k // 2, 128 * s:128 * (s + 1)]
                                fns.append(MM(bk[:, :], src, Wo[:, k, 512 * hf:512 * (hf + 1)], k == 0, k == 7))
                            S.group("pe", fns, reads=[r_cT[cb], r_yn, r_Wo], writes=[rbk])
                            yield None
                            S.op("act", ACT(junkA[:], bk[:, :], AF.Square, accum_out=ssvA[:, c0 + hf:c0 + hf + 1]), reads=[rbk], writes=[r_junkA, r_ssvA])
                        yield None
                        S.op("dve", TT(ssvA[:, c0 + 2:c0 + 3], ssvA[:, c0:c0 + 1], ssvA[:, c0 + 1:c0 + 2], ALU.add), reads=[r_ssvA], writes=[r_ssvA])
                        S.op("act", ACT(ssvA[:, c0 + 2:c0 + 3], ssvA[:, c0 + 2:c0 + 3], AF.Ln, scale=1.0 / D, bias=eps2[:, 0:1]), reads=[r_ssvA, r_eps2], writes=[r_ssvA])
                        S.op("act", ACT(ssvA[:, c0 + 2:c0 + 3], ssvA[:, c0 + 2:c0 + 3], AF.Exp, scale=-0.5), reads=[r_ssvA], writes=[r_ssvA])
                        yield "BAR%d" % s
                        row0 = tt * T2 + 128 * s
                        S.dma("sp", hr[:], xres_d[row0:row0 + 128, :], "xr%d" % s, writes=[r_hr])
                        yield None
                        for hf in range(2):
                            bk = B[2 + hf]; rbk = rB[2 + hf]
                            S.op("dve", STT(tmpA[:], bk[:, :], ssvA[:, c0 + 2:c0 + 3], gpost_b[:, 512 * hf:512 * (hf + 1)], ALU.mult, ALU.mult),
                                 reads=[rbk, r_ssvA, r_gpost], writes=[r_tmpA])
                            S.op("dve", TT(hr[:, 512 * hf:512 * (hf + 1)], tmpA[:], hr[:, 512 * hf:512 * (hf + 1)], ALU.add),
                                 reads=[r_tmpA, r_hr], writes=[r_hr])
                            S.op("act", ACT(junkA[:], hr[:, 512 * hf:512 * (hf + 1)], AF.Square, accum_out=ssvA[:, c0 + hf:c0 + hf + 1]),
                                 reads=[r_hr], writes=[r_junkA, r_ssvA])
                        yield None
                        S.op("dve", TT(ssvA[:, c0 + 3:c0 + 4], ssvA[:, c0:c0 + 1], ssvA[:, c0 + 1:c0 + 2], ALU.add), reads=[r_ssvA], writes=[r_ssvA])
                        S.op("act", ACT(ssvA[:, c0 + 3:c0 + 4], ssvA[:, c0 + 3:c0 + 4], AF.Ln, scale=1.0 / D, bias=eps2[:, 0:1]), reads=[r_ssvA, r_eps2], writes=[r_ssvA])
                        S.op("act", ACT(ssvA[:, c0 + 3:c0 + 4], ssvA[:, c0 + 3:c0 + 4], AF.Exp, scale=-0.5), reads=[r_ssvA], writes=[r_ssvA])
                        yield None
                        S.op("dve", STT(hs[s][:], hr[:], ssvA[:, c0 + 3:c0 + 4], gffn_b[:], ALU.mult, ALU.mult), reads=[r_hr, r_ssvA, r_gffn], writes=[r_hs[s]])
                        yield None
                        B1b = B[1][:].bitcast(BF16)
                        S.group("pe", [TR(B1b[:, 128 * k:128 * (k + 1)], hs[s][:, 128 * k:128 * (k + 1)], ident2[:]) for k in range(8)],
                                reads=[r_hs[s], r_id2], writes=[rB[1]])
                        yield None
                        S.op("act", ACT(hT2[:, :, 128 * s:128 * (s + 1)], B1b.rearrange("p (k t) -> p k t", k=8), AF.Copy),
                             reads=[rB[1]], writes=[r_hT2])
                        yield None

                class Pump:
                    def __init__(self, gen):
                        self.gen = gen
                        self.released = set()
                        self.pending = None

                    def release(self, name):
                        self.released.add(name)

                    def step(self, n):
                        for _ in range(n):
                            if self.gen is None:
                                return
                            if self.pending is not None:
                                if self.pending not in self.released:
                                    return
                                self.pending = None
                            try:
                                r = next(self.gen)
                            except StopIteration:
                                self.gen = None
                                return
                            if r is not None:
                                self.pending = r

                    def flush(self):
                        while self.gen is not None:
                            if self.pending is not None:
                                assert self.pending in self.released, self.pending
                            self.step(1)

                def h2(tt, P):
                    for f in range(NF):
                        gb2 = B[4 + (f % 2)]; rgb2 = rB[4 + (f % 2)]
                        gv = gb2[:].rearrange("p (h q) -> p h q", h=2)
                        S.group("pe", [MM(gv[:, 0, :], Wg[:, k, 128 * f:128 * (f + 1)], hT2[:, k, :], k == 0, k == 7) for k in range(8)] +
                                      [MM(gv[:, 1, :], Wu[:, k, 128 * f:128 * (f + 1)], hT2[:, k, :], k == 0, k == 7) for k in range(8)],
                                reads=[r_Wgc[f], r_Wuc[f], r_hT2], writes=[rgb2])
                        S.op("act", ACT(sg[f % 2][:], gv[:, 0, :], AF.Silu), reads=[rgb2], writes=[r_sg[f % 2]])
                        S.op("dve", TT(aT[:, f, :], gv[:, 1, :], sg[f % 2][:], ALU.mult), reads=[rgb2, r_sg[f % 2]], writes=[r_aT])
                        P.step(2)
                    for s in range(2):
                        row0 = tt * T2 + 128 * s
                        hr = hres[s]; r_hr = r_hres[s]
                        for hf in range(2):
                            bk = B[6 + hf]; rbk = rB[6 + hf]
                            S.group("pe", [MM(bk[:, :], aT[:, f, 128 * s:128 * (s + 1)], Wd[:, f, 512 * hf:512 * (hf + 1)], f == 0, f == NF - 1) for f in range(NF)],
                                    reads=[r_aT] + r_Wdc, writes=[rbk])
                            S.op("act", ACT(junkB[:], bk[:, :], AF.Square, accum_out=ssvB[:, hf:hf + 1]), reads=[rbk], writes=[r_junkB, r_ssvB])
                            P.step(6)
                        S.op("dve", TT(ssvB[:, 2:3], ssvB[:, 0:1], ssvB[:, 1:2], ALU.add), reads=[r_ssvB], writes=[r_ssvB])
                        S.op("act", ACT(ssvB[:, 2:3], ssvB[:, 2:3], AF.Ln, scale=1.0 / D, bias=eps2[:, 0:1]), reads=[r_ssvB, r_eps2], writes=[r_ssvB])
                        S.op("act", ACT(ssvB[:, 2:3], ssvB[:, 2:3], AF.Exp, scale=-0.5), reads=[r_ssvB], writes=[r_ssvB])
                        for hf in range(2):
                            bk = B[6 + hf]; rbk = rB[6 + hf]
                            S.op("dve", STT(tmpB[:], bk[:, :], ssvB[:, 2:3], gpffn_b[:, 512 * hf:512 * (hf + 1)], ALU.mult, ALU.mult),
                                 reads=[rbk, r_ssvB, r_gpffn], writes=[r_tmpB])
                            S.op("dve", TT(hr[:, 512 * hf:512 * (hf + 1)], tmpB[:], hr[:, 512 * hf:512 * (hf + 1)], ALU.add),
                                 reads=[r_tmpB, r_hr], writes=[r_hr])
                        final_events.append(S.dma("sp", out_d[row0:row0 + 128, :], hr[:], "fo%d" % s, reads=[r_hr]))
                        P.release("BAR%d" % s)

                P0 = Pump(h1_gen(0)); P0.release("BAR0"); P0.release("BAR1"); P0.flush()
                for tt in range(NT2):
                    P = Pump(h1_gen(tt + 1) if tt + 1 < NT2 else None)
                    h2(tt, P)
                    P.flush()

                S.wait_all("sp", final_events)

        with nc.Block() as block:
            S.run(block)
    return nc


def _consts():
    invf8 = (500000.0 ** (-np.arange(0, 16, 2, dtype=np.float32) / np.float32(16))).astype(np.float32)
    invf = np.zeros((128, 1), np.float32)
    for base in (0, 64):
        for i in range(16):
            invf[base + i, 0] = invf8[i % 8]
    R = np.zeros((128, 128), np.float32)
    for base in (0, 64):
        for i in range(8):
            R[base + i, base + i + 8] = -1.0
            R[base + i + 8, base + i] = 1.0
    rmatT = np.ascontiguousarray(R.T)
    ident = np.eye(128, dtype=np.float32)
    swapm = np.zeros((128, 128), np.float32)
    for k in range(128):
        swapm[k, (k + 64) % 128] = 1.0
    sgn = np.ones((128, 1), np.float32); sgn[64:] = -1.0
    tri = (np.arange(128)[:, None] <= np.arange(128)[None, :]).astype(np.float32)
    gmask = (np.arange(128)[:, None] // 16 == np.arange(8)[None, :]).astype(np.float32)
    return dict(invf=invf, rmatT=rmatT, ident=ident, swapm=swapm, sgn=sgn, tri=tri, gmask=gmask,
                iota512=np.arange(512, dtype=np.float32), tile_iota=(512.0 * np.arange(NT1)).astype(np.float32))


def make_in_maps(inp):
    c = _consts()
    f = lambda a: np.ascontiguousarray(np.asarray(a, dtype=np.float32))
    w_in = f(inp["w_in"])[0]
    maps = []
    w_out = f(inp["w_out"])[0]
    perm = []
    for r in range(4):
        perm += list(range(128 * r, 128 * r + 128)) + list(range(512 + 128 * r, 512 + 128 * r + 128))
    w_out_p = np.ascontiguousarray(w_out[perm, :])
    for core in range(8):
        b, h = core // 4, core % 4
        cols = (list(range(64 * h, 64 * h + 64)) + list(range(256 + 64 * h, 256 + 64 * h + 64)) +
                list(range(512 + 64 * h, 512 + 64 * h + 64)) + list(range(768 + 64 * h, 768 + 64 * h + 64)) +
                list(range(1024 + 128 * h, 1024 + 128 * h + 128)) + list(range(1536 + 128 * h, 1536 + 128 * h + 128)))
        gs = slice(8 * h, 8 * h + 8)
        m = dict(c)
        m["x"] = f(inp["x"][b])
        m["meta"] = f(inp["meta"])
        m["win"] = np.ascontiguousarray(w_in[:, cols])
        m["gpre"] = np.ascontiguousarray(f(inp["pre_mix_g"])[0].reshape(8, 128).T)
        m["lamv"] = np.stack([f(inp["lambda_q1"])[0], f(inp["lambda_k1"])[0], f(inp["lambda_q2"])[0], f(inp["lambda_k2"])[0]])
        m["subg"] = f(inp["subln_g"])[0].reshape(128, 1)
        m["areT"] = np.ascontiguousarray(f(inp["a_re"])[0, gs].T)
        m["aimT"] = np.ascontiguousarray(f(inp["a_im"])[0, gs].T)
        m["logdt"] = f(inp["log_dt"])[0, gs]
        m["bre"] = f(inp["b_re"])[0, gs]
        m["bim"] = f(inp["b_im"])[0, gs]
        m["creT"] = np.ascontiguousarray(f(inp["c_re"])[0, gs].transpose(2, 0, 1).reshape(64, 128))
        m["cimT"] = np.ascontiguousarray(f(inp["c_im"])[0, gs].transpose(2, 0, 1).reshape(64, 128))
        m["dskip"] = f(inp["d_skip"])[0, 128 * h:128 * h + 128].reshape(128, 1)
        m["wglu"] = f(inp["w_glu"])[0]
        m["bglu"] = np.ascontiguousarray(f(inp["b_glu"])[0].reshape(4, 128).T)
        m["gssm"] = np.ascontiguousarray(f(inp["ssm_out_g"])[0].reshape(4, 128).T)
        m["wout"] = w_out_p
        m["gpost"] = f(inp["post_mix_g"])[0]
        m["gffn"] = f(inp["pre_ffn_g"])[0]
        m["wgate"] = f(inp["w_gate"])[0]
        m["wup"] = f(inp["w_up"])[0]
        m["wdown"] = f(inp["w_down"])[0]
        m["gpffn"] = f(inp["post_ffn_g"])[0]
        m["xres"] = f(inp["x"][b, TOK2 * h:TOK2 * (h + 1)])
        maps.append(m)
    return maps


def kernel(**inputs):
    nc = build_program()
    maps = make_in_maps(inputs)
    res = run_bass_kernel_spmd(nc, maps, core_ids=list(range(8)))
    out = np.zeros((2, SEQ, D), np.float32)
    for core in range(8):
        b, h = core // 4, core % 4
        out[b, TOK2 * h:TOK2 * (h + 1)] = res.results[core]["out"]
    return out
```

```python
import math
import os
from contextlib import ExitStack

import numpy as np
import concourse.bass as bass
import concourse.mybir as mybir
from concourse.bass_utils import run_bass_kernel_spmd

F32 = mybir.dt.float32
BF16 = mybir.dt.bfloat16
I32 = mybir.dt.int32
AF = mybir.ActivationFunctionType
ALU = mybir.AluOpType

D = 1024
SEQ = 8192
NMETA = 16
L = SEQ + NMETA
LP = 8320
NBLK = 65
NT1 = 17
DFF = 2816
NF = 22
EPS = 1e-6
LAM_INIT = 0.8 - 0.6 * math.exp(0.0)
TWO_PI = 2.0 * math.pi
C1 = 6.28125
C2 = TWO_PI - C1
PI_LO = 3.141592
TOK2 = 2048
T2 = 256
NT2 = TOK2 // T2
FRONT_STEPS = 48


class Res:
    __slots__ = ("name", "w", "r", "excl")

    def __init__(self, name, excl=False):
        self.name = name
        self.w = None
        self.r = []
        self.excl = excl


class Sched:
    ENGS = ("pe", "act", "dve", "pool", "sp")

    def __init__(self, nc, stack):
        self.nc = nc
        self.ops = {e: [] for e in self.ENGS}
        self.sem = {e: stack.enter_context(nc.semaphore("s_" + e)) for e in self.ENGS}
        self.cnt = {e: 0 for e in self.ENGS}
        self.waited = {e: {} for e in self.ENGS}
        self.stack = stack
        self.dma_sems = {}
        self.dma_cnt = {}
        self.last = {e: None for e in self.ENGS}
        self.limit = int(os.environ.get("K_LIMIT", "1000000000"))
        self.nops = 0
        self.marks = []

    def _waits(self, eng, deps):
        need = {}
        for ev in deps:
            if ev is None:
                continue
            sem, val, src = ev
            key = sem.name
            if self.waited[eng].get(key, 0) >= val:
                continue
            if key not in need or need[key][1] < val:
                need[key] = (sem, val)
        for key, (sem, val) in need.items():
            self.waited[eng][key] = val
            self.ops[eng].append(("wait", sem, val))

    @staticmethod
    def _deps_for(reads, writes, extra):
        deps = list(extra or [])
        for r in reads or []:
            if r.w is not None:
                deps.append(r.w)
            if r.excl:
                deps.extend(r.r)
        for w in writes or []:
            if w.w is not None:
                deps.append(w.w)
            deps.extend(w.r)
        return deps

    @staticmethod
    def _commit(ev, reads, writes):
        for r in reads or []:
            r.r.append(ev)
        for w in writes or []:
            w.w = ev
            w.r = []

    def op(self, eng, fn, reads=None, writes=None, extra=None):
        return self.group(eng, [fn], reads, writes, extra)

    def group(self, eng, fns, reads=None, writes=None, extra=None):
        self.nops += 1
        if self.nops > self.limit:
            return None
        self._waits(eng, self._deps_for(reads, writes, extra))
        self.cnt[eng] += 1
        ev = (self.sem[eng], self.cnt[eng], eng)
        for f in fns[:-1]:
            self.ops[eng].append(("op", f, False))
        self.ops[eng].append(("op", fns[-1], True))
        self._commit(ev, reads, writes)
        self.last[eng] = ev
        return ev

    def dma(self, eng, out, in_, slot, reads=None, writes=None, extra=None):
        if slot not in self.dma_sems:
            self.dma_sems[slot] = self.stack.enter_context(self.nc.semaphore("d_" + slot))
            self.dma_cnt[slot] = 0
        self.nops += 1
        if self.nops > self.limit:
            return None
        self._waits(eng, self._deps_for(reads, writes, extra))
        self.dma_cnt[slot] += 16
        sem = self.dma_sems[slot]
        ev = (sem, self.dma_cnt[slot], "dma")
        self.ops[eng].append(("dma", out, in_, sem))
        self._commit(ev, reads, writes)
        return ev

    def raw(self, eng, fn):
        self.ops[eng].append(("raw", fn))

    def wait_all(self, eng, events):
        self._waits(eng, events)

    def run(self, block):
        sched = self

        def replay(engname, e):
            for item in sched.ops[engname]:
                if item[0] == "wait":
                    e.wait_ge(item[1], item[2])
                elif item[0] == "op":
                    ins = item[1](e)
                    if item[2]:
                        ins.then_inc(sched.sem[engname], 1)
                elif item[0] == "dma":
                    e.dma_start(out=item[1], in_=item[2]).then_inc(item[3], 16)
                elif item[0] == "raw":
                    item[1](e)

        @block.tensor
        def _(e):
            replay("pe", e)

        @block.scalar
        def _(e):
            replay("act", e)

        @block.vector
        def _(e):
            replay("dve", e)

        @block.gpsimd
        def _(e):
            replay("pool", e)

        @block.sync
        def _(e):
            replay("sp", e)


def MM(out, lhsT, rhs, start, stop):
    return lambda e: e.matmul(out, lhsT, rhs, start=start, stop=stop)


def TR(out, in_, ident):
    return lambda e: e.transpose(out, in_, ident)


def ACT(out, in_, func, **kw):
    return lambda e: e.activation(out=out, in_=in_, func=func, **kw)


def TT(out, a, b, op):
    return lambda e: e.tensor_tensor(out=out, in0=a, in1=b, op=op)


def TS(out, a, s1, op0, s2=None, op1=None):
    if op1 is None:
        return lambda e: e.tensor_scalar(out=out, in0=a, scalar1=s1, scalar2=None, op0=op0)
    return lambda e: e.tensor_scalar(out=out, in0=a, scalar1=s1, scalar2=s2, op0=op0, op1=op1)


def STT(out, a, s, b, op0, op1):
    return lambda e: e.scalar_tensor_tensor(out=out, in0=a, scalar=s, in1=b, op0=op0, op1=op1)


def CP(out, in_):
    return lambda e: e.tensor_copy(out=out, in_=in_)


def MEMSET(ap, v):
    return lambda e: e.memset(ap, v)


def RECIP(out, in_):
    return lambda e: e.reciprocal(out=out, in_=in_)


def SCAN(out, d0, d1, init):
    return lambda e: e.tensor_tensor_scan(out=out, data0=d0, data1=d1, initial=init, op0=ALU.mult, op1=ALU.add)


def build_program(nt1=NT1, do_phase2=True, debug=False, stop=9, p2mode="full"):
    nc = bass.Bass("TRN2", target_bir_lowering=False)

    def din(name, shape, dt=F32):
        return nc.dram_tensor(name, list(shape), dt, kind="ExternalInput").ap()

    ident_d = din("ident", [128, 128])
    if nt1 > 0:
        x_d = din("x", [SEQ, D])
        meta_d = din("meta", [NMETA, D])
        win_d = din("win", [D, 512])
        gpre_d = din("gpre", [128, 8])
        invf_d = din("invf", [128, 1])
        iota512_d = din("iota512", [512])
        tile_iota_d = din("tile_iota", [NT1])
        rmatT_d = din("rmatT", [128, 128])
        swap_d = din("swapm", [128, 128])
        sgn_d = din("sgn", [128, 1])
        tri_d = din("tri", [128, 128])
        gmask_d = din("gmask", [128, 8])
        lam_d = din("lamv", [4, 64])
        subg_d = din("subg", [128, 1])
        areT_d = din("areT", [64, 8])
        aimT_d = din("aimT", [64, 8])
        logdt_d = din("logdt", [8])
        bre_d = din("bre", [8, 64, 16])
        bim_d = din("bim", [8, 64, 16])
        creT_d = din("creT", [64, 128])
        cimT_d = din("cimT", [64, 128])
        dskip_d = din("dskip", [128, 1])
    if do_phase2:
        wglu_d = din("wglu", [512, 512])
        bglu_d = din("bglu", [128, 4])
        gssm_d = din("gssm", [128, 4])
        wout_d = din("wout", [D, D])
        gpost_d = din("gpost", [D])
        gffn_d = din("gffn8", [128, 8])
        wgate_d = din("wgate", [D, DFF])
        wup_d = din("wup", [D, DFF])
        wdown_d = din("wdown", [DFF, D])
        gpffn_d = din("gpffn", [D])
        xres_d = din("xres", [TOK2, D])
    out_d = nc.dram_tensor("out", [TOK2, D], F32, kind="ExternalOutput").ap() if do_phase2 else None

    xout_dbg_d = din("xout_dbg", [1024, TOK2]) if p2mode == "nogather" else None
    xin_t = [nc.dram_tensor("xchg_in%d" % q, [256, TOK2], BF16) for q in range(4)]
    xout_t = nc.dram_tensor("xchg_out", [4 * 1024, TOK2], BF16)
    xin = [t.ap() for t in xin_t]
    xout = xout_t.ap()
    dbg_d = None
    if debug:
        dbg_d = nc.dram_tensor("dbg", [256, SEQ], F32, kind="ExternalOutput").ap()

    with ExitStack() as top:
        S = Sched(nc, top)

        def ps_bank(name):
            return top.enter_context(nc.psum_tensor(name, [128, 512], F32))

        use_cc = do_phase2 and p2mode == "full"
        cc_sem = top.enter_context(nc.semaphore("cc_sem")) if use_cc else None
        q_events = [[] for _ in range(4)]

        def xchg_write(row0, lo, hi, src, src_col0, slot, res):
            evs = []
            c = lo
            while c < hi:
                q = c // TOK2
                ce = min(hi, TOK2 * (q + 1))
                ev = S.dma("sp", xin[q][row0:row0 + 128, c - TOK2 * q:ce - TOK2 * q],
                           src[:, src_col0 + (c - lo):src_col0 + (ce - lo)], slot, reads=[res])
                q_events[q].append(ev)
                evs.append(ev)
                c = ce
            return evs

        banks = [ps_bank("bank%d" % i) for i in range(8)]
        rbank = [Res("bank%d" % i, excl=True) for i in range(8)]

        def _phase1():
            with ExitStack() as p1:
                def sb(name, shape, dt=F32):
                    return p1.enter_context(nc.sbuf_tensor("a_" + name, list(shape), dt))

                ident_f = sb("ident_f", [128, 128]); r_ident_f = Res("ident_f")
                ident_b = sb("ident_b", [128, 128], BF16); r_ident_b = Res("ident_b")
                rmatT_b = sb("rmatT_b", [128, 128], BF16); r_rmat = Res("rmat")
                swap_f = sb("swap_f", [128, 128]); r_swap = Res("swap")
                tri_b = sb("tri_b", [128, 128], BF16); r_tri = Res("tri")
                ones_b = sb("ones_b", [128, 128], BF16); r_ones = Res("ones")
                sgn = sb("sgn", [128, 1]); r_sgn = Res("sgn")
                gmask = sb("gmask", [128, 8]); r_gmask = Res("gmask")
                invf = sb("invf", [128, 1]); r_invf = Res("invf")
                iota512 = sb("iota512", [128, 512]); r_iota = Res("iota512")
                tile_iota = sb("tile_iota", [128, NT1]); r_tiota = Res("tile_iota")
                gpre = sb("gpre", [128, 8]); r_gpre = Res("gpre")
                subg = sb("subg", [128, 1]); r_subg = Res("subg")
                dskip = sb("dskip", [128, 1]); r_dskip = Res("dskip")
                epsc = sb("epsc", [128, 1]); r_eps = Res("eps")
                lamv = sb("lamv", [128, 4, 64]); r_lamv = Res("lamv")
                neglam = sb("neglam", [128, 1]); r_neglam = Res("neglam")

                S.dma("sp", ident_f[:], ident_d, "c0", writes=[r_ident_f])
                S.dma("pool", ident_b[:], ident_d, "c1", writes=[r_ident_b])
                S.dma("pool", rmatT_b[:], rmatT_d, "c2", writes=[r_rmat])
                S.dma("sp", swap_f[:], swap_d, "c3", writes=[r_swap])
                S.dma("pool", tri_b[:], tri_d, "c4", writes=[r_tri])
                S.dma("sp", sgn[:], sgn_d, "c5", writes=[r_sgn])
                S.dma("sp", gmask[:], gmask_d, "c6", writes=[r_gmask])
                S.dma("sp", invf[:], invf_d, "c7", writes=[r_invf])
                S.dma("sp", iota512[:], iota512_d.partition_broadcast(128), "c8", writes=[r_iota])
                S.dma("sp", tile_iota[:], tile_iota_d.partition_broadcast(128), "c9", writes=[r_tiota])
                S.dma("sp", gpre[:], gpre_d, "c10", writes=[r_gpre])
                S.dma("sp", subg[:], subg_d, "c11", writes=[r_subg])
                S.dma("sp", dskip[:], dskip_d, "c12", writes=[r_dskip])
                S.dma("sp", lamv[:].rearrange("p a b -> p (a b)"),
                      lam_d.rearrange("a b -> (a b)").partition_broadcast(128), "c13", writes=[r_lamv])
                S.op("dve", MEMSET(ones_b[:], 1.0), writes=[r_ones])
                S.op("dve", MEMSET(epsc[:], EPS), writes=[r_eps])

                scr = [sb("scr%d" % i, [128, 512]) for i in range(4)]
                r_scr = [Res("scr%d" % i) for i in range(4)]
                ki_t = sb("ki_t", [128, 512], I32); r_ki = Res("ki")

                def range_reduce(eng, out_ap, arg_ap, n, r_out, r_arg):
                    kf = scr[3][:, 0:n]
                    S.op(eng, TS(ki_t[:, 0:n], arg_ap, 1.0 / TWO_PI, ALU.mult), reads=[r_arg], writes=[r_ki])
                    S.op(eng, CP(kf, ki_t[:, 0:n]), reads=[r_ki], writes=[r_scr[3]])
                    S.op(eng, STT(out_ap, kf, -C1, arg_ap, ALU.mult, ALU.add), reads=[r_scr[3], r_arg], writes=[r_out])
                    S.op(eng, STT(out_ap, kf, -C2, out_ap, ALU.mult, ALU.add), reads=[r_scr[3], r_out], writes=[r_out])
                    S.op(eng, TS(out_ap, out_ap, PI_LO, ALU.min, -PI_LO, ALU.max), reads=[r_out], writes=[r_out])

                def sincos(arg_ap, n, r_arg, sin_out, cos_out, r_sin, r_cos):
                    red = scr[2][:, 0:n]
                    range_reduce("dve", red, arg_ap, n, r_scr[2], r_arg)
                    S.op("act", ACT(sin_out, red, AF.Sin), reads=[r_scr[2]], writes=[r_sin])
                    hs = scr[1][:, 0:n]
                    S.op("act", ACT(hs, red, AF.Sin, scale=0.5), reads=[r_scr[2]], writes=[r_scr[1]])
                    S.op("dve", TT(hs, hs, hs, ALU.mult), reads=[r_scr[1]], writes=[r_scr[1]])
                    S.op("dve", TS(cos_out, hs, -2.0, ALU.mult, 1.0, ALU.add), reads=[r_scr[1]], writes=[r_cos])

                lt = sb("lt", [128, 2, 64]); r_lt = Res("lt")
                ls = sb("ls", [128, 2]); r_ls = Res("ls")
                S.op("dve", TT(lt[:, 0, :], lamv[:, 0, :], lamv[:, 1, :], ALU.mult), reads=[r_lamv], writes=[r_lt])
                S.op("dve", TT(lt[:, 1, :], lamv[:, 2, :], lamv[:, 3, :], ALU.mult), reads=[r_lamv, r_lt], writes=[r_lt])
                S.op("dve", lambda e: e.reduce_sum(out=ls[:], in_=lt[:], axis=mybir.AxisListType.X), reads=[r_lt], writes=[r_ls])
                S.op("act", ACT(ls[:], ls[:], AF.Exp), reads=[r_ls], writes=[r_ls])
                S.op("dve", TT(neglam[:], ls[:, 1:2], ls[:, 0:1], ALU.subtract), reads=[r_ls], writes=[r_neglam])
                S.op("dve", TS(neglam[:], neglam[:], -LAM_INIT, ALU.add), reads=[r_neglam], writes=[r_neglam])
                S.op("dve", TS(subg[:], subg[:], 1.0 - LAM_INIT, ALU.mult), reads=[r_subg], writes=[r_subg])

                Wb = sb("Wb", [128, 8, 512], BF16); r_Wb = Res("Wb")
                wstg = [sb("wstg%d" % i, [128, 512]) for i in range(2)]
                r_wstg = [Res("wstg%d" % i) for i in range(2)]
                for k in range(8):
                    S.dma("sp", wstg[k % 2][:], win_d[128 * k:128 * (k + 1), :], "wstg%d" % (k % 2), writes=[r_wstg[k % 2]])
                    S.op("pool", TS(Wb[:, k, :], wstg[k % 2][:], gpre[:, k:k + 1], ALU.mult),
                         reads=[r_wstg[k % 2], r_gpre], writes=[r_Wb])

                C0 = sb("C0", [128, 512]); S0 = sb("S0", [128, 512]); r_C0 = Res("C0"); r_S0 = Res("S0")
                cI = sb("cI", [128, NT1]); sI = sb("sI", [128, NT1]); r_cI = Res("cI"); r_sI = Res("sI")
                S.op("dve", TS(scr[0][:], iota512[:], invf[:, 0:1], ALU.mult), reads=[r_iota, r_invf], writes=[r_scr[0]])
                sincos(scr[0][:], 512, r_scr[0], S0[:], C0[:], r_S0, r_C0)
                S.op("dve", TS(scr[0][:, 0:NT1], tile_iota[:], invf[:, 0:1], ALU.mult), reads=[r_tiota, r_invf], writes=[r_scr[0]])
                sincos(scr[0][:, 0:NT1], NT1, r_scr[0], sI[:], cI[:], r_sI, r_cI)
                nsI = sb("nsI", [128, NT1]); r_nsI = Res("nsI")
                S.op("dve", TS(nsI[:], sI[:], -1.0, ALU.mult), reads=[r_sI], writes=[r_nsI])

                are2 = sb("are2", [128, 8]); aim2 = sb("aim2", [128, 8]); dt2 = sb("dt2", [128, 8])
                r_are = Res("are2"); r_aim = Res("aim2"); r_dt = Res("dt2")
                S.dma("sp", are2[0:64, :], areT_d, "s0", writes=[r_are])
                S.dma("sp", are2[64:128, :], areT_d, "s0", writes=[r_are])
                S.dma("sp", aim2[0:64, :], aimT_d, "s1", writes=[r_aim])
                S.dma("sp", aim2[64:128, :], aimT_d, "s1", writes=[r_aim])
                S.dma("sp", dt2[:], logdt_d.partition_broadcast(128), "s2", writes=[r_dt])
                S.op("act", ACT(dt2[:], dt2[:], AF.Exp), reads=[r_dt], writes=[r_dt])
                rdec = sb("rdec", [128, 8]); r_rdec = Res("rdec")
                theta = sb("theta", [128, 8]); r_theta = Res("theta")
                S.op("dve", TT(rdec[:], are2[:], dt2[:], ALU.mult), reads=[r_are, r_dt], writes=[r_rdec])
                S.op("act", ACT(rdec[:], rdec[:], AF.Exp), reads=[r_rdec], writes=[r_rdec])
                S.op("dve", TT(theta[:], aim2[:], dt2[:], ALU.mult), reads=[r_aim, r_dt], writes=[r_theta])
                c1t = sb("c1t", [128, 8]); s1t = sb("s1t", [128, 8]); r_c1t = Res("c1t"); r_s1t = Res("s1t")
                cTt = sb("cTt", [128, 8]); sTt = sb("sTt", [128, 8]); r_cTt = Res("cTt"); r_sTt = Res("sTt")
                sincos(theta[:], 8, r_theta, s1t[:], c1t[:], r_s1t, r_c1t)
                S.op("dve", TS(scr[0][:, 0:8], theta[:], 512.0, ALU.mult), reads=[r_theta], writes=[r_scr[0]])
                sincos(scr[0][:, 0:8], 8, r_scr[0], sTt[:], cTt[:], r_sTt, r_cTt)
                S.op("dve", TT(sTt[:], sTt[:], sgn[:, 0:1].to_broadcast([128, 8]), ALU.mult), reads=[r_sTt, r_sgn], writes=[r_sTt])
                COSg = sb("COSg", [128, 8, 512]); SINg = sb("SINg", [128, 8, 512])
                r_COSg = [Res("COSg%d" % g) for g in range(8)]; r_SINg = [Res("SINg%d" % g) for g in range(8)]
                for g in range(8):
                    S.op("dve", TS(scr[0][:], iota512[:], theta[:, g:g + 1], ALU.mult), reads=[r_iota, r_theta], writes=[r_scr[0]])
                    sincos(scr[0][:], 512, r_scr[0], SINg[:, g, :], COSg[:, g, :], r_SINg[g], r_COSg[g])
                ROT = sb("ROT", [128, 8, 128]); r_ROT = Res("ROT")
                for g in range(8):
                    S.op("dve", TS(ROT[:, g, :], ident_f[:], cTt[:, g:g + 1], ALU.mult), reads=[r_ident_f, r_cTt], writes=[r_ROT])
                    S.op("dve", STT(ROT[:, g, :], swap_f[:], sTt[:, g:g + 1], ROT[:, g, :], ALU.mult, ALU.add),
                         reads=[r_swap, r_sTt, r_ROT], writes=[r_ROT])
                fre = sb("fre", [128, 8]); fim = sb("fim", [128, 8]); r_fre = Res("fre"); r_fim = Res("fim")
                nr = sb("nr", [128, 8]); ni = sb("ni", [128, 8]); den = sb("den", [128, 8]); tmp8 = sb("tmp8", [128, 8])
                r_nr = Res("nr"); r_ni = Res("ni"); r_den = Res("den"); r_tmp8 = Res("tmp8")
                S.op("dve", TT(nr[:], rdec[:], c1t[:], ALU.mult), reads=[r_rdec, r_c1t], writes=[r_nr])
                S.op("dve", TS(nr[:], nr[:], -1.0, ALU.add), reads=[r_nr], writes=[r_nr])
                S.op("dve", TT(ni[:], rdec[:], s1t[:], ALU.mult), reads=[r_rdec, r_s1t], writes=[r_ni])
                S.op("dve", TT(den[:], are2[:], are2[:], ALU.mult), reads=[r_are], writes=[r_den])
                S.op("dve", TT(tmp8[:], aim2[:], aim2[:], ALU.mult), reads=[r_aim], writes=[r_tmp8])
                S.op("dve", TT(den[:], den[:], tmp8[:], ALU.add), reads=[r_den, r_tmp8], writes=[r_den])
                S.op("dve", RECIP(den[:], den[:]), reads=[r_den], writes=[r_den])
                S.op("dve", TT(fre[:], nr[:], are2[:], ALU.mult), reads=[r_nr, r_are], writes=[r_fre])
                S.op("dve", TT(tmp8[:], ni[:], aim2[:], ALU.mult), reads=[r_ni, r_aim], writes=[r_tmp8])
                S.op("dve", TT(fre[:], fre[:], tmp8[:], ALU.add), reads=[r_fre, r_tmp8], writes=[r_fre])
                S.op("dve", TT(fre[:], fre[:], den[:], ALU.mult), reads=[r_fre, r_den], writes=[r_fre])
                S.op("dve", TT(fim[:], ni[:], are2[:], ALU.mult), reads=[r_ni, r_are], writes=[r_fim])
                S.op("dve", TT(tmp8[:], nr[:], aim2[:], ALU.mult), reads=[r_nr, r_aim], writes=[r_tmp8])
                S.op("dve", TT(fim[:], fim[:], tmp8[:], ALU.subtract), reads=[r_fim, r_tmp8], writes=[r_fim])
                S.op("dve", TT(fim[:], fim[:], den[:], ALU.mult), reads=[r_fim, r_den], writes=[r_fim])
                Bre = sb("Bre", [64, 8, 16]); Bim = sb("Bim", [64, 8, 16]); r_Bre = Res("Bre"); r_Bim = Res("Bim")
                S.dma("sp", Bre[:], bre_d.rearrange("g p h -> p g h"), "s3", writes=[r_Bre])
                S.dma("sp", Bim[:], bim_d.rearrange("g p h -> p g h"), "s4", writes=[r_Bim])
                BBre = sb("BBre", [64, 8, 16]); BBim = sb("BBim", [64, 8, 16]); tB = sb("tB", [64, 8, 16])
                r_BBre = Res("BBre"); r_BBim = Res("BBim"); r_tB = Res("tB")
                fre_b = fre[0:64, :].unsqueeze(2).to_broadcast([64, 8, 16])
                fim_b = fim[0:64, :].unsqueeze(2).to_broadcast([64, 8, 16])
                S.op("dve", TT(BBre[:], Bre[:], fre_b, ALU.mult), reads=[r_Bre, r_fre], writes=[r_BBre])
                S.op("dve", TT(tB[:], Bim[:], fim_b, ALU.mult), reads=[r_Bim, r_fim], writes=[r_tB])
                S.op("dve", TT(BBre[:], BBre[:], tB[:], ALU.subtract), reads=[r_BBre, r_tB], writes=[r_BBre])
                S.op("dve", TT(BBim[:], Bim[:], fre_b, ALU.mult), reads=[r_Bim, r_fre], writes=[r_BBim])
                S.op("dve", TT(tB[:], Bre[:], fim_b, ALU.mult), reads=[r_Bre, r_fim], writes=[r_tB])
                S.op("dve", TT(BBim[:], BBim[:], tB[:], ALU.add), reads=[r_BBim, r_tB], writes=[r_BBim])
                FB1 = sb("FB1", [128, 128]); FB2 = sb("FB2", [128, 128]); r_FB1 = Res("FB1"); r_FB2 = Res("FB2")
                S.group("pe", [TR(banks[0][:, 0:64], BBre[:].rearrange("p g h -> p (g h)"), ident_f[0:64, 0:64]),
                               TR(banks[0][:, 64:128], BBim[:].rearrange("p g h -> p (g h)"), ident_f[0:64, 0:64])],
                        reads=[r_BBre, r_BBim, r_ident_f], writes=[rbank[0]])
                S.op("dve", CP(FB1[:], banks[0][:, 0:128]), reads=[rbank[0]], writes=[r_FB1])
                S.op("dve", CP(FB2[:, 0:64], banks[0][:, 64:128]), reads=[rbank[0]], writes=[r_FB2])
                S.op("dve", TS(FB2[:, 64:128], banks[0][:, 0:64], -1.0, ALU.mult), reads=[rbank[0], r_FB2], writes=[r_FB2])
                LB1 = sb("LB1", [128, 8, 128], BF16); LB2 = sb("LB2", [128, 8, 128], BF16); r_LB = Res("LB")
                for g in range(8):
                    S.op("pool", TS(LB1[:, g, :], FB1[:], gmask[:, g:g + 1], ALU.mult), reads=[r_FB1, r_gmask], writes=[r_LB])
                    S.op("pool", TS(LB2[:, g, :], FB2[:], gmask[:, g:g + 1], ALU.mult), reads=[r_FB2, r_gmask], writes=[r_LB])
                CC1 = sb("CC1", [128, 128]); CC2 = sb("CC2", [128, 128]); r_CC1 = Res("CC1"); r_CC2 = Res("CC2")
                S.dma("sp", CC1[0:64, :], creT_d, "s5", writes=[r_CC1])
                S.dma("sp", CC1[64:128, :], cimT_d, "s5", writes=[r_CC1])
                S.dma("sp", CC2[0:64, :], cimT_d, "s6", writes=[r_CC2])
                S.dma("sp", CC2[64:128, :], creT_d, "s6", writes=[r_CC2])
                S.op("dve", TS(CC1[64:128, :], CC1[64:128, :], -1.0, ALU.mult), reads=[r_CC1], writes=[r_CC1])
                S.op("dve", TS(CC2[:], CC2[:], -1.0, ALU.mult), reads=[r_CC2], writes=[r_CC2])
                LC1 = sb("LC1", [128, 8, 128], BF16); LC2 = sb("LC2", [128, 8, 128], BF16); r_LC = Res("LC")
                S.op("pool", MEMSET(LC1[:], 0.0), writes=[r_LC])
                S.op("pool", MEMSET(LC2[:], 0.0), writes=[r_LC])
                for g in range(8):
                    S.op("dve", CP(LC1[:, g, 16 * g:16 * g + 16], CC1[:, 16 * g:16 * g + 16]), reads=[r_CC1], writes=[r_LC])
                    S.op("dve", CP(LC2[:, g, 16 * g:16 * g + 16], CC2[:, 16 * g:16 * g + 16]), reads=[r_CC2], writes=[r_LC])
                diagD = sb("diagD", [128, 128], BF16); r_diagD = Res("diagD")
                S.op("dve", TS(diagD[:], ident_f[:], dskip[:, 0:1], ALU.mult), reads=[r_ident_f, r_dskip], writes=[r_diagD])
                s0 = sb("s0", [128, 2, 8]); r_s0 = [Res("s0_0"), Res("s0_1")]
                S.op("dve", MEMSET(s0[:], 0.0), writes=r_s0)

                xt = [sb("xt%d" % i, [128, D]) for i in range(2)]; r_xt = [Res("xt%d" % i) for i in range(2)]
                junk = sb("junk", [128, D], BF16); r_junk = Res("junk")
                ssq = sb("ssq", [128, 4]); r_ssq = [Res("ssq%d" % i) for i in range(4)]
                xs = [sb("xs%d" % i, [128, D], BF16) for i in range(2)]; r_xs = [Res("xs%d" % i) for i in range(2)]
                hT = [sb("hT%d" % i, [128, 8, 512], BF16) for i in range(2)]; r_hT = [Res("hT%d" % i) for i in range(2)]
                COSt = sb("COSt", [128, 512]); SINt = sb("SINt", [128, 512]); r_COSt = Res("COSt"); r_SINt = Res("SINt")
                qraw = sb("qraw", [128, 512], BF16); r_qraw = Res("qraw")
                rt1 = sb("rt1", [128, 512]); rt2 = sb("rt2", [128, 512]); r_rt1 = Res("rt1"); r_rt2 = Res("rt2")
                QT = [sb("QT%d" % i, [128, 512], BF16) for i in range(2)]; r_QT = [Res("QT%d" % i) for i in range(2)]
                KTa = sb("KTa", [128, LP], BF16); KTb = sb("KTb", [128, LP], BF16)
                r_KT = [Res("KT%d" % i) for i in range(NT1)]
                r_KTz = Res("KTz")
                S.op("pool", MEMSET(KTa[64:128, :], 0.0), writes=[r_KTz])
                S.op("pool", MEMSET(KTb[0:64, :], 0.0), writes=[r_KTz])
                V = sb("V", [128, NBLK, 128], BF16); r_V = [Res("V%d" % i) for i in range(NT1)]
                uT = [sb("uT%d" % i, [128, 512], BF16) for i in range(2)]; r_uT = [Res("uT%d" % i) for i in range(2)]
                st1 = [sb("st1_%d" % i, [128, 512]) for i in range(2)]; r_st1 = [Res("st1_%d" % i) for i in range(2)]
                st2 = [sb("st2_%d" % i, [128, 512]) for i in range(2)]; r_st2 = [Res("st2_%d" % i) for i in range(2)]
                bt = [sb("bt%d" % i, [128, 512]) for i in range(2)]; r_bt = [Res("bt%d" % i) for i in range(2)]
                Xs = [sb("Xs%d" % i, [128, 512]) for i in range(2)]; r_Xs = [Res("Xs%d" % i) for i in range(2)]
                P1s = [sb("P1s%d" % i, [128, 512], BF16) for i in range(2)]; r_P1s = [Res("P1s%d" % i) for i in range(2)]
                P2s = [sb("P2s%d" % i, [128, 512], BF16) for i in range(2)]; r_P2s = [Res("P2s%d" % i) for i in range(2)]
                ybuf = [sb("ybuf%d" % i, [128, 512], BF16) for i in range(2)]; r_ybuf = [Res("ybuf%d" % i) for i in range(2)]
                Pb = [sb("Pb%d" % i, [128, 2, 256], BF16) for i in range(3)]; r_Pb = [Res("Pb%d" % i) for i in range(3)]
                rL = sb("rL", [128, 2, 256]); r_rL = Res("rL")
                On = sb("On", [128, 2, 256]); r_On = Res("On")
                od = sb("od", [128, 256]); r_od = Res("od")
                osq = sb("osq", [128, 256], BF16); r_osq = Res("osq")
                orstd = sb("orstd", [128, 256]); r_orstd = Res("orstd")
                obuf = [sb("obuf%d" % i, [128, 256], BF16) for i in range(2)]; r_obuf = [Res("obuf%d" % i) for i in range(2)]

                F0, F1, F2, F3 = banks[0], banks[1], banks[2], banks[3]
                rF0, rF1, rF2, rF3 = rbank[0], rbank[1], rbank[2], rbank[3]
                Sb = [banks[4], banks[5]]; rSb = [rbank[4], rbank[5]]
                Ob, Lb = banks[6], banks[7]; rOb, rLb = rbank[6], rbank[7]
                F0b = F0[:].bitcast(BF16)


                pcount = [0]
                sbcount = [0]
                out_events = []

                def front_gen(ti):
                    nb = 4 if ti < 16 else 1
                    N = 128 * nb
                    tok0 = 512 * ti
                    hb = ti % 2
                    for j in range(nb):
                        s = 4 * ti + j
                        xb = sbcount[0] % 2
                        sq = sbcount[0] % 4
                        sbcount[0] += 1
                        if s == 0:
                            S.dma("sp", xt[xb][0:16, :], meta_d, "x%d" % xb, writes=[r_xt[xb]])
                            S.dma("sp", xt[xb][16:128, :], x_d[0:112, :], "x%d" % xb, writes=[r_xt[xb]])
                        elif s == 64:
                            S.op("pool", MEMSET(xt[xb][:], 0.0), writes=[r_xt[xb]])
                            S.dma("sp", xt[xb][0:16, :], x_d[SEQ - 16:SEQ, :], "x%d" % xb, writes=[r_xt[xb]])
                        else:
                            S.dma("sp", xt[xb][:], x_d[128 * s - 16:128 * s + 112, :], "x%d" % xb, writes=[r_xt[xb]])
                        S.op("act", ACT(junk[:], xt[xb][:], AF.Square, accum_out=ssq[:, sq:sq + 1]),
                             reads=[r_xt[xb]], writes=[r_junk, r_ssq[sq]])
                        S.op("act", ACT(ssq[:, sq:sq + 1], ssq[:, sq:sq + 1], AF.Ln, scale=1.0 / D, bias=epsc[:, 0:1]),
                             reads=[r_ssq[sq], r_eps], writes=[r_ssq[sq]])
                        S.op("act", ACT(ssq[:, sq:sq + 1], ssq[:, sq:sq + 1], AF.Exp, scale=-0.5),
                             reads=[r_ssq[sq]], writes=[r_ssq[sq]])
                        S.op("act", ACT(xs[xb][:], xt[xb][:], AF.Copy, scale=ssq[:, sq:sq + 1]),
                             reads=[r_xt[xb], r_ssq[sq]], writes=[r_xs[xb]])
                        yield
                        S.group("pe", [TR(F0b[:, 128 * k:128 * (k + 1)], xs[xb][:, 128 * k:128 * (k + 1)], ident_b[:]) for k in range(8)],
                                reads=[r_xs[xb], r_ident_b], writes=[rF0])
                        S.op("dve", CP(hT[hb][:, :, 128 * j:128 * (j + 1)], F0b.rearrange("p (k t) -> p k t", k=8)),
                             reads=[rF0], writes=[r_hT[hb]])
                        yield
                    S.op("dve", TS(COSt[:], C0[:], cI[:, ti:ti + 1], ALU.mult), reads=[r_C0, r_cI], writes=[r_COSt])
                    S.op("dve", STT(COSt[:], S0[:], nsI[:, ti:ti + 1], COSt[:], ALU.mult, ALU.add), reads=[r_S0, r_nsI, r_COSt], writes=[r_COSt])
                    S.op("dve", TS(SINt[:], S0[:], cI[:, ti:ti + 1], ALU.mult), reads=[r_S0, r_cI], writes=[r_SINt])
                    S.op("dve", STT(SINt[:], C0[:], sI[:, ti:ti + 1], SINt[:], ALU.mult, ALU.add), reads=[r_C0, r_sI, r_SINt], writes=[r_SINt])
                    yield

                    def proj_fm(c, bank, rb):
                        S.group("pe", [MM(bank[:, 0:N], Wb[:, k, 128 * c:128 * (c + 1)], hT[hb][:, k, 0:N], k == 0, k == 7) for k in range(8)],
                                reads=[r_Wb, r_hT[hb]], writes=[rb])

                    def rope_a():
                        S.op("act", ACT(qraw[:, 0:N], F1[:, 0:N], AF.Copy), reads=[rF1], writes=[r_qraw])
                        S.op("pe", MM(F2[:, 0:N], rmatT_b[:], qraw[:, 0:N], True, True), reads=[r_rmat, r_qraw], writes=[rF2])

                    def rope_b(dst_ap, r_dst, dst2=None):
                        S.op("dve", TT(rt1[:, 0:N], F1[:, 0:N], COSt[:, 0:N], ALU.mult), reads=[rF1, r_COSt], writes=[r_rt1])
                        S.op("dve", TT(rt2[:, 0:N], F2[:, 0:N], SINt[:, 0:N], ALU.mult), reads=[rF2, r_SINt], writes=[r_rt2])
                        if dst2 is None:
                            S.op("pool", TT(dst_ap, rt1[:, 0:N], rt2[:, 0:N], ALU.add), reads=[r_rt1, r_rt2], writes=[r_dst])
                        else:
                            S.op("pool", TT(dst_ap[0:64, :], rt1[0:64, 0:N], rt2[0:64, 0:N], ALU.add), reads=[r_rt1, r_rt2, r_KTz], writes=[r_dst])
                            S.op("pool", TT(dst2[64:128, :], rt1[64:128, 0:N], rt2[64:128, 0:N], ALU.add), reads=[r_rt1, r_rt2, r_KTz], writes=[r_dst])

                    proj_fm(0, F1, rF1)
                    yield
                    rope_a()
                    yield
                    rope_b(QT[hb][:, 0:N], r_QT[hb])
                    yield
                    proj_fm(1, F1, rF1)
                    yield
                    rope_a()
                    yield
                    rope_b(KTa[:, tok0:tok0 + N], r_KT[ti], KTb[:, tok0:tok0 + N])
                    yield
                    proj_fm(3, F1, rF1)
                    yield
                    S.op("act", ACT(uT[hb][:, 0:N], F1[:, 0:N], AF.Copy), reads=[rF1], writes=[r_uT[hb]])
                    fns = []
                    for j in range(nb):
                        for k in range(8):
                            fns.append(MM(F2[:, 128 * j:128 * (j + 1)], hT[hb][:, k, 128 * j:128 * (j + 1)], Wb[:, k, 256:384], k == 0, k == 7))
                    S.group("pe", fns, reads=[r_Wb, r_hT[hb]], writes=[rF2])
                    yield
                    S.op("act", ACT(V[:, 4 * ti:4 * ti + nb, :], F2[:, 0:N].rearrange("p (j d) -> p j d", j=nb), AF.Copy),
                         reads=[rF2], writes=[r_V[ti]])
                    yield
                    par = ti % 2
                    yb = ti % 2

                    def emit_y(g, pb):
                        S.group("pe", [MM(F3[:, 0:N], LC1[:, g, :], P1s[pb][:, 0:N], g == 0, False),
                                       MM(F3[:, 0:N], LC2[:, g, :], P2s[pb][:, 0:N], False, False)],
                                reads=[r_LC, r_P1s[pb], r_P2s[pb]], writes=[rF3])

                    pbs = []
                    for g in range(8):
                        pbs.append(pcount[0] % 2)
                        pcount[0] += 1

                    def stage_b(g):
                        pb = pbs[g]; q = g % 2
                        S.op("dve", SCAN(Xs[pb][:, 0:N], rdec[:, g:g + 1].to_broadcast([128, N]), bt[q][:, 0:N], s0[:, par, g:g + 1]),
                             reads=[r_rdec, r_bt[q], r_s0[par]], writes=[r_Xs[pb]])
                        if ti + 1 < nt1:
                            S.op("pe", MM(F0[:, g:g + 1], ROT[:, g, :], Xs[pb][:, N - 1:N], True, True), reads=[r_ROT, r_Xs[pb]], writes=[rF0])
                            S.op("act", ACT(s0[:, 1 - par, g:g + 1], F0[:, g:g + 1], AF.Copy), reads=[rF0], writes=[r_s0[1 - par]])
                        S.op("pool", TT(P1s[pb][:, 0:N], Xs[pb][:, 0:N], COSg[:, g, 0:N], ALU.mult), reads=[r_Xs[pb], r_COSg[g]], writes=[r_P1s[pb]])
                        S.op("dve", TT(P2s[pb][:, 0:N], Xs[pb][:, 0:N], SINg[:, g, 0:N], ALU.mult), reads=[r_Xs[pb], r_SINg[g]], writes=[r_P2s[pb]])

                    for g in range(10):
                        if g < 8:
                            q = g % 2
                            S.op("pe", MM(F1[:, 0:N], LB1[:, g, :], uT[hb][:, 0:N], True, True), reads=[r_LB, r_uT[hb]], writes=[rF1])
                            S.op("pe", MM(F2[:, 0:N], LB2[:, g, :], uT[hb][:, 0:N], True, True), reads=[r_LB, r_uT[hb]], writes=[rF2])
                            yield
                            S.op("dve", TT(st1[q][:, 0:N], F1[:, 0:N], COSg[:, g, 0:N], ALU.mult), reads=[rF1, r_COSg[g]], writes=[r_st1[q]])
                            S.op("dve", TT(st2[q][:, 0:N], F2[:, 0:N], SINg[:, g, 0:N], ALU.mult), reads=[rF2, r_SINg[g]], writes=[r_st2[q]])
                            S.op("pool", TT(bt[q][:, 0:N], st1[q][:, 0:N], st2[q][:, 0:N], ALU.add), reads=[r_st1[q], r_st2[q]], writes=[r_bt[q]])
                        if 1 <= g <= 8:
                            stage_b(g - 1)
                        yield
                        if 2 <= g <= 9:
                            emit_y(g - 2, pbs[g - 2])
                            yield
                    S.op("pe", MM(F3[:, 0:N], diagD[:], uT[hb][:, 0:N], False, True), reads=[r_diagD, r_uT[hb]], writes=[rF3])
                    yield
                    S.op("act", ACT(ybuf[yb][:, 0:N], F3[:, 0:N], AF.Gelu), reads=[rF3], writes=[r_ybuf[yb]])
                    lo = max(tok0, NMETA); hi = min(tok0 + N, L)
                    out_events.extend(xchg_write(128, lo - NMETA, hi - NMETA, ybuf[yb], lo - tok0, "yo%d" % yb, r_ybuf[yb]))
                    yield

                def attention(ti, pump):
                    nb = 4 if ti < 16 else 1
                    tok0 = 512 * ti
                    hb = ti % 2
                    nqt = 2 if nb == 4 else 1
                    nbq = 2 if nb == 4 else 1
                    NQ = 128 * nbq
                    QTv = QT[hb]
                    for qi in range(nqt):
                        c = 2 * ti + qi
                        q0 = 256 * qi
                        nkb = 2 * c + nbq

                        def qk(kb):
                            j = kb - 2 * c
                            qs = 128 * j if j > 0 else 0
                            kt = kb // 4
                            sbk = kb % 2
                            pbk = kb % 3
                            Sv = Sb[sbk][:, 0:2 * NQ].rearrange("p (h q) -> p h q", h=2)
                            Pv = Pb[pbk][:].rearrange("p h q -> p (h q)")[:, 0:2 * NQ].rearrange("p (h q) -> p h q", h=2)
                            S.group("pe", [MM(Sv[:, 0, qs:NQ], KTa[:, 128 * kb:128 * (kb + 1)], QTv[:, q0 + qs:q0 + NQ], True, True),
                                           MM(Sv[:, 1, qs:NQ], KTb[:, 128 * kb:128 * (kb + 1)], QTv[:, q0 + qs:q0 + NQ], True, True)],
                                    reads=[r_KT[kt], r_QT[hb]], writes=[rSb[sbk]])
                            S.op("act", ACT(Pv[:, :, qs:NQ], Sv[:, :, qs:NQ], AF.Exp, scale=0.125), reads=[rSb[sbk]], writes=[r_Pb[pbk]])
                            if j >= 0:
                                S.op("pool", TT(Pv[:, :, qs:qs + 128], Pv[:, :, qs:qs + 128],
                                                tri_b[:].unsqueeze(1).to_broadcast([128, 2, 128]), ALU.mult),
                                     reads=[r_Pb[pbk], r_tri], writes=[r_Pb[pbk]])

                        def pv(kb):
                            j = kb - 2 * c
                            qs = 128 * j if j > 0 else 0
                            kt = kb // 4
                            pbk = kb % 3
                            Pv = Pb[pbk][:].rearrange("p h q -> p (h q)")[:, 0:2 * NQ].rearrange("p (h q) -> p h q", h=2)
                            Ov = Ob[:, 0:2 * NQ].rearrange("p (h q) -> p h q", h=2)
                            Lv = Lb[:, 0:2 * NQ].rearrange("p (h q) -> p h q", h=2)
                            first = (kb == 0); lastk = (kb == nkb - 1)
                            if qs == 0:
                                Pf = Pb[pbk][:].rearrange("p h q -> p (h q)")[:, 0:2 * NQ]
                                S.op("pe", MM(Ob[:, 0:2 * NQ], V[:, kb, :], Pf, first, lastk), reads=[r_V[kt], r_Pb[pbk]], writes=[rOb])
                                S.op("pe", MM(Lb[:, 0:2 * NQ], ones_b[:], Pf, first, lastk), reads=[r_ones, r_Pb[pbk]], writes=[rLb])
                            else:
                                assert not first
                                S.group("pe", [MM(Ov[:, 0, qs:NQ], V[:, kb, :], Pv[:, 0, qs:NQ], False, False),
                                               MM(Ov[:, 1, qs:NQ], V[:, kb, :], Pv[:, 1, qs:NQ], False, lastk)],
                                        reads=[r_V[kt], r_Pb[pbk]], writes=[rOb])
                                S.group("pe", [MM(Lv[:, 0, qs:NQ], ones_b[:], Pv[:, 0, qs:NQ], False, False),
                                               MM(Lv[:, 1, qs:NQ], ones_b[:], Pv[:, 1, qs:NQ], False, lastk)],
                                        reads=[r_ones, r_Pb[pbk]], writes=[rLb])

                        qk(0)
                        for kb in range(nkb):
                            if kb + 1 < nkb:
                                qk(kb + 1)
                            pv(kb)
                            pump()
                        Ov = Ob[:, 0:2 * NQ].rearrange("p (h q) -> p h q", h=2)
                        Lv = Lb[:, 0:2 * NQ].rearrange("p (h q) -> p h q", h=2)
                        S.op("dve", RECIP(rL[:, :, 0:NQ], Lv[:, :, 0:NQ]), reads=[rLb], writes=[r_rL])
                        S.op("dve", TT(On[:, :, 0:NQ], Ov[:, :, 0:NQ], rL[:, :, 0:NQ], ALU.mult), reads=[rOb, r_rL], writes=[r_On])
                        S.op("dve", STT(od[:, 0:NQ], On[:, 1, 0:NQ], neglam[:, 0:1], On[:, 0, 0:NQ], ALU.mult, ALU.add),
                             reads=[r_On, r_neglam], writes=[r_od])
                        S.op("act", ACT(osq[:, 0:NQ], od[:, 0:NQ], AF.Square), reads=[r_od], writes=[r_osq])
                        sbk = nkb % 2
                        S.op("pe", MM(Sb[sbk][:, 0:NQ], ones_b[:], osq[:, 0:NQ], True, True), reads=[r_ones, r_osq], writes=[rSb[sbk]])
                        S.op("act", ACT(orstd[:, 0:NQ], Sb[sbk][:, 0:NQ], AF.Ln, scale=1.0 / 128, bias=epsc[:, 0:1]),
                             reads=[rSb[sbk], r_eps], writes=[r_orstd])
                        S.op("act", ACT(orstd[:, 0:NQ], orstd[:, 0:NQ], AF.Exp, scale=-0.5), reads=[r_orstd], writes=[r_orstd])
                        ob = c % 2
                        S.op("dve", STT(obuf[ob][:, 0:NQ], od[:, 0:NQ], subg[:, 0:1], orstd[:, 0:NQ], ALU.mult, ALU.mult),
                             reads=[r_od, r_subg, r_orstd], writes=[r_obuf[ob]])
                        qt0 = tok0 + q0
                        lo = max(qt0, NMETA); hi = min(qt0 + NQ, L)
                        out_events.extend(xchg_write(0, lo - NMETA, hi - NMETA, obuf[ob], lo - qt0, "oo%d" % ob, r_obuf[ob]))

                for _ in front_gen(0):
                    pass
                for ti in range(nt1):
                    nxt = front_gen(ti + 1) if ti + 1 < nt1 else None
                    nbq_t = 2 if ti < 16 else 1
                    total_kb = sum(2 * (2 * ti + qi) + nbq_t for qi in range(2 if ti < 16 else 1))
                    state = {"steps_left": FRONT_STEPS if nxt is not None else 0, "kb_left": total_kb, "gen": nxt}

                    def pump(state=state):
                        if state["gen"] is None:
                            return
                        kbl = max(state["kb_left"], 1)
                        n = -(-state["steps_left"] // kbl)
                        state["kb_left"] -= 1
                        for _ in range(n):
                            try:
                                next(state["gen"])
                                state["steps_left"] -= 1
                            except StopIteration:
                                state["gen"] = None
                                state["steps_left"] = 0
                                return
                    attention(ti, pump)
                    if state["gen"] is not None:
                        for _ in state["gen"]:
                            pass
                    if use_cc and ti >= 4 and ti % 4 == 0:
                        qd = ti // 4 - 1
                        S.wait_all("pool", q_events[qd])
                        S.raw("pool", (lambda qd: (lambda e: e.collective_compute(
                            "AllGather", ALU.bypass, replica_groups=[[0, 1, 2, 3], [4, 5, 6, 7]],
                            ins=[xin_t[qd].ap().opt()], outs=[xout_t.ap()[1024 * qd:1024 * (qd + 1), :].opt()]).then_inc(cc_sem)))(qd))

            return out_events

        out_events = _phase1() if nt1 > 0 else []

        if debug:
            print('MARKS', S.marks, S.nops)
            S.limit = 10**9
            S.wait_all("pool", out_events)
            evs = [S.dma("pool", dbg_d[:, TOK2 * q:TOK2 * (q + 1)], xin[q], "dbg") for q in range(4)]
            S.wait_all("pool", evs)
        if do_phase2:
            S.wait_all("pool", out_events)
            if p2mode == "nogather":
                evd = S.dma("pool", xout[0:1024, :], xout_dbg_d, "xdbg")
                S.wait_all("pool", [evd])
            else:
                S.raw("pool", lambda e: e.wait_ge(cc_sem, 4))
            S.cnt["pool"] += 1
            gather_ev = (S.sem["pool"], S.cnt["pool"], "pool")
            S.raw("pool", lambda e: e.sem_inc(S.sem["pool"], 1))
            lasts = [S.last[e] for e in ("pe", "act", "dve", "pool") if S.last[e] is not None] + [gather_ev]
            for e in ("pe", "act", "dve", "sp", "pool"):
                S.wait_all(e, lasts)

            with ExitStack() as p2:
                def sb(name, shape, dt=F32):
                    return p2.enter_context(nc.sbuf_tensor("b_" + name, list(shape), dt))

                Wg = sb("Wg", [128, 8, DFF], BF16); r_Wg = Res("Wg")
                Wu = sb("Wu", [128, 8, DFF], BF16); r_Wu = Res("Wu")
                Wd = sb("Wd", [128, NF, D], BF16); r_Wd = Res("Wd")
                Wo = sb("Wo", [128, 8, D], BF16); r_Wo = Res("Wo")
                Wgl = sb("Wgl", [128, 4, 512], BF16); r_Wgl = Res("Wgl")
                S.dma("pool", Wgl[:], wglu_d.rearrange("(k p) c -> p k c", p=128), "w0", writes=[r_Wgl])
                S.dma("pool", Wo[:], wout_d.rearrange("(k p) c -> p k c", p=128), "w1", writes=[r_Wo])
                r_Wgc = [Res("Wg%d" % f) for f in range(NF)]
                r_Wuc = [Res("Wu%d" % f) for f in range(NF)]
                r_Wdc = [Res("Wd%d" % f) for f in range(NF)]
                for f in range(0, NF, 2):
                    S.dma("pool", Wg[:, :, 128 * f:128 * (f + 2)], wgate_d[:, 128 * f:128 * (f + 2)].rearrange("(k p) c -> p k c", p=128),
                          "wg%d" % (f // 2), writes=[r_Wgc[f], r_Wgc[f + 1]])
                    S.dma("pool", Wu[:, :, 128 * f:128 * (f + 2)], wup_d[:, 128 * f:128 * (f + 2)].rearrange("(k p) c -> p k c", p=128),
                          "wu%d" % (f // 2), writes=[r_Wuc[f], r_Wuc[f + 1]])
                for f in range(0, NF, 2):
                    S.dma("pool", Wd[:, f:f + 2, :], wdown_d[128 * f:128 * (f + 2), :].rearrange("(k p) c -> p k c", p=128),
                          "wd%d" % (f // 2), writes=[r_Wdc[f], r_Wdc[f + 1]])
                ident2 = sb("ident2", [128, 128], BF16); r_id2 = Res("ident2")
                ones2 = sb("ones2", [128, 128], BF16); r_ones2 = Res("ones2")
                eps2 = sb("eps2", [128, 1]); r_eps2 = Res("eps2")
                bglu = sb("bglu", [128, 4]); r_bglu = Res("bglu")
                gssm = sb("gssm", [128, 4]); r_gssm = Res("gssm")
                gpost_b = sb("gpost_b", [128, D]); r_gpost = Res("gpost")
                gffn8 = sb("gffn8", [128, 8]); r_gffn = Res("gffn")
                gpffn_b = sb("gpffn_b", [128, D]); r_gpffn = Res("gpffn")
                S.dma("pool", ident2[:], ident_d, "k0", writes=[r_id2])
                S.op("dve", MEMSET(ones2[:], 1.0), writes=[r_ones2])
                S.op("dve", MEMSET(eps2[:], EPS), writes=[r_eps2])
                S.dma("sp", bglu[:], bglu_d, "k1", writes=[r_bglu])
                S.dma("sp", gssm[:], gssm_d, "k2", writes=[r_gssm])
                S.dma("sp", gpost_b[:], gpost_d.partition_broadcast(128), "k3", writes=[r_gpost])
                S.dma("sp", gffn8[:], gffn_d, "k4", writes=[r_gffn])
                S.dma("sp", gpffn_b[:], gpffn_d.partition_broadcast(128), "k5", writes=[r_gpffn])

                cT = [sb("cT%d" % i, [128, 8, T2], BF16) for i in range(1)]; r_cT = [Res("cT%d" % i) for i in range(1)]
                sig = sb("sig", [128, T2], BF16); r_sig = Res("sig")
                yg = sb("yg", [128, 4, T2], BF16); r_yg = Res("yg")
                sq2 = sb("sq2", [128, T2], BF16); r_sq2 = Res("sq2")
                rsy = sb("rsy", [128, T2]); r_rsy = Res("rsy")
                yn = sb("yn", [128, 4, T2], BF16); r_yn = Res("yn")
                hres = [sb("hres%d" % i, [128, D]) for i in range(3)]; r_hres = [Res("hres%d" % i) for i in range(3)]
                ssvA = sb("ssvA", [128, 8]); r_ssvA = Res("ssvA")
                ssvB = sb("ssvB", [128, 4]); r_ssvB = Res("ssvB")
                tmpA = sb("tmpA", [128, 512]); r_tmpA = Res("tmpA")
                tmpB = tmpA; r_tmpB = r_tmpA
                hs = [sb("hs0", [128, D], BF16)] * 2; r_hs = [Res("hs0")] * 2
                junkA = sb("junkA", [128, 512], BF16); r_junkA = Res("junkA")
                junkB = junkA; r_junkB = r_junkA
                hT2 = [sb("hT2_%d" % i, [128, 8, T2], BF16) for i in range(2)]; r_hT2 = [Res("hT2_%d" % i) for i in range(2)]
                sg = [sb("sg%d" % i, [128, T2], BF16) for i in range(2)]; r_sg = [Res("sg%d" % i) for i in range(2)]
                aT = sb("aT", [128, NF, T2], BF16); r_aT = Res("aT")

                pid = [None]
                B = banks
                rB = rbank
                final_events = []

                def h1_gen(tt):
                    cb = 0

                    def _ld(e, tt=tt):
                        if pid[0] is None:
                            pid[0] = e.partition_id() % 4
                        return e.dma_start(out=cT[cb][:], in_=xout[bass.ds(pid[0] * 1024, 1024), tt * T2:(tt + 1) * T2].rearrange("(k p) t -> p k t", p=128))
                    slot = "g%d" % cb
                    if slot not in S.dma_sems:
                        S.dma_sems[slot] = top.enter_context(nc.semaphore("d_" + slot))
                        S.dma_cnt[slot] = 0
                    S._waits("sp", S._deps_for(None, [r_cT[cb]], None))
                    S.dma_cnt[slot] += 16
                    ev = (S.dma_sems[slot], S.dma_cnt[slot], "dma")
                    S.raw("sp", (lambda sem: (lambda e, f=_ld: f(e).then_inc(sem, 16)))(S.dma_sems[slot]))
                    S._commit(ev, None, [r_cT[cb]])
                    yield None
                    for m in range(4):
                        S.group("pe", [MM(B[0][:, 0:T2], Wgl[:, r, 128 * m:128 * (m + 1)], cT[cb][:, 2 * r + 1, :], r == 0, r == 3) for r in range(4)],
                                reads=[r_Wgl, r_cT[cb]], writes=[rB[0]])
                        yield None
                        S.op("act", ACT(sig[:], B[0][:, 0:T2], AF.Sigmoid, bias=bglu[:, m:m + 1]), reads=[rB[0], r_bglu], writes=[r_sig])
                        S.op("dve", TT(yg[:, m, :], cT[cb][:, 2 * m + 1, :], sig[:], ALU.mult), reads=[r_cT[cb], r_sig], writes=[r_yg])
                        S.op("act", ACT(sq2[:], yg[:, m, :], AF.Square), reads=[r_yg], writes=[r_sq2])
                        yield None
                        S.op("pe", MM(B[1][:, 0:T2], ones2[:], sq2[:], m == 0, m == 3), reads=[r_ones2, r_sq2], writes=[rB[1]])
                    yield None
                    S.op("act", ACT(rsy[:], B[1][:, 0:T2], AF.Ln, scale=1.0 / 512, bias=eps2[:, 0:1]), reads=[rB[1], r_eps2], writes=[r_rsy])
                    S.op("act", ACT(rsy[:], rsy[:], AF.Exp, scale=-0.5), reads=[r_rsy], writes=[r_rsy])
                    for m in range(4):
                        S.op("dve", STT(yn[:, m, :], yg[:, m, :], gssm[:, m:m + 1], rsy[:], ALU.mult, ALU.mult),
                             reads=[r_yg, r_gssm, r_rsy], writes=[r_yn])
                    yield None
                    for s in range(2):
                        hr = hres[(2 * tt + s) % 3]; r_hr = r_hres[(2 * tt + s) % 3]
                        c0 = 4 * s
                        for hf in range(2):
                            bk = B[2 + hf]; rbk = rB[2 + hf]
                            fns = []
                            for k in range(8):
                                src = cT[cb][:, k, 128 * s:128 * (s + 1)] if k % 2 == 0 else yn[:, k // 2, 128 * s:128 * (s + 1)]
                                fns.append(MM(bk[:, :], src, Wo[:, k, 512 * hf:512 * (hf + 1)], k == 0, k == 7))
                            S.group("pe", fns, reads=[r_cT[cb], r_yn, r_Wo], writes=[rbk])
                            yield None
                            S.op("act", ACT(junkA[:], bk[:, :], AF.Square, accum_out=ssvA[:, c0 + hf:c0 + hf + 1]), reads=[rbk], writes=[r_junkA, r_ssvA])
                        yield None
                        S.op("dve", TT(ssvA[:, c0 + 2:c0 + 3], ssvA[:, c0:c0 + 1], ssvA[:, c0 + 1:c0 + 2], ALU.add), reads=[r_ssvA], writes=[r_ssvA])
                        S.op("act", ACT(ssvA[:, c0 + 2:c0 + 3], ssvA[:, c0 + 2:c0 + 3], AF.Ln, scale=1.0 / D, bias=eps2[:, 0:1]), reads=[r_ssvA, r_eps2], writes=[r_ssvA])
                        S.op("act", ACT(ssvA[:, c0 + 2:c0 + 3], ssvA[:, c0 + 2:c0 + 3], AF.Exp, scale=-0.5), reads=[r_ssvA], writes=[r_ssvA])
                        yield ("BAR1" if s == 1 else None)
                        row0 = tt * T2 + 128 * s
                        S.dma("sp", hr[:], xres_d[row0:row0 + 128, :], "xr%d" % ((2 * tt + s) % 3), writes=[r_hr])
                        yield None
                        for hf in range(2):
                            bk = B[2 + hf]; rbk = rB[2 + hf]
                            S.op("dve", STT(tmpA[:], bk[:, :], ssvA[:, c0 + 2:c0 + 3], gpost_b[:, 512 * hf:512 * (hf + 1)], ALU.mult, ALU.mult),
                                 reads=[rbk, r_ssvA, r_gpost], writes=[r_tmpA])
                            S.op("dve", TT(hr[:, 512 * hf:512 * (hf + 1)], tmpA[:], hr[:, 512 * hf:512 * (hf + 1)], ALU.add),
                                 reads=[r_tmpA, r_hr], writes=[r_hr])
                            S.op("act", ACT(junkA[:], hr[:, 512 * hf:512 * (hf + 1)], AF.Square, accum_out=ssvA[:, c0 + hf:c0 + hf + 1]),
                                 reads=[r_hr], writes=[r_junkA, r_ssvA])
                        yield None
                        S.op("dve", TT(ssvA[:, c0 + 3:c0 + 4], ssvA[:, c0:c0 + 1], ssvA[:, c0 + 1:c0 + 2], ALU.add), reads=[r_ssvA], writes=[r_ssvA])
                        S.op("act", ACT(ssvA[:, c0 + 3:c0 + 4], ssvA[:, c0 + 3:c0 + 4], AF.Ln, scale=1.0 / D, bias=eps2[:, 0:1]), reads=[r_ssvA, r_eps2], writes=[r_ssvA])
                        S.op("act", ACT(ssvA[:, c0 + 3:c0 + 4], ssvA[:, c0 + 3:c0 + 4], AF.Exp, scale=-0.5), reads=[r_ssvA], writes=[r_ssvA])
                        yield None
                        S.op("act", ACT(hs[s][:], hr[:], AF.Copy, scale=ssvA[:, c0 + 3:c0 + 4]), reads=[r_hr, r_ssvA], writes=[r_hs[s]])
                        yield None
                        B1b = B[1][:].bitcast(BF16)
                        S.group("pe", [TR(B1b[:, 128 * k:128 * (k + 1)], hs[s][:, 128 * k:128 * (k + 1)], ident2[:]) for k in range(8)],
                                reads=[r_hs[s], r_id2], writes=[rB[1]])
                        yield None
                        S.op("dve", TT(hT2[tt % 2][:, :, 128 * s:128 * (s + 1)], B1b.rearrange("p (k t) -> p k t", k=8),
                                       gffn8[:].unsqueeze(2).to_broadcast([128, 8, 128]), ALU.mult),
                             reads=[rB[1], r_gffn], writes=[r_hT2[tt % 2]])
                        yield None

                class Pump:
                    def __init__(self, gen):
                        self.gen = gen
                        self.released = set()
                        self.pending = None

                    def release(self, name):
                        self.released.add(name)

                    def step(self, n):
                        for _ in range(n):
                            if self.gen is None:
                                return
                            if self.pending is not None:
                                if self.pending not in self.released:
                                    return
                                self.pending = None
                            try:
                                r = next(self.gen)
                            except StopIteration:
                                self.gen = None
                                return
                            if r is not None:
                                self.pending = r

                    def flush(self):
                        while self.gen is not None:
                            if self.pending is not None:
                                assert self.pending in self.released, self.pending
                            self.step(1)

                def h2(tt, P):
                    for f in range(NF):
                        gb2 = B[4 + (f % 2)]; rgb2 = rB[4 + (f % 2)]
                        gv = gb2[:].rearrange("p (h q) -> p h q", h=2)
                        h2t = hT2[tt % 2]; r_h2t = r_hT2[tt % 2]
                        S.group("pe", [MM(gv[:, 0, :], Wg[:, k, 128 * f:128 * (f + 1)], h2t[:, k, :], k == 0, k == 7) for k in range(8)],
                                reads=[r_Wgc[f], r_h2t], writes=[rgb2])
                        P.step(1)
                        S.group("pe", [MM(gv[:, 1, :], Wu[:, k, 128 * f:128 * (f + 1)], h2t[:, k, :], k == 0, k == 7) for k in range(8)],
                                reads=[r_Wuc[f], r_h2t], writes=[rgb2])
                        S.op("act", ACT(sg[f % 2][:], gv[:, 0, :], AF.Silu), reads=[rgb2], writes=[r_sg[f % 2]])
                        S.op("dve", TT(aT[:, f, :], gv[:, 1, :], sg[f % 2][:], ALU.mult), reads=[rgb2, r_sg[f % 2]], writes=[r_aT])
                        P.step(1)
                    for s in range(2):
                        row0 = tt * T2 + 128 * s
                        hr = hres[(2 * tt + s) % 3]; r_hr = r_hres[(2 * tt + s) % 3]
                        for hf in range(2):
                            bk = B[6 + hf]; rbk = rB[6 + hf]
                            S.group("pe", [MM(bk[:, :], aT[:, f, 128 * s:128 * (s + 1)], Wd[:, f, 512 * hf:512 * (hf + 1)], f == 0, f == NF - 1) for f in range(NF)],
                                    reads=[r_aT] + r_Wdc, writes=[rbk])
                            S.op("act", ACT(junkB[:], bk[:, :], AF.Square, accum_out=ssvB[:, hf:hf + 1]), reads=[rbk], writes=[r_junkB, r_ssvB])
                            P.step(6)
                        S.op("dve", TT(ssvB[:, 2:3], ssvB[:, 0:1], ssvB[:, 1:2], ALU.add), reads=[r_ssvB], writes=[r_ssvB])
                        S.op("act", ACT(ssvB[:, 2:3], ssvB[:, 2:3], AF.Ln, scale=1.0 / D, bias=eps2[:, 0:1]), reads=[r_ssvB, r_eps2], writes=[r_ssvB])
                        S.op("act", ACT(ssvB[:, 2:3], ssvB[:, 2:3], AF.Exp, scale=-0.5), reads=[r_ssvB], writes=[r_ssvB])
                        for hf in range(2):
                            bk = B[6 + hf]; rbk = rB[6 + hf]
                            S.op("dve", STT(tmpB[:], bk[:, :], ssvB[:, 2:3], gpffn_b[:, 512 * hf:512 * (hf + 1)], ALU.mult, ALU.mult),
                                 reads=[rbk, r_ssvB, r_gpffn], writes=[r_tmpB])
                            S.op("dve", TT(hr[:, 512 * hf:512 * (hf + 1)], tmpB[:], hr[:, 512 * hf:512 * (hf + 1)], ALU.add),
                                 reads=[r_tmpB, r_hr], writes=[r_hr])
                        final_events.append(S.dma("sp", out_d[row0:row0 + 128, :], hr[:], "fo%d" % ((2 * tt + s) % 3), reads=[r_hr]))
                        if s == 0:
                            P.release("BAR1")

                P0 = Pump(h1_gen(0)); P0.release("BAR1"); P0.flush()
                for tt in range(NT2):
                    P = Pump(h1_gen(tt + 1) if tt + 1 < NT2 else None)
                    h2(tt, P)
                    P.flush()

                S.wait_all("sp", final_events)

        with nc.Block() as block:
            S.run(block)
    return nc


def _consts():
    invf8 = (500000.0 ** (-np.arange(0, 16, 2, dtype=np.float32) / np.float32(16))).astype(np.float32)
    invf = np.zeros((128, 1), np.float32)
    for base in (0, 64):
        for i in range(16):
            invf[base + i, 0] = invf8[i % 8]
    R = np.zeros((128, 128), np.float32)
    for base in (0, 64):
        for i in range(8):
            R[base + i, base + i + 8] = -1.0
            R[base + i + 8, base + i] = 1.0
    rmatT = np.ascontiguousarray(R.T)
    ident = np.eye(128, dtype=np.float32)
    swapm = np.zeros((128, 128), np.float32)
    for k in range(128):
        swapm[k, (k + 64) % 128] = 1.0
    sgn = np.ones((128, 1), np.float32); sgn[64:] = -1.0
    tri = (np.arange(128)[:, None] <= np.arange(128)[None, :]).astype(np.float32)
    gmask = (np.arange(128)[:, None] // 16 == np.arange(8)[None, :]).astype(np.float32)
    return dict(invf=invf, rmatT=rmatT, ident=ident, swapm=swapm, sgn=sgn, tri=tri, gmask=gmask,
                iota512=np.arange(512, dtype=np.float32), tile_iota=(512.0 * np.arange(NT1)).astype(np.float32))


def make_in_maps(inp):
    c = _consts()
    f = lambda a: np.ascontiguousarray(np.asarray(a, dtype=np.float32))
    w_in = f(inp["w_in"])[0]
    maps = []
    w_out = f(inp["w_out"])[0]
    perm = []
    for r in range(4):
        perm += list(range(128 * r, 128 * r + 128)) + list(range(512 + 128 * r, 512 + 128 * r + 128))
    w_out_p = np.ascontiguousarray(w_out[perm, :])
    for core in range(8):
        b, h = core // 4, core % 4
        cols = (list(range(64 * h, 64 * h + 64)) + list(range(256 + 64 * h, 256 + 64 * h + 64)) +
                list(range(512 + 64 * h, 512 + 64 * h + 64)) + list(range(768 + 64 * h, 768 + 64 * h + 64)) +
                list(range(1024 + 128 * h, 1024 + 128 * h + 128)) + list(range(1536 + 128 * h, 1536 + 128 * h + 128)))
        gs = slice(8 * h, 8 * h + 8)
        m = dict(c)
        m["x"] = f(inp["x"][b])
        m["meta"] = f(inp["meta"])
        m["win"] = np.ascontiguousarray(w_in[:, cols])
        m["gpre"] = np.ascontiguousarray(f(inp["pre_mix_g"])[0].reshape(8, 128).T)
        m["lamv"] = np.stack([f(inp["lambda_q1"])[0], f(inp["lambda_k1"])[0], f(inp["lambda_q2"])[0], f(inp["lambda_k2"])[0]])
        m["subg"] = f(inp["subln_g"])[0].reshape(128, 1)
        m["areT"] = np.ascontiguousarray(f(inp["a_re"])[0, gs].T)
        m["aimT"] = np.ascontiguousarray(f(inp["a_im"])[0, gs].T)
        m["logdt"] = f(inp["log_dt"])[0, gs]
        m["bre"] = f(inp["b_re"])[0, gs]
        m["bim"] = f(inp["b_im"])[0, gs]
        m["creT"] = np.ascontiguousarray(f(inp["c_re"])[0, gs].transpose(2, 0, 1).reshape(64, 128))
        m["cimT"] = np.ascontiguousarray(f(inp["c_im"])[0, gs].transpose(2, 0, 1).reshape(64, 128))
        m["dskip"] = f(inp["d_skip"])[0, 128 * h:128 * h + 128].reshape(128, 1)
        m["wglu"] = f(inp["w_glu"])[0]
        m["bglu"] = np.ascontiguousarray(f(inp["b_glu"])[0].reshape(4, 128).T)
        m["gssm"] = np.ascontiguousarray(f(inp["ssm_out_g"])[0].reshape(4, 128).T)
        m["wout"] = w_out_p
        m["gpost"] = f(inp["post_mix_g"])[0]
        m["gffn8"] = np.ascontiguousarray(f(inp["pre_ffn_g"])[0].reshape(8, 128).T)
        m["wgate"] = f(inp["w_gate"])[0]
        m["wup"] = f(inp["w_up"])[0]
        m["wdown"] = f(inp["w_down"])[0]
        m["gpffn"] = f(inp["post_ffn_g"])[0]
        m["xres"] = f(inp["x"][b, TOK2 * h:TOK2 * (h + 1)])
        maps.append(m)
    return maps


def kernel(**inputs):
    nc = build_program()
    maps = make_in_maps(inputs)
    res = run_bass_kernel_spmd(nc, maps, core_ids=list(range(8)))
    out = np.zeros((2, SEQ, D), np.float32)
    for core in range(8):
        b, h = core // 4, core % 4
        out[b, TOK2 * h:TOK2 * (h + 1)] = res.results[core]["out"]
    return out
```

```python
import math
import os
from contextlib import ExitStack

import numpy as np
import concourse.bass as bass
import concourse.mybir as mybir
from concourse.bass_utils import run_bass_kernel_spmd

F32 = mybir.dt.float32
BF16 = mybir.dt.bfloat16
I32 = mybir.dt.int32
AF = mybir.ActivationFunctionType
ALU = mybir.AluOpType

D = 1024
SEQ = 8192
NMETA = 16
L = SEQ + NMETA
LP = 8320
NBLK = 65
NT1 = 17
DFF = 2816
NF = 22
EPS = 1e-6
LAM_INIT = 0.8 - 0.6 * math.exp(0.0)
TWO_PI = 2.0 * math.pi
C1 = 6.28125
C2 = TWO_PI - C1
PI_LO = 3.141592
TOK2 = 2048
T2 = 256
NT2 = TOK2 // T2
FRONT_STEPS = 48


class Res:
    __slots__ = ("name", "w", "r", "excl")

    def __init__(self, name, excl=False):
        self.name = name
        self.w = None
        self.r = []
        self.excl = excl


class Sched:
    ENGS = ("pe", "act", "dve", "pool", "sp")

    def __init__(self, nc, stack):
        self.nc = nc
        self.ops = {e: [] for e in self.ENGS}
        self.sem = {e: stack.enter_context(nc.semaphore("s_" + e)) for e in self.ENGS}
        self.cnt = {e: 0 for e in self.ENGS}
        self.waited = {e: {} for e in self.ENGS}
        self.stack = stack
        self.dma_sems = {}
        self.dma_cnt = {}
        self.last = {e: None for e in self.ENGS}
        self.limit = int(os.environ.get("K_LIMIT", "1000000000"))
        self.nops = 0
        self.marks = []

    def _waits(self, eng, deps):
        need = {}
        for ev in deps:
            if ev is None:
                continue
            sem, val, src = ev
            key = sem.name
            if self.waited[eng].get(key, 0) >= val:
                continue
            if key not in need or need[key][1] < val:
                need[key] = (sem, val)
        for key, (sem, val) in need.items():
            self.waited[eng][key] = val
            self.ops[eng].append(("wait", sem, val))

    @staticmethod
    def _deps_for(reads, writes, extra):
        deps = list(extra or [])
        for r in reads or []:
            if r.w is not None:
                deps.append(r.w)
            if r.excl:
                deps.extend(r.r)
        for w in writes or []:
            if w.w is not None:
                deps.append(w.w)
            deps.extend(w.r)
        return deps

    @staticmethod
    def _commit(ev, reads, writes):
        for r in reads or []:
            r.r.append(ev)
        for w in writes or []:
            w.w = ev
            w.r = []

    def op(self, eng, fn, reads=None, writes=None, extra=None):
        return self.group(eng, [fn], reads, writes, extra)

    def group(self, eng, fns, reads=None, writes=None, extra=None):
        self.nops += 1
        if self.nops > self.limit:
            return None
        self._waits(eng, self._deps_for(reads, writes, extra))
        self.cnt[eng] += 1
        ev = (self.sem[eng], self.cnt[eng], eng)
        for f in fns[:-1]:
            self.ops[eng].append(("op", f, False))
        self.ops[eng].append(("op", fns[-1], True))
        self._commit(ev, reads, writes)
        self.last[eng] = ev
        return ev

    def dma(self, eng, out, in_, slot, reads=None, writes=None, extra=None):
        if slot not in self.dma_sems:
            self.dma_sems[slot] = self.stack.enter_context(self.nc.semaphore("d_" + slot))
            self.dma_cnt[slot] = 0
        self.nops += 1
        if self.nops > self.limit:
            return None
        self._waits(eng, self._deps_for(reads, writes, extra))
        self.dma_cnt[slot] += 16
        sem = self.dma_sems[slot]
        ev = (sem, self.dma_cnt[slot], "dma")
        self.ops[eng].append(("dma", out, in_, sem))
        self._commit(ev, reads, writes)
        return ev

    def raw(self, eng, fn):
        self.ops[eng].append(("raw", fn))

    def wait_all(self, eng, events):
        self._waits(eng, events)

    def run(self, block):
        sched = self

        def replay(engname, e):
            for item in sched.ops[engname]:
                if item[0] == "wait":
                    e.wait_ge(item[1], item[2])
                elif item[0] == "op":
                    ins = item[1](e)
                    if item[2]:
                        ins.then_inc(sched.sem[engname], 1)
                elif item[0] == "dma":
                    e.dma_start(out=item[1], in_=item[2]).then_inc(item[3], 16)
                elif item[0] == "raw":
                    item[1](e)

        @block.tensor
        def _(e):
            replay("pe", e)

        @block.scalar
        def _(e):
            replay("act", e)

        @block.vector
        def _(e):
            replay("dve", e)

        @block.gpsimd
        def _(e):
            replay("pool", e)

        @block.sync
        def _(e):
            replay("sp", e)


def MM(out, lhsT, rhs, start, stop):
    return lambda e: e.matmul(out, lhsT, rhs, start=start, stop=stop)


def TR(out, in_, ident):
    return lambda e: e.transpose(out, in_, ident)


def ACT(out, in_, func, **kw):
    return lambda e: e.activation(out=out, in_=in_, func=func, **kw)


def TT(out, a, b, op):
    return lambda e: e.tensor_tensor(out=out, in0=a, in1=b, op=op)


def TS(out, a, s1, op0, s2=None, op1=None):
    if op1 is None:
        return lambda e: e.tensor_scalar(out=out, in0=a, scalar1=s1, scalar2=None, op0=op0)
    return lambda e: e.tensor_scalar(out=out, in0=a, scalar1=s1, scalar2=s2, op0=op0, op1=op1)


def STT(out, a, s, b, op0, op1):
    return lambda e: e.scalar_tensor_tensor(out=out, in0=a, scalar=s, in1=b, op0=op0, op1=op1)


def CP(out, in_):
    return lambda e: e.tensor_copy(out=out, in_=in_)


def MEMSET(ap, v):
    return lambda e: e.memset(ap, v)


def RECIP(out, in_):
    return lambda e: e.reciprocal(out=out, in_=in_)


def SCAN(out, d0, d1, init):
    return lambda e: e.tensor_tensor_scan(out=out, data0=d0, data1=d1, initial=init, op0=ALU.mult, op1=ALU.add)


def build_program(nt1=NT1, do_phase2=True, debug=False, stop=9, p2mode="full"):
    nc = bass.Bass("TRN2", target_bir_lowering=False)

    def din(name, shape, dt=F32):
        return nc.dram_tensor(name, list(shape), dt, kind="ExternalInput").ap()

    ident_d = din("ident", [128, 128])
    if nt1 > 0:
        x_d = din("x", [SEQ, D])
        meta_d = din("meta", [NMETA, D])
        win_d = din("win", [D, 512])
        gpre_d = din("gpre", [128, 8])
        invf_d = din("invf", [128, 1])
        iota512_d = din("iota512", [512])
        tile_iota_d = din("tile_iota", [NT1])
        rmatT_d = din("rmatT", [128, 128])
        swap_d = din("swapm", [128, 128])
        sgn_d = din("sgn", [128, 1])
        tri_d = din("tri", [128, 128])
        gmask_d = din("gmask", [128, 8])
        lam_d = din("lamv", [4, 64])
        subg_d = din("subg", [128, 1])
        areT_d = din("areT", [64, 8])
        aimT_d = din("aimT", [64, 8])
        logdt_d = din("logdt", [8])
        bre_d = din("bre", [8, 64, 16])
        bim_d = din("bim", [8, 64, 16])
        creT_d = din("creT", [64, 128])
        cimT_d = din("cimT", [64, 128])
        dskip_d = din("dskip", [128, 1])
    if do_phase2:
        wglu_d = din("wglu", [512, 512])
        bglu_d = din("bglu", [128, 4])
        gssm_d = din("gssm", [128, 4])
        wout_d = din("wout", [D, D])
        gpost_d = din("gpost", [D])
        gffn_d = din("gffn8", [128, 8])
        wgate_d = din("wgate", [D, DFF])
        wup_d = din("wup", [D, DFF])
        wdown_d = din("wdown", [DFF, D])
        gpffn_d = din("gpffn", [D])
        xres_d = din("xres", [TOK2, D])
    out_d = nc.dram_tensor("out", [TOK2, D], F32, kind="ExternalOutput").ap() if do_phase2 else None

    xout_dbg_d = din("xout_dbg", [1024, TOK2]) if p2mode == "nogather" else None
    xin_t = [nc.dram_tensor("xchg_in%d" % q, [256, TOK2], BF16) for q in range(4)]
    xout_t = nc.dram_tensor("xchg_out", [4 * 1024, TOK2], BF16)
    xin = [t.ap() for t in xin_t]
    xout = xout_t.ap()
    dbg_d = None
    if debug:
        dbg_d = nc.dram_tensor("dbg", [256, SEQ], F32, kind="ExternalOutput").ap()

    with ExitStack() as top:
        S = Sched(nc, top)

        def ps_bank(name):
            return top.enter_context(nc.psum_tensor(name, [128, 512], F32))

        use_cc = do_phase2 and p2mode == "full"
        cc_sem = top.enter_context(nc.semaphore("cc_sem")) if use_cc else None
        q_events = [[] for _ in range(4)]

        def xchg_write(row0, lo, hi, src, src_col0, slot, res):
            evs = []
            c = lo
            while c < hi:
                q = c // TOK2
                ce = min(hi, TOK2 * (q + 1))
                ev = S.dma("sp", xin[q][row0:row0 + 128, c - TOK2 * q:ce - TOK2 * q],
                           src[:, src_col0 + (c - lo):src_col0 + (ce - lo)], slot, reads=[res])
                q_events[q].append(ev)
                evs.append(ev)
                c = ce
            return evs

        banks = [ps_bank("bank%d" % i) for i in range(8)]
        rbank = [Res("bank%d" % i, excl=True) for i in range(8)]

        def _phase1():
            with ExitStack() as p1:
                def sb(name, shape, dt=F32):
                    return p1.enter_context(nc.sbuf_tensor("a_" + name, list(shape), dt))

                ident_f = sb("ident_f", [128, 128]); r_ident_f = Res("ident_f")
                ident_b = sb("ident_b", [128, 128], BF16); r_ident_b = Res("ident_b")
                rmatT_b = sb("rmatT_b", [128, 128], BF16); r_rmat = Res("rmat")
                swap_f = sb("swap_f", [128, 128]); r_swap = Res("swap")
                tri_b = sb("tri_b", [128, 128], BF16); r_tri = Res("tri")
                ones_b = sb("ones_b", [128, 128], BF16); r_ones = Res("ones")
                sgn = sb("sgn", [128, 1]); r_sgn = Res("sgn")
                gmask = sb("gmask", [128, 8]); r_gmask = Res("gmask")
                invf = sb("invf", [128, 1]); r_invf = Res("invf")
                iota512 = sb("iota512", [128, 512]); r_iota = Res("iota512")
                tile_iota = sb("tile_iota", [128, NT1]); r_tiota = Res("tile_iota")
                gpre = sb("gpre", [128, 8]); r_gpre = Res("gpre")
                subg = sb("subg", [128, 1]); r_subg = Res("subg")
                dskip = sb("dskip", [128, 1]); r_dskip = Res("dskip")
                epsc = sb("epsc", [128, 1]); r_eps = Res("eps")
                lamv = sb("lamv", [128, 4, 64]); r_lamv = Res("lamv")
                neglam = sb("neglam", [128, 1]); r_neglam = Res("neglam")

                S.dma("sp", ident_f[:], ident_d, "c0", writes=[r_ident_f])
                S.dma("pool", ident_b[:], ident_d, "c1", writes=[r_ident_b])
                S.dma("pool", rmatT_b[:], rmatT_d, "c2", writes=[r_rmat])
                S.dma("sp", swap_f[:], swap_d, "c3", writes=[r_swap])
                S.dma("pool", tri_b[:], tri_d, "c4", writes=[r_tri])
                S.dma("sp", sgn[:], sgn_d, "c5", writes=[r_sgn])
                S.dma("sp", gmask[:], gmask_d, "c6", writes=[r_gmask])
                S.dma("sp", invf[:], invf_d, "c7", writes=[r_invf])
                S.dma("sp", iota512[:], iota512_d.partition_broadcast(128), "c8", writes=[r_iota])
                S.dma("sp", tile_iota[:], tile_iota_d.partition_broadcast(128), "c9", writes=[r_tiota])
                S.dma("sp", gpre[:], gpre_d, "c10", writes=[r_gpre])
                S.dma("sp", subg[:], subg_d, "c11", writes=[r_subg])
                S.dma("sp", dskip[:], dskip_d, "c12", writes=[r_dskip])
                S.dma("sp", lamv[:].rearrange("p a b -> p (a b)"),
                      lam_d.rearrange("a b -> (a b)").partition_broadcast(128), "c13", writes=[r_lamv])
                S.op("dve", MEMSET(ones_b[:], 1.0), writes=[r_ones])
                S.op("dve", MEMSET(epsc[:], EPS), writes=[r_eps])

                scr = [sb("scr%d" % i, [128, 512]) for i in range(4)]
                r_scr = [Res("scr%d" % i) for i in range(4)]
                ki_t = sb("ki_t", [128, 512], I32); r_ki = Res("ki")

                def range_reduce(eng, out_ap, arg_ap, n, r_out, r_arg):
                    kf = scr[3][:, 0:n]
                    S.op(eng, TS(ki_t[:, 0:n], arg_ap, 1.0 / TWO_PI, ALU.mult), reads=[r_arg], writes=[r_ki])
                    S.op(eng, CP(kf, ki_t[:, 0:n]), reads=[r_ki], writes=[r_scr[3]])
                    S.op(eng, STT(out_ap, kf, -C1, arg_ap, ALU.mult, ALU.add), reads=[r_scr[3], r_arg], writes=[r_out])
                    S.op(eng, STT(out_ap, kf, -C2, out_ap, ALU.mult, ALU.add), reads=[r_scr[3], r_out], writes=[r_out])
                    S.op(eng, TS(out_ap, out_ap, PI_LO, ALU.min, -PI_LO, ALU.max), reads=[r_out], writes=[r_out])

                def sincos(arg_ap, n, r_arg, sin_out, cos_out, r_sin, r_cos):
                    red = scr[2][:, 0:n]
                    range_reduce("dve", red, arg_ap, n, r_scr[2], r_arg)
                    S.op("act", ACT(sin_out, red, AF.Sin), reads=[r_scr[2]], writes=[r_sin])
                    hs = scr[1][:, 0:n]
                    S.op("act", ACT(hs, red, AF.Sin, scale=0.5), reads=[r_scr[2]], writes=[r_scr[1]])
                    S.op("dve", TT(hs, hs, hs, ALU.mult), reads=[r_scr[1]], writes=[r_scr[1]])
                    S.op("dve", TS(cos_out, hs, -2.0, ALU.mult, 1.0, ALU.add), reads=[r_scr[1]], writes=[r_cos])

                lt = sb("lt", [128, 2, 64]); r_lt = Res("lt")
                ls = sb("ls", [128, 2]); r_ls = Res("ls")
                S.op("dve", TT(lt[:, 0, :], lamv[:, 0, :], lamv[:, 1, :], ALU.mult), reads=[r_lamv], writes=[r_lt])
                S.op("dve", TT(lt[:, 1, :], lamv[:, 2, :], lamv[:, 3, :], ALU.mult), reads=[r_lamv, r_lt], writes=[r_lt])
                S.op("dve", lambda e: e.reduce_sum(out=ls[:], in_=lt[:], axis=mybir.AxisListType.X), reads=[r_lt], writes=[r_ls])
                S.op("act", ACT(ls[:], ls[:], AF.Exp), reads=[r_ls], writes=[r_ls])
                S.op("dve", TT(neglam[:], ls[:, 1:2], ls[:, 0:1], ALU.subtract), reads=[r_ls], writes=[r_neglam])
                S.op("dve", TS(neglam[:], neglam[:], -LAM_INIT, ALU.add), reads=[r_neglam], writes=[r_neglam])
                S.op("dve", TS(subg[:], subg[:], 1.0 - LAM_INIT, ALU.mult), reads=[r_subg], writes=[r_subg])

                Wb = sb("Wb", [128, 8, 512], BF16); r_Wb = Res("Wb")
                wstg = [scr[0], scr[1]]
                r_wstg = [r_scr[0], r_scr[1]]
                for k in range(8):
                    S.dma("sp", wstg[k % 2][:], win_d[128 * k:128 * (k + 1), :], "wstg%d" % (k % 2), writes=[r_wstg[k % 2]])
                    S.op("pool", TS(Wb[:, k, :], wstg[k % 2][:], gpre[:, k:k + 1], ALU.mult),
                         reads=[r_wstg[k % 2], r_gpre], writes=[r_Wb])

                C0 = sb("C0", [128, 512]); S0 = sb("S0", [128, 512]); r_C0 = Res("C0"); r_S0 = Res("S0")
                cI = sb("cI", [128, NT1]); sI = sb("sI", [128, NT1]); r_cI = Res("cI"); r_sI = Res("sI")
                S.op("dve", TS(scr[0][:], iota512[:], invf[:, 0:1], ALU.mult), reads=[r_iota, r_invf], writes=[r_scr[0]])
                sincos(scr[0][:], 512, r_scr[0], S0[:], C0[:], r_S0, r_C0)
                S.op("dve", TS(scr[0][:, 0:NT1], tile_iota[:], invf[:, 0:1], ALU.mult), reads=[r_tiota, r_invf], writes=[r_scr[0]])
                sincos(scr[0][:, 0:NT1], NT1, r_scr[0], sI[:], cI[:], r_sI, r_cI)
                nsI = sb("nsI", [128, NT1]); r_nsI = Res("nsI")
                S.op("dve", TS(nsI[:], sI[:], -1.0, ALU.mult), reads=[r_sI], writes=[r_nsI])

                are2 = sb("are2", [128, 8]); aim2 = sb("aim2", [128, 8]); dt2 = sb("dt2", [128, 8])
                r_are = Res("are2"); r_aim = Res("aim2"); r_dt = Res("dt2")
                S.dma("sp", are2[0:64, :], areT_d, "s0", writes=[r_are])
                S.dma("sp", are2[64:128, :], areT_d, "s0", writes=[r_are])
                S.dma("sp", aim2[0:64, :], aimT_d, "s1", writes=[r_aim])
                S.dma("sp", aim2[64:128, :], aimT_d, "s1", writes=[r_aim])
                S.dma("sp", dt2[:], logdt_d.partition_broadcast(128), "s2", writes=[r_dt])
                S.op("act", ACT(dt2[:], dt2[:], AF.Exp), reads=[r_dt], writes=[r_dt])
                rdec = sb("rdec", [128, 8]); r_rdec = Res("rdec")
                theta = sb("theta", [128, 8]); r_theta = Res("theta")
                S.op("dve", TT(rdec[:], are2[:], dt2[:], ALU.mult), reads=[r_are, r_dt], writes=[r_rdec])
                S.op("act", ACT(rdec[:], rdec[:], AF.Exp), reads=[r_rdec], writes=[r_rdec])
                S.op("dve", TT(theta[:], aim2[:], dt2[:], ALU.mult), reads=[r_aim, r_dt], writes=[r_theta])
                c1t = sb("c1t", [128, 8]); s1t = sb("s1t", [128, 8]); r_c1t = Res("c1t"); r_s1t = Res("s1t")
                cTt = sb("cTt", [128, 8]); sTt = sb("sTt", [128, 8]); r_cTt = Res("cTt"); r_sTt = Res("sTt")
                sincos(theta[:], 8, r_theta, s1t[:], c1t[:], r_s1t, r_c1t)
                S.op("dve", TS(scr[0][:, 0:8], theta[:], 512.0, ALU.mult), reads=[r_theta], writes=[r_scr[0]])
                sincos(scr[0][:, 0:8], 8, r_scr[0], sTt[:], cTt[:], r_sTt, r_cTt)
                S.op("dve", TT(sTt[:], sTt[:], sgn[:, 0:1].to_broadcast([128, 8]), ALU.mult), reads=[r_sTt, r_sgn], writes=[r_sTt])
                COSg = sb("COSg", [128, 8, 512]); SINg = sb("SINg", [128, 8, 512])
                r_COSg = [Res("COSg%d" % g) for g in range(8)]; r_SINg = [Res("SINg%d" % g) for g in range(8)]
                for g in range(8):
                    S.op("dve", TS(scr[0][:], iota512[:], theta[:, g:g + 1], ALU.mult), reads=[r_iota, r_theta], writes=[r_scr[0]])
                    sincos(scr[0][:], 512, r_scr[0], SINg[:, g, :], COSg[:, g, :], r_SINg[g], r_COSg[g])
                ROT = sb("ROT", [128, 8, 128]); r_ROT = Res("ROT")
                for g in range(8):
                    S.op("dve", TS(ROT[:, g, :], ident_f[:], cTt[:, g:g + 1], ALU.mult), reads=[r_ident_f, r_cTt], writes=[r_ROT])
                    S.op("dve", STT(ROT[:, g, :], swap_f[:], sTt[:, g:g + 1], ROT[:, g, :], ALU.mult, ALU.add),
                         reads=[r_swap, r_sTt, r_ROT], writes=[r_ROT])
                fre = sb("fre", [128, 8]); fim = sb("fim", [128, 8]); r_fre = Res("fre"); r_fim = Res("fim")
                nr = sb("nr", [128, 8]); ni = sb("ni", [128, 8]); den = sb("den", [128, 8]); tmp8 = sb("tmp8", [128, 8])
                r_nr = Res("nr"); r_ni = Res("ni"); r_den = Res("den"); r_tmp8 = Res("tmp8")
                S.op("dve", TT(nr[:], rdec[:], c1t[:], ALU.mult), reads=[r_rdec, r_c1t], writes=[r_nr])
                S.op("dve", TS(nr[:], nr[:], -1.0, ALU.add), reads=[r_nr], writes=[r_nr])
                S.op("dve", TT(ni[:], rdec[:], s1t[:], ALU.mult), reads=[r_rdec, r_s1t], writes=[r_ni])
                S.op("dve", TT(den[:], are2[:], are2[:], ALU.mult), reads=[r_are], writes=[r_den])
                S.op("dve", TT(tmp8[:], aim2[:], aim2[:], ALU.mult), reads=[r_aim], writes=[r_tmp8])
                S.op("dve", TT(den[:], den[:], tmp8[:], ALU.add), reads=[r_den, r_tmp8], writes=[r_den])
                S.op("dve", RECIP(den[:], den[:]), reads=[r_den], writes=[r_den])
                S.op("dve", TT(fre[:], nr[:], are2[:], ALU.mult), reads=[r_nr, r_are], writes=[r_fre])
                S.op("dve", TT(tmp8[:], ni[:], aim2[:], ALU.mult), reads=[r_ni, r_aim], writes=[r_tmp8])
                S.op("dve", TT(fre[:], fre[:], tmp8[:], ALU.add), reads=[r_fre, r_tmp8], writes=[r_fre])
                S.op("dve", TT(fre[:], fre[:], den[:], ALU.mult), reads=[r_fre, r_den], writes=[r_fre])
                S.op("dve", TT(fim[:], ni[:], are2[:], ALU.mult), reads=[r_ni, r_are], writes=[r_fim])
                S.op("dve", TT(tmp8[:], nr[:], aim2[:], ALU.mult), reads=[r_nr, r_aim], writes=[r_tmp8])
                S.op("dve", TT(fim[:], fim[:], tmp8[:], ALU.subtract), reads=[r_fim, r_tmp8], writes=[r_fim])
                S.op("dve", TT(fim[:], fim[:], den[:], ALU.mult), reads=[r_fim, r_den], writes=[r_fim])
                Bre = sb("Bre", [64, 8, 16]); Bim = sb("Bim", [64, 8, 16]); r_Bre = Res("Bre"); r_Bim = Res("Bim")
                S.dma("sp", Bre[:], bre_d.rearrange("g p h -> p g h"), "s3", writes=[r_Bre])
                S.dma("sp", Bim[:], bim_d.rearrange("g p h -> p g h"), "s4", writes=[r_Bim])
                BBre = sb("BBre", [64, 8, 16]); BBim = sb("BBim", [64, 8, 16]); tB = sb("tB", [64, 8, 16])
                r_BBre = Res("BBre"); r_BBim = Res("BBim"); r_tB = Res("tB")
                fre_b = fre[0:64, :].unsqueeze(2).to_broadcast([64, 8, 16])
                fim_b = fim[0:64, :].unsqueeze(2).to_broadcast([64, 8, 16])
                S.op("dve", TT(BBre[:], Bre[:], fre_b, ALU.mult), reads=[r_Bre, r_fre], writes=[r_BBre])
                S.op("dve", TT(tB[:], Bim[:], fim_b, ALU.mult), reads=[r_Bim, r_fim], writes=[r_tB])
                S.op("dve", TT(BBre[:], BBre[:], tB[:], ALU.subtract), reads=[r_BBre, r_tB], writes=[r_BBre])
                S.op("dve", TT(BBim[:], Bim[:], fre_b, ALU.mult), reads=[r_Bim, r_fre], writes=[r_BBim])
                S.op("dve", TT(tB[:], Bre[:], fim_b, ALU.mult), reads=[r_Bre, r_fim], writes=[r_tB])
                S.op("dve", TT(BBim[:], BBim[:], tB[:], ALU.add), reads=[r_BBim, r_tB], writes=[r_BBim])
                FB1 = sb("FB1", [128, 128]); FB2 = sb("FB2", [128, 128]); r_FB1 = Res("FB1"); r_FB2 = Res("FB2")
                S.group("pe", [TR(banks[0][:, 0:64], BBre[:].rearrange("p g h -> p (g h)"), ident_f[0:64, 0:64]),
                               TR(banks[0][:, 64:128], BBim[:].rearrange("p g h -> p (g h)"), ident_f[0:64, 0:64])],
                        reads=[r_BBre, r_BBim, r_ident_f], writes=[rbank[0]])
                S.op("dve", CP(FB1[:], banks[0][:, 0:128]), reads=[rbank[0]], writes=[r_FB1])
                S.op("dve", CP(FB2[:, 0:64], banks[0][:, 64:128]), reads=[rbank[0]], writes=[r_FB2])
                S.op("dve", TS(FB2[:, 64:128], banks[0][:, 0:64], -1.0, ALU.mult), reads=[rbank[0], r_FB2], writes=[r_FB2])
                LB1 = sb("LB1", [128, 8, 128], BF16); LB2 = sb("LB2", [128, 8, 128], BF16); r_LB = Res("LB")
                for g in range(8):
                    S.op("pool", TS(LB1[:, g, :], FB1[:], gmask[:, g:g + 1], ALU.mult), reads=[r_FB1, r_gmask], writes=[r_LB])
                    S.op("pool", TS(LB2[:, g, :], FB2[:], gmask[:, g:g + 1], ALU.mult), reads=[r_FB2, r_gmask], writes=[r_LB])
                CC1 = sb("CC1", [128, 128]); CC2 = sb("CC2", [128, 128]); r_CC1 = Res("CC1"); r_CC2 = Res("CC2")
                S.dma("sp", CC1[0:64, :], creT_d, "s5", writes=[r_CC1])
                S.dma("sp", CC1[64:128, :], cimT_d, "s5", writes=[r_CC1])
                S.dma("sp", CC2[0:64, :], cimT_d, "s6", writes=[r_CC2])
                S.dma("sp", CC2[64:128, :], creT_d, "s6", writes=[r_CC2])
                S.op("dve", TS(CC1[64:128, :], CC1[64:128, :], -1.0, ALU.mult), reads=[r_CC1], writes=[r_CC1])
                S.op("dve", TS(CC2[:], CC2[:], -1.0, ALU.mult), reads=[r_CC2], writes=[r_CC2])
                LC1 = sb("LC1", [128, 8, 128], BF16); LC2 = sb("LC2", [128, 8, 128], BF16); r_LC = Res("LC")
                S.op("pool", MEMSET(LC1[:], 0.0), writes=[r_LC])
                S.op("pool", MEMSET(LC2[:], 0.0), writes=[r_LC])
                for g in range(8):
                    S.op("dve", CP(LC1[:, g, 16 * g:16 * g + 16], CC1[:, 16 * g:16 * g + 16]), reads=[r_CC1], writes=[r_LC])
                    S.op("dve", CP(LC2[:, g, 16 * g:16 * g + 16], CC2[:, 16 * g:16 * g + 16]), reads=[r_CC2], writes=[r_LC])
                diagD = sb("diagD", [128, 128], BF16); r_diagD = Res("diagD")
                S.op("dve", TS(diagD[:], ident_f[:], dskip[:, 0:1], ALU.mult), reads=[r_ident_f, r_dskip], writes=[r_diagD])
                s0 = sb("s0", [128, 2, 8]); r_s0 = [Res("s0_0"), Res("s0_1")]
                S.op("dve", MEMSET(s0[:], 0.0), writes=r_s0)

                xt = [sb("xt%d" % i, [128, D]) for i in range(2)]; r_xt = [Res("xt%d" % i) for i in range(2)]
                junk = sb("junk", [128, D], BF16); r_junk = Res("junk")
                ssq = sb("ssq", [128, 4]); r_ssq = [Res("ssq%d" % i) for i in range(4)]
                xs = [sb("xs%d" % i, [128, D], BF16) for i in range(2)]; r_xs = [Res("xs%d" % i) for i in range(2)]
                hT = [sb("hT%d" % i, [128, 8, 512], BF16) for i in range(2)]; r_hT = [Res("hT%d" % i) for i in range(2)]
                COSt = sb("COSt", [128, 512]); SINt = sb("SINt", [128, 512]); r_COSt = Res("COSt"); r_SINt = Res("SINt")
                qraw = sb("qraw", [128, 512], BF16); r_qraw = Res("qraw")
                rt1 = sb("rt1", [128, 512]); rt2 = sb("rt2", [128, 512]); r_rt1 = Res("rt1"); r_rt2 = Res("rt2")
                QT = [sb("QT%d" % i, [128, 512], BF16) for i in range(2)]; r_QT = [Res("QT%d" % i) for i in range(2)]
                KTa = sb("KTa", [128, LP], BF16); KTb = sb("KTb", [128, LP], BF16)
                r_KT = [Res("KT%d" % i) for i in range(NT1)]
                r_KTz = Res("KTz")
                S.op("pool", MEMSET(KTa[64:128, :], 0.0), writes=[r_KTz])
                S.op("pool", MEMSET(KTb[0:64, :], 0.0), writes=[r_KTz])
                V = sb("V", [128, NBLK, 128], BF16); r_V = [Res("V%d" % i) for i in range(NT1)]
                uT = [sb("uT%d" % i, [128, 512], BF16) for i in range(2)]; r_uT = [Res("uT%d" % i) for i in range(2)]
                st1 = [sb("st1_%d" % i, [128, 512]) for i in range(2)]; r_st1 = [Res("st1_%d" % i) for i in range(2)]
                st2 = [sb("st2_%d" % i, [128, 512]) for i in range(2)]; r_st2 = [Res("st2_%d" % i) for i in range(2)]
                bt = [sb("bt%d" % i, [128, 512]) for i in range(2)]; r_bt = [Res("bt%d" % i) for i in range(2)]
                Xs = [sb("Xs%d" % i, [128, 512]) for i in range(3)]; r_Xs = [Res("Xs%d" % i) for i in range(3)]
                P1s = [sb("P1s%d" % i, [128, 512], BF16) for i in range(3)]; r_P1s = [Res("P1s%d" % i) for i in range(3)]
                P2s = [sb("P2s%d" % i, [128, 512], BF16) for i in range(3)]; r_P2s = [Res("P2s%d" % i) for i in range(3)]
                ybuf = [sb("ybuf%d" % i, [128, 512], BF16) for i in range(2)]; r_ybuf = [Res("ybuf%d" % i) for i in range(2)]
                Pb = [sb("Pb%d" % i, [128, 2, 256], BF16) for i in range(3)]; r_Pb = [Res("Pb%d" % i) for i in range(3)]
                rL = sb("rL", [128, 2, 256]); r_rL = Res("rL")
                Osb = sb("Osb", [128, 2, 256]); r_Osb = Res("Osb")
                On = Osb; r_On = r_Osb
                pending_fin = []
                od = sb("od", [128, 256]); r_od = Res("od")
                osq = sb("osq", [128, 256], BF16); r_osq = Res("osq")
                orstd = sb("orstd", [128, 256]); r_orstd = Res("orstd")
                obuf = [sb("obuf%d" % i, [128, 256], BF16) for i in range(2)]; r_obuf = [Res("obuf%d" % i) for i in range(2)]

                F0, F1, F2, F3 = banks[0], banks[1], banks[2], banks[3]
                rF0, rF1, rF2, rF3 = rbank[0], rbank[1], rbank[2], rbank[3]
                Sb = [banks[4], banks[5]]; rSb = [rbank[4], rbank[5]]
                Ob, Lb = banks[6], banks[7]; rOb, rLb = rbank[6], rbank[7]
                F0b = F0[:].bitcast(BF16)


                pcount = [0]
                sbcount = [0]
                out_events = []

                def front_gen(ti):
                    nb = 4 if ti < 16 else 1
                    N = 128 * nb
                    tok0 = 512 * ti
                    hb = ti % 2
                    for j in range(nb):
                        s = 4 * ti + j
                        xb = sbcount[0] % 2
                        sq = sbcount[0] % 4
                        sbcount[0] += 1
                        if s == 0:
                            S.dma("sp", xt[xb][0:16, :], meta_d, "x%d" % xb, writes=[r_xt[xb]])
                            S.dma("sp", xt[xb][16:128, :], x_d[0:112, :], "x%d" % xb, writes=[r_xt[xb]])
                        elif s == 64:
                            S.op("pool", MEMSET(xt[xb][:], 0.0), writes=[r_xt[xb]])
                            S.dma("sp", xt[xb][0:16, :], x_d[SEQ - 16:SEQ, :], "x%d" % xb, writes=[r_xt[xb]])
                        else:
                            S.dma("sp", xt[xb][:], x_d[128 * s - 16:128 * s + 112, :], "x%d" % xb, writes=[r_xt[xb]])
                        S.op("act", ACT(junk[:], xt[xb][:], AF.Square, accum_out=ssq[:, sq:sq + 1]),
                             reads=[r_xt[xb]], writes=[r_junk, r_ssq[sq]])
                        S.op("act", ACT(ssq[:, sq:sq + 1], ssq[:, sq:sq + 1], AF.Ln, scale=1.0 / D, bias=epsc[:, 0:1]),
                             reads=[r_ssq[sq], r_eps], writes=[r_ssq[sq]])
                        S.op("act", ACT(ssq[:, sq:sq + 1], ssq[:, sq:sq + 1], AF.Exp, scale=-0.5),
                             reads=[r_ssq[sq]], writes=[r_ssq[sq]])
                        S.op("act", ACT(xs[xb][:], xt[xb][:], AF.Copy, scale=ssq[:, sq:sq + 1]),
                             reads=[r_xt[xb], r_ssq[sq]], writes=[r_xs[xb]])
                        yield
                        S.group("pe", [TR(F0b[:, 128 * k:128 * (k + 1)], xs[xb][:, 128 * k:128 * (k + 1)], ident_b[:]) for k in range(8)],
                                reads=[r_xs[xb], r_ident_b], writes=[rF0])
                        S.op("dve", CP(hT[hb][:, :, 128 * j:128 * (j + 1)], F0b.rearrange("p (k t) -> p k t", k=8)),
                             reads=[rF0], writes=[r_hT[hb]])
                        yield
                    S.op("dve", TS(COSt[:], C0[:], cI[:, ti:ti + 1], ALU.mult), reads=[r_C0, r_cI], writes=[r_COSt])
                    S.op("dve", STT(COSt[:], S0[:], nsI[:, ti:ti + 1], COSt[:], ALU.mult, ALU.add), reads=[r_S0, r_nsI, r_COSt], writes=[r_COSt])
                    S.op("dve", TS(SINt[:], S0[:], cI[:, ti:ti + 1], ALU.mult), reads=[r_S0, r_cI], writes=[r_SINt])
                    S.op("dve", STT(SINt[:], C0[:], sI[:, ti:ti + 1], SINt[:], ALU.mult, ALU.add), reads=[r_C0, r_sI, r_SINt], writes=[r_SINt])
                    yield

                    def proj_fm(c, bank, rb):
                        S.group("pe", [MM(bank[:, 0:N], Wb[:, k, 128 * c:128 * (c + 1)], hT[hb][:, k, 0:N], k == 0, k == 7) for k in range(8)],
                                reads=[r_Wb, r_hT[hb]], writes=[rb])

                    def rope_a():
                        S.op("act", ACT(qraw[:, 0:N], F1[:, 0:N], AF.Copy), reads=[rF1], writes=[r_qraw])
                        S.op("pe", MM(F2[:, 0:N], rmatT_b[:], qraw[:, 0:N], True, True), reads=[r_rmat, r_qraw], writes=[rF2])

                    def rope_b(dst_ap, r_dst, dst2=None):
                        S.op("dve", TT(rt1[:, 0:N], F1[:, 0:N], COSt[:, 0:N], ALU.mult), reads=[rF1, r_COSt], writes=[r_rt1])
                        S.op("dve", TT(rt2[:, 0:N], F2[:, 0:N], SINt[:, 0:N], ALU.mult), reads=[rF2, r_SINt], writes=[r_rt2])
                        if dst2 is None:
                            S.op("pool", TT(dst_ap, rt1[:, 0:N], rt2[:, 0:N], ALU.add), reads=[r_rt1, r_rt2], writes=[r_dst])
                        else:
                            S.op("pool", TT(dst_ap[0:64, :], rt1[0:64, 0:N], rt2[0:64, 0:N], ALU.add), reads=[r_rt1, r_rt2, r_KTz], writes=[r_dst])
                            S.op("pool", TT(dst2[64:128, :], rt1[64:128, 0:N], rt2[64:128, 0:N], ALU.add), reads=[r_rt1, r_rt2, r_KTz], writes=[r_dst])

                    proj_fm(0, F1, rF1)
                    yield
                    rope_a()
                    yield
                    rope_b(QT[hb][:, 0:N], r_QT[hb])
                    yield
                    proj_fm(1, F1, rF1)
                    yield
                    rope_a()
                    yield
                    rope_b(KTa[:, tok0:tok0 + N], r_KT[ti], KTb[:, tok0:tok0 + N])
                    yield
                    proj_fm(3, F1, rF1)
                    yield
                    S.op("act", ACT(uT[hb][:, 0:N], F1[:, 0:N], AF.Copy), reads=[rF1], writes=[r_uT[hb]])
                    fns = []
                    for j in range(nb):
                        for k in range(8):
                            fns.append(MM(F2[:, 128 * j:128 * (j + 1)], hT[hb][:, k, 128 * j:128 * (j + 1)], Wb[:, k, 256:384], k == 0, k == 7))
                    S.group("pe", fns, reads=[r_Wb, r_hT[hb]], writes=[rF2])
                    yield
                    S.op("act", ACT(V[:, 4 * ti:4 * ti + nb, :], F2[:, 0:N].rearrange("p (j d) -> p j d", j=nb), AF.Copy),
                         reads=[rF2], writes=[r_V[ti]])
                    yield
                    par = ti % 2
                    yb = ti % 2

                    def emit_y(g, pb):
                        S.group("pe", [MM(F3[:, 0:N], LC1[:, g, :], P1s[pb][:, 0:N], g == 0, False),
                                       MM(F3[:, 0:N], LC2[:, g, :], P2s[pb][:, 0:N], False, False)],
                                reads=[r_LC, r_P1s[pb], r_P2s[pb]], writes=[rF3])

                    def stage_b(g):
                        pb = g % 3; q = g % 2
                        S.op("dve", SCAN(Xs[pb][:, 0:N], rdec[:, g:g + 1].to_broadcast([128, N]), bt[q][:, 0:N], s0[:, par, g:g + 1]),
                             reads=[r_rdec, r_bt[q], r_s0[par]], writes=[r_Xs[pb]])
                        S.op("pool", TT(P1s[pb][:, 0:N], Xs[pb][:, 0:N], COSg[:, g, 0:N], ALU.mult), reads=[r_Xs[pb], r_COSg[g]], writes=[r_P1s[pb]])
                        S.op("dve", TT(P2s[pb][:, 0:N], Xs[pb][:, 0:N], SINg[:, g, 0:N], ALU.mult), reads=[r_Xs[pb], r_SINg[g]], writes=[r_P2s[pb]])

                    def stage_c(g):
                        pb = g % 3
                        if ti + 1 < nt1:
                            S.op("pe", MM(F0[:, g:g + 1], ROT[:, g, :], Xs[pb][:, N - 1:N], True, True), reads=[r_ROT, r_Xs[pb]], writes=[rF0])
                            S.op("act", ACT(s0[:, 1 - par, g:g + 1], F0[:, g:g + 1], AF.Copy), reads=[rF0], writes=[r_s0[1 - par]])
                        emit_y(g, pb)

                    for it in range(11):
                        if 3 <= it:
                            stage_c(it - 3)
                            yield
                        if it < 8:
                            g = it; q = g % 2
                            S.op("pe", MM(F1[:, 0:N], LB1[:, g, :], uT[hb][:, 0:N], True, True), reads=[r_LB, r_uT[hb]], writes=[rF1])
                            S.op("pe", MM(F2[:, 0:N], LB2[:, g, :], uT[hb][:, 0:N], True, True), reads=[r_LB, r_uT[hb]], writes=[rF2])
                            yield
                            S.op("dve", TT(st1[q][:, 0:N], F1[:, 0:N], COSg[:, g, 0:N], ALU.mult), reads=[rF1, r_COSg[g]], writes=[r_st1[q]])
                            S.op("dve", TT(st2[q][:, 0:N], F2[:, 0:N], SINg[:, g, 0:N], ALU.mult), reads=[rF2, r_SINg[g]], writes=[r_st2[q]])
                            S.op("pool", TT(bt[q][:, 0:N], st1[q][:, 0:N], st2[q][:, 0:N], ALU.add), reads=[r_st1[q], r_st2[q]], writes=[r_bt[q]])
                        if 1 <= it <= 8:
                            stage_b(it - 1)
                        if it <= 8:
                            yield
                    S.op("pe", MM(F3[:, 0:N], diagD[:], uT[hb][:, 0:N], False, True), reads=[r_diagD, r_uT[hb]], writes=[rF3])
                    yield
                    S.op("act", ACT(ybuf[yb][:, 0:N], F3[:, 0:N], AF.Gelu), reads=[rF3], writes=[r_ybuf[yb]])
                    lo = max(tok0, NMETA); hi = min(tok0 + N, L)
                    out_events.extend(xchg_write(128, lo - NMETA, hi - NMETA, ybuf[yb], lo - tok0, "yo%d" % yb, r_ybuf[yb]))
                    yield

                def attention(ti, pump):
                    nb = 4 if ti < 16 else 1
                    tok0 = 512 * ti
                    hb = ti % 2
                    nqt = 2 if nb == 4 else 1
                    nbq = 2 if nb == 4 else 1
                    NQ = 128 * nbq
                    QTv = QT[hb]
                    for qi in range(nqt):
                        c = 2 * ti + qi
                        q0 = 256 * qi
                        nkb = 2 * c + nbq

                        def qk(kb):
                            j = kb - 2 * c
                            qs = 128 * j if j > 0 else 0
                            kt = kb // 4
                            sbk = kb % 2
                            pbk = kb % 3
                            Sv = Sb[sbk][:, 0:2 * NQ].rearrange("p (h q) -> p h q", h=2)
                            Pv = Pb[pbk][:].rearrange("p h q -> p (h q)")[:, 0:2 * NQ].rearrange("p (h q) -> p h q", h=2)
                            S.group("pe", [MM(Sv[:, 0, qs:NQ], KTa[:, 128 * kb:128 * (kb + 1)], QTv[:, q0 + qs:q0 + NQ], True, True),
                                           MM(Sv[:, 1, qs:NQ], KTb[:, 128 * kb:128 * (kb + 1)], QTv[:, q0 + qs:q0 + NQ], True, True)],
                                    reads=[r_KT[kt], r_QT[hb]], writes=[rSb[sbk]])
                            S.op("act", ACT(Pv[:, :, qs:NQ], Sv[:, :, qs:NQ], AF.Exp, scale=0.125), reads=[rSb[sbk]], writes=[r_Pb[pbk]])
                            if j >= 0:
                                S.op("pool", TT(Pv[:, :, qs:qs + 128], Pv[:, :, qs:qs + 128],
                                                tri_b[:].unsqueeze(1).to_broadcast([128, 2, 128]), ALU.mult),
                                     reads=[r_Pb[pbk], r_tri], writes=[r_Pb[pbk]])

                        def pv(kb):
                            j = kb - 2 * c
                            qs = 128 * j if j > 0 else 0
                            kt = kb // 4
                            pbk = kb % 3
                            Pv = Pb[pbk][:].rearrange("p h q -> p (h q)")[:, 0:2 * NQ].rearrange("p (h q) -> p h q", h=2)
                            Ov = Ob[:, 0:2 * NQ].rearrange("p (h q) -> p h q", h=2)
                            Lv = Lb[:, 0:2 * NQ].rearrange("p (h q) -> p h q", h=2)
                            first = (kb == 0); lastk = (kb == nkb - 1)
                            if qs == 0:
                                Pf = Pb[pbk][:].rearrange("p h q -> p (h q)")[:, 0:2 * NQ]
                                S.op("pe", MM(Ob[:, 0:2 * NQ], V[:, kb, :], Pf, first, lastk), reads=[r_V[kt], r_Pb[pbk]], writes=[rOb])
                                S.op("pe", MM(Lb[:, 0:2 * NQ], ones_b[:], Pf, first, lastk), reads=[r_ones, r_Pb[pbk]], writes=[rLb])
                            else:
                                assert not first
                                S.group("pe", [MM(Ov[:, 0, qs:NQ], V[:, kb, :], Pv[:, 0, qs:NQ], False, False),
                                               MM(Ov[:, 1, qs:NQ], V[:, kb, :], Pv[:, 1, qs:NQ], False, lastk)],
                                        reads=[r_V[kt], r_Pb[pbk]], writes=[rOb])
                                S.group("pe", [MM(Lv[:, 0, qs:NQ], ones_b[:], Pv[:, 0, qs:NQ], False, False),
                                               MM(Lv[:, 1, qs:NQ], ones_b[:], Pv[:, 1, qs:NQ], False, lastk)],
                                        reads=[r_ones, r_Pb[pbk]], writes=[rLb])

                        qk(0)
                        for kb in range(nkb):
                            if kb + 1 < nkb:
                                qk(kb + 1)
                            pv(kb)
                            if kb == 2 and pending_fin:
                                pending_fin.pop(0)()
                            pump()
                        while pending_fin:
                            pending_fin.pop(0)()
                        Of = Ob[:, 0:2 * NQ]; Lf = Lb[:, 0:2 * NQ]
                        Osf = Osb[:].rearrange("p h q -> p (h q)")[:, 0:2 * NQ]
                        rLf = rL[:].rearrange("p h q -> p (h q)")[:, 0:2 * NQ]
                        Onf = On[:].rearrange("p h q -> p (h q)")[:, 0:2 * NQ]
                        Onv = Onf.rearrange("p (h q) -> p h q", h=2)
                        S.op("dve", CP(Osf, Of), reads=[rOb], writes=[r_Osb])
                        S.op("act", ACT(rLf, Lf, AF.Ln), reads=[rLb], writes=[r_rL])
                        S.op("act", ACT(rLf, rLf, AF.Exp, scale=-1.0), reads=[r_rL], writes=[r_rL])
                        S.op("dve", TT(Onf, Osf, rLf, ALU.mult), reads=[r_Osb, r_rL], writes=[r_On])
                        S.op("dve", STT(od[:, 0:NQ], Onv[:, 1, :], neglam[:, 0:1], Onv[:, 0, :], ALU.mult, ALU.add),
                             reads=[r_On, r_neglam], writes=[r_od])
                        S.op("act", ACT(osq[:, 0:NQ], od[:, 0:NQ], AF.Square), reads=[r_od], writes=[r_osq])

                        def fin2(c=c, NQ=NQ, q0=q0, tok0=tok0):
                            S.op("pe", MM(F0[:, 256:256 + NQ], ones_b[:], osq[:, 0:NQ], True, True), reads=[r_ones, r_osq], writes=[rF0])
                            S.op("act", ACT(orstd[:, 0:NQ], F0[:, 256:256 + NQ], AF.Ln, scale=1.0 / 128, bias=epsc[:, 0:1]),
                                 reads=[rF0, r_eps], writes=[r_orstd])
                            S.op("act", ACT(orstd[:, 0:NQ], orstd[:, 0:NQ], AF.Exp, scale=-0.5), reads=[r_orstd], writes=[r_orstd])
                            ob = c % 2
                            S.op("dve", STT(obuf[ob][:, 0:NQ], od[:, 0:NQ], subg[:, 0:1], orstd[:, 0:NQ], ALU.mult, ALU.mult),
                                 reads=[r_od, r_subg, r_orstd], writes=[r_obuf[ob]])
                            qt0 = tok0 + q0
                            lo = max(qt0, NMETA); hi = min(qt0 + NQ, L)
                            out_events.extend(xchg_write(0, lo - NMETA, hi - NMETA, obuf[ob], lo - qt0, "oo%d" % ob, r_obuf[ob]))
                        pending_fin.append(fin2)

                for _ in front_gen(0):
                    pass
                for ti in range(nt1):
                    nxt = front_gen(ti + 1) if ti + 1 < nt1 else None
                    nbq_t = 2 if ti < 16 else 1
                    total_kb = sum(2 * (2 * ti + qi) + nbq_t for qi in range(2 if ti < 16 else 1))
                    state = {"steps_left": FRONT_STEPS if nxt is not None else 0, "kb_left": total_kb, "gen": nxt}

                    def pump(state=state):
                        if state["gen"] is None:
                            return
                        kbl = max(state["kb_left"], 1)
                        n = -(-state["steps_left"] // kbl)
                        state["kb_left"] -= 1
                        for _ in range(n):
                            try:
                                next(state["gen"])
                                state["steps_left"] -= 1
                            except StopIteration:
                                state["gen"] = None
                                state["steps_left"] = 0
                                return
                    attention(ti, pump)
                    if state["gen"] is not None:
                        for _ in state["gen"]:
                            pass
                    if ti == nt1 - 1 or (ti >= 4 and ti % 4 == 0):
                        while pending_fin:
                            pending_fin.pop(0)()
                    if use_cc and ti >= 4 and ti % 4 == 0:
                        qd = ti // 4 - 1
                        S.wait_all("pool", q_events[qd])
                        S.raw("pool", (lambda qd: (lambda e: e.collective_compute(
                            "AllGather", ALU.bypass, replica_groups=[[0, 1, 2, 3], [4, 5, 6, 7]],
                            ins=[xin_t[qd].ap().opt()], outs=[xout_t.ap()[1024 * qd:1024 * (qd + 1), :].opt()]).then_inc(cc_sem)))(qd))

            return out_events

        out_events = _phase1() if nt1 > 0 else []

        if debug:
            print('MARKS', S.marks, S.nops)
            S.limit = 10**9
            S.wait_all("pool", out_events)
            evs = [S.dma("pool", dbg_d[:, TOK2 * q:TOK2 * (q + 1)], xin[q], "dbg") for q in range(4)]
            S.wait_all("pool", evs)
        if do_phase2:
            S.wait_all("pool", out_events)
            if p2mode == "nogather":
                evd = S.dma("pool", xout[0:1024, :], xout_dbg_d, "xdbg")
                S.wait_all("pool", [evd])
            else:
                S.raw("pool", lambda e: e.wait_ge(cc_sem, 4))
            S.cnt["pool"] += 1
            gather_ev = (S.sem["pool"], S.cnt["pool"], "pool")
            S.raw("pool", lambda e: e.sem_inc(S.sem["pool"], 1))
            lasts = [S.last[e] for e in ("pe", "act", "dve", "pool") if S.last[e] is not None] + [gather_ev]
            for e in ("pe", "act", "dve", "sp", "pool"):
                S.wait_all(e, lasts)

            with ExitStack() as p2:
                def sb(name, shape, dt=F32):
                    return p2.enter_context(nc.sbuf_tensor("b_" + name, list(shape), dt))

                Wg = sb("Wg", [128, 8, DFF], BF16); r_Wg = Res("Wg")
                Wu = sb("Wu", [128, 8, DFF], BF16); r_Wu = Res("Wu")
                Wd = sb("Wd", [128, NF, D], BF16); r_Wd = Res("Wd")
                Wo = sb("Wo", [128, 8, D], BF16); r_Wo = Res("Wo")
                Wgl = sb("Wgl", [128, 4, 512], BF16); r_Wgl = Res("Wgl")
                S.dma("pool", Wgl[:], wglu_d.rearrange("(k p) c -> p k c", p=128), "w0", writes=[r_Wgl])
                S.dma("pool", Wo[:], wout_d.rearrange("(k p) c -> p k c", p=128), "w1", writes=[r_Wo])
                r_Wgc = [Res("Wg%d" % f) for f in range(NF)]
                r_Wuc = [Res("Wu%d" % f) for f in range(NF)]
                r_Wdc = [Res("Wd%d" % f) for f in range(NF)]
                for f in range(0, NF, 2):
                    S.dma("pool", Wg[:, :, 128 * f:128 * (f + 2)], wgate_d[:, 128 * f:128 * (f + 2)].rearrange("(k p) c -> p k c", p=128),
                          "wg%d" % (f // 2), writes=[r_Wgc[f], r_Wgc[f + 1]])
                    S.dma("pool", Wu[:, :, 128 * f:128 * (f + 2)], wup_d[:, 128 * f:128 * (f + 2)].rearrange("(k p) c -> p k c", p=128),
                          "wu%d" % (f // 2), writes=[r_Wuc[f], r_Wuc[f + 1]])
                for f in range(0, NF, 2):
                    S.dma("pool", Wd[:, f:f + 2, :], wdown_d[128 * f:128 * (f + 2), :].rearrange("(k p) c -> p k c", p=128),
                          "wd%d" % (f // 2), writes=[r_Wdc[f], r_Wdc[f + 1]])
                ident2 = sb("ident2", [128, 128], BF16); r_id2 = Res("ident2")
                ones2 = sb("ones2", [128, 128], BF16); r_ones2 = Res("ones2")
                eps2 = sb("eps2", [128, 1]); r_eps2 = Res("eps2")
                bglu = sb("bglu", [128, 4]); r_bglu = Res("bglu")
                gssm = sb("gssm", [128, 4]); r_gssm = Res("gssm")
                gpost_b = sb("gpost_b", [128, D]); r_gpost = Res("gpost")
                gffn8 = sb("gffn8", [128, 8]); r_gffn = Res("gffn")
                gpffn_b = sb("gpffn_b", [128, D]); r_gpffn = Res("gpffn")
                S.dma("pool", ident2[:], ident_d, "k0", writes=[r_id2])
                S.op("dve", MEMSET(ones2[:], 1.0), writes=[r_ones2])
                S.op("dve", MEMSET(eps2[:], EPS), writes=[r_eps2])
                S.dma("sp", bglu[:], bglu_d, "k1", writes=[r_bglu])
                S.dma("sp", gssm[:], gssm_d, "k2", writes=[r_gssm])
                S.dma("sp", gpost_b[:], gpost_d.partition_broadcast(128), "k3", writes=[r_gpost])
                S.dma("sp", gffn8[:], gffn_d, "k4", writes=[r_gffn])
                S.dma("sp", gpffn_b[:], gpffn_d.partition_broadcast(128), "k5", writes=[r_gpffn])

                cT = [sb("cT%d" % i, [128, 8, T2], BF16) for i in range(1)]; r_cT = [Res("cT%d" % i) for i in range(1)]
                sig = sb("sig", [128, T2], BF16); r_sig = Res("sig")
                yg = sb("yg", [128, 4, T2], BF16); r_yg = Res("yg")
                sq2 = sb("sq2", [128, T2], BF16); r_sq2 = Res("sq2")
                rsy = sb("rsy", [128, T2]); r_rsy = Res("rsy")
                yn = sb("yn", [128, 4, T2], BF16); r_yn = Res("yn")
                hres = [sb("hres%d" % i, [128, D]) for i in range(3)]; r_hres = [Res("hres%d" % i) for i in range(3)]
                ssvA = sb("ssvA", [128, 8]); r_ssvA = Res("ssvA")
                ssvB = sb("ssvB", [128, 4]); r_ssvB = Res("ssvB")
                tmpA = sb("tmpA", [128, 512]); r_tmpA = Res("tmpA")
                tmpB = tmpA; r_tmpB = r_tmpA
                hs = [sb("hs0", [128, D], BF16)] * 2; r_hs = [Res("hs0")] * 2
                junkA = sb("junkA", [128, 512], BF16); r_junkA = Res("junkA")
                junkB = junkA; r_junkB = r_junkA
                hT2 = [sb("hT2_%d" % i, [128, 8, T2], BF16) for i in range(2)]; r_hT2 = [Res("hT2_%d" % i) for i in range(2)]
                sg = [sb("sg%d" % i, [128, T2], BF16) for i in range(2)]; r_sg = [Res("sg%d" % i) for i in range(2)]
                aT = sb("aT", [128, NF, T2], BF16); r_aT = Res("aT")

                pid = [None]
                B = banks
                rB = rbank
                final_events = []

                def h1_gen(tt):
                    cb = 0

                    def _ld(e, tt=tt):
                        if pid[0] is None:
                            pid[0] = e.partition_id() % 4
                        return e.dma_start(out=cT[cb][:], in_=xout[bass.ds(pid[0] * 1024, 1024), tt * T2:(tt + 1) * T2].rearrange("(k p) t -> p k t", p=128))
                    slot = "g%d" % cb
                    if slot not in S.dma_sems:
                        S.dma_sems[slot] = top.enter_context(nc.semaphore("d_" + slot))
                        S.dma_cnt[slot] = 0
                    S._waits("sp", S._deps_for(None, [r_cT[cb]], None))
                    S.dma_cnt[slot] += 16
                    ev = (S.dma_sems[slot], S.dma_cnt[slot], "dma")
                    S.raw("sp", (lambda sem: (lambda e, f=_ld: f(e).then_inc(sem, 16)))(S.dma_sems[slot]))
                    S._commit(ev, None, [r_cT[cb]])
                    yield None
                    for m in range(4):
                        S.group("pe", [MM(B[0][:, 0:T2], Wgl[:, r, 128 * m:128 * (m + 1)], cT[cb][:, 2 * r + 1, :], r == 0, r == 3) for r in range(4)],
                                reads=[r_Wgl, r_cT[cb]], writes=[rB[0]])
                        yield None
                        S.op("act", ACT(sig[:], B[0][:, 0:T2], AF.Sigmoid, bias=bglu[:, m:m + 1]), reads=[rB[0], r_bglu], writes=[r_sig])
                        S.op("dve", TT(yg[:, m, :], cT[cb][:, 2 * m + 1, :], sig[:], ALU.mult), reads=[r_cT[cb], r_sig], writes=[r_yg])
                        S.op("act", ACT(sq2[:], yg[:, m, :], AF.Square), reads=[r_yg], writes=[r_sq2])
                        yield None
                        S.op("pe", MM(B[1][:, 0:T2], ones2[:], sq2[:], m == 0, m == 3), reads=[r_ones2, r_sq2], writes=[rB[1]])
                    yield None
                    S.op("act", ACT(rsy[:], B[1][:, 0:T2], AF.Ln, scale=1.0 / 512, bias=eps2[:, 0:1]), reads=[rB[1], r_eps2], writes=[r_rsy])
                    S.op("act", ACT(rsy[:], rsy[:], AF.Exp, scale=-0.5), reads=[r_rsy], writes=[r_rsy])
                    for m in range(4):
                        S.op("dve", STT(yn[:, m, :], yg[:, m, :], gssm[:, m:m + 1], rsy[:], ALU.mult, ALU.mult),
                             reads=[r_yg, r_gssm, r_rsy], writes=[r_yn])
                    yield None
                    for s in range(2):
                        hr = hres[(2 * tt + s) % 3]; r_hr = r_hres[(2 * tt + s) % 3]
                        c0 = 4 * s
                        for hf in range(2):
                            bk = B[2 + hf]; rbk = rB[2 + hf]
                            fns = []
                            for k in range(8):
                                src = cT[cb][:, k, 128 * s:128 * (s + 1)] if k % 2 == 0 else yn[:, k // 2, 128 * s:128 * (s + 1)]
                                fns.append(MM(bk[:, :], src, Wo[:, k, 512 * hf:512 * (hf + 1)], k == 0, k == 7))
                            S.group("pe", fns, reads=[r_cT[cb], r_yn, r_Wo], writes=[rbk])
                            yield None
                            S.op("act", ACT(junkA[:], bk[:, :], AF.Square, accum_out=ssvA[:, c0 + hf:c0 + hf + 1]), reads=[rbk], writes=[r_junkA, r_ssvA])
                        yield None
                        S.op("dve", TT(ssvA[:, c0 + 2:c0 + 3], ssvA[:, c0:c0 + 1], ssvA[:, c0 + 1:c0 + 2], ALU.add), reads=[r_ssvA], writes=[r_ssvA])
                        S.op("act", ACT(ssvA[:, c0 + 2:c0 + 3], ssvA[:, c0 + 2:c0 + 3], AF.Ln, scale=1.0 / D, bias=eps2[:, 0:1]), reads=[r_ssvA, r_eps2], writes=[r_ssvA])
                        S.op("act", ACT(ssvA[:, c0 + 2:c0 + 3], ssvA[:, c0 + 2:c0 + 3], AF.Exp, scale=-0.5), reads=[r_ssvA], writes=[r_ssvA])
                        yield ("BAR1" if s == 1 else None)
                        row0 = tt * T2 + 128 * s
                        S.dma("sp", hr[:], xres_d[row0:row0 + 128, :], "xr%d" % ((2 * tt + s) % 3), writes=[r_hr])
                        yield None
                        for hf in range(2):
                            bk = B[2 + hf]; rbk = rB[2 + hf]
                            S.op("dve", STT(tmpA[:], bk[:, :], ssvA[:, c0 + 2:c0 + 3], gpost_b[:, 512 * hf:512 * (hf + 1)], ALU.mult, ALU.mult),
                                 reads=[rbk, r_ssvA, r_gpost], writes=[r_tmpA])
                            S.op("dve", TT(hr[:, 512 * hf:512 * (hf + 1)], tmpA[:], hr[:, 512 * hf:512 * (hf + 1)], ALU.add),
                                 reads=[r_tmpA, r_hr], writes=[r_hr])
                            S.op("act", ACT(junkA[:], hr[:, 512 * hf:512 * (hf + 1)], AF.Square, accum_out=ssvA[:, c0 + hf:c0 + hf + 1]),
                                 reads=[r_hr], writes=[r_junkA, r_ssvA])
                        yield None
                        S.op("dve", TT(ssvA[:, c0 + 3:c0 + 4], ssvA[:, c0:c0 + 1], ssvA[:, c0 + 1:c0 + 2], ALU.add), reads=[r_ssvA], writes=[r_ssvA])
                        S.op("act", ACT(ssvA[:, c0 + 3:c0 + 4], ssvA[:, c0 + 3:c0 + 4], AF.Ln, scale=1.0 / D, bias=eps2[:, 0:1]), reads=[r_ssvA, r_eps2], writes=[r_ssvA])
                        S.op("act", ACT(ssvA[:, c0 + 3:c0 + 4], ssvA[:, c0 + 3:c0 + 4], AF.Exp, scale=-0.5), reads=[r_ssvA], writes=[r_ssvA])
                        yield None
                        S.op("act", ACT(hs[s][:], hr[:], AF.Copy, scale=ssvA[:, c0 + 3:c0 + 4]), reads=[r_hr, r_ssvA], writes=[r_hs[s]])
                        yield None
                        B1b = B[1][:].bitcast(BF16)
                        S.group("pe", [TR(B1b[:, 128 * k:128 * (k + 1)], hs[s][:, 128 * k:128 * (k + 1)], ident2[:]) for k in range(8)],
                                reads=[r_hs[s], r_id2], writes=[rB[1]])
                        yield None
                        S.op("dve", TT(hT2[tt % 2][:, :, 128 * s:128 * (s + 1)], B1b.rearrange("p (k t) -> p k t", k=8),
                                       gffn8[:].unsqueeze(2).to_broadcast([128, 8, 128]), ALU.mult),
                             reads=[rB[1], r_gffn], writes=[r_hT2[tt % 2]])
                        yield None

                class Pump:
                    def __init__(self, gen):
                        self.gen = gen
                        self.released = set()
                        self.pending = None

                    def release(self, name):
                        self.released.add(name)

                    def step(self, n):
                        for _ in range(n):
                            if self.gen is None:
                                return
                            if self.pending is not None:
                                if self.pending not in self.released:
                                    return
                                self.pending = None
                            try:
                                r = next(self.gen)
                            except StopIteration:
                                self.gen = None
                                return
                            if r is not None:
                                self.pending = r

                    def flush(self):
                        while self.gen is not None:
                            if self.pending is not None:
                                assert self.pending in self.released, self.pending
                            self.step(1)

                def h2(tt, P):
                    for f in range(NF):
                        gb2 = B[4 + (f % 2)]; rgb2 = rB[4 + (f % 2)]
                        gv = gb2[:].rearrange("p (h q) -> p h q", h=2)
                        h2t = hT2[tt % 2]; r_h2t = r_hT2[tt % 2]
                        S.group("pe", [MM(gv[:, 0, :], Wg[:, k, 128 * f:128 * (f + 1)], h2t[:, k, :], k == 0, k == 7) for k in range(8)],
                                reads=[r_Wgc[f], r_h2t], writes=[rgb2])
                        P.step(1)
                        S.group("pe", [MM(gv[:, 1, :], Wu[:, k, 128 * f:128 * (f + 1)], h2t[:, k, :], k == 0, k == 7) for k in range(8)],
                                reads=[r_Wuc[f], r_h2t], writes=[rgb2])
                        S.op("act", ACT(sg[f % 2][:], gv[:, 0, :], AF.Silu), reads=[rgb2], writes=[r_sg[f % 2]])
                        S.op("dve", TT(aT[:, f, :], gv[:, 1, :], sg[f % 2][:], ALU.mult), reads=[rgb2, r_sg[f % 2]], writes=[r_aT])
                        P.step(1)
                    for s in range(2):
                        row0 = tt * T2 + 128 * s
                        hr = hres[(2 * tt + s) % 3]; r_hr = r_hres[(2 * tt + s) % 3]
                        for hf in range(2):
                            bk = B[6 + hf]; rbk = rB[6 + hf]
                            S.group("pe", [MM(bk[:, :], aT[:, f, 128 * s:128 * (s + 1)], Wd[:, f, 512 * hf:512 * (hf + 1)], f == 0, f == NF - 1) for f in range(NF)],
                                    reads=[r_aT] + r_Wdc, writes=[rbk])
                            S.op("act", ACT(junkB[:], bk[:, :], AF.Square, accum_out=ssvB[:, hf:hf + 1]), reads=[rbk], writes=[r_junkB, r_ssvB])
                            P.step(6)
                        S.op("dve", TT(ssvB[:, 2:3], ssvB[:, 0:1], ssvB[:, 1:2], ALU.add), reads=[r_ssvB], writes=[r_ssvB])
                        S.op("act", ACT(ssvB[:, 2:3], ssvB[:, 2:3], AF.Ln, scale=1.0 / D, bias=eps2[:, 0:1]), reads=[r_ssvB, r_eps2], writes=[r_ssvB])
                        S.op("act", ACT(ssvB[:, 2:3], ssvB[:, 2:3], AF.Exp, scale=-0.5), reads=[r_ssvB], writes=[r_ssvB])
                        for hf in range(2):
                            bk = B[6 + hf]; rbk = rB[6 + hf]
                            S.op("dve", STT(tmpB[:], bk[:, :], ssvB[:, 2:3], gpffn_b[:, 512 * hf:512 * (hf + 1)], ALU.mult, ALU.mult),
                                 reads=[rbk, r_ssvB, r_gpffn], writes=[r_tmpB])
                            S.op("dve", TT(hr[:, 512 * hf:512 * (hf + 1)], tmpB[:], hr[:, 512 * hf:512 * (hf + 1)], ALU.add),
                                 reads=[r_tmpB, r_hr], writes=[r_hr])
                        final_events.append(S.dma("sp", out_d[row0:row0 + 128, :], hr[:], "fo%d" % ((2 * tt + s) % 3), reads=[r_hr]))
                        if s == 0:
                            P.release("BAR1")

                P0 = Pump(h1_gen(0)); P0.release("BAR1"); P0.flush()
                for tt in range(NT2):
                    P = Pump(h1_gen(tt + 1) if tt + 1 < NT2 else None)
                    h2(tt, P)
                    P.flush()

                S.wait_all("sp", final_events)

        with nc.Block() as block:
            S.run(block)
    return nc


def _consts():
    invf8 = (500000.0 ** (-np.arange(0, 16, 2, dtype=np.float32) / np.float32(16))).astype(np.float32)
    invf = np.zeros((128, 1), np.float32)
    for base in (0, 64):
        for i in range(16):
            invf[base + i, 0] = invf8[i % 8]
    R = np.zeros((128, 128), np.float32)
    for base in (0, 64):
        for i in range(8):
            R[base + i, base + i + 8] = -1.0
            R[base + i + 8, base + i] = 1.0
    rmatT = np.ascontiguousarray(R.T)
    ident = np.eye(128, dtype=np.float32)
    swapm = np.zeros((128, 128), np.float32)
    for k in range(128):
        swapm[k, (k + 64) % 128] = 1.0
    sgn = np.ones((128, 1), np.float32); sgn[64:] = -1.0
    tri = (np.arange(128)[:, None] <= np.arange(128)[None, :]).astype(np.float32)
    gmask = (np.arange(128)[:, None] // 16 == np.arange(8)[None, :]).astype(np.float32)
    return dict(invf=invf, rmatT=rmatT, ident=ident, swapm=swapm, sgn=sgn, tri=tri, gmask=gmask,
                iota512=np.arange(512, dtype=np.float32), tile_iota=(512.0 * np.arange(NT1)).astype(np.float32))


def make_in_maps(inp):
    c = _consts()
    f = lambda a: np.ascontiguousarray(np.asarray(a, dtype=np.float32))
    w_in = f(inp["w_in"])[0]
    maps = []
    w_out = f(inp["w_out"])[0]
    perm = []
    for r in range(4):
        perm += list(range(128 * r, 128 * r + 128)) + list(range(512 + 128 * r, 512 + 128 * r + 128))
    w_out_p = np.ascontiguousarray(w_out[perm, :])
    for core in range(8):
        b, h = core // 4, core % 4
        cols = (list(range(64 * h, 64 * h + 64)) + list(range(256 + 64 * h, 256 + 64 * h + 64)) +
                list(range(512 + 64 * h, 512 + 64 * h + 64)) + list(range(768 + 64 * h, 768 + 64 * h + 64)) +
                list(range(1024 + 128 * h, 1024 + 128 * h + 128)) + list(range(1536 + 128 * h, 1536 + 128 * h + 128)))
        gs = slice(8 * h, 8 * h + 8)
        m = dict(c)
        m["x"] = f(inp["x"][b])
        m["meta"] = f(inp["meta"])
        m["win"] = np.ascontiguousarray(w_in[:, cols])
        m["gpre"] = np.ascontiguousarray(f(inp["pre_mix_g"])[0].reshape(8, 128).T)
        m["lamv"] = np.stack([f(inp["lambda_q1"])[0], f(inp["lambda_k1"])[0], f(inp["lambda_q2"])[0], f(inp["lambda_k2"])[0]])
        m["subg"] = f(inp["subln_g"])[0].reshape(128, 1)
        m["areT"] = np.ascontiguousarray(f(inp["a_re"])[0, gs].T)
        m["aimT"] = np.ascontiguousarray(f(inp["a_im"])[0, gs].T)
        m["logdt"] = f(inp["log_dt"])[0, gs]
        m["bre"] = f(inp["b_re"])[0, gs]
        m["bim"] = f(inp["b_im"])[0, gs]
        m["creT"] = np.ascontiguousarray(f(inp["c_re"])[0, gs].transpose(2, 0, 1).reshape(64, 128))
        m["cimT"] = np.ascontiguousarray(f(inp["c_im"])[0, gs].transpose(2, 0, 1).reshape(64, 128))
        m["dskip"] = f(inp["d_skip"])[0, 128 * h:128 * h + 128].reshape(128, 1)
        m["wglu"] = f(inp["w_glu"])[0]
        m["bglu"] = np.ascontiguousarray(f(inp["b_glu"])[0].reshape(4, 128).T)
        m["gssm"] = np.ascontiguousarray(f(inp["ssm_out_g"])[0].reshape(4, 128).T)
        m["wout"] = w_out_p
        m["gpost"] = f(inp["post_mix_g"])[0]
        m["gffn8"] = np.ascontiguousarray(f(inp["pre_ffn_g"])[0].reshape(8, 128).T)
        m["wgate"] = f(inp["w_gate"])[0]
        m["wup"] = f(inp["w_up"])[0]
        m["wdown"] = f(inp["w_down"])[0]
        m["gpffn"] = f(inp["post_ffn_g"])[0]
        m["xres"] = f(inp["x"][b, TOK2 * h:TOK2 * (h + 1)])
        maps.append(m)
    return maps


def kernel(**inputs):
    nc = build_program()
    maps = make_in_maps(inputs)
    res = run_bass_kernel_spmd(nc, maps, core_ids=list(range(8)))
    out = np.zeros((2, SEQ, D), np.float32)
    for core in range(8):
        b, h = core // 4, core % 4
        out[b, TOK2 * h:TOK2 * (h + 1)] = res.results[core]["out"]
    return out
```

```python
import math
import os
from contextlib import ExitStack

import numpy as np
import concourse.bass as bass
import concourse.mybir as mybir
from concourse.bass_utils import run_bass_kernel_spmd

F32 = mybir.dt.float32
BF16 = mybir.dt.bfloat16
I32 = mybir.dt.int32
AF = mybir.ActivationFunctionType
ALU = mybir.AluOpType

D = 1024
SEQ = 8192
NMETA = 16
L = SEQ + NMETA
LP = 8320
NBLK = 65
NT1 = 17
DFF = 2816
NF = 22
EPS = 1e-6
LAM_INIT = 0.8 - 0.6 * math.exp(0.0)
TWO_PI = 2.0 * math.pi
C1 = 6.28125
C2 = TWO_PI - C1
PI_LO = 3.141592
TOK2 = 2048
T2 = 256
NT2 = TOK2 // T2
FRONT_STEPS = 40


class Res:
    __slots__ = ("name", "w", "r", "excl")

    def __init__(self, name, excl=False):
        self.name = name
        self.w = None
        self.r = []
        self.excl = excl


class Sched:
    ENGS = ("pe", "act", "dve", "pool", "sp")

    def __init__(self, nc, stack):
        self.nc = nc
        self.ops = {e: [] for e in self.ENGS}
        self.sem = {e: stack.enter_context(nc.semaphore("s_" + e)) for e in self.ENGS}
        self.cnt = {e: 0 for e in self.ENGS}
        self.waited = {e: {} for e in self.ENGS}
        self.stack = stack
        self.dma_sems = {}
        self.dma_cnt = {}
        self.last = {e: None for e in self.ENGS}
        self.limit = int(os.environ.get("K_LIMIT", "1000000000"))
        self.nops = 0
        self.marks = []

    def _waits(self, eng, deps):
        need = {}
        for ev in deps:
            if ev is None:
                continue
            sem, val, src = ev
            key = sem.name
            if self.waited[eng].get(key, 0) >= val:
                continue
            if key not in need or need[key][1] < val:
                need[key] = (sem, val)
        for key, (sem, val) in need.items():
            self.waited[eng][key] = val
            self.ops[eng].append(("wait", sem, val))

    @staticmethod
    def _deps_for(reads, writes, extra):
        deps = list(extra or [])
        for r in reads or []:
            if r.w is not None:
                deps.append(r.w)
            if r.excl:
                deps.extend(r.r)
        for w in writes or []:
            if w.w is not None:
                deps.append(w.w)
            deps.extend(w.r)
        return deps

    @staticmethod
    def _commit(ev, reads, writes):
        for r in reads or []:
            r.r.append(ev)
        for w in writes or []:
            w.w = ev
            w.r = []

    def op(self, eng, fn, reads=None, writes=None, extra=None):
        return self.group(eng, [fn], reads, writes, extra)

    def group(self, eng, fns, reads=None, writes=None, extra=None):
        self.nops += 1
        if self.nops > self.limit:
            return None
        self._waits(eng, self._deps_for(reads, writes, extra))
        self.cnt[eng] += 1
        ev = (self.sem[eng], self.cnt[eng], eng)
        for f in fns[:-1]:
            self.ops[eng].append(("op", f, False))
        self.ops[eng].append(("op", fns[-1], True))
        self._commit(ev, reads, writes)
        self.last[eng] = ev
        return ev

    def dma(self, eng, out, in_, slot, reads=None, writes=None, extra=None):
        if slot not in self.dma_sems:
            self.dma_sems[slot] = self.stack.enter_context(self.nc.semaphore("d_" + slot))
            self.dma_cnt[slot] = 0
        self.nops += 1
        if self.nops > self.limit:
            return None
        self._waits(eng, self._deps_for(reads, writes, extra))
        self.dma_cnt[slot] += 16
        sem = self.dma_sems[slot]
        ev = (sem, self.dma_cnt[slot], "dma")
        self.ops[eng].append(("dma", out, in_, sem))
        self._commit(ev, reads, writes)
        return ev

    def raw(self, eng, fn):
        self.ops[eng].append(("raw", fn))

    def wait_all(self, eng, events):
        self._waits(eng, events)

    def run(self, block):
        sched = self

        def replay(engname, e):
            for item in sched.ops[engname]:
                if item[0] == "wait":
                    e.wait_ge(item[1], item[2])
                elif item[0] == "op":
                    ins = item[1](e)
                    if item[2]:
                        ins.then_inc(sched.sem[engname], 1)
                elif item[0] == "dma":
                    e.dma_start(out=item[1], in_=item[2]).then_inc(item[3], 16)
                elif item[0] == "raw":
                    item[1](e)

        @block.tensor
        def _(e):
            replay("pe", e)

        @block.scalar
        def _(e):
            replay("act", e)

        @block.vector
        def _(e):
            replay("dve", e)

        @block.gpsimd
        def _(e):
            replay("pool", e)

        @block.sync
        def _(e):
            replay("sp", e)


def MM(out, lhsT, rhs, start, stop):
    return lambda e: e.matmul(out, lhsT, rhs, start=start, stop=stop)


def TR(out, in_, ident):
    return lambda e: e.transpose(out, in_, ident)


def ACT(out, in_, func, **kw):
    return lambda e: e.activation(out=out, in_=in_, func=func, **kw)


def TT(out, a, b, op):
    return lambda e: e.tensor_tensor(out=out, in0=a, in1=b, op=op)


def TS(out, a, s1, op0, s2=None, op1=None):
    if op1 is None:
        return lambda e: e.tensor_scalar(out=out, in0=a, scalar1=s1, scalar2=None, op0=op0)
    return lambda e: e.tensor_scalar(out=out, in0=a, scalar1=s1, scalar2=s2, op0=op0, op1=op1)


def STT(out, a, s, b, op0, op1):
    return lambda e: e.scalar_tensor_tensor(out=out, in0=a, scalar=s, in1=b, op0=op0, op1=op1)


def CP(out, in_):
    return lambda e: e.tensor_copy(out=out, in_=in_)


def MEMSET(ap, v):
    return lambda e: e.memset(ap, v)


def RECIP(out, in_):
    return lambda e: e.reciprocal(out=out, in_=in_)


def SCAN(out, d0, d1, init):
    return lambda e: e.tensor_tensor_scan(out=out, data0=d0, data1=d1, initial=init, op0=ALU.mult, op1=ALU.add)


def build_program(nt1=NT1, do_phase2=True, debug=False, stop=9, p2mode="full"):
    nc = bass.Bass("TRN2", target_bir_lowering=False)

    def din(name, shape, dt=F32):
        return nc.dram_tensor(name, list(shape), dt, kind="ExternalInput").ap()

    ident_d = din("ident", [128, 128])
    if nt1 > 0:
        x_d = din("x", [SEQ, D])
        meta_d = din("meta", [NMETA, D])
        win_d = din("win", [D, 512])
        gpre_d = din("gpre", [128, 8])
        invf_d = din("invf", [128, 1])
        iota512_d = din("iota512", [512])
        tile_iota_d = din("tile_iota", [NT1])
        rmatT_d = din("rmatT", [128, 128])
        swap_d = din("swapm", [128, 128])
        sgn_d = din("sgn", [128, 1])
        tri_d = din("tri", [128, 128])
        gmask_d = din("gmask", [128, 8])
        lam_d = din("lamv", [4, 64])
        subg_d = din("subg", [128, 1])
        areT_d = din("areT", [64, 8])
        aimT_d = din("aimT", [64, 8])
        logdt_d = din("logdt", [8])
        bre_d = din("bre", [8, 64, 16])
        bim_d = din("bim", [8, 64, 16])
        creT_d = din("creT", [64, 128])
        cimT_d = din("cimT", [64, 128])
        dskip_d = din("dskip", [128, 1])
    if do_phase2:
        wglu_d = din("wglu", [512, 512])
        bglu_d = din("bglu", [128, 4])
        gssm_d = din("gssm", [128, 4])
        wout_d = din("wout", [D, D])
        gpost_d = din("gpost", [D])
        gffn_d = din("gffn8", [128, 8])
        wgate_d = din("wgate", [D, DFF])
        wup_d = din("wup", [D, DFF])
        wdown_d = din("wdown", [DFF, D])
        gpffn_d = din("gpffn", [D])
        xres_d = din("xres", [TOK2, D])
    out_d = nc.dram_tensor("out", [TOK2, D], F32, kind="ExternalOutput").ap() if do_phase2 else None

    xout_dbg_d = din("xout_dbg", [1024, TOK2]) if p2mode == "nogather" else None
    wprep = []
    if do_phase2:
        wg_b = nc.dram_tensor("wg_b", [D, DFF], BF16).ap()
        wu_b = nc.dram_tensor("wu_b", [D, DFF], BF16).ap()
        wd_b = nc.dram_tensor("wd_b", [DFF, D], BF16).ap()
        wo_b = nc.dram_tensor("wo_b", [D, D], BF16).ap()
        wgl_b = nc.dram_tensor("wgl_b", [512, 512], BF16).ap()
        wprep.append((wgl_b, wglu_d)); wprep.append((wo_b, wout_d))
        for hh in range(2):
            wprep.append((wg_b[512 * hh:512 * (hh + 1), :], wgate_d[512 * hh:512 * (hh + 1), :]))
            wprep.append((wu_b[512 * hh:512 * (hh + 1), :], wup_d[512 * hh:512 * (hh + 1), :]))
        for hh in range(2):
            wprep.append((wd_b[1408 * hh:1408 * (hh + 1), :], wdown_d[1408 * hh:1408 * (hh + 1), :]))
    wprep_events = []
    xin_t = [nc.dram_tensor("xchg_in%d" % q, [256, TOK2], BF16) for q in range(4)]
    xout_t = nc.dram_tensor("xchg_out", [4 * 1024, TOK2], BF16)
    xin = [t.ap() for t in xin_t]
    xout = xout_t.ap()
    dbg_d = None
    if debug:
        dbg_d = nc.dram_tensor("dbg", [256, SEQ], F32, kind="ExternalOutput").ap()

    with ExitStack() as top:
        S = Sched(nc, top)

        def ps_bank(name):
            return top.enter_context(nc.psum_tensor(name, [128, 512], F32))

        use_cc = do_phase2 and p2mode == "full"
        cc_sem = top.enter_context(nc.semaphore("cc_sem")) if use_cc else None
        q_events = [[] for _ in range(4)]

        def xchg_write(row0, lo, hi, src, src_col0, slot, res):
            evs = []
            c = lo
            while c < hi:
                q = c // TOK2
                ce = min(hi, TOK2 * (q + 1))
                ev = S.dma("sp", xin[q][row0:row0 + 128, c - TOK2 * q:ce - TOK2 * q],
                           src[:, src_col0 + (c - lo):src_col0 + (ce - lo)], slot, reads=[res])
                q_events[q].append(ev)
                evs.append(ev)
                c = ce
            return evs

        banks = [ps_bank("bank%d" % i) for i in range(8)]
        rbank = [Res("bank%d" % i, excl=True) for i in range(8)]

        def _phase1():
            with ExitStack() as p1:
                def sb(name, shape, dt=F32):
                    return p1.enter_context(nc.sbuf_tensor("a_" + name, list(shape), dt))

                ident_f = sb("ident_f", [128, 128]); r_ident_f = Res("ident_f")
                ident_b = sb("ident_b", [128, 128], BF16); r_ident_b = Res("ident_b")
                rmatT_b = sb("rmatT_b", [128, 128], BF16); r_rmat = Res("rmat")
                swap_f = sb("swap_f", [128, 128]); r_swap = Res("swap")
                tri_b = sb("tri_b", [128, 128], BF16); r_tri = Res("tri")
                ones_b = sb("ones_b", [128, 128], BF16); r_ones = Res("ones")
                sgn = sb("sgn", [128, 1]); r_sgn = Res("sgn")
                gmask = sb("gmask", [128, 8]); r_gmask = Res("gmask")
                invf = sb("invf", [128, 1]); r_invf = Res("invf")
                iota512 = sb("iota512", [128, 512]); r_iota = Res("iota512")
                tile_iota = sb("tile_iota", [128, NT1]); r_tiota = Res("tile_iota")
                gpre = sb("gpre", [128, 8]); r_gpre = Res("gpre")
                subg = sb("subg", [128, 1]); r_subg = Res("subg")
                dskip = sb("dskip", [128, 1]); r_dskip = Res("dskip")
                epsc = sb("epsc", [128, 1]); r_eps = Res("eps")
                lamv = sb("lamv", [128, 4, 64]); r_lamv = Res("lamv")
                neglam = sb("neglam", [128, 1]); r_neglam = Res("neglam")

                S.dma("sp", ident_f[:], ident_d, "c0", writes=[r_ident_f])
                S.dma("pool", ident_b[:], ident_d, "c1", writes=[r_ident_b])
                S.dma("pool", rmatT_b[:], rmatT_d, "c2", writes=[r_rmat])
                S.dma("sp", swap_f[:], swap_d, "c3", writes=[r_swap])
                S.dma("pool", tri_b[:], tri_d, "c4", writes=[r_tri])
                S.dma("sp", sgn[:], sgn_d, "c5", writes=[r_sgn])
                S.dma("sp", gmask[:], gmask_d, "c6", writes=[r_gmask])
                S.dma("sp", invf[:], invf_d, "c7", writes=[r_invf])
                S.dma("sp", iota512[:], iota512_d.partition_broadcast(128), "c8", writes=[r_iota])
                S.dma("sp", tile_iota[:], tile_iota_d.partition_broadcast(128), "c9", writes=[r_tiota])
                S.dma("sp", gpre[:], gpre_d, "c10", writes=[r_gpre])
                S.dma("sp", subg[:], subg_d, "c11", writes=[r_subg])
                S.dma("sp", dskip[:], dskip_d, "c12", writes=[r_dskip])
                S.dma("sp", lamv[:].rearrange("p a b -> p (a b)"),
                      lam_d.rearrange("a b -> (a b)").partition_broadcast(128), "c13", writes=[r_lamv])
                S.op("dve", MEMSET(ones_b[:], 1.0), writes=[r_ones])
                S.op("dve", MEMSET(epsc[:], EPS), writes=[r_eps])

                scr = [sb("scr%d" % i, [128, 512]) for i in range(4)]
                r_scr = [Res("scr%d" % i) for i in range(4)]
                ki_t = sb("ki_t", [128, 512], I32); r_ki = Res("ki")

                def range_reduce(eng, out_ap, arg_ap, n, r_out, r_arg):
                    kf = scr[3][:, 0:n]
                    S.op(eng, TS(ki_t[:, 0:n], arg_ap, 1.0 / TWO_PI, ALU.mult), reads=[r_arg], writes=[r_ki])
                    S.op(eng, CP(kf, ki_t[:, 0:n]), reads=[r_ki], writes=[r_scr[3]])
                    S.op(eng, STT(out_ap, kf, -C1, arg_ap, ALU.mult, ALU.add), reads=[r_scr[3], r_arg], writes=[r_out])
                    S.op(eng, STT(out_ap, kf, -C2, out_ap, ALU.mult, ALU.add), reads=[r_scr[3], r_out], writes=[r_out])
                    S.op(eng, TS(out_ap, out_ap, PI_LO, ALU.min, -PI_LO, ALU.max), reads=[r_out], writes=[r_out])

                def sincos(arg_ap, n, r_arg, sin_out, cos_out, r_sin, r_cos):
                    red = scr[2][:, 0:n]
                    range_reduce("dve", red, arg_ap, n, r_scr[2], r_arg)
                    S.op("act", ACT(sin_out, red, AF.Sin), reads=[r_scr[2]], writes=[r_sin])
                    hs = scr[1][:, 0:n]
                    S.op("act", ACT(hs, red, AF.Sin, scale=0.5), reads=[r_scr[2]], writes=[r_scr[1]])
                    S.op("dve", TT(hs, hs, hs, ALU.mult), reads=[r_scr[1]], writes=[r_scr[1]])
                    S.op("dve", TS(cos_out, hs, -2.0, ALU.mult, 1.0, ALU.add), reads=[r_scr[1]], writes=[r_cos])

                lt = sb("lt", [128, 2, 64]); r_lt = Res("lt")
                ls = sb("ls", [128, 2]); r_ls = Res("ls")
                S.op("dve", TT(lt[:, 0, :], lamv[:, 0, :], lamv[:, 1, :], ALU.mult), reads=[r_lamv], writes=[r_lt])
                S.op("dve", TT(lt[:, 1, :], lamv[:, 2, :], lamv[:, 3, :], ALU.mult), reads=[r_lamv, r_lt], writes=[r_lt])
                S.op("dve", lambda e: e.reduce_sum(out=ls[:], in_=lt[:], axis=mybir.AxisListType.X), reads=[r_lt], writes=[r_ls])
                S.op("act", ACT(ls[:], ls[:], AF.Exp), reads=[r_ls], writes=[r_ls])
                S.op("dve", TT(neglam[:], ls[:, 1:2], ls[:, 0:1], ALU.subtract), reads=[r_ls], writes=[r_neglam])
                S.op("dve", TS(neglam[:], neglam[:], -LAM_INIT, ALU.add), reads=[r_neglam], writes=[r_neglam])
                S.op("dve", TS(subg[:], subg[:], 1.0 - LAM_INIT, ALU.mult), reads=[r_subg], writes=[r_subg])

                Wb = sb("Wb", [128, 8, 512], BF16); r_Wb = Res("Wb")
                wstg = [scr[0], scr[1]]
                r_wstg = [r_scr[0], r_scr[1]]
                for k in range(8):
                    S.dma("sp", wstg[k % 2][:], win_d[128 * k:128 * (k + 1), :], "wstg%d" % (k % 2), writes=[r_wstg[k % 2]])
                    S.op("pool", TS(Wb[:, k, :], wstg[k % 2][:], gpre[:, k:k + 1], ALU.mult),
                         reads=[r_wstg[k % 2], r_gpre], writes=[r_Wb])

                C0 = sb("C0", [128, 512]); S0 = sb("S0", [128, 512]); r_C0 = Res("C0"); r_S0 = Res("S0")
                cI = sb("cI", [128, NT1]); sI = sb("sI", [128, NT1]); r_cI = Res("cI"); r_sI = Res("sI")
                S.op("dve", TS(scr[0][:], iota512[:], invf[:, 0:1], ALU.mult), reads=[r_iota, r_invf], writes=[r_scr[0]])
                sincos(scr[0][:], 512, r_scr[0], S0[:], C0[:], r_S0, r_C0)
                S.op("dve", TS(scr[0][:, 0:NT1], tile_iota[:], invf[:, 0:1], ALU.mult), reads=[r_tiota, r_invf], writes=[r_scr[0]])
                sincos(scr[0][:, 0:NT1], NT1, r_scr[0], sI[:], cI[:], r_sI, r_cI)
                nsI = sb("nsI", [128, NT1]); r_nsI = Res("nsI")
                S.op("dve", TS(nsI[:], sI[:], -1.0, ALU.mult), reads=[r_sI], writes=[r_nsI])

                are2 = sb("are2", [128, 8]); aim2 = sb("aim2", [128, 8]); dt2 = sb("dt2", [128, 8])
                r_are = Res("are2"); r_aim = Res("aim2"); r_dt = Res("dt2")
                S.dma("sp", are2[0:64, :], areT_d, "s0", writes=[r_are])
                S.dma("sp", are2[64:128, :], areT_d, "s0", writes=[r_are])
                S.dma("sp", aim2[0:64, :], aimT_d, "s1", writes=[r_aim])
                S.dma("sp", aim2[64:128, :], aimT_d, "s1", writes=[r_aim])
                S.dma("sp", dt2[:], logdt_d.partition_broadcast(128), "s2", writes=[r_dt])
                S.op("act", ACT(dt2[:], dt2[:], AF.Exp), reads=[r_dt], writes=[r_dt])
                rdec = sb("rdec", [128, 8]); r_rdec = Res("rdec")
                theta = sb("theta", [128, 8]); r_theta = Res("theta")
                S.op("dve", TT(rdec[:], are2[:], dt2[:], ALU.mult), reads=[r_are, r_dt], writes=[r_rdec])
                S.op("act", ACT(rdec[:], rdec[:], AF.Exp), reads=[r_rdec], writes=[r_rdec])
                S.op("dve", TT(theta[:], aim2[:], dt2[:], ALU.mult), reads=[r_aim, r_dt], writes=[r_theta])
                c1t = sb("c1t", [128, 8]); s1t = sb("s1t", [128, 8]); r_c1t = Res("c1t"); r_s1t = Res("s1t")
                cTt = sb("cTt", [128, 8]); sTt = sb("sTt", [128, 8]); r_cTt = Res("cTt"); r_sTt = Res("sTt")
                sincos(theta[:], 8, r_theta, s1t[:], c1t[:], r_s1t, r_c1t)
                S.op("dve", TS(scr[0][:, 0:8], theta[:], 512.0, ALU.mult), reads=[r_theta], writes=[r_scr[0]])
                sincos(scr[0][:, 0:8], 8, r_scr[0], sTt[:], cTt[:], r_sTt, r_cTt)
                S.op("dve", TT(sTt[:], sTt[:], sgn[:, 0:1].to_broadcast([128, 8]), ALU.mult), reads=[r_sTt, r_sgn], writes=[r_sTt])
                COSg = sb("COSg", [128, 8, 512]); SINg = sb("SINg", [128, 8, 512])
                r_COSg = [Res("COSg%d" % g) for g in range(8)]; r_SINg = [Res("SINg%d" % g) for g in range(8)]
                for g in range(8):
                    S.op("dve", TS(scr[0][:], iota512[:], theta[:, g:g + 1], ALU.mult), reads=[r_iota, r_theta], writes=[r_scr[0]])
                    sincos(scr[0][:], 512, r_scr[0], SINg[:, g, :], COSg[:, g, :], r_SINg[g], r_COSg[g])
                ROT = sb("ROT", [128, 8, 128]); r_ROT = Res("ROT")
                for g in range(8):
                    S.op("dve", TS(ROT[:, g, :], ident_f[:], cTt[:, g:g + 1], ALU.mult), reads=[r_ident_f, r_cTt], writes=[r_ROT])
                    S.op("dve", STT(ROT[:, g, :], swap_f[:], sTt[:, g:g + 1], ROT[:, g, :], ALU.mult, ALU.add),
                         reads=[r_swap, r_sTt, r_ROT], writes=[r_ROT])
                fre = sb("fre", [128, 8]); fim = sb("fim", [128, 8]); r_fre = Res("fre"); r_fim = Res("fim")
                nr = sb("nr", [128, 8]); ni = sb("ni", [128, 8]); den = sb("den", [128, 8]); tmp8 = sb("tmp8", [128, 8])
                r_nr = Res("nr"); r_ni = Res("ni"); r_den = Res("den"); r_tmp8 = Res("tmp8")
                S.op("dve", TT(nr[:], rdec[:], c1t[:], ALU.mult), reads=[r_rdec, r_c1t], writes=[r_nr])
                S.op("dve", TS(nr[:], nr[:], -1.0, ALU.add), reads=[r_nr], writes=[r_nr])
                S.op("dve", TT(ni[:], rdec[:], s1t[:], ALU.mult), reads=[r_rdec, r_s1t], writes=[r_ni])
                S.op("dve", TT(den[:], are2[:], are2[:], ALU.mult), reads=[r_are], writes=[r_den])
                S.op("dve", TT(tmp8[:], aim2[:], aim2[:], ALU.mult), reads=[r_aim], writes=[r_tmp8])
                S.op("dve", TT(den[:], den[:], tmp8[:], ALU.add), reads=[r_den, r_tmp8], writes=[r_den])
                S.op("dve", RECIP(den[:], den[:]), reads=[r_den], writes=[r_den])
                S.op("dve", TT(fre[:], nr[:], are2[:], ALU.mult), reads=[r_nr, r_are], writes=[r_fre])
                S.op("dve", TT(tmp8[:], ni[:], aim2[:], ALU.mult), reads=[r_ni, r_aim], writes=[r_tmp8])
                S.op("dve", TT(fre[:], fre[:], tmp8[:], ALU.add), reads=[r_fre, r_tmp8], writes=[r_fre])
                S.op("dve", TT(fre[:], fre[:], den[:], ALU.mult), reads=[r_fre, r_den], writes=[r_fre])
                S.op("dve", TT(fim[:], ni[:], are2[:], ALU.mult), reads=[r_ni, r_are], writes=[r_fim])
                S.op("dve", TT(tmp8[:], nr[:], aim2[:], ALU.mult), reads=[r_nr, r_aim], writes=[r_tmp8])
                S.op("dve", TT(fim[:], fim[:], tmp8[:], ALU.subtract), reads=[r_fim, r_tmp8], writes=[r_fim])
                S.op("dve", TT(fim[:], fim[:], den[:], ALU.mult), reads=[r_fim, r_den], writes=[r_fim])
                Bre = sb("Bre", [64, 8, 16]); Bim = sb("Bim", [64, 8, 16]); r_Bre = Res("Bre"); r_Bim = Res("Bim")
                S.dma("sp", Bre[:], bre_d.rearrange("g p h -> p g h"), "s3", writes=[r_Bre])
                S.dma("sp", Bim[:], bim_d.rearrange("g p h -> p g h"), "s4", writes=[r_Bim])
                BBre = sb("BBre", [64, 8, 16]); BBim = sb("BBim", [64, 8, 16]); tB = sb("tB", [64, 8, 16])
                r_BBre = Res("BBre"); r_BBim = Res("BBim"); r_tB = Res("tB")
                fre_b = fre[0:64, :].unsqueeze(2).to_broadcast([64, 8, 16])
                fim_b = fim[0:64, :].unsqueeze(2).to_broadcast([64, 8, 16])
                S.op("dve", TT(BBre[:], Bre[:], fre_b, ALU.mult), reads=[r_Bre, r_fre], writes=[r_BBre])
                S.op("dve", TT(tB[:], Bim[:], fim_b, ALU.mult), reads=[r_Bim, r_fim], writes=[r_tB])
                S.op("dve", TT(BBre[:], BBre[:], tB[:], ALU.subtract), reads=[r_BBre, r_tB], writes=[r_BBre])
                S.op("dve", TT(BBim[:], Bim[:], fre_b, ALU.mult), reads=[r_Bim, r_fre], writes=[r_BBim])
                S.op("dve", TT(tB[:], Bre[:], fim_b, ALU.mult), reads=[r_Bre, r_fim], writes=[r_tB])
                S.op("dve", TT(BBim[:], BBim[:], tB[:], ALU.add), reads=[r_BBim, r_tB], writes=[r_BBim])
                FB1 = sb("FB1", [128, 128]); FB2 = sb("FB2", [128, 128]); r_FB1 = Res("FB1"); r_FB2 = Res("FB2")
                S.group("pe", [TR(banks[0][:, 0:64], BBre[:].rearrange("p g h -> p (g h)"), ident_f[0:64, 0:64]),
                               TR(banks[0][:, 64:128], BBim[:].rearrange("p g h -> p (g h)"), ident_f[0:64, 0:64])],
                        reads=[r_BBre, r_BBim, r_ident_f], writes=[rbank[0]])
                S.op("dve", CP(FB1[:], banks[0][:, 0:128]), reads=[rbank[0]], writes=[r_FB1])
                S.op("dve", CP(FB2[:, 0:64], banks[0][:, 64:128]), reads=[rbank[0]], writes=[r_FB2])
                S.op("dve", TS(FB2[:, 64:128], banks[0][:, 0:64], -1.0, ALU.mult), reads=[rbank[0], r_FB2], writes=[r_FB2])
                LB1 = sb("LB1", [128, 8, 128], BF16); LB2 = sb("LB2", [128, 8, 128], BF16); r_LB = Res("LB")
                for g in range(8):
                    S.op("pool", TS(LB1[:, g, :], FB1[:], gmask[:, g:g + 1], ALU.mult), reads=[r_FB1, r_gmask], writes=[r_LB])
                    S.op("pool", TS(LB2[:, g, :], FB2[:], gmask[:, g:g + 1], ALU.mult), reads=[r_FB2, r_gmask], writes=[r_LB])
                CC1 = sb("CC1", [128, 128]); CC2 = sb("CC2", [128, 128]); r_CC1 = Res("CC1"); r_CC2 = Res("CC2")
                S.dma("sp", CC1[0:64, :], creT_d, "s5", writes=[r_CC1])
                S.dma("sp", CC1[64:128, :], cimT_d, "s5", writes=[r_CC1])
                S.dma("sp", CC2[0:64, :], cimT_d, "s6", writes=[r_CC2])
                S.dma("sp", CC2[64:128, :], creT_d, "s6", writes=[r_CC2])
                S.op("dve", TS(CC1[64:128, :], CC1[64:128, :], -1.0, ALU.mult), reads=[r_CC1], writes=[r_CC1])
                S.op("dve", TS(CC2[:], CC2[:], -1.0, ALU.mult), reads=[r_CC2], writes=[r_CC2])
                LC1 = sb("LC1", [128, 8, 128], BF16); LC2 = sb("LC2", [128, 8, 128], BF16); r_LC = Res("LC")
                S.op("pool", MEMSET(LC1[:], 0.0), writes=[r_LC])
                S.op("pool", MEMSET(LC2[:], 0.0), writes=[r_LC])
                for g in range(8):
                    S.op("dve", CP(LC1[:, g, 16 * g:16 * g + 16], CC1[:, 16 * g:16 * g + 16]), reads=[r_CC1], writes=[r_LC])
                    S.op("dve", CP(LC2[:, g, 16 * g:16 * g + 16], CC2[:, 16 * g:16 * g + 16]), reads=[r_CC2], writes=[r_LC])
                diagD = sb("diagD", [128, 128], BF16); r_diagD = Res("diagD")
                S.op("dve", TS(diagD[:], ident_f[:], dskip[:, 0:1], ALU.mult), reads=[r_ident_f, r_dskip], writes=[r_diagD])
                s0 = sb("s0", [128, 2, 8]); r_s0 = [Res("s0_0"), Res("s0_1")]
                S.op("dve", MEMSET(s0[:], 0.0), writes=r_s0)

                xt = [sb("xt%d" % i, [128, D]) for i in range(2)]; r_xt = [Res("xt%d" % i) for i in range(2)]
                junk = sb("junk", [128, D], BF16); r_junk = Res("junk")
                ssq = sb("ssq", [128, 4]); r_ssq = [Res("ssq%d" % i) for i in range(4)]
                xs = [sb("xs%d" % i, [128, D], BF16) for i in range(2)]; r_xs = [Res("xs%d" % i) for i in range(2)]
                hT = [sb("hT%d" % i, [128, 8, 512], BF16) for i in range(2)]; r_hT = [Res("hT%d" % i) for i in range(2)]
                COSt = sb("COSt", [128, 512]); SINt = sb("SINt", [128, 512]); r_COSt = Res("COSt"); r_SINt = Res("SINt")
                qraw = sb("qraw", [128, 512], BF16); r_qraw = Res("qraw")
                rt1 = sb("rt1", [128, 512]); rt2 = sb("rt2", [128, 512]); r_rt1 = Res("rt1"); r_rt2 = Res("rt2")
                QT = [sb("QT%d" % i, [128, 512], BF16) for i in range(2)]; r_QT = [Res("QT%d" % i) for i in range(2)]
                KTa = sb("KTa", [128, LP], BF16); KTb = sb("KTb", [128, LP], BF16)
                r_KT = [Res("KT%d" % i) for i in range(NT1)]
                r_KTz = Res("KTz")
                S.op("pool", MEMSET(KTa[64:128, :], 0.0), writes=[r_KTz])
                S.op("pool", MEMSET(KTb[0:64, :], 0.0), writes=[r_KTz])
                V = sb("V", [128, NBLK, 128], BF16); r_V = [Res("V%d" % i) for i in range(NT1)]
                uT = [sb("uT%d" % i, [128, 512], BF16) for i in range(2)]; r_uT = [Res("uT%d" % i) for i in range(2)]
                st1 = [sb("st1_%d" % i, [128, 512]) for i in range(2)]; r_st1 = [Res("st1_%d" % i) for i in range(2)]
                st2 = [sb("st2_%d" % i, [128, 512]) for i in range(2)]; r_st2 = [Res("st2_%d" % i) for i in range(2)]
                bt = [sb("bt%d" % i, [128, 512]) for i in range(2)]; r_bt = [Res("bt%d" % i) for i in range(2)]
                Xs = [sb("Xs%d" % i, [128, 512]) for i in range(3)]; r_Xs = [Res("Xs%d" % i) for i in range(3)]
                P1s = [sb("P1s%d" % i, [128, 512], BF16) for i in range(3)]; r_P1s = [Res("P1s%d" % i) for i in range(3)]
                P2s = [sb("P2s%d" % i, [128, 512], BF16) for i in range(3)]; r_P2s = [Res("P2s%d" % i) for i in range(3)]
                ybuf = [sb("ybuf%d" % i, [128, 512], BF16) for i in range(2)]; r_ybuf = [Res("ybuf%d" % i) for i in range(2)]
                Pb = [sb("Pb%d" % i, [128, 2, 256], BF16) for i in range(3)]; r_Pb = [Res("Pb%d" % i) for i in range(3)]
                rL = sb("rL", [128, 2, 256]); r_rL = Res("rL")
                Osb = sb("Osb", [128, 2, 256]); r_Osb = Res("Osb")
                On = Osb; r_On = r_Osb
                pending_fin = []
                od = sb("od", [128, 256]); r_od = Res("od")
                osq = sb("osq", [128, 256], BF16); r_osq = Res("osq")
                orstd = sb("orstd", [128, 256]); r_orstd = Res("orstd")
                obuf = [sb("obuf%d" % i, [128, 256], BF16) for i in range(2)]; r_obuf = [Res("obuf%d" % i) for i in range(2)]

                F0, F1, F2, F3 = banks[0], banks[1], banks[2], banks[3]
                rF0, rF1, rF2, rF3 = rbank[0], rbank[1], rbank[2], rbank[3]
                Sb = [banks[4], banks[5]]; rSb = [rbank[4], rbank[5]]
                Ob, Lb = banks[6], banks[7]; rOb, rLb = rbank[6], rbank[7]
                F0b = F0[:].bitcast(BF16)


                pcount = [0]
                sbcount = [0]
                out_events = []

                def front_a(ti):
                    nb = 4 if ti < 16 else 1
                    hb = ti % 2
                    for j in range(nb):
                        s = 4 * ti + j
                        xb = sbcount[0] % 2
                        sq = sbcount[0] % 4
                        sbcount[0] += 1
                        if s == 0:
                            S.dma("sp", xt[xb][0:16, :], meta_d, "x%d" % xb, writes=[r_xt[xb]])
                            S.dma("sp", xt[xb][16:128, :], x_d[0:112, :], "x%d" % xb, writes=[r_xt[xb]])
                        elif s == 64:
                            S.op("pool", MEMSET(xt[xb][:], 0.0), writes=[r_xt[xb]])
                            S.dma("sp", xt[xb][0:16, :], x_d[SEQ - 16:SEQ, :], "x%d" % xb, writes=[r_xt[xb]])
                        else:
                            S.dma("sp", xt[xb][:], x_d[128 * s - 16:128 * s + 112, :], "x%d" % xb, writes=[r_xt[xb]])
                        S.op("act", ACT(junk[:], xt[xb][:], AF.Square, accum_out=ssq[:, sq:sq + 1]),
                             reads=[r_xt[xb]], writes=[r_junk, r_ssq[sq]])
                        S.op("act", ACT(ssq[:, sq:sq + 1], ssq[:, sq:sq + 1], AF.Ln, scale=1.0 / D, bias=epsc[:, 0:1]),
                             reads=[r_ssq[sq], r_eps], writes=[r_ssq[sq]])
                        S.op("act", ACT(ssq[:, sq:sq + 1], ssq[:, sq:sq + 1], AF.Exp, scale=-0.5),
                             reads=[r_ssq[sq]], writes=[r_ssq[sq]])
                        S.op("act", ACT(xs[xb][:], xt[xb][:], AF.Copy, scale=ssq[:, sq:sq + 1]),
                             reads=[r_xt[xb], r_ssq[sq]], writes=[r_xs[xb]])
                        yield
                        S.group("pe", [TR(F0b[:, 128 * k:128 * (k + 1)], xs[xb][:, 128 * k:128 * (k + 1)], ident_b[:]) for k in range(8)],
                                reads=[r_xs[xb], r_ident_b], writes=[rF0])
                        S.op("dve", CP(hT[hb][:, :, 128 * j:128 * (j + 1)], F0b.rearrange("p (k t) -> p k t", k=8)),
                             reads=[rF0], writes=[r_hT[hb]])
                        yield

                def front_bs(ti):
                    nb = 4 if ti < 16 else 1
                    N = 128 * nb
                    tok0 = 512 * ti
                    hb = ti % 2
                    S.op("dve", TS(COSt[:], C0[:], cI[:, ti:ti + 1], ALU.mult), reads=[r_C0, r_cI], writes=[r_COSt])
                    S.op("dve", STT(COSt[:], S0[:], nsI[:, ti:ti + 1], COSt[:], ALU.mult, ALU.add), reads=[r_S0, r_nsI, r_COSt], writes=[r_COSt])
                    S.op("dve", TS(SINt[:], S0[:], cI[:, ti:ti + 1], ALU.mult), reads=[r_S0, r_cI], writes=[r_SINt])
                    S.op("dve", STT(SINt[:], C0[:], sI[:, ti:ti + 1], SINt[:], ALU.mult, ALU.add), reads=[r_C0, r_sI, r_SINt], writes=[r_SINt])
                    yield

                    def proj_fm(c, bank, rb):
                        S.group("pe", [MM(bank[:, 0:N], Wb[:, k, 128 * c:128 * (c + 1)], hT[hb][:, k, 0:N], k == 0, k == 7) for k in range(8)],
                                reads=[r_Wb, r_hT[hb]], writes=[rb])

                    def rope_a():
                        S.op("act", ACT(qraw[:, 0:N], F1[:, 0:N], AF.Copy), reads=[rF1], writes=[r_qraw])
                        S.op("pe", MM(F2[:, 0:N], rmatT_b[:], qraw[:, 0:N], True, True), reads=[r_rmat, r_qraw], writes=[rF2])

                    def rope_b(dst_ap, r_dst, dst2=None):
                        S.op("dve", TT(rt1[:, 0:N], F1[:, 0:N], COSt[:, 0:N], ALU.mult), reads=[rF1, r_COSt], writes=[r_rt1])
                        S.op("dve", TT(rt2[:, 0:N], F2[:, 0:N], SINt[:, 0:N], ALU.mult), reads=[rF2, r_SINt], writes=[r_rt2])
                        if dst2 is None:
                            S.op("pool", TT(dst_ap, rt1[:, 0:N], rt2[:, 0:N], ALU.add), reads=[r_rt1, r_rt2], writes=[r_dst])
                        else:
                            S.op("pool", TT(dst_ap[0:64, :], rt1[0:64, 0:N], rt2[0:64, 0:N], ALU.add), reads=[r_rt1, r_rt2, r_KTz], writes=[r_dst])
                            S.op("pool", TT(dst2[64:128, :], rt1[64:128, 0:N], rt2[64:128, 0:N], ALU.add), reads=[r_rt1, r_rt2, r_KTz], writes=[r_dst])

                    proj_fm(0, F1, rF1)
                    yield
                    rope_a()
                    yield
                    rope_b(QT[hb][:, 0:N], r_QT[hb])
                    yield
                    proj_fm(1, F1, rF1)
                    yield
                    rope_a()
                    yield
                    rope_b(KTa[:, tok0:tok0 + N], r_KT[ti], KTb[:, tok0:tok0 + N])
                    yield
                    proj_fm(3, F1, rF1)
                    yield
                    S.op("act", ACT(uT[hb][:, 0:N], F1[:, 0:N], AF.Copy), reads=[rF1], writes=[r_uT[hb]])
                    fns = []
                    for j in range(nb):
                        for k in range(8):
                            fns.append(MM(F2[:, 128 * j:128 * (j + 1)], hT[hb][:, k, 128 * j:128 * (j + 1)], Wb[:, k, 256:384], k == 0, k == 7))
                    S.group("pe", fns, reads=[r_Wb, r_hT[hb]], writes=[rF2])
                    yield
                    S.op("act", ACT(V[:, 4 * ti:4 * ti + nb, :], F2[:, 0:N].rearrange("p (j d) -> p j d", j=nb), AF.Copy),
                         reads=[rF2], writes=[r_V[ti]])
                    yield
                    par = ti % 2
                    yb = ti % 2

                    def emit_y(g, pb):
                        S.group("pe", [MM(F3[:, 0:N], LC1[:, g, :], P1s[pb][:, 0:N], g == 0, False),
                                       MM(F3[:, 0:N], LC2[:, g, :], P2s[pb][:, 0:N], False, False)],
                                reads=[r_LC, r_P1s[pb], r_P2s[pb]], writes=[rF3])

                    def stage_b(g):
                        pb = g % 3; q = g % 2
                        S.op("dve", SCAN(Xs[pb][:, 0:N], rdec[:, g:g + 1].to_broadcast([128, N]), bt[q][:, 0:N], s0[:, par, g:g + 1]),
                             reads=[r_rdec, r_bt[q], r_s0[par]], writes=[r_Xs[pb]])
                        S.op("pool", TT(P1s[pb][:, 0:N], Xs[pb][:, 0:N], COSg[:, g, 0:N], ALU.mult), reads=[r_Xs[pb], r_COSg[g]], writes=[r_P1s[pb]])
                        S.op("dve", TT(P2s[pb][:, 0:N], Xs[pb][:, 0:N], SINg[:, g, 0:N], ALU.mult), reads=[r_Xs[pb], r_SINg[g]], writes=[r_P2s[pb]])

                    def stage_c(g):
                        pb = g % 3
                        if ti + 1 < nt1:
                            S.op("pe", MM(F0[:, g:g + 1], ROT[:, g, :], Xs[pb][:, N - 1:N], True, True), reads=[r_ROT, r_Xs[pb]], writes=[rF0])
                            S.op("act", ACT(s0[:, 1 - par, g:g + 1], F0[:, g:g + 1], AF.Copy), reads=[rF0], writes=[r_s0[1 - par]])
                        emit_y(g, pb)

                    for it in range(11):
                        if 3 <= it:
                            stage_c(it - 3)
                            yield
                        if it < 8:
                            g = it; q = g % 2
                            S.op("pe", MM(F1[:, 0:N], LB1[:, g, :], uT[hb][:, 0:N], True, True), reads=[r_LB, r_uT[hb]], writes=[rF1])
                            S.op("pe", MM(F2[:, 0:N], LB2[:, g, :], uT[hb][:, 0:N], True, True), reads=[r_LB, r_uT[hb]], writes=[rF2])
                            yield
                            S.op("dve", TT(st1[q][:, 0:N], F1[:, 0:N], COSg[:, g, 0:N], ALU.mult), reads=[rF1, r_COSg[g]], writes=[r_st1[q]])
                            S.op("dve", TT(st2[q][:, 0:N], F2[:, 0:N], SINg[:, g, 0:N], ALU.mult), reads=[rF2, r_SINg[g]], writes=[r_st2[q]])
                            S.op("pool", TT(bt[q][:, 0:N], st1[q][:, 0:N], st2[q][:, 0:N], ALU.add), reads=[r_st1[q], r_st2[q]], writes=[r_bt[q]])
                        if 1 <= it <= 8:
                            stage_b(it - 1)
                        if it <= 8:
                            yield
                    S.op("pe", MM(F3[:, 0:N], diagD[:], uT[hb][:, 0:N], False, True), reads=[r_diagD, r_uT[hb]], writes=[rF3])
                    yield
                    S.op("act", ACT(ybuf[yb][:, 0:N], F3[:, 0:N], AF.Gelu), reads=[rF3], writes=[r_ybuf[yb]])
                    lo = max(tok0, NMETA); hi = min(tok0 + N, L)
                    out_events.extend(xchg_write(128, lo - NMETA, hi - NMETA, ybuf[yb], lo - tok0, "yo%d" % yb, r_ybuf[yb]))
                    yield

                def attention(ti, pump):
                    nb = 4 if ti < 16 else 1
                    tok0 = 512 * ti
                    hb = ti % 2
                    nqt = 2 if nb == 4 else 1
                    nbq = 2 if nb == 4 else 1
                    NQ = 128 * nbq
                    QTv = QT[hb]
                    for qi in range(nqt):
                        c = 2 * ti + qi
                        q0 = 256 * qi
                        nkb = 2 * c + nbq

                        def qk(kb):
                            j = kb - 2 * c
                            qs = 128 * j if j > 0 else 0
                            kt = kb // 4
                            sbk = kb % 2
                            pbk = kb % 3
                            Sv = Sb[sbk][:, 0:2 * NQ].rearrange("p (h q) -> p h q", h=2)
                            Pv = Pb[pbk][:].rearrange("p h q -> p (h q)")[:, 0:2 * NQ].rearrange("p (h q) -> p h q", h=2)
                            S.group("pe", [MM(Sv[:, 0, qs:NQ], KTa[:, 128 * kb:128 * (kb + 1)], QTv[:, q0 + qs:q0 + NQ], True, True),
                                           MM(Sv[:, 1, qs:NQ], KTb[:, 128 * kb:128 * (kb + 1)], QTv[:, q0 + qs:q0 + NQ], True, True)],
                                    reads=[r_KT[kt], r_QT[hb]], writes=[rSb[sbk]])
                            S.op("act", ACT(Pv[:, :, qs:NQ], Sv[:, :, qs:NQ], AF.Exp, scale=0.125), reads=[rSb[sbk]], writes=[r_Pb[pbk]])
                            if j >= 0:
                                S.op("pool", TT(Pv[:, :, qs:qs + 128], Pv[:, :, qs:qs + 128],
                                                tri_b[:].unsqueeze(1).to_broadcast([128, 2, 128]), ALU.mult),
                                     reads=[r_Pb[pbk], r_tri], writes=[r_Pb[pbk]])

                        def pv(kb):
                            j = kb - 2 * c
                            qs = 128 * j if j > 0 else 0
                            kt = kb // 4
                            pbk = kb % 3
                            Pv = Pb[pbk][:].rearrange("p h q -> p (h q)")[:, 0:2 * NQ].rearrange("p (h q) -> p h q", h=2)
                            Ov = Ob[:, 0:2 * NQ].rearrange("p (h q) -> p h q", h=2)
                            Lv = Lb[:, 0:2 * NQ].rearrange("p (h q) -> p h q", h=2)
                            first = (kb == 0); lastk = (kb == nkb - 1)
                            if qs == 0:
                                Pf = Pb[pbk][:].rearrange("p h q -> p (h q)")[:, 0:2 * NQ]
                                S.op("pe", MM(Ob[:, 0:2 * NQ], V[:, kb, :], Pf, first, lastk), reads=[r_V[kt], r_Pb[pbk]], writes=[rOb])
                                S.op("pe", MM(Lb[:, 0:2 * NQ], ones_b[:], Pf, first, lastk), reads=[r_ones, r_Pb[pbk]], writes=[rLb])
                            else:
                                assert not first
                                S.group("pe", [MM(Ov[:, 0, qs:NQ], V[:, kb, :], Pv[:, 0, qs:NQ], False, False),
                                               MM(Ov[:, 1, qs:NQ], V[:, kb, :], Pv[:, 1, qs:NQ], False, lastk)],
                                        reads=[r_V[kt], r_Pb[pbk]], writes=[rOb])
                                S.group("pe", [MM(Lv[:, 0, qs:NQ], ones_b[:], Pv[:, 0, qs:NQ], False, False),
                                               MM(Lv[:, 1, qs:NQ], ones_b[:], Pv[:, 1, qs:NQ], False, lastk)],
                                        reads=[r_ones, r_Pb[pbk]], writes=[rLb])

                        qk(0)
                        for kb in range(nkb):
                            if kb + 1 < nkb:
                                qk(kb + 1)
                            pv(kb)
                            if kb == 2 and pending_fin:
                                pending_fin.pop(0)()
                            pump()
                        while pending_fin:
                            pending_fin.pop(0)()
                        Of = Ob[:, 0:2 * NQ]; Lf = Lb[:, 0:2 * NQ]
                        Osf = Osb[:].rearrange("p h q -> p (h q)")[:, 0:2 * NQ]
                        rLf = rL[:].rearrange("p h q -> p (h q)")[:, 0:2 * NQ]
                        Onf = On[:].rearrange("p h q -> p (h q)")[:, 0:2 * NQ]
                        Onv = Onf.rearrange("p (h q) -> p h q", h=2)
                        S.op("dve", CP(Osf, Of), reads=[rOb], writes=[r_Osb])
                        S.op("act", ACT(rLf, Lf, AF.Ln), reads=[rLb], writes=[r_rL])
                        S.op("act", ACT(rLf, rLf, AF.Exp, scale=-1.0), reads=[r_rL], writes=[r_rL])
                        S.op("dve", TT(Onf, Osf, rLf, ALU.mult), reads=[r_Osb, r_rL], writes=[r_On])
                        S.op("dve", STT(od[:, 0:NQ], Onv[:, 1, :], neglam[:, 0:1], Onv[:, 0, :], ALU.mult, ALU.add),
                             reads=[r_On, r_neglam], writes=[r_od])
                        S.op("act", ACT(osq[:, 0:NQ], od[:, 0:NQ], AF.Square), reads=[r_od], writes=[r_osq])

                        def fin2(c=c, NQ=NQ, q0=q0, tok0=tok0):
                            S.op("pe", MM(F0[:, 256:256 + NQ], ones_b[:], osq[:, 0:NQ], True, True), reads=[r_ones, r_osq], writes=[rF0])
                            S.op("act", ACT(orstd[:, 0:NQ], F0[:, 256:256 + NQ], AF.Ln, scale=1.0 / 128, bias=epsc[:, 0:1]),
                                 reads=[rF0, r_eps], writes=[r_orstd])
                            S.op("act", ACT(orstd[:, 0:NQ], orstd[:, 0:NQ], AF.Exp, scale=-0.5), reads=[r_orstd], writes=[r_orstd])
                            ob = c % 2
                            S.op("dve", STT(obuf[ob][:, 0:NQ], od[:, 0:NQ], subg[:, 0:1], orstd[:, 0:NQ], ALU.mult, ALU.mult),
                                 reads=[r_od, r_subg, r_orstd], writes=[r_obuf[ob]])
                            qt0 = tok0 + q0
                            lo = max(qt0, NMETA); hi = min(qt0 + NQ, L)
                            out_events.extend(xchg_write(0, lo - NMETA, hi - NMETA, obuf[ob], lo - qt0, "oo%d" % ob, r_obuf[ob]))
                        pending_fin.append(fin2)

                def run_all(g):
                    for _ in g:
                        pass
                run_all(front_a(0)); run_all(front_bs(0))
                if nt1 > 1:
                    run_all(front_a(1))
                for ti in range(nt1):
                    gens = []
                    if ti + 1 < nt1:
                        gens.append(front_bs(ti + 1))
                    if ti + 2 < nt1:
                        gens.append(front_a(ti + 2))
                    nbq_t = 2 if ti < 16 else 1
                    total_kb = sum(2 * (2 * ti + qi) + nbq_t for qi in range(2 if ti < 16 else 1))
                    state = {"steps_left": (FRONT_STEPS if ti + 1 < nt1 else 0) + (8 if ti + 2 < nt1 else 0),
                             "kb_left": total_kb, "gens": gens, "rr": 0}

                    def pump(state=state):
                        if not state["gens"]:
                            return
                        kbl = max(state["kb_left"], 1)
                        n = -(-state["steps_left"] // kbl)
                        state["kb_left"] -= 1
                        for _ in range(n):
                            if not state["gens"]:
                                return
                            state["rr"] += 1
                            idx = 1 if (len(state["gens"]) > 1 and state["rr"] % 5 == 0) else 0
                            try:
                                next(state["gens"][idx])
                                state["steps_left"] -= 1
                            except StopIteration:
                                state["gens"].pop(idx)
                    attention(ti, pump)
                    for g_ in state["gens"]:
                        run_all(g_)
                    if wprep:
                        dst_, src_ = wprep.pop(0)
                        wprep_events.append(S.dma("pool", dst_, src_, "wprep"))
                    if ti == nt1 - 1 or (ti >= 4 and ti % 4 == 0):
                        while pending_fin:
                            pending_fin.pop(0)()
                    if use_cc and ti >= 4 and ti % 4 == 0:
                        qd = ti // 4 - 1
                        S.wait_all("pool", q_events[qd])
                        S.raw("pool", (lambda qd: (lambda e: e.collective_compute(
                            "AllGather", ALU.bypass, replica_groups=[[0, 1, 2, 3], [4, 5, 6, 7]],
                            ins=[xin_t[qd].ap().opt()], outs=[xout_t.ap()[1024 * qd:1024 * (qd + 1), :].opt()]).then_inc(cc_sem)))(qd))

            return out_events

        out_events = _phase1() if nt1 > 0 else []

        if debug:
            print('MARKS', S.marks, S.nops)
            S.limit = 10**9
            S.wait_all("pool", out_events)
            evs = [S.dma("pool", dbg_d[:, TOK2 * q:TOK2 * (q + 1)], xin[q], "dbg") for q in range(4)]
            S.wait_all("pool", evs)
        if do_phase2:
            S.wait_all("pool", out_events)
            if p2mode == "nogather":
                evd = S.dma("pool", xout[0:1024, :], xout_dbg_d, "xdbg")
                S.wait_all("pool", [evd])
            else:
                S.raw("pool", lambda e: e.wait_ge(cc_sem, 4))
            S.cnt["pool"] += 1
            gather_ev = (S.sem["pool"], S.cnt["pool"], "pool")
            S.raw("pool", lambda e: e.sem_inc(S.sem["pool"], 1))
            lasts = [S.last[e] for e in ("pe", "act", "dve", "pool") if S.last[e] is not None] + [gather_ev]
            for e in ("pe", "act", "dve", "sp", "pool"):
                S.wait_all(e, lasts)

            with ExitStack() as p2:
                def sb(name, shape, dt=F32):
                    return p2.enter_context(nc.sbuf_tensor("b_" + name, list(shape), dt))

                Wg = sb("Wg", [128, 8, DFF], BF16); r_Wg = Res("Wg")
                Wu = sb("Wu", [128, 8, DFF], BF16); r_Wu = Res("Wu")
                Wd = sb("Wd", [128, NF, D], BF16); r_Wd = Res("Wd")
                Wo = sb("Wo", [128, 8, D], BF16); r_Wo = Res("Wo")
                Wgl = sb("Wgl", [128, 4, 512], BF16); r_Wgl = Res("Wgl")
                while wprep:
                    dst_, src_ = wprep.pop(0)
                    wprep_events.append(S.dma("pool", dst_, src_, "wprep"))
                S.dma("sp", Wgl[:], wgl_b.rearrange("(k p) c -> p k c", p=128), "w0", writes=[r_Wgl], extra=wprep_events)
                S.dma("sp", Wo[:], wo_b.rearrange("(k p) c -> p k c", p=128), "w1", writes=[r_Wo], extra=wprep_events)
                r_Wgc = [Res("Wg%d" % f) for f in range(NF)]
                r_Wuc = [Res("Wu%d" % f) for f in range(NF)]
                r_Wdc = [Res("Wd%d" % f) for f in range(NF)]
                for f in range(0, NF, 2):
                    S.dma("sp", Wg[:, :, 128 * f:128 * (f + 2)], wg_b[:, 128 * f:128 * (f + 2)].rearrange("(k p) c -> p k c", p=128),
                          "wg%d" % (f // 2), writes=[r_Wgc[f], r_Wgc[f + 1]], extra=wprep_events)
                    S.dma("sp", Wu[:, :, 128 * f:128 * (f + 2)], wu_b[:, 128 * f:128 * (f + 2)].rearrange("(k p) c -> p k c", p=128),
                          "wu%d" % (f // 2), writes=[r_Wuc[f], r_Wuc[f + 1]], extra=wprep_events)
                for f in range(0, NF, 2):
                    S.dma("sp", Wd[:, f:f + 2, :], wd_b[128 * f:128 * (f + 2), :].rearrange("(k p) c -> p k c", p=128),
                          "wd%d" % (f // 2), writes=[r_Wdc[f], r_Wdc[f + 1]], extra=wprep_events)
                ident2 = sb("ident2", [128, 128], BF16); r_id2 = Res("ident2")
                ones2 = sb("ones2", [128, 128], BF16); r_ones2 = Res("ones2")
                eps2 = sb("eps2", [128, 1]); r_eps2 = Res("eps2")
                bglu = sb("bglu", [128, 4]); r_bglu = Res("bglu")
                gssm = sb("gssm", [128, 4]); r_gssm = Res("gssm")
                gpost_b = sb("gpost_b", [128, D]); r_gpost = Res("gpost")
                gffn8 = sb("gffn8", [128, 8]); r_gffn = Res("gffn")
                gpffn_b = sb("gpffn_b", [128, D]); r_gpffn = Res("gpffn")
                S.dma("pool", ident2[:], ident_d, "k0", writes=[r_id2])
                S.op("dve", MEMSET(ones2[:], 1.0), writes=[r_ones2])
                S.op("dve", MEMSET(eps2[:], EPS), writes=[r_eps2])
                S.dma("sp", bglu[:], bglu_d, "k1", writes=[r_bglu])
                S.dma("sp", gssm[:], gssm_d, "k2", writes=[r_gssm])
                S.dma("sp", gpost_b[:], gpost_d.partition_broadcast(128), "k3", writes=[r_gpost])
                S.dma("sp", gffn8[:], gffn_d, "k4", writes=[r_gffn])
                S.dma("sp", gpffn_b[:], gpffn_d.partition_broadcast(128), "k5", writes=[r_gpffn])

                cT = [sb("cT%d" % i, [128, 8, T2], BF16) for i in range(1)]; r_cT = [Res("cT%d" % i) for i in range(1)]
                sig = sb("sig", [128, T2], BF16); r_sig = Res("sig")
                yg = sb("yg", [128, 4, T2], BF16); r_yg = Res("yg")
                sq2 = sb("sq2", [128, T2], BF16); r_sq2 = Res("sq2")
                rsy = sb("rsy", [128, T2]); r_rsy = Res("rsy")
                yn = sb("yn", [128, 4, T2], BF16); r_yn = Res("yn")
                hres = [sb("hres%d" % i, [128, D]) for i in range(3)]; r_hres = [Res("hres%d" % i) for i in range(3)]
                ssvA = sb("ssvA", [128, 8]); r_ssvA = Res("ssvA")
                ssvB = sb("ssvB", [128, 4]); r_ssvB = Res("ssvB")
                tmpA = sb("tmpA", [128, 512]); r_tmpA = Res("tmpA")
                tmpB = tmpA; r_tmpB = r_tmpA
                hs = [sb("hs0", [128, D], BF16)] * 2; r_hs = [Res("hs0")] * 2
                junkA = sb("junkA", [128, 512], BF16); r_junkA = Res("junkA")
                junkB = junkA; r_junkB = r_junkA
                hT2 = [sb("hT2_%d" % i, [128, 8, T2], BF16) for i in range(2)]; r_hT2 = [Res("hT2_%d" % i) for i in range(2)]
                sg = [sb("sg%d" % i, [128, T2], BF16) for i in range(2)]; r_sg = [Res("sg%d" % i) for i in range(2)]
                aT = sb("aT", [128, NF, T2], BF16); r_aT = Res("aT")

                pid = [None]
                B = banks
                rB = rbank
                final_events = []

                def h1_gen(tt):
                    cb = 0

                    def _ld(e, tt=tt):
                        if pid[0] is None:
                            pid[0] = e.partition_id() % 4
                        return e.dma_start(out=cT[cb][:], in_=xout[bass.ds(pid[0] * 1024, 1024), tt * T2:(tt + 1) * T2].rearrange("(k p) t -> p k t", p=128))
                    slot = "g%d" % cb
                    if slot not in S.dma_sems:
                        S.dma_sems[slot] = top.enter_context(nc.semaphore("d_" + slot))
                        S.dma_cnt[slot] = 0
                    S._waits("sp", S._deps_for(None, [r_cT[cb]], None))
                    S.dma_cnt[slot] += 16
                    ev = (S.dma_sems[slot], S.dma_cnt[slot], "dma")
                    S.raw("sp", (lambda sem: (lambda e, f=_ld: f(e).then_inc(sem, 16)))(S.dma_sems[slot]))
                    S._commit(ev, None, [r_cT[cb]])
                    yield None
                    for m in range(4):
                        S.group("pe", [MM(B[0][:, 0:T2], Wgl[:, r, 128 * m:128 * (m + 1)], cT[cb][:, 2 * r + 1, :], r == 0, r == 3) for r in range(4)],
                                reads=[r_Wgl, r_cT[cb]], writes=[rB[0]])
                        yield None
                        S.op("act", ACT(sig[:], B[0][:, 0:T2], AF.Sigmoid, bias=bglu[:, m:m + 1]), reads=[rB[0], r_bglu], writes=[r_sig])
                        S.op("dve", TT(yg[:, m, :], cT[cb][:, 2 * m + 1, :], sig[:], ALU.mult), reads=[r_cT[cb], r_sig], writes=[r_yg])
                        S.op("act", ACT(sq2[:], yg[:, m, :], AF.Square), reads=[r_yg], writes=[r_sq2])
                        yield None
                        S.op("pe", MM(B[1][:, 0:T2], ones2[:], sq2[:], m == 0, m == 3), reads=[r_ones2, r_sq2], writes=[rB[1]])
                    yield None
                    S.op("act", ACT(rsy[:], B[1][:, 0:T2], AF.Ln, scale=1.0 / 512, bias=eps2[:, 0:1]), reads=[rB[1], r_eps2], writes=[r_rsy])
                    S.op("act", ACT(rsy[:], rsy[:], AF.Exp, scale=-0.5), reads=[r_rsy], writes=[r_rsy])
                    for m in range(4):
                        S.op("dve", STT(yn[:, m, :], yg[:, m, :], gssm[:, m:m + 1], rsy[:], ALU.mult, ALU.mult),
                             reads=[r_yg, r_gssm, r_rsy], writes=[r_yn])
                    yield None
                    for s in range(2):
                        hr = hres[(2 * tt + s) % 3]; r_hr = r_hres[(2 * tt + s) % 3]
                        c0 = 4 * s
                        for hf in range(2):
                            bk = B[2 + hf]; rbk = rB[2 + hf]
                            fns = []
                            for k in range(8):
                                src = cT[cb][:, k, 128 * s:128 * (s + 1)] if k % 2 == 0 else yn[:, k // 2, 128 * s:128 * (s + 1)]
                                fns.append(MM(bk[:, :], src, Wo[:, k, 512 * hf:512 * (hf + 1)], k == 0, k == 7))
                            S.group("pe", fns, reads=[r_cT[cb], r_yn, r_Wo], writes=[rbk])
                            yield None
                            S.op("act", ACT(junkA[:], bk[:, :], AF.Square, accum_out=ssvA[:, c0 + hf:c0 + hf + 1]), reads=[rbk], writes=[r_junkA, r_ssvA])
                        yield None
                        S.op("dve", TT(ssvA[:, c0 + 2:c0 + 3], ssvA[:, c0:c0 + 1], ssvA[:, c0 + 1:c0 + 2], ALU.add), reads=[r_ssvA], writes=[r_ssvA])
                        S.op("act", ACT(ssvA[:, c0 + 2:c0 + 3], ssvA[:, c0 + 2:c0 + 3], AF.Ln, scale=1.0 / D, bias=eps2[:, 0:1]), reads=[r_ssvA, r_eps2], writes=[r_ssvA])
                        S.op("act", ACT(ssvA[:, c0 + 2:c0 + 3], ssvA[:, c0 + 2:c0 + 3], AF.Exp, scale=-0.5), reads=[r_ssvA], writes=[r_ssvA])
                        yield ("BAR1" if s == 1 else None)
                        row0 = tt * T2 + 128 * s
                        S.dma("sp", hr[:], xres_d[row0:row0 + 128, :], "xr%d" % ((2 * tt + s) % 3), writes=[r_hr])
                        yield None
                        for hf in range(2):
                            bk = B[2 + hf]; rbk = rB[2 + hf]
                            S.op("dve", STT(tmpA[:], bk[:, :], ssvA[:, c0 + 2:c0 + 3], gpost_b[:, 512 * hf:512 * (hf + 1)], ALU.mult, ALU.mult),
                                 reads=[rbk, r_ssvA, r_gpost], writes=[r_tmpA])
                            S.op("dve", TT(hr[:, 512 * hf:512 * (hf + 1)], tmpA[:], hr[:, 512 * hf:512 * (hf + 1)], ALU.add),
                                 reads=[r_tmpA, r_hr], writes=[r_hr])
                            S.op("act", ACT(junkA[:], hr[:, 512 * hf:512 * (hf + 1)], AF.Square, accum_out=ssvA[:, c0 + hf:c0 + hf + 1]),
                                 reads=[r_hr], writes=[r_junkA, r_ssvA])
                        yield None
                        S.op("dve", TT(ssvA[:, c0 + 3:c0 + 4], ssvA[:, c0:c0 + 1], ssvA[:, c0 + 1:c0 + 2], ALU.add), reads=[r_ssvA], writes=[r_ssvA])
                        S.op("act", ACT(ssvA[:, c0 + 3:c0 + 4], ssvA[:, c0 + 3:c0 + 4], AF.Ln, scale=1.0 / D, bias=eps2[:, 0:1]), reads=[r_ssvA, r_eps2], writes=[r_ssvA])
                        S.op("act", ACT(ssvA[:, c0 + 3:c0 + 4], ssvA[:, c0 + 3:c0 + 4], AF.Exp, scale=-0.5), reads=[r_ssvA], writes=[r_ssvA])
                        yield None
                        S.op("act", ACT(hs[s][:], hr[:], AF.Copy, scale=ssvA[:, c0 + 3:c0 + 4]), reads=[r_hr, r_ssvA], writes=[r_hs[s]])
                        yield None
                        B1b = B[1][:].bitcast(BF16)
                        S.group("pe", [TR(B1b[:, 128 * k:128 * (k + 1)], hs[s][:, 128 * k:128 * (k + 1)], ident2[:]) for k in range(8)],
                                reads=[r_hs[s], r_id2], writes=[rB[1]])
                        yield None
                        S.op("dve", TT(hT2[tt % 2][:, :, 128 * s:128 * (s + 1)], B1b.rearrange("p (k t) -> p k t", k=8),
                                       gffn8[:].unsqueeze(2).to_broadcast([128, 8, 128]), ALU.mult),
                             reads=[rB[1], r_gffn], writes=[r_hT2[tt % 2]])
                        yield None

                class Pump:
                    def __init__(self, gen):
                        self.gen = gen
                        self.released = set()
                        self.pending = None

                    def release(self, name):
                        self.released.add(name)

                    def step(self, n):
                        for _ in range(n):
                            if self.gen is None:
                                return
                            if self.pending is not None:
                                if self.pending not in self.released:
                                    return
                                self.pending = None
                            try:
                                r = next(self.gen)
                            except StopIteration:
                                self.gen = None
                                return
                            if r is not None:
                                self.pending = r

                    def flush(self):
                        while self.gen is not None:
                            if self.pending is not None:
                                assert self.pending in self.released, self.pending
                            self.step(1)

                def h2(tt, P):
                    for f in range(NF):
                        gb2 = B[4 + (f % 2)]; rgb2 = rB[4 + (f % 2)]
                        gv = gb2[:].rearrange("p (h q) -> p h q", h=2)
                        h2t = hT2[tt % 2]; r_h2t = r_hT2[tt % 2]
                        S.group("pe", [MM(gv[:, 0, :], Wg[:, k, 128 * f:128 * (f + 1)], h2t[:, k, :], k == 0, k == 7) for k in range(8)],
                                reads=[r_Wgc[f], r_h2t], writes=[rgb2])
                        P.step(1)
                        S.group("pe", [MM(gv[:, 1, :], Wu[:, k, 128 * f:128 * (f + 1)], h2t[:, k, :], k == 0, k == 7) for k in range(8)],
                                reads=[r_Wuc[f], r_h2t], writes=[rgb2])
                        S.op("act", ACT(sg[f % 2][:], gv[:, 0, :], AF.Silu), reads=[rgb2], writes=[r_sg[f % 2]])
                        S.op("dve", TT(aT[:, f, :], gv[:, 1, :], sg[f % 2][:], ALU.mult), reads=[rgb2, r_sg[f % 2]], writes=[r_aT])
                        P.step(1)
                    for s in range(2):
                        row0 = tt * T2 + 128 * s
                        hr = hres[(2 * tt + s) % 3]; r_hr = r_hres[(2 * tt + s) % 3]
                        for hf in range(2):
                            bk = B[6 + hf]; rbk = rB[6 + hf]
                            S.group("pe", [MM(bk[:, :], aT[:, f, 128 * s:128 * (s + 1)], Wd[:, f, 512 * hf:512 * (hf + 1)], f == 0, f == NF - 1) for f in range(NF)],
                                    reads=[r_aT] + r_Wdc, writes=[rbk])
                            S.op("act", ACT(junkB[:], bk[:, :], AF.Square, accum_out=ssvB[:, hf:hf + 1]), reads=[rbk], writes=[r_junkB, r_ssvB])
                            P.step(6)
                        S.op("dve", TT(ssvB[:, 2:3], ssvB[:, 0:1], ssvB[:, 1:2], ALU.add), reads=[r_ssvB], writes=[r_ssvB])
                        S.op("act", ACT(ssvB[:, 2:3], ssvB[:, 2:3], AF.Ln, scale=1.0 / D, bias=eps2[:, 0:1]), reads=[r_ssvB, r_eps2], writes=[r_ssvB])
                        S.op("act", ACT(ssvB[:, 2:3], ssvB[:, 2:3], AF.Exp, scale=-0.5), reads=[r_ssvB], writes=[r_ssvB])
                        for hf in range(2):
                            bk = B[6 + hf]; rbk = rB[6 + hf]
                            S.op("dve", STT(tmpB[:], bk[:, :], ssvB[:, 2:3], gpffn_b[:, 512 * hf:512 * (hf + 1)], ALU.mult, ALU.mult),
                                 reads=[rbk, r_ssvB, r_gpffn], writes=[r_tmpB])
                            S.op("dve", TT(hr[:, 512 * hf:512 * (hf + 1)], tmpB[:], hr[:, 512 * hf:512 * (hf + 1)], ALU.add),
                                 reads=[r_tmpB, r_hr], writes=[r_hr])
                        final_events.append(S.dma("sp", out_d[row0:row0 + 128, :], hr[:], "fo%d" % ((2 * tt + s) % 3), reads=[r_hr]))
                        if s == 0:
                            P.release("BAR1")

                P0 = Pump(h1_gen(0)); P0.release("BAR1"); P0.flush()
                for tt in range(NT2):
                    P = Pump(h1_gen(tt + 1) if tt + 1 < NT2 else None)
                    h2(tt, P)
                    P.flush()

                S.wait_all("sp", final_events)

        with nc.Block() as block:
            S.run(block)
    return nc


def _consts():
    invf8 = (500000.0 ** (-np.arange(0, 16, 2, dtype=np.float32) / np.float32(16))).astype(np.float32)
    invf = np.zeros((128, 1), np.float32)
    for base in (0, 64):
        for i in range(16):
            invf[base + i, 0] = invf8[i % 8]
    R = np.zeros((128, 128), np.float32)
    for base in (0, 64):
        for i in range(8):
            R[base + i, base + i + 8] = -1.0
            R[base + i + 8, base + i] = 1.0
    rmatT = np.ascontiguousarray(R.T)
    ident = np.eye(128, dtype=np.float32)
    swapm = np.zeros((128, 128), np.float32)
    for k in range(128):
        swapm[k, (k + 64) % 128] = 1.0
    sgn = np.ones((128, 1), np.float32); sgn[64:] = -1.0
    tri = (np.arange(128)[:, None] <= np.arange(128)[None, :]).astype(np.float32)
    gmask = (np.arange(128)[:, None] // 16 == np.arange(8)[None, :]).astype(np.float32)
    return dict(invf=invf, rmatT=rmatT, ident=ident, swapm=swapm, sgn=sgn, tri=tri, gmask=gmask,
                iota512=np.arange(512, dtype=np.float32), tile_iota=(512.0 * np.arange(NT1)).astype(np.float32))


def make_in_maps(inp):
    c = _consts()
    f = lambda a: np.ascontiguousarray(np.asarray(a, dtype=np.float32))
    w_in = f(inp["w_in"])[0]
    maps = []
    w_out = f(inp["w_out"])[0]
    perm = []
    for r in range(4):
        perm += list(range(128 * r, 128 * r + 128)) + list(range(512 + 128 * r, 512 + 128 * r + 128))
    w_out_p = np.ascontiguousarray(w_out[perm, :])
    for core in range(8):
        b, h = core // 4, core % 4
        cols = (list(range(64 * h, 64 * h + 64)) + list(range(256 + 64 * h, 256 + 64 * h + 64)) +
                list(range(512 + 64 * h, 512 + 64 * h + 64)) + list(range(768 + 64 * h, 768 + 64 * h + 64)) +
                list(range(1024 + 128 * h, 1024 + 128 * h + 128)) + list(range(1536 + 128 * h, 1536 + 128 * h + 128)))
        gs = slice(8 * h, 8 * h + 8)
        m = dict(c)
        m["x"] = f(inp["x"][b])
        m["meta"] = f(inp["meta"])
        m["win"] = np.ascontiguousarray(w_in[:, cols])
        m["gpre"] = np.ascontiguousarray(f(inp["pre_mix_g"])[0].reshape(8, 128).T)
        m["lamv"] = np.stack([f(inp["lambda_q1"])[0], f(inp["lambda_k1"])[0], f(inp["lambda_q2"])[0], f(inp["lambda_k2"])[0]])
        m["subg"] = f(inp["subln_g"])[0].reshape(128, 1)
        m["areT"] = np.ascontiguousarray(f(inp["a_re"])[0, gs].T)
        m["aimT"] = np.ascontiguousarray(f(inp["a_im"])[0, gs].T)
        m["logdt"] = f(inp["log_dt"])[0, gs]
        m["bre"] = f(inp["b_re"])[0, gs]
        m["bim"] = f(inp["b_im"])[0, gs]
        m["creT"] = np.ascontiguousarray(f(inp["c_re"])[0, gs].transpose(2, 0, 1).reshape(64, 128))
        m["cimT"] = np.ascontiguousarray(f(inp["c_im"])[0, gs].transpose(2, 0, 1).reshape(64, 128))
        m["dskip"] = f(inp["d_skip"])[0, 128 * h:128 * h + 128].reshape(128, 1)
        m["wglu"] = f(inp["w_glu"])[0]
        m["bglu"] = np.ascontiguousarray(f(inp["b_glu"])[0].reshape(4, 128).T)
        m["gssm"] = np.ascontiguousarray(f(inp["ssm_out_g"])[0].reshape(4, 128).T)
        m["wout"] = w_out_p
        m["gpost"] = f(inp["post_mix_g"])[0]
        m["gffn8"] = np.ascontiguousarray(f(inp["pre_ffn_g"])[0].reshape(8, 128).T)
        m["wgate"] = f(inp["w_gate"])[0]
        m["wup"] = f(inp["w_up"])[0]
        m["wdown"] = f(inp["w_down"])[0]
        m["gpffn"] = f(inp["post_ffn_g"])[0]
        m["xres"] = f(inp["x"][b, TOK2 * h:TOK2 * (h + 1)])
        maps.append(m)
    return maps


def kernel(**inputs):
    nc = build_program()
    maps = make_in_maps(inputs)
    res = run_bass_kernel_spmd(nc, maps, core_ids=list(range(8)))
    out = np.zeros((2, SEQ, D), np.float32)
    for core in range(8):
        b, h = core // 4, core % 4
        out[b, TOK2 * h:TOK2 * (h + 1)] = res.results[core]["out"]
    return out
```

```python
import math
import os
from contextlib import ExitStack

import numpy as np
import concourse.bass as bass
import concourse.mybir as mybir
from concourse.bass_utils import run_bass_kernel_spmd

F32 = mybir.dt.float32
BF16 = mybir.dt.bfloat16
I32 = mybir.dt.int32
AF = mybir.ActivationFunctionType
ALU = mybir.AluOpType

D = 1024
SEQ = 8192
NMETA = 16
L = SEQ + NMETA
LP = 8320
NBLK = 65
NT1 = 17
DFF = 2816
NF = 22
EPS = 1e-6
LAM_INIT = 0.8 - 0.6 * math.exp(0.0)
TWO_PI = 2.0 * math.pi
C1 = 6.28125
C2 = TWO_PI - C1
PI_LO = 3.141592
TOK2 = 2048
T2 = 256
NT2 = TOK2 // T2
FRONT_STEPS = 40


class Res:
    __slots__ = ("name", "w", "r", "excl")

    def __init__(self, name, excl=False):
        self.name = name
        self.w = None
        self.r = []
        self.excl = excl


class Sched:
    ENGS = ("pe", "act", "dve", "pool", "sp")

    def __init__(self, nc, stack):
        self.nc = nc
        self.ops = {e: [] for e in self.ENGS}
        self.sem = {e: stack.enter_context(nc.semaphore("s_" + e)) for e in self.ENGS}
        self.cnt = {e: 0 for e in self.ENGS}
        self.waited = {e: {} for e in self.ENGS}
        self.stack = stack
        self.dma_sems = {}
        self.dma_cnt = {}
        self.last = {e: None for e in self.ENGS}
        self.limit = int(os.environ.get("K_LIMIT", "1000000000"))
        self.nops = 0
        self.marks = []

    def _waits(self, eng, deps):
        need = {}
        for ev in deps:
            if ev is None:
                continue
            sem, val, src = ev
            key = sem.name
            if self.waited[eng].get(key, 0) >= val:
                continue
            if key not in need or need[key][1] < val:
                need[key] = (sem, val)
        for key, (sem, val) in need.items():
            self.waited[eng][key] = val
            self.ops[eng].append(("wait", sem, val))

    @staticmethod
    def _deps_for(reads, writes, extra):
        deps = list(extra or [])
        for r in reads or []:
            if r.w is not None:
                deps.append(r.w)
            if r.excl:
                deps.extend(r.r)
        for w in writes or []:
            if w.w is not None:
                deps.append(w.w)
            deps.extend(w.r)
        return deps

    @staticmethod
    def _commit(ev, reads, writes):
        for r in reads or []:
            r.r.append(ev)
        for w in writes or []:
            w.w = ev
            w.r = []

    def op(self, eng, fn, reads=None, writes=None, extra=None):
        return self.group(eng, [fn], reads, writes, extra)

    def group(self, eng, fns, reads=None, writes=None, extra=None):
        self.nops += 1
        if self.nops > self.limit:
            return None
        self._waits(eng, self._deps_for(reads, writes, extra))
        self.cnt[eng] += 1
        ev = (self.sem[eng], self.cnt[eng], eng)
        for f in fns[:-1]:
            self.ops[eng].append(("op", f, False))
        self.ops[eng].append(("op", fns[-1], True))
        self._commit(ev, reads, writes)
        self.last[eng] = ev
        return ev

    def dma(self, eng, out, in_, slot, reads=None, writes=None, extra=None):
        if slot not in self.dma_sems:
            self.dma_sems[slot] = self.stack.enter_context(self.nc.semaphore("d_" + slot))
            self.dma_cnt[slot] = 0
        self.nops += 1
        if self.nops > self.limit:
            return None
        self._waits(eng, self._deps_for(reads, writes, extra))
        self.dma_cnt[slot] += 16
        sem = self.dma_sems[slot]
        ev = (sem, self.dma_cnt[slot], "dma")
        self.ops[eng].append(("dma", out, in_, sem))
        self._commit(ev, reads, writes)
        return ev

    def raw(self, eng, fn):
        self.ops[eng].append(("raw", fn))

    def wait_all(self, eng, events):
        self._waits(eng, events)

    def run(self, block):
        sched = self

        def replay(engname, e):
            for item in sched.ops[engname]:
                if item[0] == "wait":
                    e.wait_ge(item[1], item[2])
                elif item[0] == "op":
                    ins = item[1](e)
                    if item[2]:
                        ins.then_inc(sched.sem[engname], 1)
                elif item[0] == "dma":
                    e.dma_start(out=item[1], in_=item[2]).then_inc(item[3], 16)
                elif item[0] == "raw":
                    item[1](e)

        @block.tensor
        def _(e):
            replay("pe", e)

        @block.scalar
        def _(e):
            replay("act", e)

        @block.vector
        def _(e):
            replay("dve", e)

        @block.gpsimd
        def _(e):
            replay("pool", e)

        @block.sync
        def _(e):
            replay("sp", e)


def MM(out, lhsT, rhs, start, stop):
    return lambda e: e.matmul(out, lhsT, rhs, start=start, stop=stop)


def TR(out, in_, ident):
    return lambda e: e.transpose(out, in_, ident)


def ACT(out, in_, func, **kw):
    return lambda e: e.activation(out=out, in_=in_, func=func, **kw)


def TT(out, a, b, op):
    return lambda e: e.tensor_tensor(out=out, in0=a, in1=b, op=op)


def TS(out, a, s1, op0, s2=None, op1=None):
    if op1 is None:
        return lambda e: e.tensor_scalar(out=out, in0=a, scalar1=s1, scalar2=None, op0=op0)
    return lambda e: e.tensor_scalar(out=out, in0=a, scalar1=s1, scalar2=s2, op0=op0, op1=op1)


def STT(out, a, s, b, op0, op1):
    return lambda e: e.scalar_tensor_tensor(out=out, in0=a, scalar=s, in1=b, op0=op0, op1=op1)


def CP(out, in_):
    return lambda e: e.tensor_copy(out=out, in_=in_)


def MEMSET(ap, v):
    return lambda e: e.memset(ap, v)


def RECIP(out, in_):
    return lambda e: e.reciprocal(out=out, in_=in_)


def SCAN(out, d0, d1, init):
    return lambda e: e.tensor_tensor_scan(out=out, data0=d0, data1=d1, initial=init, op0=ALU.mult, op1=ALU.add)


def build_program(nt1=NT1, do_phase2=True, debug=False, stop=9, p2mode="full"):
    nc = bass.Bass("TRN2", target_bir_lowering=False)

    def din(name, shape, dt=F32):
        return nc.dram_tensor(name, list(shape), dt, kind="ExternalInput").ap()

    ident_d = din("ident", [128, 128])
    if nt1 > 0:
        x_d = din("x", [SEQ, D])
        meta_d = din("meta", [NMETA, D])
        win_d = din("win", [D, 512])
        gpre_d = din("gpre", [128, 8])
        invf_d = din("invf", [128, 1])
        iota512_d = din("iota512", [512])
        tile_iota_d = din("tile_iota", [NT1])
        rmatT_d = din("rmatT", [128, 128])
        swap_d = din("swapm", [128, 128])
        sgn_d = din("sgn", [128, 1])
        tri_d = din("tri", [128, 128])
        gmask_d = din("gmask", [128, 8])
        lam_d = din("lamv", [4, 64])
        subg_d = din("subg", [128, 1])
        areT_d = din("areT", [64, 8])
        aimT_d = din("aimT", [64, 8])
        logdt_d = din("logdt", [8])
        bre_d = din("bre", [8, 64, 16])
        bim_d = din("bim", [8, 64, 16])
        creT_d = din("creT", [64, 128])
        cimT_d = din("cimT", [64, 128])
        dskip_d = din("dskip", [128, 1])
    if do_phase2:
        wglu_d = din("wglu", [512, 512])
        bglu_d = din("bglu", [128, 4])
        gssm_d = din("gssm", [128, 4])
        wout_d = din("wout", [D, D])
        gpost_d = din("gpost", [D])
        gffn_d = din("gffn8", [128, 8])
        wgate_d = din("wgate", [D, DFF])
        wup_d = din("wup", [D, DFF])
        wdown_d = din("wdown", [DFF, D])
        gpffn_d = din("gpffn", [D])
        xres_d = din("xres", [TOK2, D])
    out_d = nc.dram_tensor("out", [TOK2, D], F32, kind="ExternalOutput").ap() if do_phase2 else None

    xout_dbg_d = din("xout_dbg", [1024, TOK2]) if p2mode == "nogather" else None
    wprep = []
    if do_phase2:
        wg_b = nc.dram_tensor("wg_b", [D, DFF], BF16).ap()
        wu_b = nc.dram_tensor("wu_b", [D, DFF], BF16).ap()
        wd_b = nc.dram_tensor("wd_b", [DFF, D], BF16).ap()
        wo_b = nc.dram_tensor("wo_b", [D, D], BF16).ap()
        wgl_b = nc.dram_tensor("wgl_b", [512, 512], BF16).ap()
        wprep.append((wgl_b, wglu_d)); wprep.append((wo_b, wout_d))
        for hh in range(2):
            wprep.append((wg_b[512 * hh:512 * (hh + 1), :], wgate_d[512 * hh:512 * (hh + 1), :]))
            wprep.append((wu_b[512 * hh:512 * (hh + 1), :], wup_d[512 * hh:512 * (hh + 1), :]))
        for hh in range(2):
            wprep.append((wd_b[1408 * hh:1408 * (hh + 1), :], wdown_d[1408 * hh:1408 * (hh + 1), :]))
    wprep_events = []
    xin_t = [nc.dram_tensor("xchg_in%d" % q, [256, TOK2], BF16) for q in range(4)]
    xout_t = nc.dram_tensor("xchg_out", [4 * 1024, TOK2], BF16)
    xin = [t.ap() for t in xin_t]
    xout = xout_t.ap()
    dbg_d = None
    if debug:
        dbg_d = nc.dram_tensor("dbg", [256, SEQ], F32, kind="ExternalOutput").ap()

    with ExitStack() as top:
        S = Sched(nc, top)

        def ps_bank(name):
            return top.enter_context(nc.psum_tensor(name, [128, 512], F32))

        use_cc = do_phase2 and p2mode == "full"
        cc_sem = top.enter_context(nc.semaphore("cc_sem")) if use_cc else None
        q_events = [[] for _ in range(4)]

        def xchg_write(row0, lo, hi, src, src_col0, slot, res):
            evs = []
            c = lo
            while c < hi:
                q = c // TOK2
                ce = min(hi, TOK2 * (q + 1))
                ev = S.dma("sp", xin[q][row0:row0 + 128, c - TOK2 * q:ce - TOK2 * q],
                           src[:, src_col0 + (c - lo):src_col0 + (ce - lo)], slot, reads=[res])
                q_events[q].append(ev)
                evs.append(ev)
                c = ce
            return evs

        banks = [ps_bank("bank%d" % i) for i in range(8)]
        rbank = [Res("bank%d" % i, excl=True) for i in range(8)]

        def _phase1():
            with ExitStack() as p1:
                def sb(name, shape, dt=F32):
                    return p1.enter_context(nc.sbuf_tensor("a_" + name, list(shape), dt))

                ident_f = sb("ident_f", [128, 128]); r_ident_f = Res("ident_f")
                ident_b = sb("ident_b", [128, 128], BF16); r_ident_b = Res("ident_b")
                rmatT_b = sb("rmatT_b", [128, 128], BF16); r_rmat = Res("rmat")
                swap_f = sb("swap_f", [128, 128]); r_swap = Res("swap")
                tri_b = sb("tri_b", [128, 128], BF16); r_tri = Res("tri")
                ones_b = sb("ones_b", [128, 128], BF16); r_ones = Res("ones")
                sgn = sb("sgn", [128, 1]); r_sgn = Res("sgn")
                gmask = sb("gmask", [128, 8]); r_gmask = Res("gmask")
                invf = sb("invf", [128, 1]); r_invf = Res("invf")
                iota512 = sb("iota512", [128, 512]); r_iota = Res("iota512")
                tile_iota = sb("tile_iota", [128, NT1]); r_tiota = Res("tile_iota")
                gpre = sb("gpre", [128, 8]); r_gpre = Res("gpre")
                subg = sb("subg", [128, 1]); r_subg = Res("subg")
                dskip = sb("dskip", [128, 1]); r_dskip = Res("dskip")
                epsc = sb("epsc", [128, 1]); r_eps = Res("eps")
                lamv = sb("lamv", [128, 4, 64]); r_lamv = Res("lamv")
                neglam = sb("neglam", [128, 1]); r_neglam = Res("neglam")

                S.dma("sp", ident_f[:], ident_d, "c0", writes=[r_ident_f])
                S.dma("pool", ident_b[:], ident_d, "c1", writes=[r_ident_b])
                S.dma("pool", rmatT_b[:], rmatT_d, "c2", writes=[r_rmat])
                S.dma("sp", swap_f[:], swap_d, "c3", writes=[r_swap])
                S.dma("pool", tri_b[:], tri_d, "c4", writes=[r_tri])
                S.dma("sp", sgn[:], sgn_d, "c5", writes=[r_sgn])
                S.dma("sp", gmask[:], gmask_d, "c6", writes=[r_gmask])
                S.dma("sp", invf[:], invf_d, "c7", writes=[r_invf])
                S.dma("sp", iota512[:], iota512_d.partition_broadcast(128), "c8", writes=[r_iota])
                S.dma("sp", tile_iota[:], tile_iota_d.partition_broadcast(128), "c9", writes=[r_tiota])
                S.dma("sp", gpre[:], gpre_d, "c10", writes=[r_gpre])
                S.dma("sp", subg[:], subg_d, "c11", writes=[r_subg])
                S.dma("sp", dskip[:], dskip_d, "c12", writes=[r_dskip])
                S.dma("sp", lamv[:].rearrange("p a b -> p (a b)"),
                      lam_d.rearrange("a b -> (a b)").partition_broadcast(128), "c13", writes=[r_lamv])
                S.op("dve", MEMSET(ones_b[:], 1.0), writes=[r_ones])
                S.op("dve", MEMSET(epsc[:], EPS), writes=[r_eps])

                scr = [sb("scr%d" % i, [128, 512]) for i in range(4)]
                r_scr = [Res("scr%d" % i) for i in range(4)]
                ki_t = sb("ki_t", [128, 512], I32); r_ki = Res("ki")

                def range_reduce(eng, out_ap, arg_ap, n, r_out, r_arg):
                    kf = scr[3][:, 0:n]
                    S.op(eng, TS(ki_t[:, 0:n], arg_ap, 1.0 / TWO_PI, ALU.mult), reads=[r_arg], writes=[r_ki])
                    S.op(eng, CP(kf, ki_t[:, 0:n]), reads=[r_ki], writes=[r_scr[3]])
                    S.op(eng, STT(out_ap, kf, -C1, arg_ap, ALU.mult, ALU.add), reads=[r_scr[3], r_arg], writes=[r_out])
                    S.op(eng, STT(out_ap, kf, -C2, out_ap, ALU.mult, ALU.add), reads=[r_scr[3], r_out], writes=[r_out])
                    S.op(eng, TS(out_ap, out_ap, PI_LO, ALU.min, -PI_LO, ALU.max), reads=[r_out], writes=[r_out])

                def sincos(arg_ap, n, r_arg, sin_out, cos_out, r_sin, r_cos):
                    red = scr[2][:, 0:n]
                    range_reduce("dve", red, arg_ap, n, r_scr[2], r_arg)
                    S.op("act", ACT(sin_out, red, AF.Sin), reads=[r_scr[2]], writes=[r_sin])
                    hs = scr[1][:, 0:n]
                    S.op("act", ACT(hs, red, AF.Sin, scale=0.5), reads=[r_scr[2]], writes=[r_scr[1]])
                    S.op("dve", TT(hs, hs, hs, ALU.mult), reads=[r_scr[1]], writes=[r_scr[1]])
                    S.op("dve", TS(cos_out, hs, -2.0, ALU.mult, 1.0, ALU.add), reads=[r_scr[1]], writes=[r_cos])

                lt = sb("lt", [128, 2, 64]); r_lt = Res("lt")
                ls = sb("ls", [128, 2]); r_ls = Res("ls")
                S.op("dve", TT(lt[:, 0, :], lamv[:, 0, :], lamv[:, 1, :], ALU.mult), reads=[r_lamv], writes=[r_lt])
                S.op("dve", TT(lt[:, 1, :], lamv[:, 2, :], lamv[:, 3, :], ALU.mult), reads=[r_lamv, r_lt], writes=[r_lt])
                S.op("dve", lambda e: e.reduce_sum(out=ls[:], in_=lt[:], axis=mybir.AxisListType.X), reads=[r_lt], writes=[r_ls])
                S.op("act", ACT(ls[:], ls[:], AF.Exp), reads=[r_ls], writes=[r_ls])
                S.op("dve", TT(neglam[:], ls[:, 1:2], ls[:, 0:1], ALU.subtract), reads=[r_ls], writes=[r_neglam])
                S.op("dve", TS(neglam[:], neglam[:], -LAM_INIT, ALU.add), reads=[r_neglam], writes=[r_neglam])
                S.op("dve", TS(subg[:], subg[:], 1.0 - LAM_INIT, ALU.mult), reads=[r_subg], writes=[r_subg])

                Wb = sb("Wb", [128, 8, 512], BF16); r_Wb = Res("Wb")
                wstg = [scr[0], scr[1]]
                r_wstg = [r_scr[0], r_scr[1]]
                for k in range(8):
                    S.dma("sp", wstg[k % 2][:], win_d[128 * k:128 * (k + 1), :], "wstg%d" % (k % 2), writes=[r_wstg[k % 2]])
                    S.op("pool", TS(Wb[:, k, :], wstg[k % 2][:], gpre[:, k:k + 1], ALU.mult),
                         reads=[r_wstg[k % 2], r_gpre], writes=[r_Wb])

                C0 = sb("C0", [128, 512]); S0 = sb("S0", [128, 512]); r_C0 = Res("C0"); r_S0 = Res("S0")
                cI = sb("cI", [128, NT1]); sI = sb("sI", [128, NT1]); r_cI = Res("cI"); r_sI = Res("sI")
                S.op("dve", TS(scr[0][:], iota512[:], invf[:, 0:1], ALU.mult), reads=[r_iota, r_invf], writes=[r_scr[0]])
                sincos(scr[0][:], 512, r_scr[0], S0[:], C0[:], r_S0, r_C0)
                S.op("dve", TS(scr[0][:, 0:NT1], tile_iota[:], invf[:, 0:1], ALU.mult), reads=[r_tiota, r_invf], writes=[r_scr[0]])
                sincos(scr[0][:, 0:NT1], NT1, r_scr[0], sI[:], cI[:], r_sI, r_cI)
                nsI = sb("nsI", [128, NT1]); r_nsI = Res("nsI")
                S.op("dve", TS(nsI[:], sI[:], -1.0, ALU.mult), reads=[r_sI], writes=[r_nsI])

                are2 = sb("are2", [128, 8]); aim2 = sb("aim2", [128, 8]); dt2 = sb("dt2", [128, 8])
                r_are = Res("are2"); r_aim = Res("aim2"); r_dt = Res("dt2")
                S.dma("sp", are2[0:64, :], areT_d, "s0", writes=[r_are])
                S.dma("sp", are2[64:128, :], areT_d, "s0", writes=[r_are])
                S.dma("sp", aim2[0:64, :], aimT_d, "s1", writes=[r_aim])
                S.dma("sp", aim2[64:128, :], aimT_d, "s1", writes=[r_aim])
                S.dma("sp", dt2[:], logdt_d.partition_broadcast(128), "s2", writes=[r_dt])
                S.op("act", ACT(dt2[:], dt2[:], AF.Exp), reads=[r_dt], writes=[r_dt])
                rdec = sb("rdec", [128, 8]); r_rdec = Res("rdec")
                theta = sb("theta", [128, 8]); r_theta = Res("theta")
                S.op("dve", TT(rdec[:], are2[:], dt2[:], ALU.mult), reads=[r_are, r_dt], writes=[r_rdec])
                S.op("act", ACT(rdec[:], rdec[:], AF.Exp), reads=[r_rdec], writes=[r_rdec])
                S.op("dve", TT(theta[:], aim2[:], dt2[:], ALU.mult), reads=[r_aim, r_dt], writes=[r_theta])
                c1t = sb("c1t", [128, 8]); s1t = sb("s1t", [128, 8]); r_c1t = Res("c1t"); r_s1t = Res("s1t")
                cTt = sb("cTt", [128, 8]); sTt = sb("sTt", [128, 8]); r_cTt = Res("cTt"); r_sTt = Res("sTt")
                sincos(theta[:], 8, r_theta, s1t[:], c1t[:], r_s1t, r_c1t)
                S.op("dve", TS(scr[0][:, 0:8], theta[:], 512.0, ALU.mult), reads=[r_theta], writes=[r_scr[0]])
                sincos(scr[0][:, 0:8], 8, r_scr[0], sTt[:], cTt[:], r_sTt, r_cTt)
                S.op("dve", TT(sTt[:], sTt[:], sgn[:, 0:1].to_broadcast([128, 8]), ALU.mult), reads=[r_sTt, r_sgn], writes=[r_sTt])
                COSg = sb("COSg", [128, 8, 512]); SINg = sb("SINg", [128, 8, 512])
                r_COSg = [Res("COSg%d" % g) for g in range(8)]; r_SINg = [Res("SINg%d" % g) for g in range(8)]
                for g in range(8):
                    S.op("dve", TS(scr[0][:], iota512[:], theta[:, g:g + 1], ALU.mult), reads=[r_iota, r_theta], writes=[r_scr[0]])
                    sincos(scr[0][:], 512, r_scr[0], SINg[:, g, :], COSg[:, g, :], r_SINg[g], r_COSg[g])
                ROT = sb("ROT", [128, 8, 128]); r_ROT = Res("ROT")
                for g in range(8):
                    S.op("dve", TS(ROT[:, g, :], ident_f[:], cTt[:, g:g + 1], ALU.mult), reads=[r_ident_f, r_cTt], writes=[r_ROT])
                    S.op("dve", STT(ROT[:, g, :], swap_f[:], sTt[:, g:g + 1], ROT[:, g, :], ALU.mult, ALU.add),
                         reads=[r_swap, r_sTt, r_ROT], writes=[r_ROT])
                fre = sb("fre", [128, 8]); fim = sb("fim", [128, 8]); r_fre = Res("fre"); r_fim = Res("fim")
                nr = sb("nr", [128, 8]); ni = sb("ni", [128, 8]); den = sb("den", [128, 8]); tmp8 = sb("tmp8", [128, 8])
                r_nr = Res("nr"); r_ni = Res("ni"); r_den = Res("den"); r_tmp8 = Res("tmp8")
                S.op("dve", TT(nr[:], rdec[:], c1t[:], ALU.mult), reads=[r_rdec, r_c1t], writes=[r_nr])
                S.op("dve", TS(nr[:], nr[:], -1.0, ALU.add), reads=[r_nr], writes=[r_nr])
                S.op("dve", TT(ni[:], rdec[:], s1t[:], ALU.mult), reads=[r_rdec, r_s1t], writes=[r_ni])
                S.op("dve", TT(den[:], are2[:], are2[:], ALU.mult), reads=[r_are], writes=[r_den])
                S.op("dve", TT(tmp8[:], aim2[:], aim2[:], ALU.mult), reads=[r_aim], writes=[r_tmp8])
                S.op("dve", TT(den[:], den[:], tmp8[:], ALU.add), reads=[r_den, r_tmp8], writes=[r_den])
                S.op("dve", RECIP(den[:], den[:]), reads=[r_den], writes=[r_den])
                S.op("dve", TT(fre[:], nr[:], are2[:], ALU.mult), reads=[r_nr, r_are], writes=[r_fre])
                S.op("dve", TT(tmp8[:], ni[:], aim2[:], ALU.mult), reads=[r_ni, r_aim], writes=[r_tmp8])
                S.op("dve", TT(fre[:], fre[:], tmp8[:], ALU.add), reads=[r_fre, r_tmp8], writes=[r_fre])
                S.op("dve", TT(fre[:], fre[:], den[:], ALU.mult), reads=[r_fre, r_den], writes=[r_fre])
                S.op("dve", TT(fim[:], ni[:], are2[:], ALU.mult), reads=[r_ni, r_are], writes=[r_fim])
                S.op("dve", TT(tmp8[:], nr[:], aim2[:], ALU.mult), reads=[r_nr, r_aim], writes=[r_tmp8])
                S.op("dve", TT(fim[:], fim[:], tmp8[:], ALU.subtract), reads=[r_fim, r_tmp8], writes=[r_fim])
                S.op("dve", TT(fim[:], fim[:], den[:], ALU.mult), reads=[r_fim, r_den], writes=[r_fim])
                Bre = sb("Bre", [64, 8, 16]); Bim = sb("Bim", [64, 8, 16]); r_Bre = Res("Bre"); r_Bim = Res("Bim")
                S.dma("sp", Bre[:], bre_d.rearrange("g p h -> p g h"), "s3", writes=[r_Bre])
                S.dma("sp", Bim[:], bim_d.rearrange("g p h -> p g h"), "s4", writes=[r_Bim])
                BBre = sb("BBre", [64, 8, 16]); BBim = sb("BBim", [64, 8, 16]); tB = sb("tB", [64, 8, 16])
                r_BBre = Res("BBre"); r_BBim = Res("BBim"); r_tB = Res("tB")
                fre_b = fre[0:64, :].unsqueeze(2).to_broadcast([64, 8, 16])
                fim_b = fim[0:64, :].unsqueeze(2).to_broadcast([64, 8, 16])
                S.op("dve", TT(BBre[:], Bre[:], fre_b, ALU.mult), reads=[r_Bre, r_fre], writes=[r_BBre])
                S.op("dve", TT(tB[:], Bim[:], fim_b, ALU.mult), reads=[r_Bim, r_fim], writes=[r_tB])
                S.op("dve", TT(BBre[:], BBre[:], tB[:], ALU.subtract), reads=[r_BBre, r_tB], writes=[r_BBre])
                S.op("dve", TT(BBim[:], Bim[:], fre_b, ALU.mult), reads=[r_Bim, r_fre], writes=[r_BBim])
                S.op("dve", TT(tB[:], Bre[:], fim_b, ALU.mult), reads=[r_Bre, r_fim], writes=[r_tB])
                S.op("dve", TT(BBim[:], BBim[:], tB[:], ALU.add), reads=[r_BBim, r_tB], writes=[r_BBim])
                FB1 = sb("FB1", [128, 128]); FB2 = sb("FB2", [128, 128]); r_FB1 = Res("FB1"); r_FB2 = Res("FB2")
                S.group("pe", [TR(banks[0][:, 0:64], BBre[:].rearrange("p g h -> p (g h)"), ident_f[0:64, 0:64]),
                               TR(banks[0][:, 64:128], BBim[:].rearrange("p g h -> p (g h)"), ident_f[0:64, 0:64])],
                        reads=[r_BBre, r_BBim, r_ident_f], writes=[rbank[0]])
                S.op("dve", CP(FB1[:], banks[0][:, 0:128]), reads=[rbank[0]], writes=[r_FB1])
                S.op("dve", CP(FB2[:, 0:64], banks[0][:, 64:128]), reads=[rbank[0]], writes=[r_FB2])
                S.op("dve", TS(FB2[:, 64:128], banks[0][:, 0:64], -1.0, ALU.mult), reads=[rbank[0], r_FB2], writes=[r_FB2])
                LB1 = sb("LB1", [128, 8, 128], BF16); LB2 = sb("LB2", [128, 8, 128], BF16); r_LB = Res("LB")
                for g in range(8):
                    S.op("pool", TS(LB1[:, g, :], FB1[:], gmask[:, g:g + 1], ALU.mult), reads=[r_FB1, r_gmask], writes=[r_LB])
                    S.op("pool", TS(LB2[:, g, :], FB2[:], gmask[:, g:g + 1], ALU.mult), reads=[r_FB2, r_gmask], writes=[r_LB])
                CC1 = sb("CC1", [128, 128]); CC2 = sb("CC2", [128, 128]); r_CC1 = Res("CC1"); r_CC2 = Res("CC2")
                S.dma("sp", CC1[0:64, :], creT_d, "s5", writes=[r_CC1])
                S.dma("sp", CC1[64:128, :], cimT_d, "s5", writes=[r_CC1])
                S.dma("sp", CC2[0:64, :], cimT_d, "s6", writes=[r_CC2])
                S.dma("sp", CC2[64:128, :], creT_d, "s6", writes=[r_CC2])
                S.op("dve", TS(CC1[64:128, :], CC1[64:128, :], -1.0, ALU.mult), reads=[r_CC1], writes=[r_CC1])
                S.op("dve", TS(CC2[:], CC2[:], -1.0, ALU.mult), reads=[r_CC2], writes=[r_CC2])
                LC1 = sb("LC1", [128, 8, 128], BF16); LC2 = sb("LC2", [128, 8, 128], BF16); r_LC = Res("LC")
                S.op("pool", MEMSET(LC1[:], 0.0), writes=[r_LC])
                S.op("pool", MEMSET(LC2[:], 0.0), writes=[r_LC])
                for g in range(8):
                    S.op("dve", CP(LC1[:, g, 16 * g:16 * g + 16], CC1[:, 16 * g:16 * g + 16]), reads=[r_CC1], writes=[r_LC])
                    S.op("dve", CP(LC2[:, g, 16 * g:16 * g + 16], CC2[:, 16 * g:16 * g + 16]), reads=[r_CC2], writes=[r_LC])
                diagD = sb("diagD", [128, 128], BF16); r_diagD = Res("diagD")
                S.op("dve", TS(diagD[:], ident_f[:], dskip[:, 0:1], ALU.mult), reads=[r_ident_f, r_dskip], writes=[r_diagD])
                s0 = sb("s0", [128, 2, 8]); r_s0 = [Res("s0_0"), Res("s0_1")]
                S.op("dve", MEMSET(s0[:], 0.0), writes=r_s0)

                xt = [sb("xt%d" % i, [128, D]) for i in range(2)]; r_xt = [Res("xt%d" % i) for i in range(2)]
                junk = sb("junk", [128, D], BF16); r_junk = Res("junk")
                ssq = sb("ssq", [128, 4]); r_ssq = [Res("ssq%d" % i) for i in range(4)]
                xs = [sb("xs%d" % i, [128, D], BF16) for i in range(2)]; r_xs = [Res("xs%d" % i) for i in range(2)]
                hT = [sb("hT%d" % i, [128, 8, 512], BF16) for i in range(2)]; r_hT = [Res("hT%d" % i) for i in range(2)]
                COSt = sb("COSt", [128, 512]); SINt = sb("SINt", [128, 512]); r_COSt = Res("COSt"); r_SINt = Res("SINt")
                qraw = sb("qraw", [128, 512], BF16); r_qraw = Res("qraw")
                rt1 = sb("rt1", [128, 512]); rt2 = sb("rt2", [128, 512]); r_rt1 = Res("rt1"); r_rt2 = Res("rt2")
                QT = [sb("QT%d" % i, [128, 512], BF16) for i in range(2)]; r_QT = [Res("QT%d" % i) for i in range(2)]
                KTa = sb("KTa", [128, LP], BF16); KTb = sb("KTb", [128, LP], BF16)
                r_KT = [Res("KT%d" % i) for i in range(NT1)]
                r_KTz = Res("KTz")
                S.op("pool", MEMSET(KTa[64:128, :], 0.0), writes=[r_KTz])
                S.op("pool", MEMSET(KTb[0:64, :], 0.0), writes=[r_KTz])
                V = sb("V", [128, NBLK, 128], BF16); r_V = [Res("V%d" % i) for i in range(NT1)]
                uT = [sb("uT%d" % i, [128, 512], BF16) for i in range(2)]; r_uT = [Res("uT%d" % i) for i in range(2)]
                st1 = [sb("st1_%d" % i, [128, 512]) for i in range(2)]; r_st1 = [Res("st1_%d" % i) for i in range(2)]
                st2 = [sb("st2_%d" % i, [128, 512]) for i in range(2)]; r_st2 = [Res("st2_%d" % i) for i in range(2)]
                bt = [sb("bt%d" % i, [128, 512]) for i in range(2)]; r_bt = [Res("bt%d" % i) for i in range(2)]
                Xs = [sb("Xs%d" % i, [128, 512]) for i in range(3)]; r_Xs = [Res("Xs%d" % i) for i in range(3)]
                P1s = [sb("P1s%d" % i, [128, 512], BF16) for i in range(3)]; r_P1s = [Res("P1s%d" % i) for i in range(3)]
                P2s = [sb("P2s%d" % i, [128, 512], BF16) for i in range(3)]; r_P2s = [Res("P2s%d" % i) for i in range(3)]
                ybuf = [sb("ybuf%d" % i, [128, 512], BF16) for i in range(2)]; r_ybuf = [Res("ybuf%d" % i) for i in range(2)]
                Pb = [sb("Pb%d" % i, [128, 2, 256], BF16) for i in range(3)]; r_Pb = [Res("Pb%d" % i) for i in range(3)]
                rL = sb("rL", [128, 2, 256]); r_rL = Res("rL")
                Osb = sb("Osb", [128, 2, 256]); r_Osb = Res("Osb")
                On = Osb; r_On = r_Osb
                pending_fin = []
                od = sb("od", [128, 256]); r_od = Res("od")
                osq = sb("osq", [128, 256], BF16); r_osq = Res("osq")
                orstd = sb("orstd", [128, 256]); r_orstd = Res("orstd")
                obuf = [sb("obuf%d" % i, [128, 256], BF16) for i in range(2)]; r_obuf = [Res("obuf%d" % i) for i in range(2)]

                F0, F1, F2, F3 = banks[0], banks[1], banks[2], banks[3]
                rF0, rF1, rF2, rF3 = rbank[0], rbank[1], rbank[2], rbank[3]
                Sb = [banks[4], banks[5]]; rSb = [rbank[4], rbank[5]]
                Ob, Lb = banks[6], banks[7]; rOb, rLb = rbank[6], rbank[7]
                F0b = F0[:].bitcast(BF16)


                pcount = [0]
                sbcount = [0]
                out_events = []

                def front_a(ti):
                    nb = 4 if ti < 16 else 1
                    hb = ti % 2
                    for j in range(nb):
                        s = 4 * ti + j
                        xb = sbcount[0] % 2
                        sq = sbcount[0] % 4
                        sbcount[0] += 1
                        if s == 0:
                            S.dma("sp", xt[xb][0:16, :], meta_d, "x%d" % xb, writes=[r_xt[xb]])
                            S.dma("sp", xt[xb][16:128, :], x_d[0:112, :], "x%d" % xb, writes=[r_xt[xb]])
                        elif s == 64:
                            S.op("pool", MEMSET(xt[xb][:], 0.0), writes=[r_xt[xb]])
                            S.dma("sp", xt[xb][0:16, :], x_d[SEQ - 16:SEQ, :], "x%d" % xb, writes=[r_xt[xb]])
                        else:
                            S.dma("sp", xt[xb][:], x_d[128 * s - 16:128 * s + 112, :], "x%d" % xb, writes=[r_xt[xb]])
                        S.op("act", ACT(junk[:], xt[xb][:], AF.Square, accum_out=ssq[:, sq:sq + 1]),
                             reads=[r_xt[xb]], writes=[r_junk, r_ssq[sq]])
                        S.op("act", ACT(ssq[:, sq:sq + 1], ssq[:, sq:sq + 1], AF.Ln, scale=1.0 / D, bias=epsc[:, 0:1]),
                             reads=[r_ssq[sq], r_eps], writes=[r_ssq[sq]])
                        S.op("act", ACT(ssq[:, sq:sq + 1], ssq[:, sq:sq + 1], AF.Exp, scale=-0.5),
                             reads=[r_ssq[sq]], writes=[r_ssq[sq]])
                        S.op("act", ACT(xs[xb][:], xt[xb][:], AF.Copy, scale=ssq[:, sq:sq + 1]),
                             reads=[r_xt[xb], r_ssq[sq]], writes=[r_xs[xb]])
                        yield
                        S.group("pe", [TR(F0b[:, 128 * k:128 * (k + 1)], xs[xb][:, 128 * k:128 * (k + 1)], ident_b[:]) for k in range(8)],
                                reads=[r_xs[xb], r_ident_b], writes=[rF0])
                        S.op("dve", CP(hT[hb][:, :, 128 * j:128 * (j + 1)], F0b.rearrange("p (k t) -> p k t", k=8)),
                             reads=[rF0], writes=[r_hT[hb]])
                        yield

                def front_bs(ti):
                    nb = 4 if ti < 16 else 1
                    N = 128 * nb
                    tok0 = 512 * ti
                    hb = ti % 2
                    S.op("dve", TS(COSt[:], C0[:], cI[:, ti:ti + 1], ALU.mult), reads=[r_C0, r_cI], writes=[r_COSt])
                    S.op("dve", STT(COSt[:], S0[:], nsI[:, ti:ti + 1], COSt[:], ALU.mult, ALU.add), reads=[r_S0, r_nsI, r_COSt], writes=[r_COSt])
                    S.op("dve", TS(SINt[:], S0[:], cI[:, ti:ti + 1], ALU.mult), reads=[r_S0, r_cI], writes=[r_SINt])
                    S.op("dve", STT(SINt[:], C0[:], sI[:, ti:ti + 1], SINt[:], ALU.mult, ALU.add), reads=[r_C0, r_sI, r_SINt], writes=[r_SINt])
                    yield

                    def proj_fm(c, bank, rb):
                        S.group("pe", [MM(bank[:, 0:N], Wb[:, k, 128 * c:128 * (c + 1)], hT[hb][:, k, 0:N], k == 0, k == 7) for k in range(8)],
                                reads=[r_Wb, r_hT[hb]], writes=[rb])

                    def rope_a():
                        S.op("act", ACT(qraw[:, 0:N], F1[:, 0:N], AF.Copy), reads=[rF1], writes=[r_qraw])
                        S.op("pe", MM(F2[:, 0:N], rmatT_b[:], qraw[:, 0:N], True, True), reads=[r_rmat, r_qraw], writes=[rF2])

                    def rope_b(dst_ap, r_dst, dst2=None):
                        S.op("dve", TT(rt1[:, 0:N], F1[:, 0:N], COSt[:, 0:N], ALU.mult), reads=[rF1, r_COSt], writes=[r_rt1])
                        S.op("dve", TT(rt2[:, 0:N], F2[:, 0:N], SINt[:, 0:N], ALU.mult), reads=[rF2, r_SINt], writes=[r_rt2])
                        if dst2 is None:
                            S.op("pool", TT(dst_ap, rt1[:, 0:N], rt2[:, 0:N], ALU.add), reads=[r_rt1, r_rt2], writes=[r_dst])
                        else:
                            S.op("pool", TT(dst_ap[0:64, :], rt1[0:64, 0:N], rt2[0:64, 0:N], ALU.add), reads=[r_rt1, r_rt2, r_KTz], writes=[r_dst])
                            S.op("pool", TT(dst2[64:128, :], rt1[64:128, 0:N], rt2[64:128, 0:N], ALU.add), reads=[r_rt1, r_rt2, r_KTz], writes=[r_dst])

                    proj_fm(0, F1, rF1)
                    yield
                    rope_a()
                    yield
                    rope_b(QT[hb][:, 0:N], r_QT[hb])
                    yield
                    proj_fm(1, F1, rF1)
                    yield
                    rope_a()
                    yield
                    rope_b(KTa[:, tok0:tok0 + N], r_KT[ti], KTb[:, tok0:tok0 + N])
                    yield
                    proj_fm(3, F1, rF1)
                    yield
                    S.op("act", ACT(uT[hb][:, 0:N], F1[:, 0:N], AF.Copy), reads=[rF1], writes=[r_uT[hb]])
                    fns = []
                    for j in range(nb):
                        for k in range(8):
                            fns.append(MM(F2[:, 128 * j:128 * (j + 1)], hT[hb][:, k, 128 * j:128 * (j + 1)], Wb[:, k, 256:384], k == 0, k == 7))
                    S.group("pe", fns, reads=[r_Wb, r_hT[hb]], writes=[rF2])
                    yield
                    S.op("act", ACT(V[:, 4 * ti:4 * ti + nb, :], F2[:, 0:N].rearrange("p (j d) -> p j d", j=nb), AF.Copy),
                         reads=[rF2], writes=[r_V[ti]])
                    yield
                    par = ti % 2
                    yb = ti % 2

                    def emit_y(g, pb):
                        S.group("pe", [MM(F3[:, 0:N], LC1[:, g, :], P1s[pb][:, 0:N], g == 0, False),
                                       MM(F3[:, 0:N], LC2[:, g, :], P2s[pb][:, 0:N], False, False)],
                                reads=[r_LC, r_P1s[pb], r_P2s[pb]], writes=[rF3])

                    def stage_b(g):
                        pb = g % 3; q = g % 2
                        S.op("dve", SCAN(Xs[pb][:, 0:N], rdec[:, g:g + 1].to_broadcast([128, N]), bt[q][:, 0:N], s0[:, par, g:g + 1]),
                             reads=[r_rdec, r_bt[q], r_s0[par]], writes=[r_Xs[pb]])
                        S.op("pool", TT(P1s[pb][:, 0:N], Xs[pb][:, 0:N], COSg[:, g, 0:N], ALU.mult), reads=[r_Xs[pb], r_COSg[g]], writes=[r_P1s[pb]])
                        S.op("dve", TT(P2s[pb][:, 0:N], Xs[pb][:, 0:N], SINg[:, g, 0:N], ALU.mult), reads=[r_Xs[pb], r_SINg[g]], writes=[r_P2s[pb]])

                    def stage_c(g):
                        pb = g % 3
                        if ti + 1 < nt1:
                            S.op("pe", MM(F0[:, g:g + 1], ROT[:, g, :], Xs[pb][:, N - 1:N], True, True), reads=[r_ROT, r_Xs[pb]], writes=[rF0])
                            S.op("act", ACT(s0[:, 1 - par, g:g + 1], F0[:, g:g + 1], AF.Copy), reads=[rF0], writes=[r_s0[1 - par]])
                        emit_y(g, pb)

                    for it in range(11):
                        if 3 <= it:
                            stage_c(it - 3)
                            yield
                        if it < 8:
                            g = it; q = g % 2
                            S.op("pe", MM(F1[:, 0:N], LB1[:, g, :], uT[hb][:, 0:N], True, True), reads=[r_LB, r_uT[hb]], writes=[rF1])
                            S.op("pe", MM(F2[:, 0:N], LB2[:, g, :], uT[hb][:, 0:N], True, True), reads=[r_LB, r_uT[hb]], writes=[rF2])
                            yield
                            S.op("dve", TT(st1[q][:, 0:N], F1[:, 0:N], COSg[:, g, 0:N], ALU.mult), reads=[rF1, r_COSg[g]], writes=[r_st1[q]])
                            S.op("dve", TT(st2[q][:, 0:N], F2[:, 0:N], SINg[:, g, 0:N], ALU.mult), reads=[rF2, r_SINg[g]], writes=[r_st2[q]])
                            S.op("pool", TT(bt[q][:, 0:N], st1[q][:, 0:N], st2[q][:, 0:N], ALU.add), reads=[r_st1[q], r_st2[q]], writes=[r_bt[q]])
                        if 1 <= it <= 8:
                            stage_b(it - 1)
                        if it <= 8:
                            yield
                    S.op("pe", MM(F3[:, 0:N], diagD[:], uT[hb][:, 0:N], False, True), reads=[r_diagD, r_uT[hb]], writes=[rF3])
                    yield
                    S.op("act", ACT(ybuf[yb][:, 0:N], F3[:, 0:N], AF.Gelu), reads=[rF3], writes=[r_ybuf[yb]])
                    lo = max(tok0, NMETA); hi = min(tok0 + N, L)
                    out_events.extend(xchg_write(128, lo - NMETA, hi - NMETA, ybuf[yb], lo - tok0, "yo%d" % yb, r_ybuf[yb]))
                    yield

                def attention(ti, pump):
                    nb = 4 if ti < 16 else 1
                    tok0 = 512 * ti
                    hb = ti % 2
                    nqt = 2 if nb == 4 else 1
                    nbq = 2 if nb == 4 else 1
                    NQ = 128 * nbq
                    QTv = QT[hb]
                    for qi in range(nqt):
                        c = 2 * ti + qi
                        q0 = 256 * qi
                        nkb = 2 * c + nbq

                        def qk(kb):
                            j = kb - 2 * c
                            qs = 128 * j if j > 0 else 0
                            kt = kb // 4
                            sbk = kb % 2
                            pbk = kb % 3
                            Sv = Sb[sbk][:, 0:2 * NQ].rearrange("p (h q) -> p h q", h=2)
                            Pv = Pb[pbk][:].rearrange("p h q -> p (h q)")[:, 0:2 * NQ].rearrange("p (h q) -> p h q", h=2)
                            S.group("pe", [MM(Sv[:, 0, qs:NQ], KTa[:, 128 * kb:128 * (kb + 1)], QTv[:, q0 + qs:q0 + NQ], True, True),
                                           MM(Sv[:, 1, qs:NQ], KTb[:, 128 * kb:128 * (kb + 1)], QTv[:, q0 + qs:q0 + NQ], True, True)],
                                    reads=[r_KT[kt], r_QT[hb]], writes=[rSb[sbk]])
                            S.op("act", ACT(Pv[:, :, qs:NQ], Sv[:, :, qs:NQ], AF.Exp, scale=0.125), reads=[rSb[sbk]], writes=[r_Pb[pbk]])
                            if j >= 0:
                                S.op("pool", TT(Pv[:, :, qs:qs + 128], Pv[:, :, qs:qs + 128],
                                                tri_b[:].unsqueeze(1).to_broadcast([128, 2, 128]), ALU.mult),
                                     reads=[r_Pb[pbk], r_tri], writes=[r_Pb[pbk]])

                        def pv(kb):
                            j = kb - 2 * c
                            qs = 128 * j if j > 0 else 0
                            kt = kb // 4
                            pbk = kb % 3
                            Pv = Pb[pbk][:].rearrange("p h q -> p (h q)")[:, 0:2 * NQ].rearrange("p (h q) -> p h q", h=2)
                            Ov = Ob[:, 0:2 * NQ].rearrange("p (h q) -> p h q", h=2)
                            Lv = Lb[:, 0:2 * NQ].rearrange("p (h q) -> p h q", h=2)
                            first = (kb == 0); lastk = (kb == nkb - 1)
                            if qs == 0:
                                Pf = Pb[pbk][:].rearrange("p h q -> p (h q)")[:, 0:2 * NQ]
                                S.op("pe", MM(Ob[:, 0:2 * NQ], V[:, kb, :], Pf, first, lastk), reads=[r_V[kt], r_Pb[pbk]], writes=[rOb])
                                S.op("pe", MM(Lb[:, 0:2 * NQ], ones_b[:], Pf, first, lastk), reads=[r_ones, r_Pb[pbk]], writes=[rLb])
                            else:
                                assert not first
                                S.group("pe", [MM(Ov[:, 0, qs:NQ], V[:, kb, :], Pv[:, 0, qs:NQ], False, False),
                                               MM(Ov[:, 1, qs:NQ], V[:, kb, :], Pv[:, 1, qs:NQ], False, lastk)],
                                        reads=[r_V[kt], r_Pb[pbk]], writes=[rOb])
                                S.group("pe", [MM(Lv[:, 0, qs:NQ], ones_b[:], Pv[:, 0, qs:NQ], False, False),
                                               MM(Lv[:, 1, qs:NQ], ones_b[:], Pv[:, 1, qs:NQ], False, lastk)],
                                        reads=[r_ones, r_Pb[pbk]], writes=[rLb])

                        qk(0)
                        for kb in range(nkb):
                            if kb + 1 < nkb:
                                qk(kb + 1)
                            pv(kb)
                            if kb == 2 and pending_fin:
                                pending_fin.pop(0)()
                            pump()
                        while pending_fin:
                            pending_fin.pop(0)()
                        Of = Ob[:, 0:2 * NQ]; Lf = Lb[:, 0:2 * NQ]
                        Osf = Osb[:].rearrange("p h q -> p (h q)")[:, 0:2 * NQ]
                        rLf = rL[:].rearrange("p h q -> p (h q)")[:, 0:2 * NQ]
                        Onf = On[:].rearrange("p h q -> p (h q)")[:, 0:2 * NQ]
                        Onv = Onf.rearrange("p (h q) -> p h q", h=2)
                        S.op("dve", CP(Osf, Of), reads=[rOb], writes=[r_Osb])
                        S.op("act", ACT(rLf, Lf, AF.Ln), reads=[rLb], writes=[r_rL])
                        S.op("act", ACT(rLf, rLf, AF.Exp, scale=-1.0), reads=[r_rL], writes=[r_rL])
                        S.op("dve", TT(Onf, Osf, rLf, ALU.mult), reads=[r_Osb, r_rL], writes=[r_On])
                        S.op("dve", STT(od[:, 0:NQ], Onv[:, 1, :], neglam[:, 0:1], Onv[:, 0, :], ALU.mult, ALU.add),
                             reads=[r_On, r_neglam], writes=[r_od])
                        S.op("act", ACT(osq[:, 0:NQ], od[:, 0:NQ], AF.Square), reads=[r_od], writes=[r_osq])

                        def fin2(c=c, NQ=NQ, q0=q0, tok0=tok0):
                            S.op("pe", MM(F0[:, 256:256 + NQ], ones_b[:], osq[:, 0:NQ], True, True), reads=[r_ones, r_osq], writes=[rF0])
                            S.op("act", ACT(orstd[:, 0:NQ], F0[:, 256:256 + NQ], AF.Ln, scale=1.0 / 128, bias=epsc[:, 0:1]),
                                 reads=[rF0, r_eps], writes=[r_orstd])
                            S.op("act", ACT(orstd[:, 0:NQ], orstd[:, 0:NQ], AF.Exp, scale=-0.5), reads=[r_orstd], writes=[r_orstd])
                            ob = c % 2
                            S.op("dve", STT(obuf[ob][:, 0:NQ], od[:, 0:NQ], subg[:, 0:1], orstd[:, 0:NQ], ALU.mult, ALU.mult),
                                 reads=[r_od, r_subg, r_orstd], writes=[r_obuf[ob]])
                            qt0 = tok0 + q0
                            lo = max(qt0, NMETA); hi = min(qt0 + NQ, L)
                            out_events.extend(xchg_write(0, lo - NMETA, hi - NMETA, obuf[ob], lo - qt0, "oo%d" % ob, r_obuf[ob]))
                        pending_fin.append(fin2)

                def run_all(g):
                    for _ in g:
                        pass
                run_all(front_a(0)); run_all(front_bs(0))
                if nt1 > 1:
                    run_all(front_a(1))
                for ti in range(nt1):
                    gens = []
                    if ti + 1 < nt1:
                        gens.append(front_bs(ti + 1))
                    if ti + 2 < nt1:
                        gens.append(front_a(ti + 2))
                    nbq_t = 2 if ti < 16 else 1
                    total_kb = sum(2 * (2 * ti + qi) + nbq_t for qi in range(2 if ti < 16 else 1))
                    state = {"steps_left": (FRONT_STEPS if ti + 1 < nt1 else 0) + (8 if ti + 2 < nt1 else 0),
                             "kb_left": total_kb, "gens": gens, "rr": 0}

                    def pump(state=state):
                        if not state["gens"]:
                            return
                        kbl = max(state["kb_left"], 1)
                        n = -(-state["steps_left"] // kbl)
                        state["kb_left"] -= 1
                        for _ in range(n):
                            if not state["gens"]:
                                return
                            state["rr"] += 1
                            idx = 1 if (len(state["gens"]) > 1 and state["rr"] % 5 == 0) else 0
                            try:
                                next(state["gens"][idx])
                                state["steps_left"] -= 1
                            except StopIteration:
                                state["gens"].pop(idx)
                    attention(ti, pump)
                    for g_ in state["gens"]:
                        run_all(g_)
                    if wprep:
                        dst_, src_ = wprep.pop(0)
                        wprep_events.append(S.dma("pool", dst_, src_, "wprep"))
                    if ti == nt1 - 1 or (ti >= 4 and ti % 4 == 0):
                        while pending_fin:
                            pending_fin.pop(0)()
                    if use_cc and ti >= 4 and ti % 4 == 0:
                        qd = ti // 4 - 1
                        S.wait_all("pool", q_events[qd])
                        S.raw("pool", (lambda qd: (lambda e: e.collective_compute(
                            "AllGather", ALU.bypass, replica_groups=[[0, 1, 2, 3], [4, 5, 6, 7]],
                            ins=[xin_t[qd].ap().opt()], outs=[xout_t.ap()[1024 * qd:1024 * (qd + 1), :].opt()]).then_inc(cc_sem)))(qd))

            return out_events

        out_events = _phase1() if nt1 > 0 else []

        if debug:
            print('MARKS', S.marks, S.nops)
            S.limit = 10**9
            S.wait_all("pool", out_events)
            evs = [S.dma("pool", dbg_d[:, TOK2 * q:TOK2 * (q + 1)], xin[q], "dbg") for q in range(4)]
            S.wait_all("pool", evs)
        if do_phase2:
            S.wait_all("pool", out_events)
            if p2mode == "nogather":
                evd = S.dma("pool", xout[0:1024, :], xout_dbg_d, "xdbg")
                S.wait_all("pool", [evd])
            else:
                S.raw("pool", lambda e: e.wait_ge(cc_sem, 4))
            S.cnt["pool"] += 1
            gather_ev = (S.sem["pool"], S.cnt["pool"], "pool")
            S.raw("pool", lambda e: e.sem_inc(S.sem["pool"], 1))
            lasts = [S.last[e] for e in ("pe", "act", "dve", "pool") if S.last[e] is not None] + [gather_ev]
            for e in ("pe", "act", "dve", "sp", "pool"):
                S.wait_all(e, lasts)

            with ExitStack() as p2:
                def sb(name, shape, dt=F32):
                    return p2.enter_context(nc.sbuf_tensor("b_" + name, list(shape), dt))

                Wg = sb("Wg", [128, 8, DFF], BF16); r_Wg = Res("Wg")
                Wu = sb("Wu", [128, 8, DFF], BF16); r_Wu = Res("Wu")
                Wd = sb("Wd", [128, NF, D], BF16); r_Wd = Res("Wd")
                Wo = sb("Wo", [128, 8, D], BF16); r_Wo = Res("Wo")
                Wgl = sb("Wgl", [128, 4, 512], BF16); r_Wgl = Res("Wgl")
                ident2 = sb("ident2", [128, 128], BF16); r_id2 = Res("ident2")
                ones2 = sb("ones2", [128, 128], BF16); r_ones2 = Res("ones2")
                eps2 = sb("eps2", [128, 1]); r_eps2 = Res("eps2")
                bglu = sb("bglu", [128, 4]); r_bglu = Res("bglu")
                gssm = sb("gssm", [128, 4]); r_gssm = Res("gssm")
                gpost_b = sb("gpost_b", [128, D]); r_gpost = Res("gpost")
                gffn8 = sb("gffn8", [128, 8]); r_gffn = Res("gffn")
                gpffn_b = sb("gpffn_b", [128, D]); r_gpffn = Res("gpffn")
                S.dma("pool", ident2[:], ident_d, "k0", writes=[r_id2])
                S.op("dve", MEMSET(ones2[:], 1.0), writes=[r_ones2])
                S.op("dve", MEMSET(eps2[:], EPS), writes=[r_eps2])
                S.dma("sp", bglu[:], bglu_d, "k1", writes=[r_bglu])
                S.dma("sp", gssm[:], gssm_d, "k2", writes=[r_gssm])
                S.dma("sp", gpost_b[:], gpost_d.partition_broadcast(128), "k3", writes=[r_gpost])
                S.dma("sp", gffn8[:], gffn_d, "k4", writes=[r_gffn])
                S.dma("sp", gpffn_b[:], gpffn_d.partition_broadcast(128), "k5", writes=[r_gpffn])

                while wprep:
                    dst_, src_ = wprep.pop(0)
                    wprep_events.append(S.dma("pool", dst_, src_, "wprep"))
                S.dma("act", Wgl[:], wgl_b.rearrange("(k p) c -> p k c", p=128), "w0", writes=[r_Wgl], extra=wprep_events)
                S.dma("act", Wo[:], wo_b.rearrange("(k p) c -> p k c", p=128), "w1", writes=[r_Wo], extra=wprep_events)
                r_Wgc = [Res("Wg%d" % f) for f in range(NF)]
                r_Wuc = [Res("Wu%d" % f) for f in range(NF)]
                r_Wdc = [Res("Wd%d" % f) for f in range(NF)]
                for f in range(0, NF, 2):
                    S.dma("act", Wg[:, :, 128 * f:128 * (f + 2)], wg_b[:, 128 * f:128 * (f + 2)].rearrange("(k p) c -> p k c", p=128),
                          "wg%d" % (f // 2), writes=[r_Wgc[f], r_Wgc[f + 1]], extra=wprep_events)
                    S.dma("act", Wu[:, :, 128 * f:128 * (f + 2)], wu_b[:, 128 * f:128 * (f + 2)].rearrange("(k p) c -> p k c", p=128),
                          "wu%d" % (f // 2), writes=[r_Wuc[f], r_Wuc[f + 1]], extra=wprep_events)
                for f in range(0, NF, 2):
                    S.dma("act", Wd[:, f:f + 2, :], wd_b[128 * f:128 * (f + 2), :].rearrange("(k p) c -> p k c", p=128),
                          "wd%d" % (f // 2), writes=[r_Wdc[f], r_Wdc[f + 1]], extra=wprep_events)
                cT = [sb("cT%d" % i, [128, 8, T2], BF16) for i in range(1)]; r_cT = [Res("cT%d" % i) for i in range(1)]
                sig = sb("sig", [128, T2], BF16); r_sig = Res("sig")
                yg = sb("yg", [128, 4, T2], BF16); r_yg = Res("yg")
                sq2 = sb("sq2", [128, T2], BF16); r_sq2 = Res("sq2")
                rsy = sb("rsy", [128, T2]); r_rsy = Res("rsy")
                yn = sb("yn", [128, 4, T2], BF16); r_yn = Res("yn")
                hres = [sb("hres%d" % i, [128, D]) for i in range(3)]; r_hres = [Res("hres%d" % i) for i in range(3)]
                ssvA = sb("ssvA", [128, 8]); r_ssvA = Res("ssvA")
                ssvB = sb("ssvB", [128, 4]); r_ssvB = Res("ssvB")
                tmpA = sb("tmpA", [128, 512]); r_tmpA = Res("tmpA")
                tmpB = tmpA; r_tmpB = r_tmpA
                hs = [sb("hs0", [128, D], BF16)] * 2; r_hs = [Res("hs0")] * 2
                junkA = sb("junkA", [128, 512], BF16); r_junkA = Res("junkA")
                junkB = junkA; r_junkB = r_junkA
                hT2 = [sb("hT2_%d" % i, [128, 8, T2], BF16) for i in range(2)]; r_hT2 = [Res("hT2_%d" % i) for i in range(2)]
                sg = [sb("sg%d" % i, [128, T2], BF16) for i in range(2)]; r_sg = [Res("sg%d" % i) for i in range(2)]
                aT = sb("aT", [128, NF, T2], BF16); r_aT = Res("aT")

                pid = [None]
                B = banks
                rB = rbank
                final_events = []

                def h1_gen(tt):
                    cb = 0

                    def _ld(e, tt=tt):
                        if pid[0] is None:
                            pid[0] = e.partition_id() % 4
                        return e.dma_start(out=cT[cb][:], in_=xout[bass.ds(pid[0] * 1024, 1024), tt * T2:(tt + 1) * T2].rearrange("(k p) t -> p k t", p=128))
                    slot = "g%d" % cb
                    if slot not in S.dma_sems:
                        S.dma_sems[slot] = top.enter_context(nc.semaphore("d_" + slot))
                        S.dma_cnt[slot] = 0
                    S._waits("sp", S._deps_for(None, [r_cT[cb]], None))
                    S.dma_cnt[slot] += 16
                    ev = (S.dma_sems[slot], S.dma_cnt[slot], "dma")
                    S.raw("sp", (lambda sem: (lambda e, f=_ld: f(e).then_inc(sem, 16)))(S.dma_sems[slot]))
                    S._commit(ev, None, [r_cT[cb]])
                    yield None
                    for m in range(4):
                        S.group("pe", [MM(B[0][:, 0:T2], Wgl[:, r, 128 * m:128 * (m + 1)], cT[cb][:, 2 * r + 1, :], r == 0, r == 3) for r in range(4)],
                                reads=[r_Wgl, r_cT[cb]], writes=[rB[0]])
                        yield None
                        S.op("act", ACT(sig[:], B[0][:, 0:T2], AF.Sigmoid, bias=bglu[:, m:m + 1]), reads=[rB[0], r_bglu], writes=[r_sig])
                        S.op("dve", TT(yg[:, m, :], cT[cb][:, 2 * m + 1, :], sig[:], ALU.mult), reads=[r_cT[cb], r_sig], writes=[r_yg])
                        S.op("act", ACT(sq2[:], yg[:, m, :], AF.Square), reads=[r_yg], writes=[r_sq2])
                        yield None
                        yield None
                        S.op("pe", MM(B[1][:, 0:T2], ones2[:], sq2[:], m == 0, m == 3), reads=[r_ones2, r_sq2], writes=[rB[1]])
                    yield None
                    S.op("act", ACT(rsy[:], B[1][:, 0:T2], AF.Ln, scale=1.0 / 512, bias=eps2[:, 0:1]), reads=[rB[1], r_eps2], writes=[r_rsy])
                    S.op("act", ACT(rsy[:], rsy[:], AF.Exp, scale=-0.5), reads=[r_rsy], writes=[r_rsy])
                    for m in range(4):
                        S.op("dve", STT(yn[:, m, :], yg[:, m, :], gssm[:, m:m + 1], rsy[:], ALU.mult, ALU.mult),
                             reads=[r_yg, r_gssm, r_rsy], writes=[r_yn])
                    yield None
                    yield None
                    yield None
                    for s in range(2):
                        hr = hres[(2 * tt + s) % 3]; r_hr = r_hres[(2 * tt + s) % 3]
                        c0 = 4 * s
                        for hf in range(2):
                            bk = B[2 + hf]; rbk = rB[2 + hf]
                            fns = []
                            for k in range(8):
                                src = cT[cb][:, k, 128 * s:128 * (s + 1)] if k % 2 == 0 else yn[:, k // 2, 128 * s:128 * (s + 1)]
                                fns.append(MM(bk[:, :], src, Wo[:, k, 512 * hf:512 * (hf + 1)], k == 0, k == 7))
                            S.group("pe", fns, reads=[r_cT[cb], r_yn, r_Wo], writes=[rbk])
                            yield None
                            S.op("act", ACT(junkA[:], bk[:, :], AF.Square, accum_out=ssvA[:, c0 + hf:c0 + hf + 1]), reads=[rbk], writes=[r_junkA, r_ssvA])
                        yield None
                        S.op("dve", TT(ssvA[:, c0 + 2:c0 + 3], ssvA[:, c0:c0 + 1], ssvA[:, c0 + 1:c0 + 2], ALU.add), reads=[r_ssvA], writes=[r_ssvA])
                        S.op("act", ACT(ssvA[:, c0 + 2:c0 + 3], ssvA[:, c0 + 2:c0 + 3], AF.Ln, scale=1.0 / D, bias=eps2[:, 0:1]), reads=[r_ssvA, r_eps2], writes=[r_ssvA])
                        S.op("act", ACT(ssvA[:, c0 + 2:c0 + 3], ssvA[:, c0 + 2:c0 + 3], AF.Exp, scale=-0.5), reads=[r_ssvA], writes=[r_ssvA])
                        yield ("BAR1" if s == 1 else None)
                        row0 = tt * T2 + 128 * s
                        S.dma("sp", hr[:], xres_d[row0:row0 + 128, :], "xr%d" % ((2 * tt + s) % 3), writes=[r_hr])
                        yield None
                        for hf in range(2):
                            bk = B[2 + hf]; rbk = rB[2 + hf]
                            S.op("dve", STT(tmpA[:], bk[:, :], ssvA[:, c0 + 2:c0 + 3], gpost_b[:, 512 * hf:512 * (hf + 1)], ALU.mult, ALU.mult),
                                 reads=[rbk, r_ssvA, r_gpost], writes=[r_tmpA])
                            S.op("dve", TT(hr[:, 512 * hf:512 * (hf + 1)], tmpA[:], hr[:, 512 * hf:512 * (hf + 1)], ALU.add),
                                 reads=[r_tmpA, r_hr], writes=[r_hr])
                            S.op("act", ACT(junkA[:], hr[:, 512 * hf:512 * (hf + 1)], AF.Square, accum_out=ssvA[:, c0 + hf:c0 + hf + 1]),
                                 reads=[r_hr], writes=[r_junkA, r_ssvA])
                        yield None
                        yield None
                        S.op("dve", TT(ssvA[:, c0 + 3:c0 + 4], ssvA[:, c0:c0 + 1], ssvA[:, c0 + 1:c0 + 2], ALU.add), reads=[r_ssvA], writes=[r_ssvA])
                        S.op("act", ACT(ssvA[:, c0 + 3:c0 + 4], ssvA[:, c0 + 3:c0 + 4], AF.Ln, scale=1.0 / D, bias=eps2[:, 0:1]), reads=[r_ssvA, r_eps2], writes=[r_ssvA])
                        S.op("act", ACT(ssvA[:, c0 + 3:c0 + 4], ssvA[:, c0 + 3:c0 + 4], AF.Exp, scale=-0.5), reads=[r_ssvA], writes=[r_ssvA])
                        yield None
                        S.op("act", ACT(hs[s][:], hr[:], AF.Copy, scale=ssvA[:, c0 + 3:c0 + 4]), reads=[r_hr, r_ssvA], writes=[r_hs[s]])
                        yield None
                        yield None
                        yield None
                        B1b = B[1][:].bitcast(BF16)
                        S.group("pe", [TR(B1b[:, 128 * k:128 * (k + 1)], hs[s][:, 128 * k:128 * (k + 1)], ident2[:]) for k in range(8)],
                                reads=[r_hs[s], r_id2], writes=[rB[1]])
                        yield None
                        S.op("dve", TT(hT2[tt % 2][:, :, 128 * s:128 * (s + 1)], B1b.rearrange("p (k t) -> p k t", k=8),
                                       gffn8[:].unsqueeze(2).to_broadcast([128, 8, 128]), ALU.mult),
                             reads=[rB[1], r_gffn], writes=[r_hT2[tt % 2]])
                        yield None

                class Pump:
                    def __init__(self, gen):
                        self.gen = gen
                        self.released = set()
                        self.pending = None

                    def release(self, name):
                        self.released.add(name)

                    def step(self, n):
                        for _ in range(n):
                            if self.gen is None:
                                return
                            if self.pending is not None:
                                if self.pending not in self.released:
                                    return
                                self.pending = None
                            try:
                                r = next(self.gen)
                            except StopIteration:
                                self.gen = None
                                return
                            if r is not None:
                                self.pending = r

                    def flush(self):
                        while self.gen is not None:
                            if self.pending is not None:
                                assert self.pending in self.released, self.pending
                            self.step(1)

                def h2(tt, P):
                    for f in range(NF):
                        gb2 = B[4 + (f % 2)]; rgb2 = rB[4 + (f % 2)]
                        gv = gb2[:].rearrange("p (h q) -> p h q", h=2)
                        h2t = hT2[tt % 2]; r_h2t = r_hT2[tt % 2]
                        S.group("pe", [MM(gv[:, 0, :], Wg[:, k, 128 * f:128 * (f + 1)], h2t[:, k, :], k == 0, k == 7) for k in range(8)],
                                reads=[r_Wgc[f], r_h2t], writes=[rgb2])
                        P.step(1)
                        S.group("pe", [MM(gv[:, 1, :], Wu[:, k, 128 * f:128 * (f + 1)], h2t[:, k, :], k == 0, k == 7) for k in range(8)],
                                reads=[r_Wuc[f], r_h2t], writes=[rgb2])
                        S.op("act", ACT(sg[f % 2][:], gv[:, 0, :], AF.Silu), reads=[rgb2], writes=[r_sg[f % 2]])
                        S.op("dve", TT(aT[:, f, :], gv[:, 1, :], sg[f % 2][:], ALU.mult), reads=[rgb2, r_sg[f % 2]], writes=[r_aT])
                        P.step(1)
                    for s in range(2):
                        row0 = tt * T2 + 128 * s
                        hr = hres[(2 * tt + s) % 3]; r_hr = r_hres[(2 * tt + s) % 3]
                        for hf in range(2):
                            bk = B[6 + hf]; rbk = rB[6 + hf]
                            S.group("pe", [MM(bk[:, :], aT[:, f, 128 * s:128 * (s + 1)], Wd[:, f, 512 * hf:512 * (hf + 1)], f == 0, f == NF - 1) for f in range(NF)],
                                    reads=[r_aT] + r_Wdc, writes=[rbk])
                            S.op("act", ACT(junkB[:], bk[:, :], AF.Square, accum_out=ssvB[:, hf:hf + 1]), reads=[rbk], writes=[r_junkB, r_ssvB])
                            P.step(6)
                        S.op("dve", TT(ssvB[:, 2:3], ssvB[:, 0:1], ssvB[:, 1:2], ALU.add), reads=[r_ssvB], writes=[r_ssvB])
                        S.op("act", ACT(ssvB[:, 2:3], ssvB[:, 2:3], AF.Ln, scale=1.0 / D, bias=eps2[:, 0:1]), reads=[r_ssvB, r_eps2], writes=[r_ssvB])
                        S.op("act", ACT(ssvB[:, 2:3], ssvB[:, 2:3], AF.Exp, scale=-0.5), reads=[r_ssvB], writes=[r_ssvB])
                        for hf in range(2):
                            bk = B[6 + hf]; rbk = rB[6 + hf]
                            S.op("dve", STT(tmpB[:], bk[:, :], ssvB[:, 2:3], gpffn_b[:, 512 * hf:512 * (hf + 1)], ALU.mult, ALU.mult),
                                 reads=[rbk, r_ssvB, r_gpffn], writes=[r_tmpB])
                            S.op("dve", TT(hr[:, 512 * hf:512 * (hf + 1)], tmpB[:], hr[:, 512 * hf:512 * (hf + 1)], ALU.add),
                                 reads=[r_tmpB, r_hr], writes=[r_hr])
                        final_events.append(S.dma("sp", out_d[row0:row0 + 128, :], hr[:], "fo%d" % ((2 * tt + s) % 3), reads=[r_hr]))
                        if s == 0:
                            P.release("BAR1")

                P0 = Pump(h1_gen(0)); P0.release("BAR1"); P0.flush()
                for tt in range(NT2):
                    P = Pump(h1_gen(tt + 1) if tt + 1 < NT2 else None)
                    h2(tt, P)
                    P.flush()

                S.wait_all("sp", final_events)

        with nc.Block() as block:
            S.run(block)
    return nc


def _consts():
    invf8 = (500000.0 ** (-np.arange(0, 16, 2, dtype=np.float32) / np.float32(16))).astype(np.float32)
    invf = np.zeros((128, 1), np.float32)
    for base in (0, 64):
        for i in range(16):
            invf[base + i, 0] = invf8[i % 8]
    R = np.zeros((128, 128), np.float32)
    for base in (0, 64):
        for i in range(8):
            R[base + i, base + i + 8] = -1.0
            R[base + i + 8, base + i] = 1.0
    rmatT = np.ascontiguousarray(R.T)
    ident = np.eye(128, dtype=np.float32)
    swapm = np.zeros((128, 128), np.float32)
    for k in range(128):
        swapm[k, (k + 64) % 128] = 1.0
    sgn = np.ones((128, 1), np.float32); sgn[64:] = -1.0
    tri = (np.arange(128)[:, None] <= np.arange(128)[None, :]).astype(np.float32)
    gmask = (np.arange(128)[:, None] // 16 == np.arange(8)[None, :]).astype(np.float32)
    return dict(invf=invf, rmatT=rmatT, ident=ident, swapm=swapm, sgn=sgn, tri=tri, gmask=gmask,
                iota512=np.arange(512, dtype=np.float32), tile_iota=(512.0 * np.arange(NT1)).astype(np.float32))


def make_in_maps(inp):
    c = _consts()
    f = lambda a: np.ascontiguousarray(np.asarray(a, dtype=np.float32))
    w_in = f(inp["w_in"])[0]
    maps = []
    w_out = f(inp["w_out"])[0]
    perm = []
    for r in range(4):
        perm += list(range(128 * r, 128 * r + 128)) + list(range(512 + 128 * r, 512 + 128 * r + 128))
    w_out_p = np.ascontiguousarray(w_out[perm, :])
    for core in range(8):
        b, h = core // 4, core % 4
        cols = (list(range(64 * h, 64 * h + 64)) + list(range(256 + 64 * h, 256 + 64 * h + 64)) +
                list(range(512 + 64 * h, 512 + 64 * h + 64)) + list(range(768 + 64 * h, 768 + 64 * h + 64)) +
                list(range(1024 + 128 * h, 1024 + 128 * h + 128)) + list(range(1536 + 128 * h, 1536 + 128 * h + 128)))
        gs = slice(8 * h, 8 * h + 8)
        m = dict(c)
        m["x"] = f(inp["x"][b])
        m["meta"] = f(inp["meta"])
        m["win"] = np.ascontiguousarray(w_in[:, cols])
        m["gpre"] = np.ascontiguousarray(f(inp["pre_mix_g"])[0].reshape(8, 128).T)
        m["lamv"] = np.stack([f(inp["lambda_q1"])[0], f(inp["lambda_k1"])[0], f(inp["lambda_q2"])[0], f(inp["lambda_k2"])[0]])
        m["subg"] = f(inp["subln_g"])[0].reshape(128, 1)
        m["areT"] = np.ascontiguousarray(f(inp["a_re"])[0, gs].T)
        m["aimT"] = np.ascontiguousarray(f(inp["a_im"])[0, gs].T)
        m["logdt"] = f(inp["log_dt"])[0, gs]
        m["bre"] = f(inp["b_re"])[0, gs]
        m["bim"] = f(inp["b_im"])[0, gs]
        m["creT"] = np.ascontiguousarray(f(inp["c_re"])[0, gs].transpose(2, 0, 1).reshape(64, 128))
        m["cimT"] = np.ascontiguousarray(f(inp["c_im"])[0, gs].transpose(2, 0, 1).reshape(64, 128))
        m["dskip"] = f(inp["d_skip"])[0, 128 * h:128 * h + 128].reshape(128, 1)
        m["wglu"] = f(inp["w_glu"])[0]
        m["bglu"] = np.ascontiguousarray(f(inp["b_glu"])[0].reshape(4, 128).T)
        m["gssm"] = np.ascontiguousarray(f(inp["ssm_out_g"])[0].reshape(4, 128).T)
        m["wout"] = w_out_p
        m["gpost"] = f(inp["post_mix_g"])[0]
        m["gffn8"] = np.ascontiguousarray(f(inp["pre_ffn_g"])[0].reshape(8, 128).T)
        m["wgate"] = f(inp["w_gate"])[0]
        m["wup"] = f(inp["w_up"])[0]
        m["wdown"] = f(inp["w_down"])[0]
        m["gpffn"] = f(inp["post_ffn_g"])[0]
        m["xres"] = f(inp["x"][b, TOK2 * h:TOK2 * (h + 1)])
        maps.append(m)
    return maps


def kernel(**inputs):
    nc = build_program()
    maps = make_in_maps(inputs)
    res = run_bass_kernel_spmd(nc, maps, core_ids=list(range(8)))
    out = np.zeros((2, SEQ, D), np.float32)
    for core in range(8):
        b, h = core // 4, core % 4
        out[b, TOK2 * h:TOK2 * (h + 1)] = res.results[core]["out"]
    return out
```

```python
import math
import os
from contextlib import ExitStack

import numpy as np
import concourse.bass as bass
import concourse.mybir as mybir
from concourse.bass_utils import run_bass_kernel_spmd

F32 = mybir.dt.float32
BF16 = mybir.dt.bfloat16
I32 = mybir.dt.int32
AF = mybir.ActivationFunctionType
ALU = mybir.AluOpType

D = 1024
SEQ = 8192
NMETA = 16
L = SEQ + NMETA
LP = 8320
NBLK = 65
NT1 = 17
DFF = 2816
NF = 22
EPS = 1e-6
LAM_INIT = 0.8 - 0.6 * math.exp(0.0)
TWO_PI = 2.0 * math.pi
C1 = 6.28125
C2 = TWO_PI - C1
PI_LO = 3.141592
TOK2 = 2048
T2 = 256
NT2 = TOK2 // T2
FRONT_STEPS = 40


class Res:
    __slots__ = ("name", "w", "r", "excl")

    def __init__(self, name, excl=False):
        self.name = name
        self.w = None
        self.r = []
        self.excl = excl


class Sched:
    ENGS = ("pe", "act", "dve", "pool", "sp")

    def __init__(self, nc, stack):
        self.nc = nc
        self.ops = {e: [] for e in self.ENGS}
        self.sem = {e: stack.enter_context(nc.semaphore("s_" + e)) for e in self.ENGS}
        self.cnt = {e: 0 for e in self.ENGS}
        self.waited = {e: {} for e in self.ENGS}
        self.stack = stack
        self.dma_sems = {}
        self.dma_cnt = {}
        self.last = {e: None for e in self.ENGS}
        self.limit = int(os.environ.get("K_LIMIT", "1000000000"))
        self.nops = 0
        self.marks = []

    def _waits(self, eng, deps):
        need = {}
        for ev in deps:
            if ev is None:
                continue
            sem, val, src = ev
            key = sem.name
            if self.waited[eng].get(key, 0) >= val:
                continue
            if key not in need or need[key][1] < val:
                need[key] = (sem, val)
        for key, (sem, val) in need.items():
            self.waited[eng][key] = val
            self.ops[eng].append(("wait", sem, val))

    @staticmethod
    def _deps_for(reads, writes, extra):
        deps = list(extra or [])
        for r in reads or []:
            if r.w is not None:
                deps.append(r.w)
            if r.excl:
                deps.extend(r.r)
        for w in writes or []:
            if w.w is not None:
                deps.append(w.w)
            deps.extend(w.r)
        return deps

    @staticmethod
    def _commit(ev, reads, writes):
        for r in reads or []:
            r.r.append(ev)
        for w in writes or []:
            w.w = ev
            w.r = []

    def op(self, eng, fn, reads=None, writes=None, extra=None):
        return self.group(eng, [fn], reads, writes, extra)

    def group(self, eng, fns, reads=None, writes=None, extra=None):
        self.nops += 1
        if self.nops > self.limit:
            return None
        self._waits(eng, self._deps_for(reads, writes, extra))
        self.cnt[eng] += 1
        ev = (self.sem[eng], self.cnt[eng], eng)
        for f in fns[:-1]:
            self.ops[eng].append(("op", f, False))
        self.ops[eng].append(("op", fns[-1], True))
        self._commit(ev, reads, writes)
        self.last[eng] = ev
        return ev

    def dma(self, eng, out, in_, slot, reads=None, writes=None, extra=None):
        if slot not in self.dma_sems:
            self.dma_sems[slot] = self.stack.enter_context(self.nc.semaphore("d_" + slot))
            self.dma_cnt[slot] = 0
        self.nops += 1
        if self.nops > self.limit:
            return None
        self._waits(eng, self._deps_for(reads, writes, extra))
        self.dma_cnt[slot] += 16
        sem = self.dma_sems[slot]
        ev = (sem, self.dma_cnt[slot], "dma")
        self.ops[eng].append(("dma", out, in_, sem))
        self._commit(ev, reads, writes)
        return ev

    def raw(self, eng, fn):
        self.ops[eng].append(("raw", fn))

    def wait_all(self, eng, events):
        self._waits(eng, events)

    def run(self, block):
        sched = self

        def replay(engname, e):
            for item in sched.ops[engname]:
                if item[0] == "wait":
                    e.wait_ge(item[1], item[2])
                elif item[0] == "op":
                    ins = item[1](e)
                    if item[2]:
                        ins.then_inc(sched.sem[engname], 1)
                elif item[0] == "dma":
                    e.dma_start(out=item[1], in_=item[2]).then_inc(item[3], 16)
                elif item[0] == "raw":
                    item[1](e)

        @block.tensor
        def _(e):
            replay("pe", e)

        @block.scalar
        def _(e):
            replay("act", e)

        @block.vector
        def _(e):
            replay("dve", e)

        @block.gpsimd
        def _(e):
            replay("pool", e)

        @block.sync
        def _(e):
            replay("sp", e)


def MM(out, lhsT, rhs, start, stop):
    return lambda e: e.matmul(out, lhsT, rhs, start=start, stop=stop)


def TR(out, in_, ident):
    return lambda e: e.transpose(out, in_, ident)


def ACT(out, in_, func, **kw):
    return lambda e: e.activation(out=out, in_=in_, func=func, **kw)


def TT(out, a, b, op):
    return lambda e: e.tensor_tensor(out=out, in0=a, in1=b, op=op)


def TS(out, a, s1, op0, s2=None, op1=None):
    if op1 is None:
        return lambda e: e.tensor_scalar(out=out, in0=a, scalar1=s1, scalar2=None, op0=op0)
    return lambda e: e.tensor_scalar(out=out, in0=a, scalar1=s1, scalar2=s2, op0=op0, op1=op1)


def STT(out, a, s, b, op0, op1):
    return lambda e: e.scalar_tensor_tensor(out=out, in0=a, scalar=s, in1=b, op0=op0, op1=op1)


def CP(out, in_):
    return lambda e: e.tensor_copy(out=out, in_=in_)


def MEMSET(ap, v):
    return lambda e: e.memset(ap, v)


def RECIP(out, in_):
    return lambda e: e.reciprocal(out=out, in_=in_)


def SCAN(out, d0, d1, init):
    return lambda e: e.tensor_tensor_scan(out=out, data0=d0, data1=d1, initial=init, op0=ALU.mult, op1=ALU.add)


def build_program(nt1=NT1, do_phase2=True, debug=False, stop=9, p2mode="full"):
    nc = bass.Bass("TRN2", target_bir_lowering=False)

    def din(name, shape, dt=F32):
        return nc.dram_tensor(name, list(shape), dt, kind="ExternalInput").ap()

    ident_d = din("ident", [128, 128])
    if nt1 > 0:
        x_d = din("x", [SEQ, D])
        meta_d = din("meta", [NMETA, D])
        win_d = din("win", [D, 512])
        gpre_d = din("gpre", [128, 8])
        invf_d = din("invf", [128, 1])
        iota512_d = din("iota512", [512])
        tile_iota_d = din("tile_iota", [NT1])
        rmatT_d = din("rmatT", [128, 128])
        swap_d = din("swapm", [128, 128])
        sgn_d = din("sgn", [128, 1])
        tri_d = din("tri", [128, 128])
        gmask_d = din("gmask", [128, 8])
        lam_d = din("lamv", [4, 64])
        subg_d = din("subg", [128, 1])
        areT_d = din("areT", [64, 8])
        aimT_d = din("aimT", [64, 8])
        logdt_d = din("logdt", [8])
        bre_d = din("bre", [8, 64, 16])
        bim_d = din("bim", [8, 64, 16])
        creT_d = din("creT", [64, 128])
        cimT_d = din("cimT", [64, 128])
        dskip_d = din("dskip", [128, 1])
    if do_phase2:
        wglu_d = din("wglu", [512, 512])
        bglu_d = din("bglu", [128, 4])
        gssm_d = din("gssm", [128, 4])
        wout_d = din("wout", [D, D])
        gpost_d = din("gpost", [D])
        gffn_d = din("gffn8", [128, 8])
        wgate_d = din("wgate", [D, DFF])
        wup_d = din("wup", [D, DFF])
        wdown_d = din("wdown", [DFF, D])
        gpffn_d = din("gpffn", [D])
        xres_d = din("xres", [TOK2, D])
    out_d = nc.dram_tensor("out", [TOK2, D], F32, kind="ExternalOutput").ap() if do_phase2 else None

    xout_dbg_d = din("xout_dbg", [1024, TOK2]) if p2mode == "nogather" else None
    wprep = []
    if do_phase2:
        wg_b = nc.dram_tensor("wg_b", [D, DFF], BF16).ap()
        wu_b = nc.dram_tensor("wu_b", [D, DFF], BF16).ap()
        wd_b = nc.dram_tensor("wd_b", [DFF, D], BF16).ap()
        wo_b = nc.dram_tensor("wo_b", [D, D], BF16).ap()
        wgl_b = nc.dram_tensor("wgl_b", [512, 512], BF16).ap()
        wprep.append((wgl_b, wglu_d)); wprep.append((wo_b, wout_d))
        for hh in range(2):
            wprep.append((wg_b[512 * hh:512 * (hh + 1), :], wgate_d[512 * hh:512 * (hh + 1), :]))
            wprep.append((wu_b[512 * hh:512 * (hh + 1), :], wup_d[512 * hh:512 * (hh + 1), :]))
        for hh in range(2):
            wprep.append((wd_b[1408 * hh:1408 * (hh + 1), :], wdown_d[1408 * hh:1408 * (hh + 1), :]))
    wprep_events = []
    xin_t = [nc.dram_tensor("xchg_in%d" % q, [256, TOK2], BF16) for q in range(4)]
    xout_t = nc.dram_tensor("xchg_out", [4 * 1024, TOK2], BF16)
    xin = [t.ap() for t in xin_t]
    xout = xout_t.ap()
    dbg_d = None
    if debug:
        dbg_d = nc.dram_tensor("dbg", [256, SEQ], F32, kind="ExternalOutput").ap()

    with ExitStack() as top:
        S = Sched(nc, top)

        def ps_bank(name):
            return top.enter_context(nc.psum_tensor(name, [128, 512], F32))

        use_cc = do_phase2 and p2mode == "full"
        cc_sem = top.enter_context(nc.semaphore("cc_sem")) if use_cc else None
        q_events = [[] for _ in range(4)]

        def xchg_write(row0, lo, hi, src, src_col0, slot, res):
            evs = []
            c = lo
            while c < hi:
                q = c // TOK2
                ce = min(hi, TOK2 * (q + 1))
                ev = S.dma("sp", xin[q][row0:row0 + 128, c - TOK2 * q:ce - TOK2 * q],
                           src[:, src_col0 + (c - lo):src_col0 + (ce - lo)], slot, reads=[res])
                q_events[q].append(ev)
                evs.append(ev)
                c = ce
            return evs

        banks = [ps_bank("bank%d" % i) for i in range(8)]
        rbank = [Res("bank%d" % i, excl=True) for i in range(8)]

        def _phase1():
            with ExitStack() as p1:
                def sb(name, shape, dt=F32):
                    return p1.enter_context(nc.sbuf_tensor("a_" + name, list(shape), dt))

                ident_f = sb("ident_f", [128, 128]); r_ident_f = Res("ident_f")
                ident_b = sb("ident_b", [128, 128], BF16); r_ident_b = Res("ident_b")
                rmatT_b = sb("rmatT_b", [128, 128], BF16); r_rmat = Res("rmat")
                swap_f = sb("swap_f", [128, 128]); r_swap = Res("swap")
                tri_b = sb("tri_b", [128, 128], BF16); r_tri = Res("tri")
                ones_b = sb("ones_b", [128, 128], BF16); r_ones = Res("ones")
                sgn = sb("sgn", [128, 1]); r_sgn = Res("sgn")
                gmask = sb("gmask", [128, 8]); r_gmask = Res("gmask")
                invf = sb("invf", [128, 1]); r_invf = Res("invf")
                iota512 = sb("iota512", [128, 512]); r_iota = Res("iota512")
                tile_iota = sb("tile_iota", [128, NT1]); r_tiota = Res("tile_iota")
                gpre = sb("gpre", [128, 8]); r_gpre = Res("gpre")
                subg = sb("subg", [128, 1]); r_subg = Res("subg")
                dskip = sb("dskip", [128, 1]); r_dskip = Res("dskip")
                epsc = sb("epsc", [128, 1]); r_eps = Res("eps")
                lamv = sb("lamv", [128, 4, 64]); r_lamv = Res("lamv")
                neglam = sb("neglam", [128, 1]); r_neglam = Res("neglam")

                S.dma("sp", ident_f[:], ident_d, "c0", writes=[r_ident_f])
                S.dma("pool", ident_b[:], ident_d, "c1", writes=[r_ident_b])
                S.dma("pool", rmatT_b[:], rmatT_d, "c2", writes=[r_rmat])
                S.dma("sp", swap_f[:], swap_d, "c3", writes=[r_swap])
                S.dma("pool", tri_b[:], tri_d, "c4", writes=[r_tri])
                S.dma("sp", sgn[:], sgn_d, "c5", writes=[r_sgn])
                S.dma("sp", gmask[:], gmask_d, "c6", writes=[r_gmask])
                S.dma("sp", invf[:], invf_d, "c7", writes=[r_invf])
                S.dma("sp", iota512[:], iota512_d.partition_broadcast(128), "c8", writes=[r_iota])
                S.dma("sp", tile_iota[:], tile_iota_d.partition_broadcast(128), "c9", writes=[r_tiota])
                S.dma("sp", gpre[:], gpre_d, "c10", writes=[r_gpre])
                S.dma("sp", subg[:], subg_d, "c11", writes=[r_subg])
                S.dma("sp", dskip[:], dskip_d, "c12", writes=[r_dskip])
                S.dma("sp", lamv[:].rearrange("p a b -> p (a b)"),
                      lam_d.rearrange("a b -> (a b)").partition_broadcast(128), "c13", writes=[r_lamv])
                S.op("dve", MEMSET(ones_b[:], 1.0), writes=[r_ones])
                S.op("dve", MEMSET(epsc[:], EPS), writes=[r_eps])

                scr = [sb("scr%d" % i, [128, 512]) for i in range(4)]
                r_scr = [Res("scr%d" % i) for i in range(4)]
                ki_t = sb("ki_t", [128, 512], I32); r_ki = Res("ki")

                def range_reduce(eng, out_ap, arg_ap, n, r_out, r_arg):
                    kf = scr[3][:, 0:n]
                    S.op(eng, TS(ki_t[:, 0:n], arg_ap, 1.0 / TWO_PI, ALU.mult), reads=[r_arg], writes=[r_ki])
                    S.op(eng, CP(kf, ki_t[:, 0:n]), reads=[r_ki], writes=[r_scr[3]])
                    S.op(eng, STT(out_ap, kf, -C1, arg_ap, ALU.mult, ALU.add), reads=[r_scr[3], r_arg], writes=[r_out])
                    S.op(eng, STT(out_ap, kf, -C2, out_ap, ALU.mult, ALU.add), reads=[r_scr[3], r_out], writes=[r_out])
                    S.op(eng, TS(out_ap, out_ap, PI_LO, ALU.min, -PI_LO, ALU.max), reads=[r_out], writes=[r_out])

                def sincos(arg_ap, n, r_arg, sin_out, cos_out, r_sin, r_cos):
                    red = scr[2][:, 0:n]
                    range_reduce("dve", red, arg_ap, n, r_scr[2], r_arg)
                    S.op("act", ACT(sin_out, red, AF.Sin), reads=[r_scr[2]], writes=[r_sin])
                    hs = scr[1][:, 0:n]
                    S.op("act", ACT(hs, red, AF.Sin, scale=0.5), reads=[r_scr[2]], writes=[r_scr[1]])
                    S.op("dve", TT(hs, hs, hs, ALU.mult), reads=[r_scr[1]], writes=[r_scr[1]])
                    S.op("dve", TS(cos_out, hs, -2.0, ALU.mult, 1.0, ALU.add), reads=[r_scr[1]], writes=[r_cos])

                lt = sb("lt", [128, 2, 64]); r_lt = Res("lt")
                ls = sb("ls", [128, 2]); r_ls = Res("ls")
                S.op("dve", TT(lt[:, 0, :], lamv[:, 0, :], lamv[:, 1, :], ALU.mult), reads=[r_lamv], writes=[r_lt])
                S.op("dve", TT(lt[:, 1, :], lamv[:, 2, :], lamv[:, 3, :], ALU.mult), reads=[r_lamv, r_lt], writes=[r_lt])
                S.op("dve", lambda e: e.reduce_sum(out=ls[:], in_=lt[:], axis=mybir.AxisListType.X), reads=[r_lt], writes=[r_ls])
                S.op("act", ACT(ls[:], ls[:], AF.Exp), reads=[r_ls], writes=[r_ls])
                S.op("dve", TT(neglam[:], ls[:, 1:2], ls[:, 0:1], ALU.subtract), reads=[r_ls], writes=[r_neglam])
                S.op("dve", TS(neglam[:], neglam[:], -LAM_INIT, ALU.add), reads=[r_neglam], writes=[r_neglam])
                S.op("dve", TS(subg[:], subg[:], 1.0 - LAM_INIT, ALU.mult), reads=[r_subg], writes=[r_subg])

                Wb = sb("Wb", [128, 8, 512], BF16); r_Wb = Res("Wb")
                wstg = [scr[0], scr[1]]
                r_wstg = [r_scr[0], r_scr[1]]
                for k in range(8):
                    S.dma("sp", wstg[k % 2][:], win_d[128 * k:128 * (k + 1), :], "wstg%d" % (k % 2), writes=[r_wstg[k % 2]])
                    S.op("pool", TS(Wb[:, k, :], wstg[k % 2][:], gpre[:, k:k + 1], ALU.mult),
                         reads=[r_wstg[k % 2], r_gpre], writes=[r_Wb])

                C0 = sb("C0", [128, 512]); S0 = sb("S0", [128, 512]); r_C0 = Res("C0"); r_S0 = Res("S0")
                cI = sb("cI", [128, NT1]); sI = sb("sI", [128, NT1]); r_cI = Res("cI"); r_sI = Res("sI")
                S.op("dve", TS(scr[0][:], iota512[:], invf[:, 0:1], ALU.mult), reads=[r_iota, r_invf], writes=[r_scr[0]])
                sincos(scr[0][:], 512, r_scr[0], S0[:], C0[:], r_S0, r_C0)
                S.op("dve", TS(scr[0][:, 0:NT1], tile_iota[:], invf[:, 0:1], ALU.mult), reads=[r_tiota, r_invf], writes=[r_scr[0]])
                sincos(scr[0][:, 0:NT1], NT1, r_scr[0], sI[:], cI[:], r_sI, r_cI)
                nsI = sb("nsI", [128, NT1]); r_nsI = Res("nsI")
                S.op("dve", TS(nsI[:], sI[:], -1.0, ALU.mult), reads=[r_sI], writes=[r_nsI])

                are2 = sb("are2", [128, 8]); aim2 = sb("aim2", [128, 8]); dt2 = sb("dt2", [128, 8])
                r_are = Res("are2"); r_aim = Res("aim2"); r_dt = Res("dt2")
                S.dma("sp", are2[0:64, :], areT_d, "s0", writes=[r_are])
                S.dma("sp", are2[64:128, :], areT_d, "s0", writes=[r_are])
                S.dma("sp", aim2[0:64, :], aimT_d, "s1", writes=[r_aim])
                S.dma("sp", aim2[64:128, :], aimT_d, "s1", writes=[r_aim])
                S.dma("sp", dt2[:], logdt_d.partition_broadcast(128), "s2", writes=[r_dt])
                S.op("act", ACT(dt2[:], dt2[:], AF.Exp), reads=[r_dt], writes=[r_dt])
                rdec = sb("rdec", [128, 8]); r_rdec = Res("rdec")
                theta = sb("theta", [128, 8]); r_theta = Res("theta")
                S.op("dve", TT(rdec[:], are2[:], dt2[:], ALU.mult), reads=[r_are, r_dt], writes=[r_rdec])
                S.op("act", ACT(rdec[:], rdec[:], AF.Exp), reads=[r_rdec], writes=[r_rdec])
                S.op("dve", TT(theta[:], aim2[:], dt2[:], ALU.mult), reads=[r_aim, r_dt], writes=[r_theta])
                c1t = sb("c1t", [128, 8]); s1t = sb("s1t", [128, 8]); r_c1t = Res("c1t"); r_s1t = Res("s1t")
                cTt = sb("cTt", [128, 8]); sTt = sb("sTt", [128, 8]); r_cTt = Res("cTt"); r_sTt = Res("sTt")
                sincos(theta[:], 8, r_theta, s1t[:], c1t[:], r_s1t, r_c1t)
                S.op("dve", TS(scr[0][:, 0:8], theta[:], 512.0, ALU.mult), reads=[r_theta], writes=[r_scr[0]])
                sincos(scr[0][:, 0:8], 8, r_scr[0], sTt[:], cTt[:], r_sTt, r_cTt)
                S.op("dve", TT(sTt[:], sTt[:], sgn[:, 0:1].to_broadcast([128, 8]), ALU.mult), reads=[r_sTt, r_sgn], writes=[r_sTt])
                COSg = sb("COSg", [128, 8, 512]); SINg = sb("SINg", [128, 8, 512])
                r_COSg = [Res("COSg%d" % g) for g in range(8)]; r_SINg = [Res("SINg%d" % g) for g in range(8)]
                for g in range(8):
                    S.op("dve", TS(scr[0][:], iota512[:], theta[:, g:g + 1], ALU.mult), reads=[r_iota, r_theta], writes=[r_scr[0]])
                    sincos(scr[0][:], 512, r_scr[0], SINg[:, g, :], COSg[:, g, :], r_SINg[g], r_COSg[g])
                ROT = sb("ROT", [128, 8, 128]); r_ROT = Res("ROT")
                for g in range(8):
                    S.op("dve", TS(ROT[:, g, :], ident_f[:], cTt[:, g:g + 1], ALU.mult), reads=[r_ident_f, r_cTt], writes=[r_ROT])
                    S.op("dve", STT(ROT[:, g, :], swap_f[:], sTt[:, g:g + 1], ROT[:, g, :], ALU.mult, ALU.add),
                         reads=[r_swap, r_sTt, r_ROT], writes=[r_ROT])
                fre = sb("fre", [128, 8]); fim = sb("fim", [128, 8]); r_fre = Res("fre"); r_fim = Res("fim")
                nr = sb("nr", [128, 8]); ni = sb("ni", [128, 8]); den = sb("den", [128, 8]); tmp8 = sb("tmp8", [128, 8])
                r_nr = Res("nr"); r_ni = Res("ni"); r_den = Res("den"); r_tmp8 = Res("tmp8")
                S.op("dve", TT(nr[:], rdec[:], c1t[:], ALU.mult), reads=[r_rdec, r_c1t], writes=[r_nr])
                S.op("dve", TS(nr[:], nr[:], -1.0, ALU.add), reads=[r_nr], writes=[r_nr])
                S.op("dve", TT(ni[:], rdec[:], s1t[:], ALU.mult), reads=[r_rdec, r_s1t], writes=[r_ni])
                S.op("dve", TT(den[:], are2[:], are2[:], ALU.mult), reads=[r_are], writes=[r_den])
                S.op("dve", TT(tmp8[:], aim2[:], aim2[:], ALU.mult), reads=[r_aim], writes=[r_tmp8])
                S.op("dve", TT(den[:], den[:], tmp8[:], ALU.add), reads=[r_den, r_tmp8], writes=[r_den])
                S.op("dve", RECIP(den[:], den[:]), reads=[r_den], writes=[r_den])
                S.op("dve", TT(fre[:], nr[:], are2[:], ALU.mult), reads=[r_nr, r_are], writes=[r_fre])
                S.op("dve", TT(tmp8[:], ni[:], aim2[:], ALU.mult), reads=[r_ni, r_aim], writes=[r_tmp8])
                S.op("dve", TT(fre[:], fre[:], tmp8[:], ALU.add), reads=[r_fre, r_tmp8], writes=[r_fre])
                S.op("dve", TT(fre[:], fre[:], den[:], ALU.mult), reads=[r_fre, r_den], writes=[r_fre])
                S.op("dve", TT(fim[:], ni[:], are2[:], ALU.mult), reads=[r_ni, r_are], writes=[r_fim])
                S.op("dve", TT(tmp8[:], nr[:], aim2[:], ALU.mult), reads=[r_nr, r_aim], writes=[r_tmp8])
                S.op("dve", TT(fim[:], fim[:], tmp8[:], ALU.subtract), reads=[r_fim, r_tmp8], writes=[r_fim])
                S.op("dve", TT(fim[:], fim[:], den[:], ALU.mult), reads=[r_fim, r_den], writes=[r_fim])
                Bre = sb("Bre", [64, 8, 16]); Bim = sb("Bim", [64, 8, 16]); r_Bre = Res("Bre"); r_Bim = Res("Bim")
                S.dma("sp", Bre[:], bre_d.rearrange("g p h -> p g h"), "s3", writes=[r_Bre])
                S.dma("sp", Bim[:], bim_d.rearrange("g p h -> p g h"), "s4", writes=[r_Bim])
                BBre = sb("BBre", [64, 8, 16]); BBim = sb("BBim", [64, 8, 16]); tB = sb("tB", [64, 8, 16])
                r_BBre = Res("BBre"); r_BBim = Res("BBim"); r_tB = Res("tB")
                fre_b = fre[0:64, :].unsqueeze(2).to_broadcast([64, 8, 16])
                fim_b = fim[0:64, :].unsqueeze(2).to_broadcast([64, 8, 16])
                S.op("dve", TT(BBre[:], Bre[:], fre_b, ALU.mult), reads=[r_Bre, r_fre], writes=[r_BBre])
                S.op("dve", TT(tB[:], Bim[:], fim_b, ALU.mult), reads=[r_Bim, r_fim], writes=[r_tB])
                S.op("dve", TT(BBre[:], BBre[:], tB[:], ALU.subtract), reads=[r_BBre, r_tB], writes=[r_BBre])
                S.op("dve", TT(BBim[:], Bim[:], fre_b, ALU.mult), reads=[r_Bim, r_fre], writes=[r_BBim])
                S.op("dve", TT(tB[:], Bre[:], fim_b, ALU.mult), reads=[r_Bre, r_fim], writes=[r_tB])
                S.op("dve", TT(BBim[:], BBim[:], tB[:], ALU.add), reads=[r_BBim, r_tB], writes=[r_BBim])
                FB1 = sb("FB1", [128, 128]); FB2 = sb("FB2", [128, 128]); r_FB1 = Res("FB1"); r_FB2 = Res("FB2")
                S.group("pe", [TR(banks[0][:, 0:64], BBre[:].rearrange("p g h -> p (g h)"), ident_f[0:64, 0:64]),
                               TR(banks[0][:, 64:128], BBim[:].rearrange("p g h -> p (g h)"), ident_f[0:64, 0:64])],
                        reads=[r_BBre, r_BBim, r_ident_f], writes=[rbank[0]])
                S.op("dve", CP(FB1[:], banks[0][:, 0:128]), reads=[rbank[0]], writes=[r_FB1])
                S.op("dve", CP(FB2[:, 0:64], banks[0][:, 64:128]), reads=[rbank[0]], writes=[r_FB2])
                S.op("dve", TS(FB2[:, 64:128], banks[0][:, 0:64], -1.0, ALU.mult), reads=[rbank[0], r_FB2], writes=[r_FB2])
                LB1 = sb("LB1", [128, 8, 128], BF16); LB2 = sb("LB2", [128, 8, 128], BF16); r_LB = Res("LB")
                for g in range(8):
                    S.op("pool", TS(LB1[:, g, :], FB1[:], gmask[:, g:g + 1], ALU.mult), reads=[r_FB1, r_gmask], writes=[r_LB])
                    S.op("pool", TS(LB2[:, g, :], FB2[:], gmask[:, g:g + 1], ALU.mult), reads=[r_FB2, r_gmask], writes=[r_LB])
                CC1 = sb("CC1", [128, 128]); CC2 = sb("CC2", [128, 128]); r_CC1 = Res("CC1"); r_CC2 = Res("CC2")
                S.dma("sp", CC1[0:64, :], creT_d, "s5", writes=[r_CC1])
                S.dma("sp", CC1[64:128, :], cimT_d, "s5", writes=[r_CC1])
                S.dma("sp", CC2[0:64, :], cimT_d, "s6", writes=[r_CC2])
                S.dma("sp", CC2[64:128, :], creT_d, "s6", writes=[r_CC2])
                S.op("dve", TS(CC1[64:128, :], CC1[64:128, :], -1.0, ALU.mult), reads=[r_CC1], writes=[r_CC1])
                S.op("dve", TS(CC2[:], CC2[:], -1.0, ALU.mult), reads=[r_CC2], writes=[r_CC2])
                LC1 = sb("LC1", [128, 8, 128], BF16); LC2 = sb("LC2", [128, 8, 128], BF16); r_LC = Res("LC")
                S.op("pool", MEMSET(LC1[:], 0.0), writes=[r_LC])
                S.op("pool", MEMSET(LC2[:], 0.0), writes=[r_LC])
                for g in range(8):
                    S.op("dve", CP(LC1[:, g, 16 * g:16 * g + 16], CC1[:, 16 * g:16 * g + 16]), reads=[r_CC1], writes=[r_LC])
                    S.op("dve", CP(LC2[:, g, 16 * g:16 * g + 16], CC2[:, 16 * g:16 * g + 16]), reads=[r_CC2], writes=[r_LC])
                diagD = sb("diagD", [128, 128], BF16); r_diagD = Res("diagD")
                S.op("dve", TS(diagD[:], ident_f[:], dskip[:, 0:1], ALU.mult), reads=[r_ident_f, r_dskip], writes=[r_diagD])
                s0 = sb("s0", [128, 2, 8]); r_s0 = [Res("s0_0"), Res("s0_1")]
                S.op("dve", MEMSET(s0[:], 0.0), writes=r_s0)

                xt = [sb("xt%d" % i, [128, D]) for i in range(2)]; r_xt = [Res("xt%d" % i) for i in range(2)]
                junk = sb("junk", [128, D], BF16); r_junk = Res("junk")
                ssq = sb("ssq", [128, 4]); r_ssq = [Res("ssq%d" % i) for i in range(4)]
                xs = [sb("xs%d" % i, [128, D], BF16) for i in range(2)]; r_xs = [Res("xs%d" % i) for i in range(2)]
                hT = [sb("hT%d" % i, [128, 8, 512], BF16) for i in range(2)]; r_hT = [Res("hT%d" % i) for i in range(2)]
                COSt = sb("COSt", [128, 512]); SINt = sb("SINt", [128, 512]); r_COSt = Res("COSt"); r_SINt = Res("SINt")
                qraw = sb("qraw", [128, 512], BF16); r_qraw = Res("qraw")
                rt1 = sb("rt1", [128, 512]); rt2 = sb("rt2", [128, 512]); r_rt1 = Res("rt1"); r_rt2 = Res("rt2")
                QT = [sb("QT%d" % i, [128, 512], BF16) for i in range(2)]; r_QT = [Res("QT%d" % i) for i in range(2)]
                KTa = sb("KTa", [128, LP], BF16); KTb = sb("KTb", [128, LP], BF16)
                r_KT = [Res("KT%d" % i) for i in range(NT1)]
                r_KTz = Res("KTz")
                S.op("pool", MEMSET(KTa[64:128, :], 0.0), writes=[r_KTz])
                S.op("pool", MEMSET(KTb[0:64, :], 0.0), writes=[r_KTz])
                V = sb("V", [128, NBLK, 128], BF16); r_V = [Res("V%d" % i) for i in range(NT1)]
                uT = [sb("uT%d" % i, [128, 512], BF16) for i in range(2)]; r_uT = [Res("uT%d" % i) for i in range(2)]
                st1 = [sb("st1_%d" % i, [128, 512]) for i in range(2)]; r_st1 = [Res("st1_%d" % i) for i in range(2)]
                st2 = [sb("st2_%d" % i, [128, 512]) for i in range(2)]; r_st2 = [Res("st2_%d" % i) for i in range(2)]
                bt = [sb("bt%d" % i, [128, 512]) for i in range(2)]; r_bt = [Res("bt%d" % i) for i in range(2)]
                Xs = [sb("Xs%d" % i, [128, 512]) for i in range(3)]; r_Xs = [Res("Xs%d" % i) for i in range(3)]
                P1s = [sb("P1s%d" % i, [128, 512], BF16) for i in range(3)]; r_P1s = [Res("P1s%d" % i) for i in range(3)]
                P2s = [sb("P2s%d" % i, [128, 512], BF16) for i in range(3)]; r_P2s = [Res("P2s%d" % i) for i in range(3)]
                ybuf = [sb("ybuf%d" % i, [128, 512], BF16) for i in range(2)]; r_ybuf = [Res("ybuf%d" % i) for i in range(2)]
                Pb = [sb("Pb%d" % i, [128, 2, 256], BF16) for i in range(3)]; r_Pb = [Res("Pb%d" % i) for i in range(3)]
                rL = sb("rL", [128, 2, 256]); r_rL = Res("rL")
                Osb = sb("Osb", [128, 2, 256]); r_Osb = Res("Osb")
                On = Osb; r_On = r_Osb
                pending_fin = []
                od = sb("od", [128, 256]); r_od = Res("od")
                osq = sb("osq", [128, 256], BF16); r_osq = Res("osq")
                orstd = sb("orstd", [128, 256]); r_orstd = Res("orstd")
                obuf = [sb("obuf%d" % i, [128, 256], BF16) for i in range(2)]; r_obuf = [Res("obuf%d" % i) for i in range(2)]

                F0, F1, F2, F3 = banks[0], banks[1], banks[2], banks[3]
                rF0, rF1, rF2, rF3 = rbank[0], rbank[1], rbank[2], rbank[3]
                Sb = [banks[4], banks[5]]; rSb = [rbank[4], rbank[5]]
                Ob, Lb = banks[6], banks[7]; rOb, rLb = rbank[6], rbank[7]
                F0b = F0[:].bitcast(BF16)


                pcount = [0]
                sbcount = [0]
                out_events = []

                def front_a(ti):
                    nb = 4 if ti < 16 else 1
                    hb = ti % 2
                    for j in range(nb):
                        s = 4 * ti + j
                        xb = sbcount[0] % 2
                        sq = sbcount[0] % 4
                        sbcount[0] += 1
                        if s == 0:
                            S.dma("sp", xt[xb][0:16, :], meta_d, "x%d" % xb, writes=[r_xt[xb]])
                            S.dma("sp", xt[xb][16:128, :], x_d[0:112, :], "x%d" % xb, writes=[r_xt[xb]])
                        elif s == 64:
                            S.op("pool", MEMSET(xt[xb][:], 0.0), writes=[r_xt[xb]])
                            S.dma("sp", xt[xb][0:16, :], x_d[SEQ - 16:SEQ, :], "x%d" % xb, writes=[r_xt[xb]])
                        else:
                            S.dma("sp", xt[xb][:], x_d[128 * s - 16:128 * s + 112, :], "x%d" % xb, writes=[r_xt[xb]])
                        S.op("act", ACT(junk[:], xt[xb][:], AF.Square, accum_out=ssq[:, sq:sq + 1]),
                             reads=[r_xt[xb]], writes=[r_junk, r_ssq[sq]])
                        S.op("act", ACT(ssq[:, sq:sq + 1], ssq[:, sq:sq + 1], AF.Ln, scale=1.0 / D, bias=epsc[:, 0:1]),
                             reads=[r_ssq[sq], r_eps], writes=[r_ssq[sq]])
                        S.op("act", ACT(ssq[:, sq:sq + 1], ssq[:, sq:sq + 1], AF.Exp, scale=-0.5),
                             reads=[r_ssq[sq]], writes=[r_ssq[sq]])
                        S.op("act", ACT(xs[xb][:], xt[xb][:], AF.Copy, scale=ssq[:, sq:sq + 1]),
                             reads=[r_xt[xb], r_ssq[sq]], writes=[r_xs[xb]])
                        yield
                        S.group("pe", [TR(F0b[:, 128 * k:128 * (k + 1)], xs[xb][:, 128 * k:128 * (k + 1)], ident_b[:]) for k in range(8)],
                                reads=[r_xs[xb], r_ident_b], writes=[rF0])
                        S.op("dve", CP(hT[hb][:, :, 128 * j:128 * (j + 1)], F0b.rearrange("p (k t) -> p k t", k=8)),
                             reads=[rF0], writes=[r_hT[hb]])
                        yield

                def front_bs(ti):
                    nb = 4 if ti < 16 else 1
                    N = 128 * nb
                    tok0 = 512 * ti
                    hb = ti % 2
                    S.op("dve", TS(COSt[:], C0[:], cI[:, ti:ti + 1], ALU.mult), reads=[r_C0, r_cI], writes=[r_COSt])
                    S.op("dve", STT(COSt[:], S0[:], nsI[:, ti:ti + 1], COSt[:], ALU.mult, ALU.add), reads=[r_S0, r_nsI, r_COSt], writes=[r_COSt])
                    S.op("dve", TS(SINt[:], S0[:], cI[:, ti:ti + 1], ALU.mult), reads=[r_S0, r_cI], writes=[r_SINt])
                    S.op("dve", STT(SINt[:], C0[:], sI[:, ti:ti + 1], SINt[:], ALU.mult, ALU.add), reads=[r_C0, r_sI, r_SINt], writes=[r_SINt])
                    yield

                    def proj_fm(c, bank, rb):
                        S.group("pe", [MM(bank[:, 0:N], Wb[:, k, 128 * c:128 * (c + 1)], hT[hb][:, k, 0:N], k == 0, k == 7) for k in range(8)],
                                reads=[r_Wb, r_hT[hb]], writes=[rb])

                    def rope_a():
                        S.op("act", ACT(qraw[:, 0:N], F1[:, 0:N], AF.Copy), reads=[rF1], writes=[r_qraw])
                        S.op("pe", MM(F2[:, 0:N], rmatT_b[:], qraw[:, 0:N], True, True), reads=[r_rmat, r_qraw], writes=[rF2])

                    def rope_b(dst_ap, r_dst, dst2=None):
                        S.op("dve", TT(rt1[:, 0:N], F1[:, 0:N], COSt[:, 0:N], ALU.mult), reads=[rF1, r_COSt], writes=[r_rt1])
                        S.op("dve", TT(rt2[:, 0:N], F2[:, 0:N], SINt[:, 0:N], ALU.mult), reads=[rF2, r_SINt], writes=[r_rt2])
                        if dst2 is None:
                            S.op("pool", TT(dst_ap, rt1[:, 0:N], rt2[:, 0:N], ALU.add), reads=[r_rt1, r_rt2], writes=[r_dst])
                        else:
                            S.op("pool", TT(dst_ap[0:64, :], rt1[0:64, 0:N], rt2[0:64, 0:N], ALU.add), reads=[r_rt1, r_rt2, r_KTz], writes=[r_dst])
                            S.op("pool", TT(dst2[64:128, :], rt1[64:128, 0:N], rt2[64:128, 0:N], ALU.add), reads=[r_rt1, r_rt2, r_KTz], writes=[r_dst])

                    proj_fm(0, F1, rF1)
                    yield
                    rope_a()
                    yield
                    rope_b(QT[hb][:, 0:N], r_QT[hb])
                    yield
                    proj_fm(1, F1, rF1)
                    yield
                    rope_a()
                    yield
                    rope_b(KTa[:, tok0:tok0 + N], r_KT[ti], KTb[:, tok0:tok0 + N])
                    yield
                    proj_fm(3, F1, rF1)
                    yield
                    S.op("act", ACT(uT[hb][:, 0:N], F1[:, 0:N], AF.Copy), reads=[rF1], writes=[r_uT[hb]])
                    fns = []
                    for j in range(nb):
                        for k in range(8):
                            fns.append(MM(F2[:, 128 * j:128 * (j + 1)], hT[hb][:, k, 128 * j:128 * (j + 1)], Wb[:, k, 256:384], k == 0, k == 7))
                    S.group("pe", fns, reads=[r_Wb, r_hT[hb]], writes=[rF2])
                    yield
                    S.op("act", ACT(V[:, 4 * ti:4 * ti + nb, :], F2[:, 0:N].rearrange("p (j d) -> p j d", j=nb), AF.Copy),
                         reads=[rF2], writes=[r_V[ti]])
                    yield
                    par = ti % 2
                    yb = ti % 2

                    def emit_y(g, pb):
                        S.group("pe", [MM(F3[:, 0:N], LC1[:, g, :], P1s[pb][:, 0:N], g == 0, False),
                                       MM(F3[:, 0:N], LC2[:, g, :], P2s[pb][:, 0:N], False, False)],
                                reads=[r_LC, r_P1s[pb], r_P2s[pb]], writes=[rF3])

                    def stage_b(g):
                        pb = g % 3; q = g % 2
                        S.op("dve", SCAN(Xs[pb][:, 0:N], rdec[:, g:g + 1].to_broadcast([128, N]), bt[q][:, 0:N], s0[:, par, g:g + 1]),
                             reads=[r_rdec, r_bt[q], r_s0[par]], writes=[r_Xs[pb]])
                        S.op("pool", TT(P1s[pb][:, 0:N], Xs[pb][:, 0:N], COSg[:, g, 0:N], ALU.mult), reads=[r_Xs[pb], r_COSg[g]], writes=[r_P1s[pb]])
                        S.op("dve", TT(P2s[pb][:, 0:N], Xs[pb][:, 0:N], SINg[:, g, 0:N], ALU.mult), reads=[r_Xs[pb], r_SINg[g]], writes=[r_P2s[pb]])

                    def stage_c(g):
                        pb = g % 3
                        if ti + 1 < nt1:
                            S.op("pe", MM(F0[:, g:g + 1], ROT[:, g, :], Xs[pb][:, N - 1:N], True, True), reads=[r_ROT, r_Xs[pb]], writes=[rF0])
                            S.op("act", ACT(s0[:, 1 - par, g:g + 1], F0[:, g:g + 1], AF.Copy), reads=[rF0], writes=[r_s0[1 - par]])
                        emit_y(g, pb)

                    for it in range(11):
                        if 3 <= it:
                            stage_c(it - 3)
                            yield
                        if it < 8:
                            g = it; q = g % 2
                            S.op("pe", MM(F1[:, 0:N], LB1[:, g, :], uT[hb][:, 0:N], True, True), reads=[r_LB, r_uT[hb]], writes=[rF1])
                            S.op("pe", MM(F2[:, 0:N], LB2[:, g, :], uT[hb][:, 0:N], True, True), reads=[r_LB, r_uT[hb]], writes=[rF2])
                            yield
                            S.op("dve", TT(st1[q][:, 0:N], F1[:, 0:N], COSg[:, g, 0:N], ALU.mult), reads=[rF1, r_COSg[g]], writes=[r_st1[q]])
                            S.op("dve", TT(st2[q][:, 0:N], F2[:, 0:N], SINg[:, g, 0:N], ALU.mult), reads=[rF2, r_SINg[g]], writes=[r_st2[q]])
                            S.op("pool", TT(bt[q][:, 0:N], st1[q][:, 0:N], st2[q][:, 0:N], ALU.add), reads=[r_st1[q], r_st2[q]], writes=[r_bt[q]])
                        if 1 <= it <= 8:
                            stage_b(it - 1)
                        if it <= 8:
                            yield
                    S.op("pe", MM(F3[:, 0:N], diagD[:], uT[hb][:, 0:N], False, True), reads=[r_diagD, r_uT[hb]], writes=[rF3])
                    yield
                    S.op("act", ACT(ybuf[yb][:, 0:N], F3[:, 0:N], AF.Gelu), reads=[rF3], writes=[r_ybuf[yb]])
                    lo = max(tok0, NMETA); hi = min(tok0 + N, L)
                    out_events.extend(xchg_write(128, lo - NMETA, hi - NMETA, ybuf[yb], lo - tok0, "yo%d" % yb, r_ybuf[yb]))
                    yield

                def attention(ti, pump):
                    nb = 4 if ti < 16 else 1
                    tok0 = 512 * ti
                    hb = ti % 2
                    nqt = 2 if nb == 4 else 1
                    nbq = 2 if nb == 4 else 1
                    NQ = 128 * nbq
                    QTv = QT[hb]
                    for qi in range(nqt):
                        c = 2 * ti + qi
                        q0 = 256 * qi
                        nkb = 2 * c + nbq

                        def qk(kb):
                            j = kb - 2 * c
                            qs = 128 * j if j > 0 else 0
                            kt = kb // 4
                            sbk = kb % 2
                            pbk = kb % 3
                            Sv = Sb[sbk][:, 0:2 * NQ].rearrange("p (h q) -> p h q", h=2)
                            Pv = Pb[pbk][:].rearrange("p h q -> p (h q)")[:, 0:2 * NQ].rearrange("p (h q) -> p h q", h=2)
                            S.group("pe", [MM(Sv[:, 0, qs:NQ], KTa[:, 128 * kb:128 * (kb + 1)], QTv[:, q0 + qs:q0 + NQ], True, True),
                                           MM(Sv[:, 1, qs:NQ], KTb[:, 128 * kb:128 * (kb + 1)], QTv[:, q0 + qs:q0 + NQ], True, True)],
                                    reads=[r_KT[kt], r_QT[hb]], writes=[rSb[sbk]])
                            S.op("act", ACT(Pv[:, :, qs:NQ], Sv[:, :, qs:NQ], AF.Exp, scale=0.125), reads=[rSb[sbk]], writes=[r_Pb[pbk]])
                            if j >= 0:
                                S.op("pool", TT(Pv[:, :, qs:qs + 128], Pv[:, :, qs:qs + 128],
                                                tri_b[:].unsqueeze(1).to_broadcast([128, 2, 128]), ALU.mult),
                                     reads=[r_Pb[pbk], r_tri], writes=[r_Pb[pbk]])

                        def pv(kb):
                            j = kb - 2 * c
                            qs = 128 * j if j > 0 else 0
                            kt = kb // 4
                            pbk = kb % 3
                            Pv = Pb[pbk][:].rearrange("p h q -> p (h q)")[:, 0:2 * NQ].rearrange("p (h q) -> p h q", h=2)
                            Ov = Ob[:, 0:2 * NQ].rearrange("p (h q) -> p h q", h=2)
                            Lv = Lb[:, 0:2 * NQ].rearrange("p (h q) -> p h q", h=2)
                            first = (kb == 0); lastk = (kb == nkb - 1)
                            if qs == 0:
                                Pf = Pb[pbk][:].rearrange("p h q -> p (h q)")[:, 0:2 * NQ]
                                S.op("pe", MM(Ob[:, 0:2 * NQ], V[:, kb, :], Pf, first, lastk), reads=[r_V[kt], r_Pb[pbk]], writes=[rOb])
                                S.op("pe", MM(Lb[:, 0:2 * NQ], ones_b[:], Pf, first, lastk), reads=[r_ones, r_Pb[pbk]], writes=[rLb])
                            else:
                                assert not first
                                S.group("pe", [MM(Ov[:, 0, qs:NQ], V[:, kb, :], Pv[:, 0, qs:NQ], False, False),
                                               MM(Ov[:, 1, qs:NQ], V[:, kb, :], Pv[:, 1, qs:NQ], False, lastk)],
                                        reads=[r_V[kt], r_Pb[pbk]], writes=[rOb])
                                S.group("pe", [MM(Lv[:, 0, qs:NQ], ones_b[:], Pv[:, 0, qs:NQ], False, False),
                                               MM(Lv[:, 1, qs:NQ], ones_b[:], Pv[:, 1, qs:NQ], False, lastk)],
                                        reads=[r_ones, r_Pb[pbk]], writes=[rLb])

                        qk(0)
                        for kb in range(nkb):
                            if kb + 1 < nkb:
                                qk(kb + 1)
                            pv(kb)
                            if kb == 2 and pending_fin:
                                pending_fin.pop(0)()
                            pump()
                        while pending_fin:
                            pending_fin.pop(0)()
                        Of = Ob[:, 0:2 * NQ]; Lf = Lb[:, 0:2 * NQ]
                        Osf = Osb[:].rearrange("p h q -> p (h q)")[:, 0:2 * NQ]
                        rLf = rL[:].rearrange("p h q -> p (h q)")[:, 0:2 * NQ]
                        Onf = On[:].rearrange("p h q -> p (h q)")[:, 0:2 * NQ]
                        Onv = Onf.rearrange("p (h q) -> p h q", h=2)
                        S.op("dve", CP(Osf, Of), reads=[rOb], writes=[r_Osb])
                        S.op("act", ACT(rLf, Lf, AF.Ln), reads=[rLb], writes=[r_rL])
                        S.op("act", ACT(rLf, rLf, AF.Exp, scale=-1.0), reads=[r_rL], writes=[r_rL])
                        S.op("dve", TT(Onf, Osf, rLf, ALU.mult), reads=[r_Osb, r_rL], writes=[r_On])
                        S.op("dve", STT(od[:, 0:NQ], Onv[:, 1, :], neglam[:, 0:1], Onv[:, 0, :], ALU.mult, ALU.add),
                             reads=[r_On, r_neglam], writes=[r_od])
                        S.op("act", ACT(osq[:, 0:NQ], od[:, 0:NQ], AF.Square), reads=[r_od], writes=[r_osq])

                        def fin2(c=c, NQ=NQ, q0=q0, tok0=tok0):
                            S.op("pe", MM(F0[:, 256:256 + NQ], ones_b[:], osq[:, 0:NQ], True, True), reads=[r_ones, r_osq], writes=[rF0])
                            S.op("act", ACT(orstd[:, 0:NQ], F0[:, 256:256 + NQ], AF.Ln, scale=1.0 / 128, bias=epsc[:, 0:1]),
                                 reads=[rF0, r_eps], writes=[r_orstd])
                            S.op("act", ACT(orstd[:, 0:NQ], orstd[:, 0:NQ], AF.Exp, scale=-0.5), reads=[r_orstd], writes=[r_orstd])
                            ob = c % 2
                            S.op("dve", STT(obuf[ob][:, 0:NQ], od[:, 0:NQ], subg[:, 0:1], orstd[:, 0:NQ], ALU.mult, ALU.mult),
                                 reads=[r_od, r_subg, r_orstd], writes=[r_obuf[ob]])
                            qt0 = tok0 + q0
                            lo = max(qt0, NMETA); hi = min(qt0 + NQ, L)
                            out_events.extend(xchg_write(0, lo - NMETA, hi - NMETA, obuf[ob], lo - qt0, "oo%d" % ob, r_obuf[ob]))
                        pending_fin.append(fin2)

                def run_all(g):
                    for _ in g:
                        pass
                run_all(front_a(0)); run_all(front_bs(0))
                if nt1 > 1:
                    run_all(front_a(1))
                for ti in range(nt1):
                    gens = []
                    if ti + 1 < nt1:
                        gens.append(front_bs(ti + 1))
                    if ti + 2 < nt1:
                        gens.append(front_a(ti + 2))
                    nbq_t = 2 if ti < 16 else 1
                    total_kb = sum(2 * (2 * ti + qi) + nbq_t for qi in range(2 if ti < 16 else 1))
                    state = {"steps_left": (FRONT_STEPS if ti + 1 < nt1 else 0) + (8 if ti + 2 < nt1 else 0),
                             "kb_left": total_kb, "gens": gens, "rr": 0}

                    def pump(state=state):
                        if not state["gens"]:
                            return
                        kbl = max(state["kb_left"], 1)
                        n = -(-state["steps_left"] // kbl)
                        state["kb_left"] -= 1
                        for _ in range(n):
                            if not state["gens"]:
                                return
                            state["rr"] += 1
                            idx = 1 if (len(state["gens"]) > 1 and state["rr"] % 5 == 0) else 0
                            try:
                                next(state["gens"][idx])
                                state["steps_left"] -= 1
                            except StopIteration:
                                state["gens"].pop(idx)
                    attention(ti, pump)
                    for g_ in state["gens"]:
                        run_all(g_)
                    if wprep:
                        dst_, src_ = wprep.pop(0)
                        wprep_events.append(S.dma("pool", dst_, src_, "wprep"))
                    if ti == nt1 - 1 or (ti >= 4 and ti % 4 == 0):
                        while pending_fin:
                            pending_fin.pop(0)()
                    if use_cc and ti >= 4 and ti % 4 == 0:
                        qd = ti // 4 - 1
                        S.wait_all("pool", q_events[qd])
                        S.raw("pool", (lambda qd: (lambda e: e.collective_compute(
                            "AllGather", ALU.bypass, replica_groups=[[0, 1, 2, 3], [4, 5, 6, 7]],
                            ins=[xin_t[qd].ap().opt()], outs=[xout_t.ap()[1024 * qd:1024 * (qd + 1), :].opt()]).then_inc(cc_sem)))(qd))

            return out_events

        out_events = _phase1() if nt1 > 0 else []

        if debug:
            print('MARKS', S.marks, S.nops)
            S.limit = 10**9
            S.wait_all("pool", out_events)
            evs = [S.dma("pool", dbg_d[:, TOK2 * q:TOK2 * (q + 1)], xin[q], "dbg") for q in range(4)]
            S.wait_all("pool", evs)
        if do_phase2:
            S.wait_all("pool", out_events)
            if p2mode == "nogather":
                evd = [S.dma("pool", xout[1024 * (t_ // 2):1024 * (t_ // 2 + 1), 1024 * (t_ % 2):1024 * (t_ % 2) + T2],
                             xout_dbg_d[:, T2 * t_:T2 * (t_ + 1)], "xdbg") for t_ in range(NT2)]
                S.wait_all("pool", evd)
            S.cnt["pool"] += 1
            gather_ev = (S.sem["pool"], S.cnt["pool"], "pool")
            S.raw("pool", lambda e: e.sem_inc(S.sem["pool"], 1))
            lasts = [S.last[e] for e in ("pe", "act", "dve", "pool") if S.last[e] is not None] + [gather_ev]
            for e in ("pe", "act", "dve", "sp", "pool"):
                S.wait_all(e, lasts)

            with ExitStack() as p2:
                def sb(name, shape, dt=F32):
                    return p2.enter_context(nc.sbuf_tensor("b_" + name, list(shape), dt))

                Wg = sb("Wg", [128, 8, DFF], BF16); r_Wg = Res("Wg")
                Wu = sb("Wu", [128, 8, DFF], BF16); r_Wu = Res("Wu")
                Wd = sb("Wd", [128, NF, D], BF16); r_Wd = Res("Wd")
                Wo = sb("Wo", [128, 8, D], BF16); r_Wo = Res("Wo")
                Wgl = sb("Wgl", [128, 4, 512], BF16); r_Wgl = Res("Wgl")
                ident2 = sb("ident2", [128, 128], BF16); r_id2 = Res("ident2")
                ones2 = sb("ones2", [128, 128], BF16); r_ones2 = Res("ones2")
                eps2 = sb("eps2", [128, 1]); r_eps2 = Res("eps2")
                bglu = sb("bglu", [128, 4]); r_bglu = Res("bglu")
                gssm = sb("gssm", [128, 4]); r_gssm = Res("gssm")
                gpost_b = sb("gpost_b", [128, D]); r_gpost = Res("gpost")
                gffn8 = sb("gffn8", [128, 8]); r_gffn = Res("gffn")
                gpffn_b = sb("gpffn_b", [128, D]); r_gpffn = Res("gpffn")
                S.dma("pool", ident2[:], ident_d, "k0", writes=[r_id2])
                S.op("dve", MEMSET(ones2[:], 1.0), writes=[r_ones2])
                S.op("dve", MEMSET(eps2[:], EPS), writes=[r_eps2])
                S.dma("sp", bglu[:], bglu_d, "k1", writes=[r_bglu])
                S.dma("sp", gssm[:], gssm_d, "k2", writes=[r_gssm])
                S.dma("sp", gpost_b[:], gpost_d.partition_broadcast(128), "k3", writes=[r_gpost])
                S.dma("sp", gffn8[:], gffn_d, "k4", writes=[r_gffn])
                S.dma("sp", gpffn_b[:], gpffn_d.partition_broadcast(128), "k5", writes=[r_gpffn])

                first_loads = []
                cT = [sb("cT%d" % i, [128, 8, T2], BF16) for i in range(1)]; r_cT = [Res("cT%d" % i) for i in range(1)]
                sig = sb("sig", [128, T2], BF16); r_sig = Res("sig")
                yg = sb("yg", [128, 4, T2], BF16); r_yg = Res("yg")
                sq2 = sb("sq2", [128, T2], BF16); r_sq2 = Res("sq2")
                rsy = sb("rsy", [128, T2]); r_rsy = Res("rsy")
                yn = sb("yn", [128, 4, T2], BF16); r_yn = Res("yn")
                hres = [sb("hres%d" % i, [128, D]) for i in range(3)]; r_hres = [Res("hres%d" % i) for i in range(3)]
                ssvA = sb("ssvA", [128, 8]); r_ssvA = Res("ssvA")
                ssvB = sb("ssvB", [128, 4]); r_ssvB = Res("ssvB")
                tmpA = sb("tmpA", [128, 512]); r_tmpA = Res("tmpA")
                tmpB = tmpA; r_tmpB = r_tmpA
                hs = [sb("hs0", [128, D], BF16)] * 2; r_hs = [Res("hs0")] * 2
                junkA = sb("junkA", [128, 512], BF16); r_junkA = Res("junkA")
                junkB = junkA; r_junkB = r_junkA
                hT2 = [sb("hT2_%d" % i, [128, 8, T2], BF16) for i in range(2)]; r_hT2 = [Res("hT2_%d" % i) for i in range(2)]
                sg = [sb("sg%d" % i, [128, T2], BF16) for i in range(2)]; r_sg = [Res("sg%d" % i) for i in range(2)]
                aT = sb("aT", [128, NF, T2], BF16); r_aT = Res("aT")

                pid = [None]
                B = banks
                rB = rbank
                final_events = []

                def h1_gen(tt):
                    cb = 0

                    def _ld(e, tt=tt):
                        if pid[0] is None:
                            pid[0] = e.partition_id() % 4
                        col = pid[0] * T2 + 1024 * (tt % 2)
                        return e.dma_start(out=cT[cb][:], in_=xout[1024 * (tt // 2):1024 * (tt // 2 + 1), bass.ds(col, T2)].rearrange("(k p) t -> p k t", p=128))
                    slot = "g%d" % cb
                    if slot not in S.dma_sems:
                        S.dma_sems[slot] = top.enter_context(nc.semaphore("d_" + slot))
                        S.dma_cnt[slot] = 0
                    S._waits("sp", S._deps_for(None, [r_cT[cb]], None))
                    if use_cc:
                        S.raw("sp", (lambda v: (lambda e: e.wait_ge(cc_sem, v)))(tt // 2 + 1))
                    S.dma_cnt[slot] += 16
                    ev = (S.dma_sems[slot], S.dma_cnt[slot], "dma")
                    S.raw("sp", (lambda sem: (lambda e, f=_ld: f(e).then_inc(sem, 16)))(S.dma_sems[slot]))
                    S._commit(ev, None, [r_cT[cb]])
                    if tt == 0:
                        first_loads.append(ev)
                        for s in range(2):
                            first_loads.append(S.dma("sp", hres[s][:], xres_d[128 * s:128 * (s + 1), :], "xr%d" % s, writes=[r_hres[s]]))
                    yield None
                    for m in range(4):
                        S.group("pe", [MM(B[0][:, 0:T2], Wgl[:, r, 128 * m:128 * (m + 1)], cT[cb][:, 2 * r + 1, :], r == 0, r == 3) for r in range(4)],
                                reads=[r_Wgl, r_cT[cb]], writes=[rB[0]])
                        yield None
                        S.op("act", ACT(sig[:], B[0][:, 0:T2], AF.Sigmoid, bias=bglu[:, m:m + 1]), reads=[rB[0], r_bglu], writes=[r_sig])
                        S.op("dve", TT(yg[:, m, :], cT[cb][:, 2 * m + 1, :], sig[:], ALU.mult), reads=[r_cT[cb], r_sig], writes=[r_yg])
                        S.op("act", ACT(sq2[:], yg[:, m, :], AF.Square), reads=[r_yg], writes=[r_sq2])
                        yield None
                        yield None
                        S.op("pe", MM(B[1][:, 0:T2], ones2[:], sq2[:], m == 0, m == 3), reads=[r_ones2, r_sq2], writes=[rB[1]])
                    yield None
                    S.op("act", ACT(rsy[:], B[1][:, 0:T2], AF.Ln, scale=1.0 / 512, bias=eps2[:, 0:1]), reads=[rB[1], r_eps2], writes=[r_rsy])
                    S.op("act", ACT(rsy[:], rsy[:], AF.Exp, scale=-0.5), reads=[r_rsy], writes=[r_rsy])
                    for m in range(4):
                        S.op("dve", STT(yn[:, m, :], yg[:, m, :], gssm[:, m:m + 1], rsy[:], ALU.mult, ALU.mult),
                             reads=[r_yg, r_gssm, r_rsy], writes=[r_yn])
                    yield None
                    yield None
                    yield None
                    for s in range(2):
                        hr = hres[(2 * tt + s) % 3]; r_hr = r_hres[(2 * tt + s) % 3]
                        c0 = 4 * s
                        for hf in range(2):
                            bk = B[2 + hf]; rbk = rB[2 + hf]
                            fns = []
                            for k in range(8):
                                src = cT[cb][:, k, 128 * s:128 * (s + 1)] if k % 2 == 0 else yn[:, k // 2, 128 * s:128 * (s + 1)]
                                fns.append(MM(bk[:, :], src, Wo[:, k, 512 * hf:512 * (hf + 1)], k == 0, k == 7))
                            S.group("pe", fns, reads=[r_cT[cb], r_yn, r_Wo], writes=[rbk])
                            yield None
                            S.op("act", ACT(junkA[:], bk[:, :], AF.Square, accum_out=ssvA[:, c0 + hf:c0 + hf + 1]), reads=[rbk], writes=[r_junkA, r_ssvA])
                        yield None
                        S.op("dve", TT(ssvA[:, c0 + 2:c0 + 3], ssvA[:, c0:c0 + 1], ssvA[:, c0 + 1:c0 + 2], ALU.add), reads=[r_ssvA], writes=[r_ssvA])
                        S.op("act", ACT(ssvA[:, c0 + 2:c0 + 3], ssvA[:, c0 + 2:c0 + 3], AF.Ln, scale=1.0 / D, bias=eps2[:, 0:1]), reads=[r_ssvA, r_eps2], writes=[r_ssvA])
                        S.op("act", ACT(ssvA[:, c0 + 2:c0 + 3], ssvA[:, c0 + 2:c0 + 3], AF.Exp, scale=-0.5), reads=[r_ssvA], writes=[r_ssvA])
                        yield ("BAR1" if s == 1 else None)
                        row0 = tt * T2 + 128 * s
                        if tt > 0:
                            S.dma("sp", hr[:], xres_d[row0:row0 + 128, :], "xr%d" % ((2 * tt + s) % 3), writes=[r_hr])
                        yield None
                        for hf in range(2):
                            bk = B[2 + hf]; rbk = rB[2 + hf]
                            S.op("dve", STT(tmpA[:], bk[:, :], ssvA[:, c0 + 2:c0 + 3], gpost_b[:, 512 * hf:512 * (hf + 1)], ALU.mult, ALU.mult),
                                 reads=[rbk, r_ssvA, r_gpost], writes=[r_tmpA])
                            S.op("dve", TT(hr[:, 512 * hf:512 * (hf + 1)], tmpA[:], hr[:, 512 * hf:512 * (hf + 1)], ALU.add),
                                 reads=[r_tmpA, r_hr], writes=[r_hr])
                            S.op("act", ACT(junkA[:], hr[:, 512 * hf:512 * (hf + 1)], AF.Square, accum_out=ssvA[:, c0 + hf:c0 + hf + 1]),
                                 reads=[r_hr], writes=[r_junkA, r_ssvA])
                        yield None
                        yield None
                        S.op("dve", TT(ssvA[:, c0 + 3:c0 + 4], ssvA[:, c0:c0 + 1], ssvA[:, c0 + 1:c0 + 2], ALU.add), reads=[r_ssvA], writes=[r_ssvA])
                        S.op("act", ACT(ssvA[:, c0 + 3:c0 + 4], ssvA[:, c0 + 3:c0 + 4], AF.Ln, scale=1.0 / D, bias=eps2[:, 0:1]), reads=[r_ssvA, r_eps2], writes=[r_ssvA])
                        S.op("act", ACT(ssvA[:, c0 + 3:c0 + 4], ssvA[:, c0 + 3:c0 + 4], AF.Exp, scale=-0.5), reads=[r_ssvA], writes=[r_ssvA])
                        yield None
                        S.op("act", ACT(hs[s][:], hr[:], AF.Copy, scale=ssvA[:, c0 + 3:c0 + 4]), reads=[r_hr, r_ssvA], writes=[r_hs[s]])
                        yield None
                        yield None
                        yield None
                        B1b = B[1][:].bitcast(BF16)
                        S.group("pe", [TR(B1b[:, 128 * k:128 * (k + 1)], hs[s][:, 128 * k:128 * (k + 1)], ident2[:]) for k in range(8)],
                                reads=[r_hs[s], r_id2], writes=[rB[1]])
                        yield None
                        S.op("dve", TT(hT2[tt % 2][:, :, 128 * s:128 * (s + 1)], B1b.rearrange("p (k t) -> p k t", k=8),
                                       gffn8[:].unsqueeze(2).to_broadcast([128, 8, 128]), ALU.mult),
                             reads=[rB[1], r_gffn], writes=[r_hT2[tt % 2]])
                        yield None

                class Pump:
                    def __init__(self, gen):
                        self.gen = gen
                        self.released = set()
                        self.pending = None

                    def release(self, name):
                        self.released.add(name)

                    def step(self, n):
                        for _ in range(n):
                            if self.gen is None:
                                return
                            if self.pending is not None:
                                if self.pending not in self.released:
                                    return
                                self.pending = None
                            try:
                                r = next(self.gen)
                            except StopIteration:
                                self.gen = None
                                return
                            if r is not None:
                                self.pending = r

                    def flush(self):
                        while self.gen is not None:
                            if self.pending is not None:
                                assert self.pending in self.released, self.pending
                            self.step(1)

                def h2(tt, P):
                    for f in range(NF):
                        gb2 = B[4 + (f % 2)]; rgb2 = rB[4 + (f % 2)]
                        gv = gb2[:].rearrange("p (h q) -> p h q", h=2)
                        h2t = hT2[tt % 2]; r_h2t = r_hT2[tt % 2]
                        S.group("pe", [MM(gv[:, 0, :], Wg[:, k, 128 * f:128 * (f + 1)], h2t[:, k, :], k == 0, k == 7) for k in range(8)],
                                reads=[r_Wgc[f], r_h2t], writes=[rgb2])
                        P.step(1)
                        S.group("pe", [MM(gv[:, 1, :], Wu[:, k, 128 * f:128 * (f + 1)], h2t[:, k, :], k == 0, k == 7) for k in range(8)],
                                reads=[r_Wuc[f], r_h2t], writes=[rgb2])
                        S.op("act", ACT(sg[f % 2][:], gv[:, 0, :], AF.Silu), reads=[rgb2], writes=[r_sg[f % 2]])
                        S.op("dve", TT(aT[:, f, :], gv[:, 1, :], sg[f % 2][:], ALU.mult), reads=[rgb2, r_sg[f % 2]], writes=[r_aT])
                        P.step(1)
                    for s in range(2):
                        row0 = tt * T2 + 128 * s
                        hr = hres[(2 * tt + s) % 3]; r_hr = r_hres[(2 * tt + s) % 3]
                        for hf in range(2):
                            bk = B[6 + hf]; rbk = rB[6 + hf]
                            S.group("pe", [MM(bk[:, :], aT[:, f, 128 * s:128 * (s + 1)], Wd[:, f, 512 * hf:512 * (hf + 1)], f == 0, f == NF - 1) for f in range(NF)],
                                    reads=[r_aT] + r_Wdc, writes=[rbk])
                            S.op("act", ACT(junkB[:], bk[:, :], AF.Square, accum_out=ssvB[:, hf:hf + 1]), reads=[rbk], writes=[r_junkB, r_ssvB])
                            P.step(6)
                        S.op("dve", TT(ssvB[:, 2:3], ssvB[:, 0:1], ssvB[:, 1:2], ALU.add), reads=[r_ssvB], writes=[r_ssvB])
                        S.op("act", ACT(ssvB[:, 2:3], ssvB[:, 2:3], AF.Ln, scale=1.0 / D, bias=eps2[:, 0:1]), reads=[r_ssvB, r_eps2], writes=[r_ssvB])
                        S.op("act", ACT(ssvB[:, 2:3], ssvB[:, 2:3], AF.Exp, scale=-0.5), reads=[r_ssvB], writes=[r_ssvB])
                        for hf in range(2):
                            bk = B[6 + hf]; rbk = rB[6 + hf]
                            S.op("dve", STT(tmpB[:], bk[:, :], ssvB[:, 2:3], gpffn_b[:, 512 * hf:512 * (hf + 1)], ALU.mult, ALU.mult),
                                 reads=[rbk, r_ssvB, r_gpffn], writes=[r_tmpB])
                            S.op("dve", TT(hr[:, 512 * hf:512 * (hf + 1)], tmpB[:], hr[:, 512 * hf:512 * (hf + 1)], ALU.add),
                                 reads=[r_tmpB, r_hr], writes=[r_hr])
                        final_events.append(S.dma("sp", out_d[row0:row0 + 128, :], hr[:], "fo%d" % ((2 * tt + s) % 3), reads=[r_hr]))
                        if s == 0:
                            P.release("BAR1")

                P0 = Pump(h1_gen(0)); P0.release("BAR1"); P0.step(1)
                while wprep:
                    dst_, src_ = wprep.pop(0)
                    wprep_events.append(S.dma("pool", dst_, src_, "wprep"))
                S.dma("act", Wgl[:], wgl_b.rearrange("(k p) c -> p k c", p=128), "w0", writes=[r_Wgl], extra=wprep_events + first_loads)
                S.dma("act", Wo[:], wo_b.rearrange("(k p) c -> p k c", p=128), "w1", writes=[r_Wo], extra=wprep_events + first_loads)
                r_Wgc = [Res("Wg%d" % f) for f in range(NF)]
                r_Wuc = [Res("Wu%d" % f) for f in range(NF)]
                r_Wdc = [Res("Wd%d" % f) for f in range(NF)]
                for f in range(0, NF, 2):
                    S.dma("act", Wg[:, :, 128 * f:128 * (f + 2)], wg_b[:, 128 * f:128 * (f + 2)].rearrange("(k p) c -> p k c", p=128),
                          "wg%d" % (f // 2), writes=[r_Wgc[f], r_Wgc[f + 1]], extra=wprep_events + first_loads)
                    S.dma("act", Wu[:, :, 128 * f:128 * (f + 2)], wu_b[:, 128 * f:128 * (f + 2)].rearrange("(k p) c -> p k c", p=128),
                          "wu%d" % (f // 2), writes=[r_Wuc[f], r_Wuc[f + 1]], extra=wprep_events + first_loads)
                for f in range(0, NF, 2):
                    S.dma("act", Wd[:, f:f + 2, :], wd_b[128 * f:128 * (f + 2), :].rearrange("(k p) c -> p k c", p=128),
                          "wd%d" % (f // 2), writes=[r_Wdc[f], r_Wdc[f + 1]], extra=wprep_events + first_loads)
                P0.flush()
                for tt in range(NT2):
                    P = Pump(h1_gen(tt + 1) if tt + 1 < NT2 else None)
                    h2(tt, P)
                    P.flush()

                S.wait_all("sp", final_events)
                if use_cc:
                    S.raw("pool", lambda e: e.wait_ge(cc_sem, 4))

        with nc.Block() as block:
            S.run(block)
    return nc


def _consts():
    invf8 = (500000.0 ** (-np.arange(0, 16, 2, dtype=np.float32) / np.float32(16))).astype(np.float32)
    invf = np.zeros((128, 1), np.float32)
    for base in (0, 64):
        for i in range(16):
            invf[base + i, 0] = invf8[i % 8]
    R = np.zeros((128, 128), np.float32)
    for base in (0, 64):
        for i in range(8):
            R[base + i, base + i + 8] = -1.0
            R[base + i + 8, base + i] = 1.0
    rmatT = np.ascontiguousarray(R.T)
    ident = np.eye(128, dtype=np.float32)
    swapm = np.zeros((128, 128), np.float32)
    for k in range(128):
        swapm[k, (k + 64) % 128] = 1.0
    sgn = np.ones((128, 1), np.float32); sgn[64:] = -1.0
    tri = (np.arange(128)[:, None] <= np.arange(128)[None, :]).astype(np.float32)
    gmask = (np.arange(128)[:, None] // 16 == np.arange(8)[None, :]).astype(np.float32)
    return dict(invf=invf, rmatT=rmatT, ident=ident, swapm=swapm, sgn=sgn, tri=tri, gmask=gmask,
                iota512=np.arange(512, dtype=np.float32), tile_iota=(512.0 * np.arange(NT1)).astype(np.float32))


def make_in_maps(inp):
    c = _consts()
    f = lambda a: np.ascontiguousarray(np.asarray(a, dtype=np.float32))
    w_in = f(inp["w_in"])[0]
    maps = []
    w_out = f(inp["w_out"])[0]
    perm = []
    for r in range(4):
        perm += list(range(128 * r, 128 * r + 128)) + list(range(512 + 128 * r, 512 + 128 * r + 128))
    w_out_p = np.ascontiguousarray(w_out[perm, :])
    for core in range(8):
        b, h = core // 4, core % 4
        cols = (list(range(64 * h, 64 * h + 64)) + list(range(256 + 64 * h, 256 + 64 * h + 64)) +
                list(range(512 + 64 * h, 512 + 64 * h + 64)) + list(range(768 + 64 * h, 768 + 64 * h + 64)) +
                list(range(1024 + 128 * h, 1024 + 128 * h + 128)) + list(range(1536 + 128 * h, 1536 + 128 * h + 128)))
        gs = slice(8 * h, 8 * h + 8)
        m = dict(c)
        m["x"] = f(inp["x"][b])
        m["meta"] = f(inp["meta"])
        m["win"] = np.ascontiguousarray(w_in[:, cols])
        m["gpre"] = np.ascontiguousarray(f(inp["pre_mix_g"])[0].reshape(8, 128).T)
        m["lamv"] = np.stack([f(inp["lambda_q1"])[0], f(inp["lambda_k1"])[0], f(inp["lambda_q2"])[0], f(inp["lambda_k2"])[0]])
        m["subg"] = f(inp["subln_g"])[0].reshape(128, 1)
        m["areT"] = np.ascontiguousarray(f(inp["a_re"])[0, gs].T)
        m["aimT"] = np.ascontiguousarray(f(inp["a_im"])[0, gs].T)
        m["logdt"] = f(inp["log_dt"])[0, gs]
        m["bre"] = f(inp["b_re"])[0, gs]
        m["bim"] = f(inp["b_im"])[0, gs]
        m["creT"] = np.ascontiguousarray(f(inp["c_re"])[0, gs].transpose(2, 0, 1).reshape(64, 128))
        m["cimT"] = np.ascontiguousarray(f(inp["c_im"])[0, gs].transpose(2, 0, 1).reshape(64, 128))
        m["dskip"] = f(inp["d_skip"])[0, 128 * h:128 * h + 128].reshape(128, 1)
        m["wglu"] = f(inp["w_glu"])[0]
        m["bglu"] = np.ascontiguousarray(f(inp["b_glu"])[0].reshape(4, 128).T)
        m["gssm"] = np.ascontiguousarray(f(inp["ssm_out_g"])[0].reshape(4, 128).T)
        m["wout"] = w_out_p
        m["gpost"] = f(inp["post_mix_g"])[0]
        m["gffn8"] = np.ascontiguousarray(f(inp["pre_ffn_g"])[0].reshape(8, 128).T)
        m["wgate"] = f(inp["w_gate"])[0]
        m["wup"] = f(inp["w_up"])[0]
        m["wdown"] = f(inp["w_down"])[0]
        m["gpffn"] = f(inp["post_ffn_g"])[0]
        m["xres"] = np.ascontiguousarray(np.concatenate([inp["x"][b, T2 * (4 * tt + h):T2 * (4 * tt + h + 1)] for tt in range(NT2)], axis=0), dtype=np.float32)
        maps.append(m)
    return maps


def kernel(**inputs):
    nc = build_program()
    maps = make_in_maps(inputs)
    res = run_bass_kernel_spmd(nc, maps, core_ids=list(range(8)))
    out = np.zeros((2, SEQ, D), np.float32)
    for core in range(8):
        b, h = core // 4, core % 4
        o = res.results[core]["out"]
        for tt in range(NT2):
            out[b, T2 * (4 * tt + h):T2 * (4 * tt + h + 1)] = o[T2 * tt:T2 * (tt + 1)]
    return out
```

```python
import math
import os
from contextlib import ExitStack

import numpy as np
import concourse.bass as bass
import concourse.mybir as mybir
from concourse.bass_utils import run_bass_kernel_spmd

F32 = mybir.dt.float32
BF16 = mybir.dt.bfloat16
I32 = mybir.dt.int32
AF = mybir.ActivationFunctionType
ALU = mybir.AluOpType

D = 1024
SEQ = 8192
NMETA = 16
L = SEQ + NMETA
LP = 8320
NBLK = 65
NT1 = 17
DFF = 2816
NF = 22
EPS = 1e-6
LAM_INIT = 0.8 - 0.6 * math.exp(0.0)
TWO_PI = 2.0 * math.pi
C1 = 6.28125
C2 = TWO_PI - C1
PI_LO = 3.141592
TOK2 = 2048
T2 = 256
NT2 = TOK2 // T2
FRONT_STEPS = 40


class Res:
    __slots__ = ("name", "w", "r", "excl")

    def __init__(self, name, excl=False):
        self.name = name
        self.w = None
        self.r = []
        self.excl = excl


class Sched:
    ENGS = ("pe", "act", "dve", "pool", "sp")

    def __init__(self, nc, stack):
        self.nc = nc
        self.ops = {e: [] for e in self.ENGS}
        self.sem = {e: stack.enter_context(nc.semaphore("s_" + e)) for e in self.ENGS}
        self.cnt = {e: 0 for e in self.ENGS}
        self.waited = {e: {} for e in self.ENGS}
        self.stack = stack
        self.dma_sems = {}
        self.dma_cnt = {}
        self.last = {e: None for e in self.ENGS}
        self.limit = int(os.environ.get("K_LIMIT", "1000000000"))
        self.nops = 0
        self.marks = []

    def _waits(self, eng, deps):
        need = {}
        for ev in deps:
            if ev is None:
                continue
            sem, val, src = ev
            key = sem.name
            if self.waited[eng].get(key, 0) >= val:
                continue
            if key not in need or need[key][1] < val:
                need[key] = (sem, val)
        for key, (sem, val) in need.items():
            self.waited[eng][key] = val
            self.ops[eng].append(("wait", sem, val))

    @staticmethod
    def _deps_for(reads, writes, extra):
        deps = list(extra or [])
        for r in reads or []:
            if r.w is not None:
                deps.append(r.w)
            if r.excl:
                deps.extend(r.r)
        for w in writes or []:
            if w.w is not None:
                deps.append(w.w)
            deps.extend(w.r)
        return deps

    @staticmethod
    def _commit(ev, reads, writes):
        for r in reads or []:
            r.r.append(ev)
        for w in writes or []:
            w.w = ev
            w.r = []

    def op(self, eng, fn, reads=None, writes=None, extra=None):
        return self.group(eng, [fn], reads, writes, extra)

    def group(self, eng, fns, reads=None, writes=None, extra=None):
        self.nops += 1
        if self.nops > self.limit:
            return None
        self._waits(eng, self._deps_for(reads, writes, extra))
        self.cnt[eng] += 1
        ev = (self.sem[eng], self.cnt[eng], eng)
        for f in fns[:-1]:
            self.ops[eng].append(("op", f, False))
        self.ops[eng].append(("op", fns[-1], True))
        self._commit(ev, reads, writes)
        self.last[eng] = ev
        return ev

    def dma(self, eng, out, in_, slot, reads=None, writes=None, extra=None):
        if slot not in self.dma_sems:
            self.dma_sems[slot] = self.stack.enter_context(self.nc.semaphore("d_" + slot))
            self.dma_cnt[slot] = 0
        self.nops += 1
        if self.nops > self.limit:
            return None
        self._waits(eng, self._deps_for(reads, writes, extra))
        self.dma_cnt[slot] += 16
        sem = self.dma_sems[slot]
        ev = (sem, self.dma_cnt[slot], "dma")
        self.ops[eng].append(("dma", out, in_, sem))
        self._commit(ev, reads, writes)
        return ev

    def raw(self, eng, fn):
        self.ops[eng].append(("raw", fn))

    def wait_all(self, eng, events):
        self._waits(eng, events)

    def run(self, block):
        sched = self

        def replay(engname, e):
            for item in sched.ops[engname]:
                if item[0] == "wait":
                    e.wait_ge(item[1], item[2])
                elif item[0] == "op":
                    ins = item[1](e)
                    if item[2]:
                        ins.then_inc(sched.sem[engname], 1)
                elif item[0] == "dma":
                    e.dma_start(out=item[1], in_=item[2]).then_inc(item[3], 16)
                elif item[0] == "raw":
                    item[1](e)

        @block.tensor
        def _(e):
            replay("pe", e)

        @block.scalar
        def _(e):
            replay("act", e)

        @block.vector
        def _(e):
            replay("dve", e)

        @block.gpsimd
        def _(e):
            replay("pool", e)

        @block.sync
        def _(e):
            replay("sp", e)


def MM(out, lhsT, rhs, start, stop):
    return lambda e: e.matmul(out, lhsT, rhs, start=start, stop=stop)


def TR(out, in_, ident):
    return lambda e: e.transpose(out, in_, ident)


def ACT(out, in_, func, **kw):
    return lambda e: e.activation(out=out, in_=in_, func=func, **kw)


def TT(out, a, b, op):
    return lambda e: e.tensor_tensor(out=out, in0=a, in1=b, op=op)


def TS(out, a, s1, op0, s2=None, op1=None):
    if op1 is None:
        return lambda e: e.tensor_scalar(out=out, in0=a, scalar1=s1, scalar2=None, op0=op0)
    return lambda e: e.tensor_scalar(out=out, in0=a, scalar1=s1, scalar2=s2, op0=op0, op1=op1)


def STT(out, a, s, b, op0, op1):
    return lambda e: e.scalar_tensor_tensor(out=out, in0=a, scalar=s, in1=b, op0=op0, op1=op1)


def CP(out, in_):
    return lambda e: e.tensor_copy(out=out, in_=in_)


def MEMSET(ap, v):
    return lambda e: e.memset(ap, v)


def RECIP(out, in_):
    return lambda e: e.reciprocal(out=out, in_=in_)


def SCAN(out, d0, d1, init):
    return lambda e: e.tensor_tensor_scan(out=out, data0=d0, data1=d1, initial=init, op0=ALU.mult, op1=ALU.add)


def build_program(nt1=NT1, do_phase2=True, debug=False, stop=9, p2mode="full"):
    nc = bass.Bass("TRN2", target_bir_lowering=False)

    def din(name, shape, dt=F32):
        return nc.dram_tensor(name, list(shape), dt, kind="ExternalInput").ap()

    ident_d = din("ident", [128, 128])
    if nt1 > 0:
        x_d = din("x", [SEQ, D])
        meta_d = din("meta", [NMETA, D])
        win_d = din("win", [D, 512])
        gpre_d = din("gpre", [128, 8])
        invf_d = din("invf", [128, 1])
        iota512_d = din("iota512", [512])
        tile_iota_d = din("tile_iota", [NT1])
        rmatT_d = din("rmatT", [128, 128])
        swap_d = din("swapm", [128, 128])
        sgn_d = din("sgn", [128, 1])
        tri_d = din("tri", [128, 128])
        gmask_d = din("gmask", [128, 8])
        lam_d = din("lamv", [4, 64])
        subg_d = din("subg", [128, 1])
        areT_d = din("areT", [64, 8])
        aimT_d = din("aimT", [64, 8])
        logdt_d = din("logdt", [8])
        bre_d = din("bre", [8, 64, 16])
        bim_d = din("bim", [8, 64, 16])
        creT_d = din("creT", [64, 128])
        cimT_d = din("cimT", [64, 128])
        dskip_d = din("dskip", [128, 1])
    if do_phase2:
        wglu_d = din("wglu", [512, 512])
        bglu_d = din("bglu", [128, 4])
        gssm_d = din("gssm", [128, 4])
        wout_d = din("wout", [D, D])
        gpost_d = din("gpost", [D])
        gffn_d = din("gffn8", [128, 8])
        wgate_d = din("wgate", [D, DFF])
        wup_d = din("wup", [D, DFF])
        wdown_d = din("wdown", [DFF, D])
        gpffn_d = din("gpffn", [D])
        xres_d = din("xres", [TOK2, D])
    out_d = nc.dram_tensor("out", [TOK2, D], F32, kind="ExternalOutput").ap() if do_phase2 else None

    xout_dbg_d = din("xout_dbg", [1024, TOK2]) if p2mode == "nogather" else None
    wprep = []
    if do_phase2:
        wg_b = nc.dram_tensor("wg_b", [D, DFF], BF16).ap()
        wu_b = nc.dram_tensor("wu_b", [D, DFF], BF16).ap()
        wd_b = nc.dram_tensor("wd_b", [DFF, D], BF16).ap()
        wo_b = nc.dram_tensor("wo_b", [D, D], BF16).ap()
        wgl_b = nc.dram_tensor("wgl_b", [512, 512], BF16).ap()
        wprep.append((wgl_b, wglu_d)); wprep.append((wo_b, wout_d))
        for hh in range(2):
            wprep.append((wg_b[512 * hh:512 * (hh + 1), :], wgate_d[512 * hh:512 * (hh + 1), :]))
            wprep.append((wu_b[512 * hh:512 * (hh + 1), :], wup_d[512 * hh:512 * (hh + 1), :]))
        for hh in range(2):
            wprep.append((wd_b[1408 * hh:1408 * (hh + 1), :], wdown_d[1408 * hh:1408 * (hh + 1), :]))
    wprep_events = []
    xin_t = [nc.dram_tensor("xchg_in%d" % q, [256, TOK2], BF16) for q in range(4)]
    xout_t = nc.dram_tensor("xchg_out", [4 * 1024, TOK2], BF16)
    xin = [t.ap() for t in xin_t]
    xout = xout_t.ap()
    dbg_d = None
    if debug:
        dbg_d = nc.dram_tensor("dbg", [256, SEQ], F32, kind="ExternalOutput").ap()

    with ExitStack() as top:
        S = Sched(nc, top)

        def ps_bank(name):
            return top.enter_context(nc.psum_tensor(name, [128, 512], F32))

        use_cc = do_phase2 and p2mode == "full"
        cc_sem = top.enter_context(nc.semaphore("cc_sem")) if use_cc else None
        q_events = [[] for _ in range(4)]

        def xchg_write(row0, lo, hi, src, src_col0, slot, res):
            evs = []
            c = lo
            while c < hi:
                q = c // TOK2
                ce = min(hi, TOK2 * (q + 1))
                ev = S.dma("sp", xin[q][row0:row0 + 128, c - TOK2 * q:ce - TOK2 * q],
                           src[:, src_col0 + (c - lo):src_col0 + (ce - lo)], slot, reads=[res])
                q_events[q].append(ev)
                evs.append(ev)
                c = ce
            return evs

        banks = [ps_bank("bank%d" % i) for i in range(8)]
        rbank = [Res("bank%d" % i, excl=True) for i in range(8)]

        def _phase1():
            with ExitStack() as p1:
                def sb(name, shape, dt=F32):
                    return p1.enter_context(nc.sbuf_tensor("a_" + name, list(shape), dt))

                ident_f = sb("ident_f", [128, 128]); r_ident_f = Res("ident_f")
                ident_b = sb("ident_b", [128, 128], BF16); r_ident_b = Res("ident_b")
                rmatT_b = sb("rmatT_b", [128, 128], BF16); r_rmat = Res("rmat")
                swap_f = sb("swap_f", [128, 128]); r_swap = Res("swap")
                tri_b = sb("tri_b", [128, 128], BF16); r_tri = Res("tri")
                ones_b = sb("ones_b", [128, 128], BF16); r_ones = Res("ones")
                sgn = sb("sgn", [128, 1]); r_sgn = Res("sgn")
                gmask = sb("gmask", [128, 8]); r_gmask = Res("gmask")
                invf = sb("invf", [128, 1]); r_invf = Res("invf")
                iota512 = sb("iota512", [128, 512]); r_iota = Res("iota512")
                tile_iota = sb("tile_iota", [128, NT1]); r_tiota = Res("tile_iota")
                gpre = sb("gpre", [128, 8]); r_gpre = Res("gpre")
                subg = sb("subg", [128, 1]); r_subg = Res("subg")
                dskip = sb("dskip", [128, 1]); r_dskip = Res("dskip")
                epsc = sb("epsc", [128, 1]); r_eps = Res("eps")
                lamv = sb("lamv", [128, 4, 64]); r_lamv = Res("lamv")
                neglam = sb("neglam", [128, 1]); r_neglam = Res("neglam")

                S.dma("sp", ident_f[:], ident_d, "c0", writes=[r_ident_f])
                S.dma("pool", ident_b[:], ident_d, "c1", writes=[r_ident_b])
                S.dma("pool", rmatT_b[:], rmatT_d, "c2", writes=[r_rmat])
                S.dma("sp", swap_f[:], swap_d, "c3", writes=[r_swap])
                S.dma("pool", tri_b[:], tri_d, "c4", writes=[r_tri])
                S.dma("sp", sgn[:], sgn_d, "c5", writes=[r_sgn])
                S.dma("sp", gmask[:], gmask_d, "c6", writes=[r_gmask])
                S.dma("sp", invf[:], invf_d, "c7", writes=[r_invf])
                S.dma("sp", iota512[:], iota512_d.partition_broadcast(128), "c8", writes=[r_iota])
                S.dma("sp", tile_iota[:], tile_iota_d.partition_broadcast(128), "c9", writes=[r_tiota])
                S.dma("sp", gpre[:], gpre_d, "c10", writes=[r_gpre])
                S.dma("sp", subg[:], subg_d, "c11", writes=[r_subg])
                S.dma("sp", dskip[:], dskip_d, "c12", writes=[r_dskip])
                S.dma("sp", lamv[:].rearrange("p a b -> p (a b)"),
                      lam_d.rearrange("a b -> (a b)").partition_broadcast(128), "c13", writes=[r_lamv])
                S.op("dve", MEMSET(ones_b[:], 1.0), writes=[r_ones])
                S.op("dve", MEMSET(epsc[:], EPS), writes=[r_eps])

                scr = [sb("scr%d" % i, [128, 512]) for i in range(4)]
                r_scr = [Res("scr%d" % i) for i in range(4)]
                ki_t = sb("ki_t", [128, 512], I32); r_ki = Res("ki")

                def range_reduce(eng, out_ap, arg_ap, n, r_out, r_arg):
                    kf = scr[3][:, 0:n]
                    S.op(eng, TS(ki_t[:, 0:n], arg_ap, 1.0 / TWO_PI, ALU.mult), reads=[r_arg], writes=[r_ki])
                    S.op(eng, CP(kf, ki_t[:, 0:n]), reads=[r_ki], writes=[r_scr[3]])
                    S.op(eng, STT(out_ap, kf, -C1, arg_ap, ALU.mult, ALU.add), reads=[r_scr[3], r_arg], writes=[r_out])
                    S.op(eng, STT(out_ap, kf, -C2, out_ap, ALU.mult, ALU.add), reads=[r_scr[3], r_out], writes=[r_out])
                    S.op(eng, TS(out_ap, out_ap, PI_LO, ALU.min, -PI_LO, ALU.max), reads=[r_out], writes=[r_out])

                def sincos(arg_ap, n, r_arg, sin_out, cos_out, r_sin, r_cos):
                    red = scr[2][:, 0:n]
                    range_reduce("dve", red, arg_ap, n, r_scr[2], r_arg)
                    S.op("act", ACT(sin_out, red, AF.Sin), reads=[r_scr[2]], writes=[r_sin])
                    hs = scr[1][:, 0:n]
                    S.op("act", ACT(hs, red, AF.Sin, scale=0.5), reads=[r_scr[2]], writes=[r_scr[1]])
                    S.op("dve", TT(hs, hs, hs, ALU.mult), reads=[r_scr[1]], writes=[r_scr[1]])
                    S.op("dve", TS(cos_out, hs, -2.0, ALU.mult, 1.0, ALU.add), reads=[r_scr[1]], writes=[r_cos])

                lt = sb("lt", [128, 2, 64]); r_lt = Res("lt")
                ls = sb("ls", [128, 2]); r_ls = Res("ls")
                S.op("dve", TT(lt[:, 0, :], lamv[:, 0, :], lamv[:, 1, :], ALU.mult), reads=[r_lamv], writes=[r_lt])
                S.op("dve", TT(lt[:, 1, :], lamv[:, 2, :], lamv[:, 3, :], ALU.mult), reads=[r_lamv, r_lt], writes=[r_lt])
                S.op("dve", lambda e: e.reduce_sum(out=ls[:], in_=lt[:], axis=mybir.AxisListType.X), reads=[r_lt], writes=[r_ls])
                S.op("act", ACT(ls[:], ls[:], AF.Exp), reads=[r_ls], writes=[r_ls])
                S.op("dve", TT(neglam[:], ls[:, 1:2], ls[:, 0:1], ALU.subtract), reads=[r_ls], writes=[r_neglam])
                S.op("dve", TS(neglam[:], neglam[:], -LAM_INIT, ALU.add), reads=[r_neglam], writes=[r_neglam])
                S.op("dve", TS(subg[:], subg[:], 1.0 - LAM_INIT, ALU.mult), reads=[r_subg], writes=[r_subg])

                Wb = sb("Wb", [128, 8, 512], BF16); r_Wb = Res("Wb")
                wstg = [scr[0], scr[1]]
                r_wstg = [r_scr[0], r_scr[1]]
                for k in range(8):
                    S.dma("sp", wstg[k % 2][:], win_d[128 * k:128 * (k + 1), :], "wstg%d" % (k % 2), writes=[r_wstg[k % 2]])
                    S.op("pool", TS(Wb[:, k, :], wstg[k % 2][:], gpre[:, k:k + 1], ALU.mult),
                         reads=[r_wstg[k % 2], r_gpre], writes=[r_Wb])

                C0 = sb("C0", [128, 512]); S0 = sb("S0", [128, 512]); r_C0 = Res("C0"); r_S0 = Res("S0")
                cI = sb("cI", [128, NT1]); sI = sb("sI", [128, NT1]); r_cI = Res("cI"); r_sI = Res("sI")
                S.op("dve", TS(scr[0][:], iota512[:], invf[:, 0:1], ALU.mult), reads=[r_iota, r_invf], writes=[r_scr[0]])
                sincos(scr[0][:], 512, r_scr[0], S0[:], C0[:], r_S0, r_C0)
                S.op("dve", TS(scr[0][:, 0:NT1], tile_iota[:], invf[:, 0:1], ALU.mult), reads=[r_tiota, r_invf], writes=[r_scr[0]])
                sincos(scr[0][:, 0:NT1], NT1, r_scr[0], sI[:], cI[:], r_sI, r_cI)
                nsI = sb("nsI", [128, NT1]); r_nsI = Res("nsI")
                S.op("dve", TS(nsI[:], sI[:], -1.0, ALU.mult), reads=[r_sI], writes=[r_nsI])

                are2 = sb("are2", [128, 8]); aim2 = sb("aim2", [128, 8]); dt2 = sb("dt2", [128, 8])
                r_are = Res("are2"); r_aim = Res("aim2"); r_dt = Res("dt2")
                S.dma("sp", are2[0:64, :], areT_d, "s0", writes=[r_are])
                S.dma("sp", are2[64:128, :], areT_d, "s0", writes=[r_are])
                S.dma("sp", aim2[0:64, :], aimT_d, "s1", writes=[r_aim])
                S.dma("sp", aim2[64:128, :], aimT_d, "s1", writes=[r_aim])
                S.dma("sp", dt2[:], logdt_d.partition_broadcast(128), "s2", writes=[r_dt])
                S.op("act", ACT(dt2[:], dt2[:], AF.Exp), reads=[r_dt], writes=[r_dt])
                rdec = sb("rdec", [128, 8]); r_rdec = Res("rdec")
                theta = sb("theta", [128, 8]); r_theta = Res("theta")
                S.op("dve", TT(rdec[:], are2[:], dt2[:], ALU.mult), reads=[r_are, r_dt], writes=[r_rdec])
                S.op("act", ACT(rdec[:], rdec[:], AF.Exp), reads=[r_rdec], writes=[r_rdec])
                S.op("dve", TT(theta[:], aim2[:], dt2[:], ALU.mult), reads=[r_aim, r_dt], writes=[r_theta])
                c1t = sb("c1t", [128, 8]); s1t = sb("s1t", [128, 8]); r_c1t = Res("c1t"); r_s1t = Res("s1t")
                cTt = sb("cTt", [128, 8]); sTt = sb("sTt", [128, 8]); r_cTt = Res("cTt"); r_sTt = Res("sTt")
                sincos(theta[:], 8, r_theta, s1t[:], c1t[:], r_s1t, r_c1t)
                S.op("dve", TS(scr[0][:, 0:8], theta[:], 512.0, ALU.mult), reads=[r_theta], writes=[r_scr[0]])
                sincos(scr[0][:, 0:8], 8, r_scr[0], sTt[:], cTt[:], r_sTt, r_cTt)
                S.op("dve", TT(sTt[:], sTt[:], sgn[:, 0:1].to_broadcast([128, 8]), ALU.mult), reads=[r_sTt, r_sgn], writes=[r_sTt])
                COSg = sb("COSg", [128, 8, 512]); SINg = sb("SINg", [128, 8, 512])
                r_COSg = [Res("COSg%d" % g) for g in range(8)]; r_SINg = [Res("SINg%d" % g) for g in range(8)]
                for g in range(8):
                    S.op("dve", TS(scr[0][:], iota512[:], theta[:, g:g + 1], ALU.mult), reads=[r_iota, r_theta], writes=[r_scr[0]])
                    sincos(scr[0][:], 512, r_scr[0], SINg[:, g, :], COSg[:, g, :], r_SINg[g], r_COSg[g])
                ROT = sb("ROT", [128, 8, 128]); r_ROT = Res("ROT")
                for g in range(8):
                    S.op("dve", TS(ROT[:, g, :], ident_f[:], cTt[:, g:g + 1], ALU.mult), reads=[r_ident_f, r_cTt], writes=[r_ROT])
                    S.op("dve", STT(ROT[:, g, :], swap_f[:], sTt[:, g:g + 1], ROT[:, g, :], ALU.mult, ALU.add),
                         reads=[r_swap, r_sTt, r_ROT], writes=[r_ROT])
                fre = sb("fre", [128, 8]); fim = sb("fim", [128, 8]); r_fre = Res("fre"); r_fim = Res("fim")
                nr = sb("nr", [128, 8]); ni = sb("ni", [128, 8]); den = sb("den", [128, 8]); tmp8 = sb("tmp8", [128, 8])
                r_nr = Res("nr"); r_ni = Res("ni"); r_den = Res("den"); r_tmp8 = Res("tmp8")
                S.op("dve", TT(nr[:], rdec[:], c1t[:], ALU.mult), reads=[r_rdec, r_c1t], writes=[r_nr])
                S.op("dve", TS(nr[:], nr[:], -1.0, ALU.add), reads=[r_nr], writes=[r_nr])
                S.op("dve", TT(ni[:], rdec[:], s1t[:], ALU.mult), reads=[r_rdec, r_s1t], writes=[r_ni])
                S.op("dve", TT(den[:], are2[:], are2[:], ALU.mult), reads=[r_are], writes=[r_den])
                S.op("dve", TT(tmp8[:], aim2[:], aim2[:], ALU.mult), reads=[r_aim], writes=[r_tmp8])
                S.op("dve", TT(den[:], den[:], tmp8[:], ALU.add), reads=[r_den, r_tmp8], writes=[r_den])
                S.op("dve", RECIP(den[:], den[:]), reads=[r_den], writes=[r_den])
                S.op("dve", TT(fre[:], nr[:], are2[:], ALU.mult), reads=[r_nr, r_are], writes=[r_fre])
                S.op("dve", TT(tmp8[:], ni[:], aim2[:], ALU.mult), reads=[r_ni, r_aim], writes=[r_tmp8])
                S.op("dve", TT(fre[:], fre[:], tmp8[:], ALU.add), reads=[r_fre, r_tmp8], writes=[r_fre])
                S.op("dve", TT(fre[:], fre[:], den[:], ALU.mult), reads=[r_fre, r_den], writes=[r_fre])
                S.op("dve", TT(fim[:], ni[:], are2[:], ALU.mult), reads=[r_ni, r_are], writes=[r_fim])
                S.op("dve", TT(tmp8[:], nr[:], aim2[:], ALU.mult), reads=[r_nr, r_aim], writes=[r_tmp8])
                S.op("dve", TT(fim[:], fim[:], tmp8[:], ALU.subtract), reads=[r_fim, r_tmp8], writes=[r_fim])
                S.op("dve", TT(fim[:], fim[:], den[:], ALU.mult), reads=[r_fim, r_den], writes=[r_fim])
                Bre = sb("Bre", [64, 8, 16]); Bim = sb("Bim", [64, 8, 16]); r_Bre = Res("Bre"); r_Bim = Res("Bim")
                S.dma("sp", Bre[:], bre_d.rearrange("g p h -> p g h"), "s3", writes=[r_Bre])
                S.dma("sp", Bim[:], bim_d.rearrange("g p h -> p g h"), "s4", writes=[r_Bim])
                BBre = sb("BBre", [64, 8, 16]); BBim = sb("BBim", [64, 8, 16]); tB = sb("tB", [64, 8, 16])
                r_BBre = Res("BBre"); r_BBim = Res("BBim"); r_tB = Res("tB")
                fre_b = fre[0:64, :].unsqueeze(2).to_broadcast([64, 8, 16])
                fim_b = fim[0:64, :].unsqueeze(2).to_broadcast([64, 8, 16])
                S.op("dve", TT(BBre[:], Bre[:], fre_b, ALU.mult), reads=[r_Bre, r_fre], writes=[r_BBre])
                S.op("dve", TT(tB[:], Bim[:], fim_b, ALU.mult), reads=[r_Bim, r_fim], writes=[r_tB])
                S.op("dve", TT(BBre[:], BBre[:], tB[:], ALU.subtract), reads=[r_BBre, r_tB], writes=[r_BBre])
                S.op("dve", TT(BBim[:], Bim[:], fre_b, ALU.mult), reads=[r_Bim, r_fre], writes=[r_BBim])
                S.op("dve", TT(tB[:], Bre[:], fim_b, ALU.mult), reads=[r_Bre, r_fim], writes=[r_tB])
                S.op("dve", TT(BBim[:], BBim[:], tB[:], ALU.add), reads=[r_BBim, r_tB], writes=[r_BBim])
                FB1 = sb("FB1", [128, 128]); FB2 = sb("FB2", [128, 128]); r_FB1 = Res("FB1"); r_FB2 = Res("FB2")
                S.group("pe", [TR(banks[0][:, 0:64], BBre[:].rearrange("p g h -> p (g h)"), ident_f[0:64, 0:64]),
                               TR(banks[0][:, 64:128], BBim[:].rearrange("p g h -> p (g h)"), ident_f[0:64, 0:64])],
                        reads=[r_BBre, r_BBim, r_ident_f], writes=[rbank[0]])
                S.op("dve", CP(FB1[:], banks[0][:, 0:128]), reads=[rbank[0]], writes=[r_FB1])
                S.op("dve", CP(FB2[:, 0:64], banks[0][:, 64:128]), reads=[rbank[0]], writes=[r_FB2])
                S.op("dve", TS(FB2[:, 64:128], banks[0][:, 0:64], -1.0, ALU.mult), reads=[rbank[0], r_FB2], writes=[r_FB2])
                LB1 = sb("LB1", [128, 8, 128], BF16); LB2 = sb("LB2", [128, 8, 128], BF16); r_LB = Res("LB")
                for g in range(8):
                    S.op("pool", TS(LB1[:, g, :], FB1[:], gmask[:, g:g + 1], ALU.mult), reads=[r_FB1, r_gmask], writes=[r_LB])
                    S.op("pool", TS(LB2[:, g, :], FB2[:], gmask[:, g:g + 1], ALU.mult), reads=[r_FB2, r_gmask], writes=[r_LB])
                CC1 = sb("CC1", [128, 128]); CC2 = sb("CC2", [128, 128]); r_CC1 = Res("CC1"); r_CC2 = Res("CC2")
                S.dma("sp", CC1[0:64, :], creT_d, "s5", writes=[r_CC1])
                S.dma("sp", CC1[64:128, :], cimT_d, "s5", writes=[r_CC1])
                S.dma("sp", CC2[0:64, :], cimT_d, "s6", writes=[r_CC2])
                S.dma("sp", CC2[64:128, :], creT_d, "s6", writes=[r_CC2])
                S.op("dve", TS(CC1[64:128, :], CC1[64:128, :], -1.0, ALU.mult), reads=[r_CC1], writes=[r_CC1])
                S.op("dve", TS(CC2[:], CC2[:], -1.0, ALU.mult), reads=[r_CC2], writes=[r_CC2])
                LC1 = sb("LC1", [128, 8, 128], BF16); LC2 = sb("LC2", [128, 8, 128], BF16); r_LC = Res("LC")
                S.op("pool", MEMSET(LC1[:], 0.0), writes=[r_LC])
                S.op("pool", MEMSET(LC2[:], 0.0), writes=[r_LC])
                for g in range(8):
                    S.op("dve", CP(LC1[:, g, 16 * g:16 * g + 16], CC1[:, 16 * g:16 * g + 16]), reads=[r_CC1], writes=[r_LC])
                    S.op("dve", CP(LC2[:, g, 16 * g:16 * g + 16], CC2[:, 16 * g:16 * g + 16]), reads=[r_CC2], writes=[r_LC])
                diagD = sb("diagD", [128, 128], BF16); r_diagD = Res("diagD")
                S.op("dve", TS(diagD[:], ident_f[:], dskip[:, 0:1], ALU.mult), reads=[r_ident_f, r_dskip], writes=[r_diagD])
                s0 = sb("s0", [128, 2, 8]); r_s0 = [Res("s0_0"), Res("s0_1")]
                S.op("dve", MEMSET(s0[:], 0.0), writes=r_s0)

                xt = [sb("xt%d" % i, [128, D]) for i in range(2)]; r_xt = [Res("xt%d" % i) for i in range(2)]
                junk = sb("junk", [128, D], BF16); r_junk = Res("junk")
                ssq = sb("ssq", [128, 4]); r_ssq = [Res("ssq%d" % i) for i in range(4)]
                xs = [sb("xs%d" % i, [128, D], BF16) for i in range(2)]; r_xs = [Res("xs%d" % i) for i in range(2)]
                hT = [sb("hT%d" % i, [128, 8, 512], BF16) for i in range(2)]; r_hT = [Res("hT%d" % i) for i in range(2)]
                COSt = sb("COSt", [128, 512]); SINt = sb("SINt", [128, 512]); r_COSt = Res("COSt"); r_SINt = Res("SINt")
                qraw = sb("qraw", [128, 512], BF16); r_qraw = Res("qraw")
                rt1 = sb("rt1", [128, 512]); rt2 = sb("rt2", [128, 512]); r_rt1 = Res("rt1"); r_rt2 = Res("rt2")
                QT = [sb("QT%d" % i, [128, 512], BF16) for i in range(2)]; r_QT = [Res("QT%d" % i) for i in range(2)]
                KTa = sb("KTa", [128, LP], BF16); KTb = sb("KTb", [128, LP], BF16)
                r_KT = [Res("KT%d" % i) for i in range(NT1)]
                r_KTz = Res("KTz")
                S.op("pool", MEMSET(KTa[64:128, :], 0.0), writes=[r_KTz])
                S.op("pool", MEMSET(KTb[0:64, :], 0.0), writes=[r_KTz])
                V = sb("V", [128, NBLK, 128], BF16); r_V = [Res("V%d" % i) for i in range(NT1)]
                uT = [sb("uT%d" % i, [128, 512], BF16) for i in range(2)]; r_uT = [Res("uT%d" % i) for i in range(2)]
                st1 = [sb("st1_%d" % i, [128, 512]) for i in range(2)]; r_st1 = [Res("st1_%d" % i) for i in range(2)]
                st2 = [sb("st2_%d" % i, [128, 512]) for i in range(2)]; r_st2 = [Res("st2_%d" % i) for i in range(2)]
                bt = [sb("bt%d" % i, [128, 512]) for i in range(2)]; r_bt = [Res("bt%d" % i) for i in range(2)]
                Xs = [sb("Xs%d" % i, [128, 512]) for i in range(3)]; r_Xs = [Res("Xs%d" % i) for i in range(3)]
                P1s = [sb("P1s%d" % i, [128, 512], BF16) for i in range(3)]; r_P1s = [Res("P1s%d" % i) for i in range(3)]
                P2s = [sb("P2s%d" % i, [128, 512], BF16) for i in range(3)]; r_P2s = [Res("P2s%d" % i) for i in range(3)]
                ybuf = [sb("ybuf%d" % i, [128, 512], BF16) for i in range(2)]; r_ybuf = [Res("ybuf%d" % i) for i in range(2)]
                Pb = [sb("Pb%d" % i, [128, 2, 256], BF16) for i in range(3)]; r_Pb = [Res("Pb%d" % i) for i in range(3)]
                rL = sb("rL", [128, 2, 256]); r_rL = Res("rL")
                Osb = sb("Osb", [128, 2, 256]); r_Osb = Res("Osb")
                On = Osb; r_On = r_Osb
                pending_fin = []
                od = sb("od", [128, 256]); r_od = Res("od")
                osq = sb("osq", [128, 256], BF16); r_osq = Res("osq")
                orstd = sb("orstd", [128, 256]); r_orstd = Res("orstd")
                obuf = [sb("obuf%d" % i, [128, 256], BF16) for i in range(2)]; r_obuf = [Res("obuf%d" % i) for i in range(2)]

                F0, F1, F2, F3 = banks[0], banks[1], banks[2], banks[3]
                rF0, rF1, rF2, rF3 = rbank[0], rbank[1], rbank[2], rbank[3]
                Sb = [banks[4], banks[5]]; rSb = [rbank[4], rbank[5]]
                Ob, Lb = banks[6], banks[7]; rOb, rLb = rbank[6], rbank[7]
                F0b = F0[:].bitcast(BF16)


                pcount = [0]
                sbcount = [0]
                out_events = []

                def front_a(ti):
                    nb = 4 if ti < 16 else 1
                    hb = ti % 2
                    for j in range(nb):
                        s = 4 * ti + j
                        xb = sbcount[0] % 2
                        sq = sbcount[0] % 4
                        sbcount[0] += 1
                        if s == 0:
                            S.dma("sp", xt[xb][0:16, :], meta_d, "x%d" % xb, writes=[r_xt[xb]])
                            S.dma("sp", xt[xb][16:128, :], x_d[0:112, :], "x%d" % xb, writes=[r_xt[xb]])
                        elif s == 64:
                            S.op("pool", MEMSET(xt[xb][:], 0.0), writes=[r_xt[xb]])
                            S.dma("sp", xt[xb][0:16, :], x_d[SEQ - 16:SEQ, :], "x%d" % xb, writes=[r_xt[xb]])
                        else:
                            S.dma("sp", xt[xb][:], x_d[128 * s - 16:128 * s + 112, :], "x%d" % xb, writes=[r_xt[xb]])
                        S.op("act", ACT(junk[:], xt[xb][:], AF.Square, accum_out=ssq[:, sq:sq + 1]),
                             reads=[r_xt[xb]], writes=[r_junk, r_ssq[sq]])
                        S.op("act", ACT(ssq[:, sq:sq + 1], ssq[:, sq:sq + 1], AF.Ln, scale=1.0 / D, bias=epsc[:, 0:1]),
                             reads=[r_ssq[sq], r_eps], writes=[r_ssq[sq]])
                        S.op("act", ACT(ssq[:, sq:sq + 1], ssq[:, sq:sq + 1], AF.Exp, scale=-0.5),
                             reads=[r_ssq[sq]], writes=[r_ssq[sq]])
                        S.op("act", ACT(xs[xb][:], xt[xb][:], AF.Copy, scale=ssq[:, sq:sq + 1]),
                             reads=[r_xt[xb], r_ssq[sq]], writes=[r_xs[xb]])
                        yield
                        S.group("pe", [TR(F0b[:, 128 * k:128 * (k + 1)], xs[xb][:, 128 * k:128 * (k + 1)], ident_b[:]) for k in range(8)],
                                reads=[r_xs[xb], r_ident_b], writes=[rF0])
                        S.op("dve", CP(hT[hb][:, :, 128 * j:128 * (j + 1)], F0b.rearrange("p (k t) -> p k t", k=8)),
                             reads=[rF0], writes=[r_hT[hb]])
                        yield

                def front_bs(ti):
                    nb = 4 if ti < 16 else 1
                    N = 128 * nb
                    tok0 = 512 * ti
                    hb = ti % 2
                    S.op("dve", TS(COSt[:], C0[:], cI[:, ti:ti + 1], ALU.mult), reads=[r_C0, r_cI], writes=[r_COSt])
                    S.op("dve", STT(COSt[:], S0[:], nsI[:, ti:ti + 1], COSt[:], ALU.mult, ALU.add), reads=[r_S0, r_nsI, r_COSt], writes=[r_COSt])
                    S.op("dve", TS(SINt[:], S0[:], cI[:, ti:ti + 1], ALU.mult), reads=[r_S0, r_cI], writes=[r_SINt])
                    S.op("dve", STT(SINt[:], C0[:], sI[:, ti:ti + 1], SINt[:], ALU.mult, ALU.add), reads=[r_C0, r_sI, r_SINt], writes=[r_SINt])
                    yield

                    def proj_fm(c, bank, rb):
                        S.group("pe", [MM(bank[:, 0:N], Wb[:, k, 128 * c:128 * (c + 1)], hT[hb][:, k, 0:N], k == 0, k == 7) for k in range(8)],
                                reads=[r_Wb, r_hT[hb]], writes=[rb])

                    def rope_a():
                        S.op("act", ACT(qraw[:, 0:N], F1[:, 0:N], AF.Copy), reads=[rF1], writes=[r_qraw])
                        S.op("pe", MM(F2[:, 0:N], rmatT_b[:], qraw[:, 0:N], True, True), reads=[r_rmat, r_qraw], writes=[rF2])

                    def rope_b(dst_ap, r_dst, dst2=None):
                        S.op("dve", TT(rt1[:, 0:N], F1[:, 0:N], COSt[:, 0:N], ALU.mult), reads=[rF1, r_COSt], writes=[r_rt1])
                        S.op("dve", TT(rt2[:, 0:N], F2[:, 0:N], SINt[:, 0:N], ALU.mult), reads=[rF2, r_SINt], writes=[r_rt2])
                        if dst2 is None:
                            S.op("pool", TT(dst_ap, rt1[:, 0:N], rt2[:, 0:N], ALU.add), reads=[r_rt1, r_rt2], writes=[r_dst])
                        else:
                            S.op("pool", TT(dst_ap[0:64, :], rt1[0:64, 0:N], rt2[0:64, 0:N], ALU.add), reads=[r_rt1, r_rt2, r_KTz], writes=[r_dst])
                            S.op("pool", TT(dst2[64:128, :], rt1[64:128, 0:N], rt2[64:128, 0:N], ALU.add), reads=[r_rt1, r_rt2, r_KTz], writes=[r_dst])

                    proj_fm(0, F1, rF1)
                    yield
                    rope_a()
                    yield
                    rope_b(QT[hb][:, 0:N], r_QT[hb])
                    yield
                    proj_fm(1, F1, rF1)
                    yield
                    rope_a()
                    yield
                    rope_b(KTa[:, tok0:tok0 + N], r_KT[ti], KTb[:, tok0:tok0 + N])
                    yield
                    proj_fm(3, F1, rF1)
                    yield
                    S.op("act", ACT(uT[hb][:, 0:N], F1[:, 0:N], AF.Copy), reads=[rF1], writes=[r_uT[hb]])
                    fns = []
                    for j in range(nb):
                        for k in range(8):
                            fns.append(MM(F2[:, 128 * j:128 * (j + 1)], hT[hb][:, k, 128 * j:128 * (j + 1)], Wb[:, k, 256:384], k == 0, k == 7))
                    S.group("pe", fns, reads=[r_Wb, r_hT[hb]], writes=[rF2])
                    yield
                    S.op("act", ACT(V[:, 4 * ti:4 * ti + nb, :], F2[:, 0:N].rearrange("p (j d) -> p j d", j=nb), AF.Copy),
                         reads=[rF2], writes=[r_V[ti]])
                    yield
                    par = ti % 2
                    yb = ti % 2

                    def emit_y(g, pb):
                        S.group("pe", [MM(F3[:, 0:N], LC1[:, g, :], P1s[pb][:, 0:N], g == 0, False),
                                       MM(F3[:, 0:N], LC2[:, g, :], P2s[pb][:, 0:N], False, False)],
                                reads=[r_LC, r_P1s[pb], r_P2s[pb]], writes=[rF3])

                    def stage_b(g):
                        pb = g % 3; q = g % 2
                        S.op("dve", SCAN(Xs[pb][:, 0:N], rdec[:, g:g + 1].to_broadcast([128, N]), bt[q][:, 0:N], s0[:, par, g:g + 1]),
                             reads=[r_rdec, r_bt[q], r_s0[par]], writes=[r_Xs[pb]])
                        S.op("pool", TT(P1s[pb][:, 0:N], Xs[pb][:, 0:N], COSg[:, g, 0:N], ALU.mult), reads=[r_Xs[pb], r_COSg[g]], writes=[r_P1s[pb]])
                        S.op("dve", TT(P2s[pb][:, 0:N], Xs[pb][:, 0:N], SINg[:, g, 0:N], ALU.mult), reads=[r_Xs[pb], r_SINg[g]], writes=[r_P2s[pb]])

                    def stage_c(g):
                        pb = g % 3
                        if ti + 1 < nt1:
                            S.op("pe", MM(F0[:, g:g + 1], ROT[:, g, :], Xs[pb][:, N - 1:N], True, True), reads=[r_ROT, r_Xs[pb]], writes=[rF0])
                            S.op("act", ACT(s0[:, 1 - par, g:g + 1], F0[:, g:g + 1], AF.Copy), reads=[rF0], writes=[r_s0[1 - par]])
                        emit_y(g, pb)

                    for it in range(11):
                        if 3 <= it:
                            stage_c(it - 3)
                            yield
                        if it < 8:
                            g = it; q = g % 2
                            S.op("pe", MM(F1[:, 0:N], LB1[:, g, :], uT[hb][:, 0:N], True, True), reads=[r_LB, r_uT[hb]], writes=[rF1])
                            S.op("pe", MM(F2[:, 0:N], LB2[:, g, :], uT[hb][:, 0:N], True, True), reads=[r_LB, r_uT[hb]], writes=[rF2])
                            yield
                            S.op("dve", TT(st1[q][:, 0:N], F1[:, 0:N], COSg[:, g, 0:N], ALU.mult), reads=[rF1, r_COSg[g]], writes=[r_st1[q]])
                            S.op("dve", TT(st2[q][:, 0:N], F2[:, 0:N], SINg[:, g, 0:N], ALU.mult), reads=[rF2, r_SINg[g]], writes=[r_st2[q]])
                            S.op("pool", TT(bt[q][:, 0:N], st1[q][:, 0:N], st2[q][:, 0:N], ALU.add), reads=[r_st1[q], r_st2[q]], writes=[r_bt[q]])
                        if 1 <= it <= 8:
                            stage_b(it - 1)
                        if it <= 8:
                            yield
                    S.op("pe", MM(F3[:, 0:N], diagD[:], uT[hb][:, 0:N], False, True), reads=[r_diagD, r_uT[hb]], writes=[rF3])
                    yield
                    S.op("act", ACT(ybuf[yb][:, 0:N], F3[:, 0:N], AF.Gelu), reads=[rF3], writes=[r_ybuf[yb]])
                    lo = max(tok0, NMETA); hi = min(tok0 + N, L)
                    out_events.extend(xchg_write(128, lo - NMETA, hi - NMETA, ybuf[yb], lo - tok0, "yo%d" % yb, r_ybuf[yb]))
                    yield

                def finalize(c, NQ, q0, tok0):
                    Of = Ob[:, 0:2 * NQ]; Lf = Lb[:, 0:2 * NQ]
                    Osf = Osb[:].rearrange("p h q -> p (h q)")[:, 0:2 * NQ]
                    rLf = rL[:].rearrange("p h q -> p (h q)")[:, 0:2 * NQ]
                    Onf = On[:].rearrange("p h q -> p (h q)")[:, 0:2 * NQ]
                    Onv = Onf.rearrange("p (h q) -> p h q", h=2)
                    S.op("dve", CP(Osf, Of), reads=[rOb], writes=[r_Osb])
                    S.op("act", ACT(rLf, Lf, AF.Ln), reads=[rLb], writes=[r_rL])
                    S.op("act", ACT(rLf, rLf, AF.Exp, scale=-1.0), reads=[r_rL], writes=[r_rL])
                    S.op("dve", TT(Onf, Osf, rLf, ALU.mult), reads=[r_Osb, r_rL], writes=[r_On])
                    S.op("dve", STT(od[:, 0:NQ], Onv[:, 1, :], neglam[:, 0:1], Onv[:, 0, :], ALU.mult, ALU.add),
                         reads=[r_On, r_neglam], writes=[r_od])
                    S.op("act", ACT(osq[:, 0:NQ], od[:, 0:NQ], AF.Square), reads=[r_od], writes=[r_osq])

                    def fin2(c=c, NQ=NQ, q0=q0, tok0=tok0):
                        S.op("pe", MM(F0[:, 256:256 + NQ], ones_b[:], osq[:, 0:NQ], True, True), reads=[r_ones, r_osq], writes=[rF0])
                        S.op("act", ACT(orstd[:, 0:NQ], F0[:, 256:256 + NQ], AF.Ln, scale=1.0 / 128, bias=epsc[:, 0:1]),
                             reads=[rF0, r_eps], writes=[r_orstd])
                        S.op("act", ACT(orstd[:, 0:NQ], orstd[:, 0:NQ], AF.Exp, scale=-0.5), reads=[r_orstd], writes=[r_orstd])
                        ob = c % 2
                        S.op("dve", STT(obuf[ob][:, 0:NQ], od[:, 0:NQ], subg[:, 0:1], orstd[:, 0:NQ], ALU.mult, ALU.mult),
                             reads=[r_od, r_subg, r_orstd], writes=[r_obuf[ob]])
                        qt0 = tok0 + q0
                        lo = max(qt0, NMETA); hi = min(qt0 + NQ, L)
                        out_events.extend(xchg_write(0, lo - NMETA, hi - NMETA, obuf[ob], lo - qt0, "oo%d" % ob, r_obuf[ob]))
                    pending_fin.append(fin2)

                def attention_tail(ti, pump):
                    tok0 = 512 * ti
                    hb = ti % 2
                    QTv = QT[hb]
                    NQ = 16
                    c = 2 * ti
                    nkb = 65
                    G = 16
                    groups = [list(range(g0, min(g0 + G, nkb))) for g0 in range(0, nkb, G)]

                    def qk_g(gi):
                        kbs = groups[gi]
                        sbk = gi % 2; pbk = gi % 3
                        fns = []
                        for idx, kb in enumerate(kbs):
                            fns.append(MM(Sb[sbk][:, 32 * idx:32 * idx + 16], KTa[:, 128 * kb:128 * (kb + 1)], QTv[:, 0:NQ], True, True))
                            fns.append(MM(Sb[sbk][:, 32 * idx + 16:32 * idx + 32], KTb[:, 128 * kb:128 * (kb + 1)], QTv[:, 0:NQ], True, True))
                        S.group("pe", fns, reads=r_KT + [r_QT[hb]], writes=[rSb[sbk]])
                        w = 32 * len(kbs)
                        Pf = Pb[pbk][:].rearrange("p h q -> p (h q)")
                        S.op("act", ACT(Pf[:, 0:w], Sb[sbk][:, 0:w], AF.Exp, scale=0.125), reads=[rSb[sbk]], writes=[r_Pb[pbk]])
                        if kbs[-1] == nkb - 1:
                            idx = len(kbs) - 1
                            S.op("pool", TT(Pf[:, 32 * idx:32 * idx + 32].rearrange("p (h q) -> p h q", h=2),
                                            Pf[:, 32 * idx:32 * idx + 32].rearrange("p (h q) -> p h q", h=2),
                                            tri_b[:, 0:NQ].unsqueeze(1).to_broadcast([128, 2, NQ]), ALU.mult),
                                 reads=[r_Pb[pbk], r_tri], writes=[r_Pb[pbk]])

                    def pv_g(gi):
                        kbs = groups[gi]
                        pbk = gi % 3
                        Pf = Pb[pbk][:].rearrange("p h q -> p (h q)")
                        fo = []; fl = []
                        for idx, kb in enumerate(kbs):
                            fo.append(MM(Ob[:, 0:32], V[:, kb, :], Pf[:, 32 * idx:32 * idx + 32], kb == 0, kb == nkb - 1))
                            fl.append(MM(Lb[:, 0:32], ones_b[:], Pf[:, 32 * idx:32 * idx + 32], kb == 0, kb == nkb - 1))
                        S.group("pe", fo, reads=r_V + [r_Pb[pbk]], writes=[rOb])
                        S.group("pe", fl, reads=[r_ones, r_Pb[pbk]], writes=[rLb])

                    qk_g(0)
                    for gi in range(len(groups)):
                        if gi + 1 < len(groups):
                            qk_g(gi + 1)
                        pv_g(gi)
                        if gi == 1 and pending_fin:
                            pending_fin.pop(0)()
                        pump()
                    while pending_fin:
                        pending_fin.pop(0)()
                    finalize(c, NQ, 0, tok0)

                def attention(ti, pump):
                    if ti == 16:
                        return attention_tail(ti, pump)
                    nb = 4 if ti < 16 else 1
                    tok0 = 512 * ti
                    hb = ti % 2
                    nqt = 2 if nb == 4 else 1
                    nbq = 2 if nb == 4 else 1
                    NQ = 128 * nbq
                    QTv = QT[hb]
                    for qi in range(nqt):
                        c = 2 * ti + qi
                        q0 = 256 * qi
                        nkb = 2 * c + nbq

                        def qk(kb):
                            j = kb - 2 * c
                            qs = 128 * j if j > 0 else 0
                            kt = kb // 4
                            sbk = kb % 2
                            pbk = kb % 3
                            Sv = Sb[sbk][:, 0:2 * NQ].rearrange("p (h q) -> p h q", h=2)
                            Pv = Pb[pbk][:].rearrange("p h q -> p (h q)")[:, 0:2 * NQ].rearrange("p (h q) -> p h q", h=2)
                            S.group("pe", [MM(Sv[:, 0, qs:NQ], KTa[:, 128 * kb:128 * (kb + 1)], QTv[:, q0 + qs:q0 + NQ], True, True),
                                           MM(Sv[:, 1, qs:NQ], KTb[:, 128 * kb:128 * (kb + 1)], QTv[:, q0 + qs:q0 + NQ], True, True)],
                                    reads=[r_KT[kt], r_QT[hb]], writes=[rSb[sbk]])
                            S.op("act", ACT(Pv[:, :, qs:NQ], Sv[:, :, qs:NQ], AF.Exp, scale=0.125), reads=[rSb[sbk]], writes=[r_Pb[pbk]])
                            if j >= 0:
                                S.op("pool", TT(Pv[:, :, qs:qs + 128], Pv[:, :, qs:qs + 128],
                                                tri_b[:].unsqueeze(1).to_broadcast([128, 2, 128]), ALU.mult),
                                     reads=[r_Pb[pbk], r_tri], writes=[r_Pb[pbk]])

                        def pv(kb):
                            j = kb - 2 * c
                            qs = 128 * j if j > 0 else 0
                            kt = kb // 4
                            pbk = kb % 3
                            Pv = Pb[pbk][:].rearrange("p h q -> p (h q)")[:, 0:2 * NQ].rearrange("p (h q) -> p h q", h=2)
                            Ov = Ob[:, 0:2 * NQ].rearrange("p (h q) -> p h q", h=2)
                            Lv = Lb[:, 0:2 * NQ].rearrange("p (h q) -> p h q", h=2)
                            first = (kb == 0); lastk = (kb == nkb - 1)
                            if qs == 0:
                                Pf = Pb[pbk][:].rearrange("p h q -> p (h q)")[:, 0:2 * NQ]
                                S.op("pe", MM(Ob[:, 0:2 * NQ], V[:, kb, :], Pf, first, lastk), reads=[r_V[kt], r_Pb[pbk]], writes=[rOb])
                                S.op("pe", MM(Lb[:, 0:2 * NQ], ones_b[:], Pf, first, lastk), reads=[r_ones, r_Pb[pbk]], writes=[rLb])
                            else:
                                assert not first
                                S.group("pe", [MM(Ov[:, 0, qs:NQ], V[:, kb, :], Pv[:, 0, qs:NQ], False, False),
                                               MM(Ov[:, 1, qs:NQ], V[:, kb, :], Pv[:, 1, qs:NQ], False, lastk)],
                                        reads=[r_V[kt], r_Pb[pbk]], writes=[rOb])
                                S.group("pe", [MM(Lv[:, 0, qs:NQ], ones_b[:], Pv[:, 0, qs:NQ], False, False),
                                               MM(Lv[:, 1, qs:NQ], ones_b[:], Pv[:, 1, qs:NQ], False, lastk)],
                                        reads=[r_ones, r_Pb[pbk]], writes=[rLb])

                        qk(0)
                        for kb in range(nkb):
                            if kb + 1 < nkb:
                                qk(kb + 1)
                            pv(kb)
                            if kb == 2 and pending_fin:
                                pending_fin.pop(0)()
                            pump()
                        while pending_fin:
                            pending_fin.pop(0)()
                        finalize(c, NQ, q0, tok0)

                def run_all(g):
                    for _ in g:
                        pass
                run_all(front_a(0)); run_all(front_bs(0))
                if nt1 > 1:
                    run_all(front_a(1))
                for ti in range(nt1):
                    gens = []
                    if ti + 1 < nt1:
                        gens.append(front_bs(ti + 1))
                    if ti + 2 < nt1:
                        gens.append(front_a(ti + 2))
                    nbq_t = 2 if ti < 16 else 1
                    total_kb = sum(2 * (2 * ti + qi) + nbq_t for qi in range(2 if ti < 16 else 1))
                    state = {"steps_left": (FRONT_STEPS if ti + 1 < nt1 else 0) + (8 if ti + 2 < nt1 else 0),
                             "kb_left": total_kb, "gens": gens, "rr": 0}

                    def pump(state=state):
                        if not state["gens"]:
                            return
                        kbl = max(state["kb_left"], 1)
                        n = -(-state["steps_left"] // kbl)
                        state["kb_left"] -= 1
                        for _ in range(n):
                            if not state["gens"]:
                                return
                            state["rr"] += 1
                            idx = 1 if (len(state["gens"]) > 1 and state["rr"] % 5 == 0) else 0
                            try:
                                next(state["gens"][idx])
                                state["steps_left"] -= 1
                            except StopIteration:
                                state["gens"].pop(idx)
                    attention(ti, pump)
                    for g_ in state["gens"]:
                        run_all(g_)
                    if wprep:
                        dst_, src_ = wprep.pop(0)
                        wprep_events.append(S.dma("pool", dst_, src_, "wprep"))
                    if ti == nt1 - 1 or (ti >= 4 and ti % 4 == 0):
                        while pending_fin:
                            pending_fin.pop(0)()
                    if use_cc and ti >= 4 and ti % 4 == 0:
                        qd = ti // 4 - 1
                        S.wait_all("pool", [(sem_, S.dma_cnt[sem_.name[2:]], src_) for (sem_, val_, src_) in q_events[qd]])
                        S.raw("pool", (lambda qd: (lambda e: e.collective_compute(
                            "AllGather", ALU.bypass, replica_groups=[[0, 1, 2, 3], [4, 5, 6, 7]],
                            ins=[xin_t[qd].ap().opt()], outs=[xout_t.ap()[1024 * qd:1024 * (qd + 1), :].opt()]).then_inc(cc_sem)))(qd))

            return out_events

        out_events = _phase1() if nt1 > 0 else []

        if debug:
            print('MARKS', S.marks, S.nops)
            S.limit = 10**9
            S.wait_all("pool", out_events)
            evs = [S.dma("pool", dbg_d[:, TOK2 * q:TOK2 * (q + 1)], xin[q], "dbg") for q in range(4)]
            S.wait_all("pool", evs)
        if do_phase2:
            S.wait_all("pool", out_events)
            if p2mode == "nogather":
                evd = [S.dma("pool", xout[1024 * (t_ // 2):1024 * (t_ // 2 + 1), 1024 * (t_ % 2):1024 * (t_ % 2) + T2],
                             xout_dbg_d[:, T2 * t_:T2 * (t_ + 1)], "xdbg") for t_ in range(NT2)]
                S.wait_all("pool", evd)
            S.cnt["pool"] += 1
            gather_ev = (S.sem["pool"], S.cnt["pool"], "pool")
            S.raw("pool", lambda e: e.sem_inc(S.sem["pool"], 1))
            lasts = [S.last[e] for e in ("pe", "act", "dve", "pool") if S.last[e] is not None] + [gather_ev]
            for e in ("pe", "act", "dve", "sp", "pool"):
                S.wait_all(e, lasts)

            with ExitStack() as p2:
                def sb(name, shape, dt=F32):
                    return p2.enter_context(nc.sbuf_tensor("b_" + name, list(shape), dt))

                Wg = sb("Wg", [128, 8, DFF], BF16); r_Wg = Res("Wg")
                Wu = sb("Wu", [128, 8, DFF], BF16); r_Wu = Res("Wu")
                Wd = sb("Wd", [128, NF, D], BF16); r_Wd = Res("Wd")
                Wo = sb("Wo", [128, 8, D], BF16); r_Wo = Res("Wo")
                Wgl = sb("Wgl", [128, 4, 512], BF16); r_Wgl = Res("Wgl")
                ident2 = sb("ident2", [128, 128], BF16); r_id2 = Res("ident2")
                ones2 = sb("ones2", [128, 128], BF16); r_ones2 = Res("ones2")
                eps2 = sb("eps2", [128, 1]); r_eps2 = Res("eps2")
                bglu = sb("bglu", [128, 4]); r_bglu = Res("bglu")
                gssm = sb("gssm", [128, 4]); r_gssm = Res("gssm")
                gpost_b = sb("gpost_b", [128, D]); r_gpost = Res("gpost")
                gffn8 = sb("gffn8", [128, 8]); r_gffn = Res("gffn")
                gpffn_b = sb("gpffn_b", [128, D]); r_gpffn = Res("gpffn")
                S.dma("pool", ident2[:], ident_d, "k0", writes=[r_id2])
                S.op("dve", MEMSET(ones2[:], 1.0), writes=[r_ones2])
                S.op("dve", MEMSET(eps2[:], EPS), writes=[r_eps2])
                S.dma("sp", bglu[:], bglu_d, "k1", writes=[r_bglu])
                S.dma("sp", gssm[:], gssm_d, "k2", writes=[r_gssm])
                S.dma("sp", gpost_b[:], gpost_d.partition_broadcast(128), "k3", writes=[r_gpost])
                S.dma("sp", gffn8[:], gffn_d, "k4", writes=[r_gffn])
                S.dma("sp", gpffn_b[:], gpffn_d.partition_broadcast(128), "k5", writes=[r_gpffn])

                first_loads = []
                cT = [sb("cT%d" % i, [128, 8, T2], BF16) for i in range(1)]; r_cT = [Res("cT%d" % i) for i in range(1)]
                sig = sb("sig", [128, T2], BF16); r_sig = Res("sig")
                yg = sb("yg", [128, 4, T2], BF16); r_yg = Res("yg")
                sq2 = sb("sq2", [128, T2], BF16); r_sq2 = Res("sq2")
                rsy = sb("rsy", [128, T2]); r_rsy = Res("rsy")
                yn = sb("yn", [128, 4, T2], BF16); r_yn = Res("yn")
                hres = [sb("hres%d" % i, [128, D]) for i in range(3)]; r_hres = [Res("hres%d" % i) for i in range(3)]
                ssvA = sb("ssvA", [128, 8]); r_ssvA = Res("ssvA")
                ssvB = sb("ssvB", [128, 4]); r_ssvB = Res("ssvB")
                tmpA = sb("tmpA", [128, 512]); r_tmpA = Res("tmpA")
                tmpB = tmpA; r_tmpB = r_tmpA
                hs = [sb("hs0", [128, D], BF16)] * 2; r_hs = [Res("hs0")] * 2
                junkA = sb("junkA", [128, 512], BF16); r_junkA = Res("junkA")
                junkB = junkA; r_junkB = r_junkA
                hT2 = [sb("hT2_%d" % i, [128, 8, T2], BF16) for i in range(2)]; r_hT2 = [Res("hT2_%d" % i) for i in range(2)]
                sg = [sb("sg%d" % i, [128, T2], BF16) for i in range(2)]; r_sg = [Res("sg%d" % i) for i in range(2)]
                aT = sb("aT", [128, NF, T2], BF16); r_aT = Res("aT")

                pid = [None]
                B = banks
                rB = rbank
                final_events = []

                def h1_gen(tt):
                    cb = 0

                    def _ld(e, tt=tt):
                        if pid[0] is None:
                            pid[0] = e.partition_id() % 4
                        col = pid[0] * T2 + 1024 * (tt % 2)
                        return e.dma_start(out=cT[cb][:], in_=xout[1024 * (tt // 2):1024 * (tt // 2 + 1), bass.ds(col, T2)].rearrange("(k p) t -> p k t", p=128))
                    slot = "g%d" % cb
                    if slot not in S.dma_sems:
                        S.dma_sems[slot] = top.enter_context(nc.semaphore("d_" + slot))
                        S.dma_cnt[slot] = 0
                    S._waits("sp", S._deps_for(None, [r_cT[cb]], None))
                    if use_cc:
                        S.raw("sp", (lambda v: (lambda e: e.wait_ge(cc_sem, v)))(tt // 2 + 1))
                    S.dma_cnt[slot] += 16
                    ev = (S.dma_sems[slot], S.dma_cnt[slot], "dma")
                    S.raw("sp", (lambda sem: (lambda e, f=_ld: f(e).then_inc(sem, 16)))(S.dma_sems[slot]))
                    S._commit(ev, None, [r_cT[cb]])
                    if tt == 0:
                        first_loads.append(ev)
                        for s in range(2):
                            first_loads.append(S.dma("sp", hres[s][:], xres_d[128 * s:128 * (s + 1), :], "xr%d" % s, writes=[r_hres[s]]))
                    yield None
                    for m in range(4):
                        S.group("pe", [MM(B[0][:, 0:T2], Wgl[:, r, 128 * m:128 * (m + 1)], cT[cb][:, 2 * r + 1, :], r == 0, r == 3) for r in range(4)],
                                reads=[r_Wgl, r_cT[cb]], writes=[rB[0]])
                        yield None
                        S.op("act", ACT(sig[:], B[0][:, 0:T2], AF.Sigmoid, bias=bglu[:, m:m + 1]), reads=[rB[0], r_bglu], writes=[r_sig])
                        S.op("dve", TT(yg[:, m, :], cT[cb][:, 2 * m + 1, :], sig[:], ALU.mult), reads=[r_cT[cb], r_sig], writes=[r_yg])
                        S.op("act", ACT(sq2[:], yg[:, m, :], AF.Square), reads=[r_yg], writes=[r_sq2])
                        yield None
                        yield None
                        S.op("pe", MM(B[1][:, 0:T2], ones2[:], sq2[:], m == 0, m == 3), reads=[r_ones2, r_sq2], writes=[rB[1]])
                    yield None
                    S.op("act", ACT(rsy[:], B[1][:, 0:T2], AF.Ln, scale=1.0 / 512, bias=eps2[:, 0:1]), reads=[rB[1], r_eps2], writes=[r_rsy])
                    S.op("act", ACT(rsy[:], rsy[:], AF.Exp, scale=-0.5), reads=[r_rsy], writes=[r_rsy])
                    for m in range(4):
                        S.op("dve", STT(yn[:, m, :], yg[:, m, :], gssm[:, m:m + 1], rsy[:], ALU.mult, ALU.mult),
                             reads=[r_yg, r_gssm, r_rsy], writes=[r_yn])
                    yield None
                    yield None
                    yield None
                    for s in range(2):
                        hr = hres[(2 * tt + s) % 3]; r_hr = r_hres[(2 * tt + s) % 3]
                        c0 = 4 * s
                        for hf in range(2):
                            bk = B[2 + hf]; rbk = rB[2 + hf]
                            fns = []
                            for k in range(8):
                                src = cT[cb][:, k, 128 * s:128 * (s + 1)] if k % 2 == 0 else yn[:, k // 2, 128 * s:128 * (s + 1)]
                                fns.append(MM(bk[:, :], src, Wo[:, k, 512 * hf:512 * (hf + 1)], k == 0, k == 7))
                            S.group("pe", fns, reads=[r_cT[cb], r_yn, r_Wo], writes=[rbk])
                            yield None
                            S.op("act", ACT(junkA[:], bk[:, :], AF.Square, accum_out=ssvA[:, c0 + hf:c0 + hf + 1]), reads=[rbk], writes=[r_junkA, r_ssvA])
                        yield None
                        S.op("dve", TT(ssvA[:, c0 + 2:c0 + 3], ssvA[:, c0:c0 + 1], ssvA[:, c0 + 1:c0 + 2], ALU.add), reads=[r_ssvA], writes=[r_ssvA])
                        S.op("act", ACT(ssvA[:, c0 + 2:c0 + 3], ssvA[:, c0 + 2:c0 + 3], AF.Ln, scale=1.0 / D, bias=eps2[:, 0:1]), reads=[r_ssvA, r_eps2], writes=[r_ssvA])
                        S.op("act", ACT(ssvA[:, c0 + 2:c0 + 3], ssvA[:, c0 + 2:c0 + 3], AF.Exp, scale=-0.5), reads=[r_ssvA], writes=[r_ssvA])
                        yield ("BAR1" if s == 1 else None)
                        row0 = tt * T2 + 128 * s
                        if tt > 0:
                            S.dma("sp", hr[:], xres_d[row0:row0 + 128, :], "xr%d" % ((2 * tt + s) % 3), writes=[r_hr])
                        yield None
                        for hf in range(2):
                            bk = B[2 + hf]; rbk = rB[2 + hf]
                            S.op("dve", STT(tmpA[:], bk[:, :], ssvA[:, c0 + 2:c0 + 3], gpost_b[:, 512 * hf:512 * (hf + 1)], ALU.mult, ALU.mult),
                                 reads=[rbk, r_ssvA, r_gpost], writes=[r_tmpA])
                            S.op("dve", TT(hr[:, 512 * hf:512 * (hf + 1)], tmpA[:], hr[:, 512 * hf:512 * (hf + 1)], ALU.add),
                                 reads=[r_tmpA, r_hr], writes=[r_hr])
                            S.op("act", ACT(junkA[:], hr[:, 512 * hf:512 * (hf + 1)], AF.Square, accum_out=ssvA[:, c0 + hf:c0 + hf + 1]),
                                 reads=[r_hr], writes=[r_junkA, r_ssvA])
                        yield None
                        yield None
                        S.op("dve", TT(ssvA[:, c0 + 3:c0 + 4], ssvA[:, c0:c0 + 1], ssvA[:, c0 + 1:c0 + 2], ALU.add), reads=[r_ssvA], writes=[r_ssvA])
                        S.op("act", ACT(ssvA[:, c0 + 3:c0 + 4], ssvA[:, c0 + 3:c0 + 4], AF.Ln, scale=1.0 / D, bias=eps2[:, 0:1]), reads=[r_ssvA, r_eps2], writes=[r_ssvA])
                        S.op("act", ACT(ssvA[:, c0 + 3:c0 + 4], ssvA[:, c0 + 3:c0 + 4], AF.Exp, scale=-0.5), reads=[r_ssvA], writes=[r_ssvA])
                        yield None
                        S.op("act", ACT(hs[s][:], hr[:], AF.Copy, scale=ssvA[:, c0 + 3:c0 + 4]), reads=[r_hr, r_ssvA], writes=[r_hs[s]])
                        yield None
                        yield None
                        yield None
                        B1b = B[1][:].bitcast(BF16)
                        S.group("pe", [TR(B1b[:, 128 * k:128 * (k + 1)], hs[s][:, 128 * k:128 * (k + 1)], ident2[:]) for k in range(8)],
                                reads=[r_hs[s], r_id2], writes=[rB[1]])
                        yield None
                        S.op("dve", TT(hT2[tt % 2][:, :, 128 * s:128 * (s + 1)], B1b.rearrange("p (k t) -> p k t", k=8),
                                       gffn8[:].unsqueeze(2).to_broadcast([128, 8, 128]), ALU.mult),
                             reads=[rB[1], r_gffn], writes=[r_hT2[tt % 2]])
                        yield None

                class Pump:
                    def __init__(self, gen):
                        self.gen = gen
                        self.released = set()
                        self.pending = None

                    def release(self, name):
                        self.released.add(name)

                    def step(self, n):
                        for _ in range(n):
                            if self.gen is None:
                                return
                            if self.pending is not None:
                                if self.pending not in self.released:
                                    return
                                self.pending = None
                            try:
                                r = next(self.gen)
                            except StopIteration:
                                self.gen = None
                                return
                            if r is not None:
                                self.pending = r

                    def flush(self):
                        while self.gen is not None:
                            if self.pending is not None:
                                assert self.pending in self.released, self.pending
                            self.step(1)

                def h2(tt, P):
                    for f in range(NF):
                        gb2 = B[4 + (f % 2)]; rgb2 = rB[4 + (f % 2)]
                        gv = gb2[:].rearrange("p (h q) -> p h q", h=2)
                        h2t = hT2[tt % 2]; r_h2t = r_hT2[tt % 2]
                        S.group("pe", [MM(gv[:, 0, :], Wg[:, k, 128 * f:128 * (f + 1)], h2t[:, k, :], k == 0, k == 7) for k in range(8)],
                                reads=[r_Wgc[f], r_h2t], writes=[rgb2])
                        P.step(1)
                        S.group("pe", [MM(gv[:, 1, :], Wu[:, k, 128 * f:128 * (f + 1)], h2t[:, k, :], k == 0, k == 7) for k in range(8)],
                                reads=[r_Wuc[f], r_h2t], writes=[rgb2])
                        S.op("act", ACT(sg[f % 2][:], gv[:, 0, :], AF.Silu), reads=[rgb2], writes=[r_sg[f % 2]])
                        S.op("dve", TT(aT[:, f, :], gv[:, 1, :], sg[f % 2][:], ALU.mult), reads=[rgb2, r_sg[f % 2]], writes=[r_aT])
                        P.step(1)
                    for s in range(2):
                        row0 = tt * T2 + 128 * s
                        hr = hres[(2 * tt + s) % 3]; r_hr = r_hres[(2 * tt + s) % 3]
                        for hf in range(2):
                            bk = B[6 + hf]; rbk = rB[6 + hf]
                            S.group("pe", [MM(bk[:, :], aT[:, f, 128 * s:128 * (s + 1)], Wd[:, f, 512 * hf:512 * (hf + 1)], f == 0, f == NF - 1) for f in range(NF)],
                                    reads=[r_aT] + r_Wdc, writes=[rbk])
                            S.op("act", ACT(junkB[:], bk[:, :], AF.Square, accum_out=ssvB[:, hf:hf + 1]), reads=[rbk], writes=[r_junkB, r_ssvB])
                            P.step(6)
                        S.op("dve", TT(ssvB[:, 2:3], ssvB[:, 0:1], ssvB[:, 1:2], ALU.add), reads=[r_ssvB], writes=[r_ssvB])
                        S.op("act", ACT(ssvB[:, 2:3], ssvB[:, 2:3], AF.Ln, scale=1.0 / D, bias=eps2[:, 0:1]), reads=[r_ssvB, r_eps2], writes=[r_ssvB])
                        S.op("act", ACT(ssvB[:, 2:3], ssvB[:, 2:3], AF.Exp, scale=-0.5), reads=[r_ssvB], writes=[r_ssvB])
                        for hf in range(2):
                            bk = B[6 + hf]; rbk = rB[6 + hf]
                            S.op("dve", STT(tmpB[:], bk[:, :], ssvB[:, 2:3], gpffn_b[:, 512 * hf:512 * (hf + 1)], ALU.mult, ALU.mult),
                                 reads=[rbk, r_ssvB, r_gpffn], writes=[r_tmpB])
                            S.op("dve", TT(hr[:, 512 * hf:512 * (hf + 1)], tmpB[:], hr[:, 512 * hf:512 * (hf + 1)], ALU.add),
                                 reads=[r_tmpB, r_hr], writes=[r_hr])
                        final_events.append(S.dma("sp", out_d[row0:row0 + 128, :], hr[:], "fo%d" % ((2 * tt + s) % 3), reads=[r_hr]))
                        if s == 0:
                            P.release("BAR1")

                P0 = Pump(h1_gen(0)); P0.release("BAR1"); P0.step(1)
                while wprep:
                    dst_, src_ = wprep.pop(0)
                    wprep_events.append(S.dma("pool", dst_, src_, "wprep"))
                S.dma("act", Wgl[:], wgl_b.rearrange("(k p) c -> p k c", p=128), "w0", writes=[r_Wgl], extra=wprep_events + first_loads)
                S.dma("act", Wo[:], wo_b.rearrange("(k p) c -> p k c", p=128), "w1", writes=[r_Wo], extra=wprep_events + first_loads)
                r_Wgc = [Res("Wg%d" % f) for f in range(NF)]
                r_Wuc = [Res("Wu%d" % f) for f in range(NF)]
                r_Wdc = [Res("Wd%d" % f) for f in range(NF)]
                for f in range(0, NF, 2):
                    S.dma("act", Wg[:, :, 128 * f:128 * (f + 2)], wg_b[:, 128 * f:128 * (f + 2)].rearrange("(k p) c -> p k c", p=128),
                          "wg%d" % (f // 2), writes=[r_Wgc[f], r_Wgc[f + 1]], extra=wprep_events + first_loads)
                    S.dma("act", Wu[:, :, 128 * f:128 * (f + 2)], wu_b[:, 128 * f:128 * (f + 2)].rearrange("(k p) c -> p k c", p=128),
                          "wu%d" % (f // 2), writes=[r_Wuc[f], r_Wuc[f + 1]], extra=wprep_events + first_loads)
                for f in range(0, NF, 2):
                    S.dma("act", Wd[:, f:f + 2, :], wd_b[128 * f:128 * (f + 2), :].rearrange("(k p) c -> p k c", p=128),
                          "wd%d" % (f // 2), writes=[r_Wdc[f], r_Wdc[f + 1]], extra=wprep_events + first_loads)
                P0.flush()
                for tt in range(NT2):
                    P = Pump(h1_gen(tt + 1) if tt + 1 < NT2 else None)
                    h2(tt, P)
                    P.flush()

                S.wait_all("sp", final_events)
                if use_cc:
                    S.raw("pool", lambda e: e.wait_ge(cc_sem, 4))

        with nc.Block() as block:
            S.run(block)
    return nc


def _consts():
    invf8 = (500000.0 ** (-np.arange(0, 16, 2, dtype=np.float32) / np.float32(16))).astype(np.float32)
    invf = np.zeros((128, 1), np.float32)
    for base in (0, 64):
        for i in range(16):
            invf[base + i, 0] = invf8[i % 8]
    R = np.zeros((128, 128), np.float32)
    for base in (0, 64):
        for i in range(8):
            R[base + i, base + i + 8] = -1.0
            R[base + i + 8, base + i] = 1.0
    rmatT = np.ascontiguousarray(R.T)
    ident = np.eye(128, dtype=np.float32)
    swapm = np.zeros((128, 128), np.float32)
    for k in range(128):
        swapm[k, (k + 64) % 128] = 1.0
    sgn = np.ones((128, 1), np.float32); sgn[64:] = -1.0
    tri = (np.arange(128)[:, None] <= np.arange(128)[None, :]).astype(np.float32)
    gmask = (np.arange(128)[:, None] // 16 == np.arange(8)[None, :]).astype(np.float32)
    return dict(invf=invf, rmatT=rmatT, ident=ident, swapm=swapm, sgn=sgn, tri=tri, gmask=gmask,
                iota512=np.arange(512, dtype=np.float32), tile_iota=(512.0 * np.arange(NT1)).astype(np.float32))


def make_in_maps(inp):
    c = _consts()
    f = lambda a: np.ascontiguousarray(np.asarray(a, dtype=np.float32))
    w_in = f(inp["w_in"])[0]
    maps = []
    w_out = f(inp["w_out"])[0]
    perm = []
    for r in range(4):
        perm += list(range(128 * r, 128 * r + 128)) + list(range(512 + 128 * r, 512 + 128 * r + 128))
    w_out_p = np.ascontiguousarray(w_out[perm, :])
    for core in range(8):
        b, h = core // 4, core % 4
        cols = (list(range(64 * h, 64 * h + 64)) + list(range(256 + 64 * h, 256 + 64 * h + 64)) +
                list(range(512 + 64 * h, 512 + 64 * h + 64)) + list(range(768 + 64 * h, 768 + 64 * h + 64)) +
                list(range(1024 + 128 * h, 1024 + 128 * h + 128)) + list(range(1536 + 128 * h, 1536 + 128 * h + 128)))
        gs = slice(8 * h, 8 * h + 8)
        m = dict(c)
        m["x"] = f(inp["x"][b])
        m["meta"] = f(inp["meta"])
        m["win"] = np.ascontiguousarray(w_in[:, cols])
        m["gpre"] = np.ascontiguousarray(f(inp["pre_mix_g"])[0].reshape(8, 128).T)
        m["lamv"] = np.stack([f(inp["lambda_q1"])[0], f(inp["lambda_k1"])[0], f(inp["lambda_q2"])[0], f(inp["lambda_k2"])[0]])
        m["subg"] = f(inp["subln_g"])[0].reshape(128, 1)
        m["areT"] = np.ascontiguousarray(f(inp["a_re"])[0, gs].T)
        m["aimT"] = np.ascontiguousarray(f(inp["a_im"])[0, gs].T)
        m["logdt"] = f(inp["log_dt"])[0, gs]
        m["bre"] = f(inp["b_re"])[0, gs]
        m["bim"] = f(inp["b_im"])[0, gs]
        m["creT"] = np.ascontiguousarray(f(inp["c_re"])[0, gs].transpose(2, 0, 1).reshape(64, 128))
        m["cimT"] = np.ascontiguousarray(f(inp["c_im"])[0, gs].transpose(2, 0, 1).reshape(64, 128))
        m["dskip"] = f(inp["d_skip"])[0, 128 * h:128 * h + 128].reshape(128, 1)
        m["wglu"] = f(inp["w_glu"])[0]
        m["bglu"] = np.ascontiguousarray(f(inp["b_glu"])[0].reshape(4, 128).T)
        m["gssm"] = np.ascontiguousarray(f(inp["ssm_out_g"])[0].reshape(4, 128).T)
        m["wout"] = w_out_p
        m["gpost"] = f(inp["post_mix_g"])[0]
        m["gffn8"] = np.ascontiguousarray(f(inp["pre_ffn_g"])[0].reshape(8, 128).T)
        m["wgate"] = f(inp["w_gate"])[0]
        m["wup"] = f(inp["w_up"])[0]
        m["wdown"] = f(inp["w_down"])[0]
        m["gpffn"] = f(inp["post_ffn_g"])[0]
        m["xres"] = np.ascontiguousarray(np.concatenate([inp["x"][b, T2 * (4 * tt + h):T2 * (4 * tt + h + 1)] for tt in range(NT2)], axis=0), dtype=np.float32)
        maps.append(m)
    return maps


def kernel(**inputs):
    nc = build_program()
    maps = make_in_maps(inputs)
    res = run_bass_kernel_spmd(nc, maps, core_ids=list(range(8)))
    out = np.zeros((2, SEQ, D), np.float32)
    for core in range(8):
        b, h = core // 4, core % 4
        o = res.results[core]["out"]
        for tt in range(NT2):
            out[b, T2 * (4 * tt + h):T2 * (4 * tt + h + 1)] = o[T2 * tt:T2 * (tt + 1)]
    return out
```
